# Optimizing a Trainium2 kernel written in Bass

```python
import jax, jax.numpy as jnp
from jax import lax
import numpy as np

D_MODEL = 1024
BATCH = 8
SEQ = 2048
DEPTH = 2
DEC_BATCH = 16
DEC_SEQ = 16
PAST_LEN = 4096

CHUNK = 64
N_MIXERS = 2
N_LAYERS_A = (DEPTH + 1) // 2
N_LAYERS_B = DEPTH // 2
H_A = 8
DK_A = 128
DV_A = 256
QK_A = H_A * DK_A
V_A = H_A * DV_A
CONV_W = 4
CONV_CH = 2 * QK_A + V_A
IN_A = CONV_CH + V_A + 2 * H_A
HEAD_B = 64
H_B = D_MODEL // HEAD_B
W_B = H_B * HEAD_B
LORA_W = 64
LORA_A = 64
RMS_EPS = 1e-6
GN_EPS = 64e-5

kernel_name = 'hybrid_gdn_rwkv7_stream_step'


def rms_norm(x, w):
    xf = x.astype(jnp.float32)
    y = xf * lax.rsqrt(jnp.mean(xf * xf, axis=-1, keepdims=True) + RMS_EPS)
    return (y * w.astype(jnp.float32)).astype(x.dtype)


def l2_normalize(x):
    xf = x.astype(jnp.float32)
    return xf * lax.rsqrt(jnp.sum(xf * xf, axis=-1, keepdims=True) + 1e-6)


def causal_conv(u, hist, w):
    full = jnp.concatenate([hist.astype(u.dtype), u], axis=1)
    out = lax.conv_general_dilated(full, w[:, None, :].astype(u.dtype), window_strides=(1,),
                                   padding='VALID', dimension_numbers=('NWC', 'WIO', 'NWC'),
                                   feature_group_count=u.shape[-1])
    return out, full[:, -(CONV_W - 1):]


def gated_delta_rule(q, k, v, beta, g, s0):
    f32 = jnp.float32
    B, T, H, DK = q.shape
    DV = v.shape[-1]
    n = -(-T // CHUNK)
    pad = n * CHUNK - T

    def blocks(a):
        a = jnp.pad(a.astype(f32), [(0, 0), (0, pad)] + [(0, 0)] * (a.ndim - 2))
        a = a.reshape((B, n, CHUNK) + a.shape[2:])
        return jnp.swapaxes(a, 2, 3)

    q, k, v, beta, g = (blocks(a) for a in (q, k, v, beta, g))
    G = jnp.cumsum(g, axis=-1)
    pos = jnp.arange(CHUNK)
    incl = pos[:, None] >= pos[None, :]
    strict = pos[:, None] > pos[None, :]
    dmask = jnp.exp(jnp.where(incl, G[..., :, None] - G[..., None, :], -jnp.inf))
    kb = k * beta[..., None]
    A = jnp.where(strict, jnp.einsum('bnhik,bnhjk->bnhij', kb, k) * dmask, 0.0)
    tri = A + jnp.eye(CHUNK, dtype=f32)
    eG = jnp.exp(G)[..., None]
    rhs = jnp.concatenate([v * beta[..., None], kb * eG], axis=-1)
    sol = lax.linalg.triangular_solve(tri, rhs, left_side=True, lower=True, unit_diagonal=True)
    u0, wk = sol[..., :DV], sol[..., DV:]
    qk = jnp.einsum('bnhik,bnhjk->bnhij', q, k) * dmask
    qg = q * eG
    kt = k * jnp.exp(G[..., -1:] - G)[..., None]
    gl = jnp.exp(G[..., -1])[..., None, None]

    def step(S, xs):
        u0_c, wk_c, qk_c, qg_c, kt_c, gl_c = xs
        u = u0_c - jnp.einsum('bhck,bhkv->bhcv', wk_c, S)
        o = jnp.einsum('bhck,bhkv->bhcv', qg_c, S) + jnp.einsum('bhij,bhjv->bhiv', qk_c, u)
        S = S * gl_c + jnp.einsum('bhck,bhcv->bhkv', kt_c, u)
        return S, o

    xs = tuple(jnp.moveaxis(a, 1, 0) for a in (u0, wk, qk, qg, kt, gl))
    S, o = lax.scan(step, s0.astype(f32), xs)
    o = jnp.transpose(o, (1, 0, 3, 2, 4)).reshape(B, n * CHUNK, H, DV)[:, :T]
    return o, S


def mixer_a(h, conv_hist, s0, w_in, conv_w, a_log, dt_bias, norm_w, w_out):
    B, T, _ = h.shape
    f32 = jnp.float32
    proj = h @ w_in
    qkv = proj[..., :CONV_CH]
    z = proj[..., CONV_CH:CONV_CH + V_A]
    b_raw = proj[..., CONV_CH + V_A:CONV_CH + V_A + H_A]
    a_raw = proj[..., CONV_CH + V_A + H_A:]
    qkv, new_hist = causal_conv(qkv, conv_hist, conv_w)
    qkv = jax.nn.silu(qkv.astype(f32))
    q = l2_normalize(qkv[..., :QK_A].reshape(B, T, H_A, DK_A)) * (DK_A ** -0.5)
    k = l2_normalize(qkv[..., QK_A:2 * QK_A].reshape(B, T, H_A, DK_A))
    v = qkv[..., 2 * QK_A:].reshape(B, T, H_A, DV_A)
    beta = jax.nn.sigmoid(b_raw.astype(f32))
    g = -jnp.exp(a_log.astype(f32)) * jax.nn.softplus(a_raw.astype(f32) + dt_bias.astype(f32))
    o, s = gated_delta_rule(q, k, v, beta, g, s0)
    o = rms_norm(o, norm_w) * jax.nn.silu(z.astype(f32).reshape(B, T, H_A, DV_A))
    y = o.reshape(B, T, V_A).astype(h.dtype) @ w_out
    return y, new_hist, s.astype(s0.dtype)


def mixer_b(h, x_prev, s0, mu, w_in, w0, w_w1, w_w2, a0, a_w1, a_w2, k_k, k_a, r_k, gn_w, gn_b, w_out):
    B, T, D = h.shape
    f32 = jnp.float32
    xx = jnp.concatenate([x_prev[:, None].astype(h.dtype), h[:, :-1]], axis=1) - h
    xs = h[None] + xx[None] * mu[:, None, None, :]
    rkvz = jnp.einsum('gbtd,dge->gbte', xs[:4], w_in.reshape(D, 4, W_B)).astype(f32)
    r, k, v, z = rkvz[0], rkvz[1], rkvz[2], rkvz[3]
    w_log = -jax.nn.softplus(-(w0 + jnp.tanh(xs[4] @ w_w1) @ w_w2).astype(f32)) - 0.5
    decay = jnp.exp(-jnp.exp(w_log))
    a = jax.nn.sigmoid((a0 + (xs[5] @ a_w1) @ a_w2).astype(f32))
    hd = (B, T, H_B, HEAD_B)
    kk = l2_normalize((k * k_k.astype(f32)).reshape(hd))
    k = k * (1.0 + (a - 1.0) * k_a.astype(f32))
    r, k, v, a, decay = (t.reshape(hd) for t in (r, k, v, a, decay))

    def step(S, inp):
        r_t, w_t, k_t, v_t, a_t, b_t = inp
        sa = jnp.einsum('bhvk,bhk->bhv', S, a_t)
        S = S * w_t[:, :, None, :] + sa[..., None] * b_t[:, :, None, :] + v_t[..., None] * k_t[:, :, None, :]
        return S, jnp.einsum('bhvk,bhk->bhv', S, r_t)

    seq = tuple(jnp.swapaxes(t, 0, 1) for t in (r, decay, k, v, -kk, kk * a))
    S, o = lax.scan(step, s0.astype(f32), seq)
    o = jnp.swapaxes(o, 0, 1)
    mean = jnp.mean(o, axis=-1, keepdims=True)
    var = jnp.mean(jnp.square(o - mean), axis=-1, keepdims=True)
    o = (o - mean) * lax.rsqrt(var + GN_EPS) * gn_w.astype(f32).reshape(H_B, HEAD_B) + gn_b.astype(f32).reshape(H_B, HEAD_B)
    o = o + jnp.sum(r * k * r_k.astype(f32), axis=-1, keepdims=True) * v
    o = o.reshape(B, T, W_B) * jax.nn.silu(z)
    y = o.astype(h.dtype) @ w_out
    return y, h[:, -1], S.astype(s0.dtype)


def trunk(x, conv_a, delta_a, shift_b, wkv_b, norm_w, final_norm_w, a_w_in, a_conv_w, a_log, a_dt_bias,
          a_norm_w, a_w_out, b_mu, b_w_in, b_w0, b_w_w1, b_w_w2, b_a0, b_a_w1, b_a_w2, b_k_k, b_k_a,
          b_r_k, b_gn_w, b_gn_b, b_w_out):
    conv_out, delta_out, shift_out, wkv_out = [], [], [], []
    for i in range(DEPTH):
        j = i // N_MIXERS
        h = rms_norm(x, norm_w[i])
        if i % N_MIXERS == 0:
            y, c, s = mixer_a(h, conv_a[j], delta_a[j], a_w_in[j], a_conv_w[j], a_log[j], a_dt_bias[j],
                              a_norm_w[j], a_w_out[j])
            conv_out.append(c)
            delta_out.append(s)
        else:
            y, sh, s = mixer_b(h, shift_b[j], wkv_b[j], b_mu[j], b_w_in[j], b_w0[j], b_w_w1[j], b_w_w2[j],
                               b_a0[j], b_a_w1[j], b_a_w2[j], b_k_k[j], b_k_a[j], b_r_k[j], b_gn_w[j],
                               b_gn_b[j], b_w_out[j])
            shift_out.append(sh)
            wkv_out.append(s)
        x = x + y
    return (rms_norm(x, final_norm_w), jnp.stack(conv_out), jnp.stack(delta_out),
            jnp.stack(shift_out), jnp.stack(wkv_out))


def setup_inputs(seed: int = 0) -> dict:
    key = jax.random.key(seed)
    ks = iter(jax.random.split(key, 40))
    f32 = jnp.float32
    D = D_MODEL
    NA, NB = N_LAYERS_A, N_LAYERS_B

    def nrm(shape, s):
        return jax.random.normal(next(ks), shape, f32) * s

    def uni(shape, lo, hi):
        return jax.random.uniform(next(ks), shape, f32, minval=lo, maxval=hi)

    dt = jnp.exp(uni((NA, H_A), float(np.log(1e-3)), float(np.log(1e-1))))
    return {
        'x_prompt': nrm((BATCH, SEQ, D), 1.0),
        'x_sample': nrm((DEC_BATCH, DEC_SEQ, D), 1.0),
        'cache_conv_a': nrm((NA, DEC_BATCH, CONV_W - 1, CONV_CH), 1.0),
        'state_delta_a': nrm((NA, DEC_BATCH, H_A, DK_A, DV_A), 0.05),
        'state_shift_b': nrm((NB, DEC_BATCH, D), 1.0),
        'state_wkv_b': nrm((NB, DEC_BATCH, H_B, HEAD_B, HEAD_B), 0.05),
        'norm_w': 1.0 + nrm((DEPTH, D), 0.05),
        'final_norm_w': 1.0 + nrm((D,), 0.05),
        'a_w_in': nrm((NA, D, IN_A), D ** -0.5),
        'a_conv_w': nrm((NA, CONV_W, CONV_CH), CONV_W ** -0.5),
        'a_log': jnp.log(uni((NA, H_A), 1.0, 16.0)),
        'a_dt_bias': jnp.log(jnp.expm1(dt)),
        'a_norm_w': 1.0 + nrm((NA, DV_A), 0.05),
        'a_w_out': nrm((NA, V_A, D), 0.5 * V_A ** -0.5),
        'b_mu': uni((NB, 6, D), 0.0, 1.0),
        'b_w_in': nrm((NB, D, 4 * W_B), D ** -0.5),
        'b_w0': uni((NB, W_B), -6.0, -0.5),
        'b_w_w1': nrm((NB, D, LORA_W), D ** -0.5),
        'b_w_w2': nrm((NB, LORA_W, W_B), 0.1),
        'b_a0': nrm((NB, W_B), 0.1),
        'b_a_w1': nrm((NB, D, LORA_A), D ** -0.5),
        'b_a_w2': nrm((NB, LORA_A, W_B), 0.1),
        'b_k_k': 0.85 + nrm((NB, W_B), 0.05),
        'b_k_a': 1.0 + nrm((NB, W_B), 0.05),
        'b_r_k': nrm((NB, H_B, HEAD_B), 0.1),
        'b_gn_w': 1.0 + nrm((NB, W_B), 0.05),
        'b_gn_b': nrm((NB, W_B), 0.01),
        'b_w_out': nrm((NB, W_B, D), 0.5 * W_B ** -0.5),
    }


def reference(x_prompt, x_sample, cache_conv_a, state_delta_a, state_shift_b, state_wkv_b, norm_w,
              final_norm_w, a_w_in, a_conv_w, a_log, a_dt_bias, a_norm_w, a_w_out, b_mu, b_w_in, b_w0,
              b_w_w1, b_w_w2, b_a0, b_a_w1, b_a_w2, b_k_k, b_k_a, b_r_k, b_gn_w, b_gn_b, b_w_out):
    weights = (norm_w, final_norm_w, a_w_in, a_conv_w, a_log, a_dt_bias, a_norm_w, a_w_out, b_mu, b_w_in,
               b_w0, b_w_w1, b_w_w2, b_a0, b_a_w1, b_a_w2, b_k_k, b_k_a, b_r_k, b_gn_w, b_gn_b, b_w_out)
    bp, dt = x_prompt.shape[0], x_prompt.dtype
    zero_conv = jnp.zeros((N_LAYERS_A, bp, CONV_W - 1, CONV_CH), dt)
    zero_delta = jnp.zeros((N_LAYERS_A, bp, H_A, DK_A, DV_A), dt)
    zero_shift = jnp.zeros((N_LAYERS_B, bp, D_MODEL), dt)
    zero_wkv = jnp.zeros((N_LAYERS_B, bp, H_B, HEAD_B, HEAD_B), dt)
    y_prompt, p_conv_a, p_delta_a, p_shift_b, p_wkv_b = trunk(
        x_prompt, zero_conv, zero_delta, zero_shift, zero_wkv, *weights)
    y_sample, s_conv_a, s_delta_a, s_shift_b, s_wkv_b = trunk(
        x_sample, cache_conv_a, state_delta_a, state_shift_b, state_wkv_b, *weights)
    return (y_prompt, y_sample, p_conv_a, p_delta_a, p_shift_b, p_wkv_b, s_conv_a, s_delta_a, s_shift_b, s_wkv_b)
```

```python
import numpy as np
import concourse.bass as bass
import concourse.mybir as mybir
from concourse.bass_utils import run_bass_kernel_spmd

F32 = mybir.dt.float32
BF16 = mybir.dt.bfloat16
AF = mybir.ActivationFunctionType
ALU = mybir.AluOpType
AX = mybir.AxisListType

D = 1024
KC = 8
TP = 2048
TS = 16
NTILE = 17
H_A = 8
RMS_EPS = 1e-6
GN_EPS = 64e-5
NEG = -30000.0


class Trk:
    __slots__ = ("lw", "rd", "dsem", "dcount")

    def __init__(self):
        self.lw = None
        self.rd = []
        self.dsem = None
        self.dcount = 0


def fsz(ap):
    n = 1
    for d in ap.shape[1:]:
        n *= d
    return n


class Prog:
    WINDOW = 4000
    LAT = 120.0
    EPS = 150.0

    def __init__(self, nc):
        self.nc = nc
        self.h = {"pe": nc.tensor, "act": nc.scalar, "dve": nc.vector, "pool": nc.gpsimd, "sp": nc.sync}
        self.sem = {k: nc.alloc_semaphore("sem_" + k) for k in self.h}
        self.cnt = {k: 0 for k in self.h}
        self.known = {k: {} for k in self.h}
        self.trk = {}
        self.ops = []
        self.nps = 0
        self.npool = {}
        self.psb = []
        self.nsem = 0
        self.sb_off = nc.sbuf_base
        self.sb_top = nc.sbuf_top
        self.seg_start = 0
        self.sel = {}
        self.fill_before = {}
        self.filler = None

    def sb(self, name, shape, dt=F32):
        n = 1
        for d in shape[1:]:
            n *= d
        nbytes = n * (2 if dt == BF16 else 4)
        off = (self.sb_off + 63) // 64 * 64
        assert off + nbytes <= self.sb_top, "SBUF overflow at %s: need %d have %d" % (name, nbytes, self.sb_top - off)
        t = self.nc.alloc_sbuf_tensor_at(name, list(shape), dt, offset=off)
        self.sb_off = off + nbytes
        self.trk[t.name] = Trk()
        return t

    def barrier(self):
        self.ops.append(("fence", None, None, (), 0.0, False, None))
        for t in self._all_trk():
            t.lw = None
            t.rd = []

    def _all_trk(self):
        for t in self.trk.values():
            if isinstance(t, list):
                for x in t:
                    yield x
            else:
                yield t

    def init_psum(self):
        for i in range(8):
            t = self.nc.alloc_psum_tensor("psb%d" % i, [128, 512], F32)
            self.trk["psb%d" % i] = Trk()
            self.psb.append(t)

    POOLS = {0: (0, 1, 2, 3, 4, 5), 2: (0, 1, 2, 3, 4, 5),
             "a1a": (0, 1), "a1m": (2,), "a1b": (3, 4), "a2": (5,),
             "b1": (0, 1, 2, 3, 4), "b2": (5,)}
    FILL = 0.7
    FILL_NS = 110.0
    GAP_MIN = 200.0

    def ps(self, pool=0):
        banks = self.POOLS[pool]
        n = self.npool.get(pool, 0)
        self.npool[pool] = n + 1
        return self.psb[banks[n % len(banks)]]

    def split(self, tensor, n):
        self.trk[tensor.name] = [Trk() for _ in range(n)]

    def only(self, **sel):
        prog = self

        class _Ctx:
            def __enter__(self_):
                self_.old = dict(prog.sel)
                prog.sel.update(sel)

            def __exit__(self_, *a):
                prog.sel = self_.old
        return _Ctx()

    def _tks(self, ap):
        t = self.trk[ap.tensor.name]
        if isinstance(t, list):
            idx = self.sel.get(ap.tensor.name.rsplit("_", 1)[0])
            return [t[i] for i in idx] if idx is not None else list(t)
        return [t]

    def _record(self, kind, e, payload, reads, writes, cost, final=False):
        rt = []
        for a in reads:
            for t in self._tks(a):
                if t not in rt:
                    rt.append(t)
        wt = []
        for a in writes:
            for t in self._tks(a):
                if t not in wt:
                    wt.append(t)
        i = len(self.ops)
        preds = set()
        for t in rt:
            if t.lw is not None:
                preds.add(t.lw)
        for t in wt:
            if t.lw is not None:
                preds.add(t.lw)
            preds.update(t.rd)
        preds.discard(i)
        for t in rt:
            t.rd.append(i)
        for t in wt:
            t.lw = i
            t.rd = []
        t0 = (wt + rt)[0] if kind == "dma" else None
        self.ops.append((kind, e, payload, tuple(preds), float(cost), final, t0))
        return i

    def op(self, e, fn, reads, writes, cost=None):
        if cost is None:
            n = fsz(writes[0]) if writes else 64
            cost = {"act": 200.0 + 0.85 * n, "dve": 110.0 + 1.05 * n, "pool": 260.0 + 1.0 * n, "pe": 150.0}[e]
        return self._record("op", e, fn, reads, writes, cost)

    def dma(self, q, out, in_, reads=(), writes=(), final=False, **kw):
        return self._record("dma", q, (out, in_, kw), reads, writes, 150.0 if q == "sp" else 1200.0, final)

    def _wait(self, e, key, semh, val):
        k = self.known[e]
        if k.get(key, 0) < val:
            self.h[e].wait_ge(semh, val)
            k[key] = val

    def _schedule(self, lo, hi):
        ops = self.ops
        n = hi - lo
        indeg = [0] * n
        succ = [[] for _ in range(n)]
        for i in range(lo, hi):
            ps_ = [p for p in ops[i][3] if p >= lo]
            indeg[i - lo] = len(ps_)
            for p in ps_:
                succ[p - lo].append(i)
        blev = [0.0] * n
        for k in range(n - 1, -1, -1):
            o = ops[lo + k]
            c = o[4] + (2500.0 if o[0] == "dma" else 0.0)
            m = 0.0
            for j in succ[k]:
                v = blev[j - lo] + self.LAT
                if v > m:
                    m = v
            blev[k] = c + m
        dready = [0.0] * n
        etime = {k: 0.0 for k in self.h}
        ready = {k: [] for k in self.h}
        for i in range(lo, hi):
            if indeg[i - lo] == 0:
                ready[ops[i][1]].append(i)
        order = []
        done = [False] * n
        minp = lo
        W = self.WINDOW
        EPS = self.EPS
        while len(order) < n:
            while minp < hi and done[minp - lo]:
                minp += 1
            lim = minp + W
            best = None
            for e, lst in ready.items():
                if not lst:
                    continue
                te = etime[e]
                cand = None
                for i in lst:
                    if i >= lim:
                        continue
                    dr = dready[i - lo]
                    st = te if dr <= te + EPS else dr
                    key = (st, -blev[i - lo], i)
                    if cand is None or key < cand:
                        cand = key
                if cand is not None and (best is None or cand < best[0]):
                    best = (cand, e)
            (st, _, i), e = best
            st = max(st, dready[i - lo], etime[e])
            ready[e].remove(i)
            kind = ops[i][0]
            cost = ops[i][4]
            if e == "pe":
                gap = st - etime[e]
                if gap > self.GAP_MIN and self.FILL > 0:
                    self.fill_before[i] = int(self.FILL * gap / self.FILL_NS)
            etime[e] = st + cost
            f = st + cost + (2500.0 if kind == "dma" else 0.0)
            done[i - lo] = True
            order.append(i)
            for j in succ[i - lo]:
                v = f + self.LAT
                if v > dready[j - lo]:
                    dready[j - lo] = v
                indeg[j - lo] -= 1
                if indeg[j - lo] == 0:
                    ready[ops[j][1]].append(j)
        return order, max(etime.values())

    def finish(self):
        ops = self.ops
        bounds = [i for i, o in enumerate(ops) if o[0] == "fence"] + [len(ops)]
        info = {}
        finals = []
        lo = 0
        est_total = 0.0
        nfill = 0
        for b in bounds:
            order, est = self._schedule(lo, b)
            est_total += est
            for i in order:
                kind, e, payload, preds, cost, final, t0 = ops[i]
                for p in sorted(preds):
                    if p < lo:
                        continue
                    pi = info[p]
                    if pi[0] == "op":
                        f, c = pi[1], pi[2]
                        if f == "pe" and e == "pe":
                            continue
                        self._wait(e, f, self.sem[f], c)
                    else:
                        self._wait(e, pi[3], pi[1], pi[2])
                if kind == "op":
                    if e == "pe" and self.filler is not None:
                        for _ in range(self.fill_before.get(i, 0)):
                            self.filler(self.h["pe"])
                            nfill += 1
                    ins = payload(self.h[e])
                    self.cnt[e] += 1
                    ins.then_inc(self.sem[e], 1)
                    info[i] = ("op", e, self.cnt[e])
                else:
                    out, in_, kw = payload
                    if t0.dsem is None:
                        t0.dsem = self.nc.alloc_semaphore("dsem%d" % self.nsem)
                        self.nsem += 1
                    ins = self.h[e].dma_start(out=out, in_=in_, **kw)
                    t0.dcount += 16
                    ins.then_inc(t0.dsem, 16)
                    info[i] = ("dma", t0.dsem, t0.dcount, "d%d" % id(t0))
                    if final:
                        finals.append(info[i])
            lo = b + 1
            if b < len(ops):
                for e in self.h:
                    for f in self.h:
                        if f != e and self.cnt[f] > 0:
                            self._wait(e, f, self.sem[f], self.cnt[f])
                    for t in self._all_trk():
                        if t.dsem is not None and t.dcount > 0:
                            self._wait(e, "d%d" % id(t), t.dsem, t.dcount)
        for (_, semh, c, key) in finals:
            self._wait("sp", key, semh, c)
        print("scheduler estimate: %.1f us, %d ops, %d PE fillers" % (est_total / 1e3, len(ops), nfill))

    def mmr(self, out, lhsT, rhs, start=True, stop=True):
        return self.mm(out, R(lhsT), R(rhs), start=start, stop=stop)

    def mm(self, out, lhsT, rhs, start=True, stop=True):
        passes = 4.0 if rhs.dtype == F32 else 1.0
        cost = 70.0 + passes * 0.42 * (fsz(rhs) + min(fsz(lhsT), 128))
        return self.op("pe", lambda h: h.matmul(out, lhsT=lhsT, rhs=rhs, start=start, stop=stop),
                       [lhsT, rhs], [out], cost=cost)

    def tr(self, out, in_, ident):
        return self.op("pe", lambda h: h.transpose(out, in_, ident), [in_, ident], [out], cost=160.0)

    def act(self, out, in_, func, e="act", **kw):
        rd = [in_] + [v for v in kw.values() if hasattr(v, "tensor")]
        wr = [out]
        if "accum_out" in kw:
            wr.append(kw["accum_out"])
            rd.remove(kw["accum_out"])
        return self.op("act", lambda h: h.activation(out=out, in_=in_, func=func, **kw), rd, wr)

    def tt(self, out, in0, in1, op, e="dve"):
        return self.op(e, lambda h: h.tensor_tensor(out=out, in0=in0, in1=in1, op=op), [in0, in1], [out])

    def ts(self, out, in0, s1, op0, s2=None, op1=None, e="dve"):
        rd = [in0] + [v for v in (s1, s2) if hasattr(v, "tensor")]
        if op1 is None:
            return self.op(e, lambda h: h.tensor_scalar(out=out, in0=in0, scalar1=s1, scalar2=None, op0=op0),
                           rd, [out])
        return self.op(e, lambda h: h.tensor_scalar(out=out, in0=in0, scalar1=s1, scalar2=s2, op0=op0, op1=op1),
                       rd, [out])

    def stt(self, out, in0, scalar, in1, op0, op1):
        rd = [in0, in1] + ([scalar] if hasattr(scalar, "tensor") else [])
        return self.op("dve", lambda h: h.scalar_tensor_tensor(out=out, in0=in0, scalar=scalar, in1=in1,
                                                                 op0=op0, op1=op1), rd, [out])

    def cp(self, out, in_, e="dve"):
        if e == "act":
            return self.act(out, in_, AF.Copy)
        return self.op(e, lambda h: h.tensor_copy(out=out, in_=in_), [in_], [out])

    def scan(self, out, d0, d1):
        return self.op("dve", lambda h: h.tensor_tensor_scan(out=out, data0=d0, data1=d1, initial=0.0,
                                                              op0=ALU.mult, op1=ALU.add),
                       [d0, d1], [out], cost=110.0 + 2.1 * fsz(out))

    def rsqrt_pool(self, out, in_, mhalf):
        return self.op("pool", lambda h: h.tensor_tensor(out=out, in0=in_, in1=mhalf, op=ALU.pow), [in_, mhalf], [out])

    def recip(self, out, in_):
        return self.op("dve", lambda h: h.reciprocal(out=out, in_=in_), [in_], [out], cost=110.0 + 3.0 * fsz(out))

    def memset(self, ap, val, e="pool"):
        return self.op(e, lambda h: h.memset(ap, val), [], [ap])

    def asel(self, out, in_, pattern, cmp, fill, base, cm):
        return self.op("pool", lambda h: h.affine_select(out=out, in_=in_, pattern=pattern, compare_op=cmp,
                                                          fill=fill, base=base, channel_multiplier=cm),
                       [in_], [out])


F32R = mybir.dt.float32r


def R(ap):
    return ap.bitcast(F32R)


def neumann(P, ident, MT0, M0, Mbuf, MTT, nx, C, pool=0):
    L = {128: 7, 64: 6, 16: 4}[C]
    G = 512 // (2 * C)

    def grp3(ps, n, w):
        return ps[0:C, 0:n * w].rearrange("p (x i) -> p x i", x=n)

    psa = P.ps(pool)
    psb = P.ps(pool)
    for x in range(nx):
        P.mm(psa[0:C, x * C:(x + 1) * C], R(MT0[:, x, :]), R(Mbuf[0][0:C, x, 0:C]))
        P.mm(psb[0:C, x * C:(x + 1) * C], R(Mbuf[0][0:C, x, 0:C]), R(MT0[:, x, :]))
    P.cp(R(Mbuf[1][0:C, 0:nx, 0:C]), grp3(psa, nx, C), e="act")
    P.cp(R(MTT[0][0:C, 0:nx, 0, 0:C]), grp3(psb, nx, C), e="act")
    P.tt(R(MTT[0][0:C, 0:nx, 1, 0:C]), MT0, bcm(ident[0:C, 0:C], [C, nx, C]), ALU.add)
    yield
    cm, ct = 1, 0
    for lev in range(2, L + 1):
        last = lev == L
        Mc = Mbuf[cm]
        cur = MTT[ct]
        nxt = MTT[1 - ct]
        if not last:
            psa = P.ps(pool)
            for x in range(nx):
                P.mm(psa[0:C, x * C:(x + 1) * C], R(cur[0:C, x, 0, 0:C]), R(Mc[0:C, x, 0:C]))
        for x0 in range(0, nx, G):
            n = min(G, nx - x0)
            psx = P.ps(pool)
            for j in range(n):
                x = x0 + j
                if last:
                    P.mm(psx[0:C, j * C:(j + 1) * C], R(Mc[0:C, x, 0:C]), R(cur[0:C, x, 1, 0:C]))
                else:
                    P.mm(psx[0:C, j * 2 * C:(j + 1) * 2 * C], R(Mc[0:C, x, 0:C]), R(cur[0:C, x, :, 0:C]))
            if last:
                P.tt(R(nxt[0:C, x0:x0 + n, 1, 0:C]), grp3(psx, n, C), cur[0:C, x0:x0 + n, 1, 0:C], ALU.add)
            else:
                pv = psx[0:C, 0:n * 2 * C].rearrange("p (x a i) -> p x a i", x=n, a=2)
                P.cp(R(nxt[0:C, x0:x0 + n, 0, 0:C]), pv[:, :, 0, :], e="act")
                P.tt(R(nxt[0:C, x0:x0 + n, 1, 0:C]), pv[:, :, 1, :], cur[0:C, x0:x0 + n, 1, 0:C], ALU.add)
        if not last:
            P.cp(R(Mbuf[1 - cm][0:C, 0:nx, 0:C]), grp3(psa, nx, C), e="act")
        cm = 1 - cm
        ct = 1 - ct
        yield
    return MTT[ct][0:C, 0:nx, 1, 0:C]


def bc(ap, shape):
    return ap.unsqueeze(len(ap.shape)).broadcast_to(list(shape))


def bcm(ap, shape):
    return ap.unsqueeze(1).broadcast_to(list(shape))


PCOL = 1
S1COL = TP + 1 + 1
S2COL = S1COL + 32
NTOK = S2COL + 16 + 1
NBA = 256
CA = 128
NCH = TP // CA + 2


def build(stop=None):
    nc = bass.Bass("TRN2", target_bir_lowering=False)
    P = Prog(nc)
    P.init_psum()

    def din(name, shape):
        return nc.dram_tensor(name, list(shape), F32, kind="ExternalInput").ap()

    def dout(name, shape):
        return nc.dram_tensor(name, list(shape), F32, kind="ExternalOutput").ap()

    x_p = din("x_p", [TP, D])
    x_s = din("x_s", [2 * TS, D])
    conv_s = din("conv_s", [2, 3, 4096])
    delta_s = din("delta_s", [2, 8, 128, 256])
    shift_s = din("shift_s", [2, D])
    wkv_s = din("wkv_s", [2, 16, 64, 64])
    norm_w = din("norm_w", [2, D])
    final_norm_w = din("final_norm_w", [D])
    a_w_in = din("a_w_in", [D, 6160])
    a_conv_w = din("a_conv_w", [4, 4096])
    a_log = din("a_log", [8])
    a_dt_bias = din("a_dt_bias", [8])
    a_norm_w = din("a_norm_w", [256])
    a_w_out = din("a_w_out", [2048, D])
    b_mu = din("b_mu", [6, D])
    b_w_in = din("b_w_in", [D, 4096])
    b_w0 = din("b_w0", [D])
    b_w_w1 = din("b_w_w1", [D, 64])
    b_w_w2 = din("b_w_w2", [64, D])
    b_a0 = din("b_a0", [D])
    b_a_w1 = din("b_a_w1", [D, 64])
    b_a_w2 = din("b_a_w2", [64, D])
    b_k_k = din("b_k_k", [D])
    b_k_a = din("b_k_a", [D])
    b_r_k = din("b_r_k", [D])
    b_gn_w = din("b_gn_w", [D])
    b_gn_b = din("b_gn_b", [D])
    b_w_out = din("b_w_out", [D, D])

    y_p = dout("y_p", [TP, D])
    y_s = dout("y_s", [2 * TS, D])
    o_conv = dout("o_conv", [3, 3, 4096])
    o_delta = dout("o_delta", [3, 8, 128, 256])
    o_shift = dout("o_shift", [3, D])
    o_wkv = dout("o_wkv", [3, 16, 64, 64])
    dbg = dout("dbg", [NTILE * 128, D]) if stop else None

    ident = P.sb("ident", [128, 128])
    ones = P.sb("ones", [128, 128])
    mones = P.sb("mones", [128, 128])
    zeros = P.sb("zeros", [128, 128])
    Utri = P.sb("Utri", [128, 128])
    NEGT = P.sb("NEGT", [128, 128])
    MsT = P.sb("MsT", [128, 128])
    P.memset(ones[:], 1.0)
    ones_r = P.sb("ones_r", [128, 128])
    P.cp(R(ones_r[:]), ones[:], e="act")
    P.memset(mones[:], -1.0)
    P.memset(zeros[:], 0.0)
    fillz = P.sb("fillz", [128, 128], BF16)
    P.memset(fillz[:], 0.0)
    fill_out = P.psb[6][:, 0:128]
    P.filler = lambda h: h.matmul(fill_out, lhsT=fillz[:, :], rhs=fillz[:, :], start=True, stop=True)
    P.asel(ident[:], ones[:], [[-1, 128]], ALU.is_equal, 0.0, 0, 1)
    P.asel(Utri[:], ones[:, :], [[1, 128]], ALU.is_ge, 0.0, 0, -1)
    P.asel(NEGT[:], zeros[:, :], [[1, 128]], ALU.is_ge, NEG, 0, -1)
    P.asel(MsT[:], ones[:, :], [[1, 128]], ALU.is_gt, 0.0, 0, -1)

    xres = [P.sb("xres%d" % i, [128, D]) for i in range(NTILE)]
    nw = P.sb("nw", [128, 2, KC])
    fnw = P.sb("fnw", [128, KC])
    P.dma("sp", nw[:], norm_w.rearrange("l (k p) -> p l k", p=128), writes=[nw[:]], allow_slow_non_contiguous=True)
    P.dma("sp", fnw[:], final_norm_w.rearrange("(k p) -> p k", p=128), writes=[fnw[:]],
          allow_slow_non_contiguous=True)
    ssq = P.sb("ssq", [128, 4])
    rstd = P.sb("rstd", [128, 4])
    phase_mark = P.sb_off
    hT = P.sb("hT", [128, KC, NTOK], BF16)
    xn = [P.sb("xn0", [128, D])] * 2

    def tile_rows(i):
        return 128 if i < 16 else 48

    def tile_col(i):
        return PCOL + i * 128 if i < 16 else S1COL

    def norm_to_hT(layer, hT):
        for i in range(NTILE):
            nt = tile_rows(i)
            xt = xres[i]
            xb = xn[i % 2]
            sl = slice(i % 4, i % 4 + 1)
            P.act(xb[0:nt, :], xt[0:nt, :], AF.Square, accum_out=ssq[0:nt, sl])
            P.act(rstd[0:nt, sl], ssq[0:nt, sl], AF.Sqrt, scale=1.0 / D, bias=RMS_EPS)
            P.recip(rstd[0:nt, sl], rstd[0:nt, sl])
            P.ts(xb[0:nt, :], xt[0:nt, :], rstd[0:nt, sl], ALU.mult)
            c0 = tile_col(i)
            for half in range(2):
                ps = P.ps()
                for j in range(4):
                    kc = half * 4 + j
                    P.tr(ps[:, j * 128:j * 128 + nt], xb[0:nt, kc * 128:(kc + 1) * 128], ident[0:nt, 0:nt])
                pv = ps[:, :].rearrange("p (j t) -> p j t", j=4)[:, :, 0:nt]
                P.tt(hT[:, half * 4:half * 4 + 4, c0:c0 + nt], pv,
                     bc(nw[:, layer, half * 4:half * 4 + 4], [128, 4, nt]), ALU.mult)

    for i in range(NTILE):
        if i < 16:
            P.dma("sp", xres[i][:], x_p[i * 128:(i + 1) * 128, :], writes=[xres[i][:]])
        else:
            P.memset(xres[i][:], 0.0)
            P.dma("sp", xres[i][0:16, :], x_s[0:16, :], writes=[xres[i][:]])
            P.dma("sp", xres[i][32:48, :], x_s[16:32, :], writes=[xres[i][:]])
    P.memset(hT[:, :, 0:1], 0.0)
    norm_to_hT(0, hT)

    blocks = [(0, PCOL + i * NBA, NBA, CA, NBA // CA, i * (NBA // CA)) for i in range(TP // NBA)] + \
             [(1, S1COL, 16, 16, 1, NCH - 2), (2, S2COL, 16, 16, 1, NCH - 1)]

    cwt = P.sb("cwt", [32, 4, 128])
    cw = P.sb("cw", [128, 4, 32])
    P.dma("sp", cwt[:], a_conv_w.rearrange("t (g c) -> g t c", c=128), writes=[cwt[:]])
    ps = P.ps()
    for t in range(4):
        P.tr(ps[:, t * 32:(t + 1) * 32], cwt[:, t, :], ident[0:32, 0:32])
    P.cp(cw[:].rearrange("p t g -> p (t g)"), ps[:, 0:128])
    halo_all = P.sb("halo_all", [128, 2, 3, 32])
    hrow = P.sb("hrow", [96, 2, 128])
    for s in range(2):
        P.dma("sp", hrow[:, s, :], conv_s[s].rearrange("t (g c) -> (t g) c", c=128), writes=[hrow[:]])
    ps = P.ps()
    for s in range(2):
        P.tr(ps[:, s * 96:(s + 1) * 96], hrow[:, s, :], ident[0:96, 0:96])
    P.cp(halo_all[:].rearrange("p s t g -> p (s t g)"), ps[:, 0:192])
    fin_all = P.sb("fin_all", [128, 3, 3, 32])
    anw = P.sb("anw", [128, 2])
    P.dma("sp", anw[:], a_norm_w.rearrange("(h p) -> p h", p=128), writes=[anw[:]], allow_slow_non_contiguous=True)
    P.ts(anw[:], anw[:], 0.5, ALU.mult)

    wba = P.sb("wba", [128, KC, 16], BF16)
    P.dma("pool", wba[:], a_w_in.rearrange("(k p) c -> p k c", p=128)[:, :, 6144:6160], writes=[wba[:]])
    NCP = TP // CA
    BA = P.sb("BA", [CA, NCH, 16])
    P.memset(BA[:], 0.0)
    ps = P.ps()
    for c in range(NCP):
        for kc in range(KC):
            P.mm(ps[0:CA, c * 16:(c + 1) * 16], hT[:, kc, PCOL + c * CA:PCOL + (c + 1) * CA], wba[:, kc, :],
                 start=(kc == 0), stop=(kc == KC - 1))
    P.cp(BA[:, 0:NCP, :].rearrange("p c k -> p (c k)"), ps[0:CA, 0:NCP * 16])
    ps = P.ps()
    for s, sc in enumerate((S1COL, S2COL)):
        for kc in range(KC):
            P.mm(ps[0:16, s * 16:(s + 1) * 16], hT[:, kc, sc:sc + 16], wba[:, kc, :],
                 start=(kc == 0), stop=(kc == KC - 1))
    P.cp(BA[0:16, NCP:NCP + 2, :].rearrange("p c k -> p (c k)"), ps[0:16, 0:32])
    alg = P.sb("alg", [CA, 8])
    dtb = P.sb("dtb", [CA, 8])
    P.dma("sp", alg[:], a_log.partition_broadcast(CA), writes=[alg[:]])
    P.dma("sp", dtb[:], a_dt_bias.partition_broadcast(CA), writes=[dtb[:]])
    P.act(alg[:], alg[:], AF.Exp)
    P.ts(alg[:], alg[:], -1.0, ALU.mult)
    beta = P.sb("beta", [CA, NCH, 8])
    gg = P.sb("gg", [CA, NCH, 8])
    Gc = P.sb("Gc", [CA, NCH, 8])
    Glb = P.sb("Glb", [128, NCH, 8])
    gl = P.sb("gl", [128, NCH, 8])
    eG = P.sb("eG", [CA, NCH, 8])
    bG = P.sb("bG", [CA, NCH, 8])
    dte = P.sb("dte", [CA, NCH, 8])
    P.act(beta[:], BA[:, :, 0:8], AF.Sigmoid)
    P.tt(gg[:], BA[:, :, 8:16], bcm(dtb[:], [CA, NCH, 8]), ALU.add)
    P.act(gg[:], gg[:], AF.Exp)
    P.act(gg[:], gg[:], AF.Ln, bias=1.0)
    P.tt(gg[:], gg[:], bcm(alg[:], [CA, NCH, 8]), ALU.mult)
    psG = P.ps()
    psL = P.ps()
    g2 = gg[:].rearrange("p c k -> p (c k)")
    GP = NCP * 8
    P.mm(psG[0:CA, 0:GP], Utri[0:CA, 0:CA], g2[:, 0:GP])
    P.mm(psG[0:16, GP:GP + 16], Utri[0:16, 0:16], g2[0:16, GP:GP + 16])
    P.mm(psL[:, 0:GP], ones[0:CA, :], g2[:, 0:GP])
    P.mm(psL[:, GP:GP + 16], ones[0:16, :], g2[0:16, GP:GP + 16])
    P.memset(Gc[:], 0.0)
    P.cp(Gc[:, 0:NCP, :].rearrange("p c k -> p (c k)"), psG[0:CA, 0:GP])
    P.cp(Gc[0:16, NCP:NCP + 2, :].rearrange("p c k -> p (c k)"), psG[0:16, GP:GP + 16])
    P.cp(Glb[:].rearrange("p c k -> p (c k)"), psL[:, 0:GP + 16])
    P.act(gl[:], Glb[:], AF.Exp)
    P.act(eG[:], Gc[:], AF.Exp)
    P.tt(bG[:], beta[:], eG[:], ALU.mult)
    hbeta = eG
    P.ts(hbeta[:], beta[:], 0.5, ALU.mult)
    P.tt(dte[:], Glb[0:CA], Gc[:], ALU.subtract)
    P.act(dte[:], dte[:], AF.Exp)

    Wh = [P.sb("Wh0", [128, KC, 768], BF16)] * 2
    Wo = [P.sb("Wo0", [128, 2, D], BF16)] * 2
    w_in_v = a_w_in.rearrange("(k p) c -> p k c", p=128)

    def load_head_w(h):
        sl = h % 2
        for (c0, n, o) in ((h * 128, 128, 0), (1024 + h * 128, 128, 128), (2048 + h * 256, 256, 256),
                           (4096 + h * 256, 256, 512)):
            P.dma("pool", Wh[sl][:, :, o:o + n], w_in_v[:, :, c0:c0 + n], writes=[Wh[sl][:]])

    def load_head_wo(h):
        sl = h % 2
        P.dma("pool", Wo[sl][:], a_w_out[h * 256:(h + 1) * 256, :].rearrange("(hh p) c -> p hh c", p=128),
              writes=[Wo[sl][:]])

    NB_ = NBA
    NC_ = NBA // CA
    pre = P.sb("pre", [128, 4, NB_ + 3])
    acc = P.sb("acc", [128, 4, NB_])
    P.split(acc, 4)
    P.split(pre, 4)
    qkv = P.sb("qkv", [128, 4, NB_])
    zs2 = [P.sb("zs%d" % i, [128, 2, NB_]) for i in range(2)]
    sqr = P.sb("sqr", [128, 2, NB_])
    sq = sqr
    rq = acc[:, 2:4]
    oT = P.sb("oTp", [128, 2, NB_])
    osq = P.sb("osq2", [128, 2, NB_])
    qT = P.sb("qT", [128, NB_])
    kT = P.sb("kT", [128, NB_])
    kbT = P.sb("kbT", [128, NB_])
    qgT2 = [P.sb("qgT%d" % i, [128, NB_]) for i in range(2)]
    dg = P.sb("dg", [128, 2, NB_])
    eGb = P.sb("eGb", [128, NB_])
    betab = P.sb("betab", [128, NB_])
    kbeG2 = [P.sb("kbeG%d" % i, [CA, NC_, 128]) for i in range(2)]
    ktk2 = [P.sb("ktk%d" % i, [CA, NC_, 128]) for i in range(2)]
    vb2 = [P.sb("vb%d" % i, [CA, NC_, 256]) for i in range(2)]
    gU = P.sb("gU", [CA, NC_, CA])
    DT = P.sb("DT", [CA, NC_, CA])
    DTs = P.sb("DTs", [CA, NC_, CA])
    qkT2 = [P.sb("qkT%d" % i, [CA, NC_, CA]) for i in range(2)]
    Mn = [P.sb("Mn%d" % i, [CA, NC_, CA]) for i in range(2)]
    MT0a2 = [P.sb("MT0a%d" % i, [CA, NC_, CA]) for i in range(2)]
    MTTa = [P.sb("MTTa%d" % i, [CA, NC_, 2, CA]) for i in range(2)]
    u0 = P.sb("u0", [CA, NC_, 256])
    wkT = P.sb("wkT", [128, NB_])
    uu = [P.sb("uu%d" % i, [CA, 256]) for i in range(2)]
    Sp_ = [P.sb("Sp%d" % i, [128, 256]) for i in range(2)]
    Ss_ = [P.sb("Ss%d" % i, [128, 256]) for i in range(2)]
    Sst = [Sp_, Ss_, Ss_]
    ors = P.sb("ors", [128, NB_])
    og = P.sb("og", [128, 2, NB_], BF16)
    print("SBUF left after layer-A alloc:", P.sb_top - P.sb_off)

    DONE = object()

    def A_s1(it, h, blk):
        (seq, t0, NB, C, nch, c0) = blk
        par = it % 2
        W = Wh[0]
        ggrp = (h, 8 + h, 16 + 2 * h, 17 + 2 * h)
        zs, qgT, ktk, qkT = zs2[par], qgT2[par], ktk2[par], qkT2[par]
        kbeG, vb, MT0a = kbeG2[par], vb2[par], MT0a2[par]
        first_of_seq = (seq == 0 and t0 == PCOL) or seq > 0
        if seq == 0 and t0 == PCOL:
            load_head_w(h)
        if first_of_seq:
            if seq == 0:
                P.memset(pre[:, :, 0:3], 0.0)
            else:
                for gi, g in enumerate(ggrp):
                    with P.only(pre=[gi]):
                        P.cp(pre[:, gi, 0:3], halo_all[:, seq - 1, :, g], e="pool")
        for m in range(6):
            ps = P.ps("a1a")
            for kc in range(KC):
                P.mm(ps[:, 0:NB], W[:, kc, m * 128:(m + 1) * 128], hT[:, kc, t0:t0 + NB],
                     start=(kc == 0), stop=(kc == KC - 1))
            if m < 4:
                with P.only(pre=[m]):
                    P.cp(pre[:, m, 3:3 + NB], ps[:, 0:NB], e="act")
            else:
                P.act(zs[:, m - 4, 0:NB], ps[:, 0:NB], AF.Tanh, scale=0.5)
                P.stt(zs[:, m - 4, 0:NB], zs[:, m - 4, 0:NB], 1.0, ps[:, 0:NB], ALU.add, ALU.mult)
            yield
        for gi, g in enumerate(ggrp):
            with P.only(acc=[gi], pre=[gi]):
                P.act(acc[:, gi, 0:NB], pre[:, gi, 3:3 + NB], AF.Copy, scale=cw[:, 3, g:g + 1])
                for tap in (2, 1, 0):
                    P.stt(acc[:, gi, 0:NB], pre[:, gi, tap:tap + NB], cw[:, tap, g:g + 1], acc[:, gi, 0:NB],
                          ALU.mult, ALU.add)
            yield
        P.act(qkv[:, :, 0:NB], acc[:, :, 0:NB], AF.Tanh, scale=0.5)
        P.stt(qkv[:, :, 0:NB], qkv[:, :, 0:NB], 1.0, acc[:, :, 0:NB], ALU.add, ALU.mult)
        last = (t0 + NB == PCOL + TP) or seq > 0
        for gi, g in enumerate(ggrp):
            with P.only(pre=[gi]):
                if last:
                    P.cp(fin_all[:, seq, :, g], pre[:, gi, NB:NB + 3], e="pool")
                else:
                    P.cp(pre[:, gi, 0:3], pre[:, gi, NB:NB + 3], e="pool")
        yield
        P.act(R(sq[:, :, 0:NB]), qkv[:, 0:2, 0:NB], AF.Square)
        for j in range(2):
            ps = P.ps("a1m")
            P.mmr(ps[:, 0:NB], ones_r[:, :], sq[:, j, 0:NB])
            with P.only(acc=[2 + j]):
                if j == 0:
                    P.act(rq[:, j, 0:NB], ps[:, 0:NB], AF.Ln, scale=128.0, bias=512.0 * 1e-6)
                else:
                    P.act(rq[:, j, 0:NB], ps[:, 0:NB], AF.Ln, bias=4e-6)
        yield
        with P.only(acc=[2, 3]):
            P.act(rq[:, :, 0:NB], rq[:, :, 0:NB], AF.Exp, scale=-0.5)
        with P.only(acc=[2]):
            P.tt(R(qT[:, 0:NB]), qkv[:, 0, 0:NB], rq[:, 0, 0:NB], ALU.mult)
        with P.only(acc=[3]):
            P.tt(R(kT[:, 0:NB]), qkv[:, 1, 0:NB], rq[:, 1, 0:NB], ALU.mult)
        yield
        cs = slice(c0, c0 + nch)
        idb = bcm(ident[0:C, 0:C], [C, nch, C])
        dgv = dg[0:C, :, 0:NB].rearrange("p a (c i) -> p a c i", c=nch)
        P.tt(dgv[:, 0], idb, bc(Gc[0:C, cs, h], [C, nch, C]), ALU.mult)
        P.tt(dgv[:, 1], idb, bc(beta[0:C, cs, h], [C, nch, C]), ALU.mult)
        ps = P.ps("a1m")
        P.mm(ps[:, 0:NB], ones[0:C, :], dg[0:C, 0, 0:NB])
        P.act(eGb[:, 0:NB], ps[:, 0:NB], AF.Exp)
        ps = P.ps("a1m")
        P.mm(ps[:, 0:NB], ones[0:C, :], dg[0:C, 1, 0:NB])
        P.cp(betab[:, 0:NB], ps[:, 0:NB], e="act")
        yield
        P.tt(R(kbT[:, 0:NB]), kT[:, 0:NB], betab[:, 0:NB], ALU.mult)
        P.tt(R(qgT[:, 0:NB]), qT[:, 0:NB], eGb[:, 0:NB], ALU.mult)
        yield
        for c4 in range(0, nch, 4):
            n4 = min(4, nch - c4)
            ps = P.ps("a1m")
            for j in range(n4):
                c = c4 + j
                P.tr(ps[0:C, j * 128:(j + 1) * 128], kT[:, c * C:(c + 1) * C], ident[:, :])
            pv = ps[0:C, 0:n4 * 128].rearrange("p (j d) -> p j d", j=n4)
            P.tt(R(kbeG[0:C, c4:c4 + n4, :]), pv, bc(bG[0:C, c0 + c4:c0 + c4 + n4, h], [C, n4, 128]), ALU.mult)
            P.tt(R(ktk[0:C, c4:c4 + n4, :]), pv, bc(dte[0:C, c0 + c4:c0 + c4 + n4, h], [C, n4, 128]), ALU.mult)
            yield
        for c2 in range(0, nch, 2):
            n2 = min(2, nch - c2)
            ps = P.ps("a1m")
            for j in range(n2):
                c = c2 + j
                for half in range(2):
                    P.tr(ps[0:C, j * 256 + half * 128:j * 256 + (half + 1) * 128],
                         qkv[:, 2 + half, c * C:(c + 1) * C], ident[:, :])
            pv = ps[0:C, 0:n2 * 256].rearrange("p (j d) -> p j d", j=n2)
            P.tt(R(vb[0:C, c2:c2 + n2, :]), pv, bc(hbeta[0:C, c0 + c2:c0 + c2 + n2, h], [C, n2, 256]), ALU.mult)
            yield
        P.tt(gU[0:C, 0:nch, 0:C], bcm(Utri[0:C, 0:C], [C, nch, C]), bc(gg[0:C, cs, h], [C, nch, C]), ALU.mult)
        ps = P.ps("a1m")
        for c in range(nch):
            o = ps[0:C, c * C:(c + 1) * C]
            P.mm(o, ones[0:C, 0:C], gU[0:C, c, 0:C], start=True, stop=False)
            P.mm(o, gU[0:C, c, 0:C], mones[0:C, 0:C], start=False, stop=False)
            P.mm(o, ident[0:C, 0:C], NEGT[0:C, 0:C], start=False, stop=True)
        pv = ps[0:C, 0:nch * C].rearrange("p (c i) -> p c i", c=nch)
        P.act(DT[0:C, 0:nch, 0:C], pv, AF.Exp)
        P.tt(DTs[0:C, 0:nch, 0:C], DT[0:C, 0:nch, 0:C], bcm(MsT[0:C, 0:C], [C, nch, C]), ALU.mult, e="pool")
        yield
        ps = P.ps("a1m")
        for c in range(nch):
            P.mmr(ps[0:C, c * C:(c + 1) * C], kT[:, c * C:(c + 1) * C], kbT[:, c * C:(c + 1) * C])
        for c in range(nch):
            P.mmr(ps[0:C, 256 + c * C:256 + (c + 1) * C], kT[:, c * C:(c + 1) * C], qT[:, c * C:(c + 1) * C])
        pv = ps[0:C, 0:nch * C].rearrange("p (c i) -> p c i", c=nch)
        pv2 = ps[0:C, 256:256 + nch * C].rearrange("p (c i) -> p c i", c=nch)
        P.stt(R(MT0a[0:C, 0:nch, 0:C]), pv, -1.0, DTs[0:C, 0:nch, 0:C], ALU.mult, ALU.mult)
        P.tt(R(qkT[0:C, 0:nch, 0:C]), pv2, DT[0:C, 0:nch, 0:C], ALU.mult)
        yield
        ps = P.ps("a1m")
        for c in range(nch):
            P.tr(ps[0:C, c * C:(c + 1) * C], MT0a[0:C, c, 0:C], ident[0:C, 0:C])
        P.cp(R(Mn[0][0:C, 0:nch, 0:C]), ps[0:C, 0:nch * C].rearrange("p (c i) -> p c i", c=nch), e="act")
        yield
        TTf = yield from neumann(P, ident, MT0a[0:C, 0:nch, 0:C], None, Mn, MTTa, nch, C, pool="a1b")
        yield "DRAIN2"
        for c2 in range(0, nch, 2):
            n2 = min(2, nch - c2)
            ps = P.ps("a1b")
            for j in range(n2):
                P.mmr(ps[0:C, j * 256:(j + 1) * 256], TTf[:, c2 + j, :], vb[0:C, c2 + j, :])
            P.cp(u0[0:C, c2:c2 + n2, :], ps[0:C, 0:n2 * 256].rearrange("p (j d) -> p j d", j=n2), e="act")
        ps = P.ps("a1b")
        for c in range(nch):
            P.mmr(ps[:, c * C:(c + 1) * C], kbeG[0:C, c, :], TTf[:, c, :])
        P.cp(R(wkT[:, 0:NB]), ps[:, 0:NB], e="act")
        yield

    spar = [0, 0, 0]

    def A_s2(it, h, blk):
        (seq, t0, NB, C, nch, c0) = blk
        par = it % 2
        WO = Wo[0]
        zs, qgT, ktk, qkT = zs2[par], qgT2[par], ktk2[par], qkT2[par]
        first_of_seq = (seq == 0 and t0 == PCOL) or seq > 0
        if seq == 0 and t0 == PCOL:
            load_head_wo(h)
        if first_of_seq:
            spar[seq] = 0
            if seq == 0:
                P.cp(R(Sst[0][0][:]), zeros[:, 0:1].broadcast_to([128, 256]), e="act")
            else:
                P.dma("sp", Sst[seq][1][:], delta_s[seq - 1, h], writes=[Sst[seq][1][:]])
                P.cp(R(Sst[seq][0][:]), Sst[seq][1][:], e="act")
        pso = P.psb[7]
        for c in range(nch):
            Sc = Sst[seq][spar[seq]]
            Sn = Sst[seq][1 - spar[seq]]
            u = uu[c % 2]
            ps = P.ps("a2")
            P.mmr(ps[0:C, 0:256], wkT[:, c * C:(c + 1) * C], Sc[:, :])
            P.tt(R(u[0:C, :]), u0[0:C, c, :], ps[0:C, 0:256], ALU.subtract)
            yield
            ps2 = P.ps("a2")
            P.mmr(ps2[:, 0:256], ktk[0:C, c, :], u[0:C, :])
            P.stt(R(Sn[:, :]), Sc[:, :], gl[:, c0 + c, h:h + 1], ps2[:, 0:256], ALU.mult, ALU.add)
            for half in range(2):
                oo = pso[:, (half * nch + c) * C:(half * nch + c + 1) * C]
                P.mmr(oo, Sc[:, half * 128:(half + 1) * 128], qgT[:, c * C:(c + 1) * C], start=True, stop=False)
                P.mmr(oo, u[0:C, half * 128:(half + 1) * 128], qkT[0:C, c, 0:C], start=False, stop=True)
            spar[seq] = 1 - spar[seq]
            yield
        P.cp(oT[:, :, 0:NB], pso[:, 0:2 * NB].rearrange("p (a t) -> p a t", a=2), e="act")
        P.act(R(osq[:, :, 0:NB]), oT[:, :, 0:NB], AF.Square)
        ps = P.ps("a2")
        P.mmr(ps[:, 0:NB], ones_r[:, :], osq[:, 0, 0:NB], start=True, stop=False)
        P.mmr(ps[:, 0:NB], ones_r[:, :], osq[:, 1, 0:NB], start=False, stop=True)
        P.act(ors[:, 0:NB], ps[:, 0:NB], AF.Ln, scale=1.0 / 256.0, bias=RMS_EPS)
        yield
        P.act(ors[:, 0:NB], ors[:, 0:NB], AF.Exp, scale=-0.5)
        for half in range(2):
            P.stt(oT[:, half, 0:NB], oT[:, half, 0:NB], anw[:, half:half + 1], ors[:, 0:NB], ALU.mult, ALU.mult)
        P.tt(og[:, :, 0:NB], oT[:, :, 0:NB], zs[:, :, 0:NB], ALU.mult)
        yield
        for tt0 in range(0, NB, 128):
            nt = min(128, NB - tt0)
            if seq == 0:
                tile_i, prow = (t0 - PCOL + tt0) // 128, 0
            else:
                tile_i, prow = 16, (seq - 1) * 32
            for nh in range(2):
                ps = P.ps("a2")
                for half in range(2):
                    P.mm(ps[prow:prow + nt, :], og[:, half, tt0:tt0 + nt], WO[:, half, nh * 512:(nh + 1) * 512],
                         start=(half == 0), stop=(half == 1))
                xr = xres[tile_i][prow:prow + nt, nh * 512:(nh + 1) * 512]
                P.tt(xr, xr, ps[prow:prow + nt, :], ALU.add)
            yield
        last = (t0 + NB == PCOL + TP) or seq > 0
        if last:
            P.dma("sp", o_delta[seq, h], Sst[seq][spar[seq]][:], reads=[Sst[seq][spar[seq]][:]], final=True)

    def pipeline(items, s1, s2, ratio):
        g2 = None
        for it, item in enumerate(list(items) + [None]):
            g1 = s1(it, *item) if item is not None else None
            while g1 is not None or g2 is not None:
                if g2 is not None:
                    if next(g2, DONE) is DONE:
                        g2 = None
                if g1 is not None:
                    for _ in range(ratio if g2 is not None else 1000000):
                        r = next(g1, DONE)
                        if r is DONE:
                            g1 = None
                            break
                        if r == "DRAIN2":
                            while g2 is not None:
                                if next(g2, DONE) is DONE:
                                    g2 = None
            g2 = s2(it, *item) if item is not None else None

    pipeline([(h, blk) for h in range(H_A) for blk in blocks], A_s1, A_s2, 3)

    ps = P.ps()
    for s in range(3):
        P.tr(ps[0:96, s * 128:(s + 1) * 128], fin_all[:, s].rearrange("p t g -> p (t g)"), ident[:, :])
    for s in range(3):
        P.cp(acc[0:96, s, 0:128], ps[0:96, s * 128:(s + 1) * 128])
        P.dma("sp", o_conv[s].rearrange("t (g c) -> (t g) c", c=128), acc[0:96, s, 0:128], reads=[acc[:]], final=True)

    if stop == "A":
        for i in range(NTILE):
            nt = tile_rows(i)
            P.dma("sp", dbg[i * 128:i * 128 + nt, :], xres[i][0:nt, :], reads=[xres[i][:]], final=True)
        P.finish()
        return nc

    P.barrier()
    P.sb_off = phase_mark
    hT = P.sb("hT2", [128, KC, NTOK], BF16)
    shout = P.sb("shout", [128, 3, KC])
    P.memset(hT[:, :, 0:1], 0.0)
    mark_b0 = P.sb_off
    xn = [P.sb("xnB", [128, D])] * 2

    def norm_to_hT_B():
        for i in range(NTILE):
            nt = tile_rows(i)
            xt = xres[i]
            xb = xn[i % 2]
            sl = slice(i % 4, i % 4 + 1)
            P.act(xb[0:nt, :], xt[0:nt, :], AF.Square, accum_out=ssq[0:nt, sl])
            P.act(rstd[0:nt, sl], ssq[0:nt, sl], AF.Sqrt, scale=1.0 / D, bias=RMS_EPS)
            P.recip(rstd[0:nt, sl], rstd[0:nt, sl])
            P.ts(xb[0:nt, :], xt[0:nt, :], rstd[0:nt, sl], ALU.mult)
            c0 = tile_col(i)
            for half in range(2):
                ps = P.ps(2)
                for j in range(4):
                    kc = half * 4 + j
                    P.tr(ps[:, j * 128:j * 128 + nt], xb[0:nt, kc * 128:(kc + 1) * 128], ident[0:nt, 0:nt])
                pv4 = ps[:, :].rearrange("p (j t) -> p j t", j=4)
                pv = pv4[:, :, 0:nt]
                P.tt(hT[:, half * 4:half * 4 + 4, c0:c0 + nt], pv,
                     bc(nw[:, 1, half * 4:half * 4 + 4], [128, 4, nt]), ALU.mult)
                lastcols = {15: [(0, 127)], 16: [(1, 15), (2, 47)]}.get(i, [])
                for (sq_, col) in lastcols:
                    P.tt(shout[:, sq_, half * 4:half * 4 + 4], pv4[:, :, col], nw[:, 1, half * 4:half * 4 + 4], ALU.mult)

    norm_to_hT_B()
    P.barrier()
    P.sb_off = mark_b0
    for s_ in range(3):
        P.dma("sp", o_shift[s_].rearrange("(k p) -> p k", p=128), shout[:, s_, :], reads=[shout[:]], final=True,
              allow_slow_non_contiguous=True)
    shin = P.sb("shin", [128, 2, KC])
    P.dma("sp", shin[:], shift_s.rearrange("s (k p) -> p s k", p=128), writes=[shin[:]], allow_slow_non_contiguous=True)
    P.cp(hT[:, :, S1COL - 1], shin[:, 0, :])
    P.cp(hT[:, :, S2COL - 1], shin[:, 1, :])

    vecs = P.sb("vecs", [128, 13, KC])
    P.dma("sp", vecs[:, 0:6, :], b_mu.rearrange("g (k p) -> p g k", p=128), writes=[vecs[:]], allow_slow_non_contiguous=True)
    for vi, v_ in enumerate((b_w0, b_a0, b_k_k, b_k_a, b_r_k, b_gn_w, b_gn_b)):
        P.dma("sp", vecs[:, 6 + vi, :], v_.rearrange("(k p) -> p k", p=128), writes=[vecs[:]], allow_slow_non_contiguous=True)
    V_W0, V_A0, V_KK, V_KA, V_RK, V_GW, V_GB = range(6, 13)
    hvec = P.sb("hvec", [128, 2, KC])
    P.ts(hvec[:], vecs[:, 6:8, :], 0.5, ALU.mult)
    blk1 = P.sb("blk1", [128, 128])
    cst32 = P.sb("cst32", [128, 192])
    P.asel(cst32[:, 0:64], ones[:, 0:64], [[0, 64]], ALU.is_ge, 0.0, 63, -1)
    P.asel(cst32[:, 64:128], ones[:, 0:64], [[0, 64]], ALU.is_ge, 0.0, -64, 1)
    P.cp(R(blk1[:]), cst32[:, 0:128])
    CB = 128
    MXT = P.sb("MXT", [CB, 2 * CB])
    P.cp(MXT[:, 0:CB], MsT[0:CB, 0:CB], e="pool")
    P.cp(MXT[:, CB:2 * CB], Utri[0:CB, 0:CB], e="pool")
    MsL = P.sb("MsL", [CB, CB])
    Sh = P.sb("Sh", [128, 64])
    P.asel(cst32[:, 128:192], ones[:, 0:64], [[-1, 64]], ALU.is_equal, 0.0, -64, 1)
    P.cp(R(Sh[:]), cst32[:, 128:192])
    P.asel(MsL[:], ones[0:CB, 0:CB], [[-1, CB]], ALU.is_gt, 0.0, 0, 1)
    rmask = P.sb("rmask", [128, 256])
    P.memset(rmask[:], 1.0)
    for c in range(256 // CB):
        P.memset(rmask[:, c * CB:c * CB + 1], 0.0)

    NBB = 256
    t1T = P.sb("t1T", [64, NTOK], BF16)
    a1T = P.sb("a1T", [64, NTOK], BF16)
    w2b = P.sb("w2b", [64, D], BF16)
    a2b = P.sb("a2b", [64, D], BF16)
    blk_mark = P.sb_off
    lw1 = P.sb("lw1", [128, KC, 2, 64], BF16)
    lw1p = P.sb("lw1p", [128, KC, 2, 64], BF16)
    lw1pp = P.sb("lw1pp", [128, KC, 2, 64], BF16)
    P.dma("pool", lw1[:, :, 0, :], b_w_w1.rearrange("(k p) c -> p k c", p=128), writes=[lw1[:]])
    P.dma("pool", lw1[:, :, 1, :], b_a_w1.rearrange("(k p) c -> p k c", p=128), writes=[lw1[:]])
    for j in range(2):
        P.tt(lw1p[:, :, j, :], lw1[:, :, j, :], bc(vecs[:, 4 + j, :], [128, KC, 64]), ALU.mult)
    P.tt(lw1pp[:], lw1[:], lw1p[:], ALU.subtract)
    P.dma("pool", w2b[:], b_w_w2[:, :], writes=[w2b[:]])
    P.dma("pool", a2b[:], b_a_w2[:, :], writes=[a2b[:]])
    col_ranges = [(PCOL + i * 512, 512) for i in range(4)] + [(S1COL, 16), (S2COL, 16)]
    for (cc0, n) in col_ranges:
        for j, dst in enumerate((t1T, a1T)):
            ps = P.ps(2)
            for kc in range(KC):
                P.mm(ps[0:64, 0:n], lw1pp[:, kc, j, :], hT[:, kc, cc0:cc0 + n], start=(kc == 0), stop=False)
                P.mm(ps[0:64, 0:n], lw1p[:, kc, j, :], hT[:, kc, cc0 - 1:cc0 - 1 + n], start=False, stop=(kc == KC - 1))
            P.act(dst[:, cc0:cc0 + n], ps[0:64, 0:n], AF.Tanh if j == 0 else AF.Copy)

    P.barrier()
    P.sb_off = blk_mark
    Wp = [P.sb("Wp0", [128, KC, 4, 128], BF16)] * 2
    Wq = P.sb("Wq", [128, KC, 4, 128], BF16)
    Wob = [P.sb("Wob0", [128, D], BF16)] * 2
    b_in_v = b_w_in.rearrange("(k p) (g c) -> p k g c", p=128, g=4)

    def load_pair_w(pr):
        for g_ in range(4):
            P.dma("pool", Wp[pr % 2][:, :, g_, :], b_in_v[:, :, g_, pr * 128:(pr + 1) * 128], writes=[Wp[pr % 2][:]])
        P.dma("pool", Wob[pr % 2][:], b_w_out[pr * 128:(pr + 1) * 128, :], writes=[Wob[pr % 2][:]])

    NCB = NBB // CB
    NX = 2 * NCB
    rkvT = P.sb("rkvT", [128, 3, NBB])
    zsB2 = [P.sb("zsB%d" % i, [128, NBB]) for i in range(2)]
    s2tmp = P.sb("s2tmp", [128, 2, NBB])
    lwT = P.sb("lwT", [128, NBB])
    aT = P.sb("aT", [128, NBB])
    cwv = P.sb("cwv", [128, NBB])
    eW = P.sb("eW", [128, 3, NBB])
    tmpB = P.sb("tmpB", [128, 4, NBB])
    kkT = P.sb("kkT", [128, NBB])
    k2T = P.sb("k2T", [128, NBB])
    arT2 = [P.sb("arT%d" % i, [128, 2, NBB]) for i in range(2)]
    bkT = P.sb("bkT", [128, 2, NBB])
    bkh = P.sb("bkh", [128, 2, NBB])
    Wc = P.sb("Wc", [128, NCB])
    rkb = P.sb("rkb", [128, NBB])
    vt = P.sb("vt", [CB, NCB, 128])
    bht = P.sb("bht", [CB, NCB, 128])
    kht = P.sb("kht", [CB, NCB, 128])
    XA = P.sb("XA", [CB, NX, 2 * CB])
    XB = P.sb("XB", [CB, NX, 2 * CB])
    MnB = [P.sb("MnB%d" % i, [CB, NX, CB]) for i in range(2)]
    MTTb = [P.sb("MTTb%d" % i, [CB, NX, 2, CB]) for i in range(2)]
    Rsb = [P.sb("Rsb%d" % i, [CB, 128]) for i in range(2)]
    Usb = [P.sb("Usb%d" % i, [CB, 128]) for i in range(2)]
    StP = [[P.sb("StP%d_%d" % (i, hd), [64, 64]) for hd in range(2)] for i in range(2)]
    StS = [[P.sb("StS%d_%d" % (i, hd), [64, 64]) for hd in range(2)] for i in range(2)]
    StB = [StP, StS, StS]
    stio = P.sb("stio", [64, 128])
    ar12 = [P.sb("ar1_%d" % i, [64, 2, NBB]) for i in range(2)]
    bk1 = P.sb("bk1", [64, 2, NBB])
    Wc1 = P.sb("Wc1", [64, NCB])
    sqB = P.sb("sqB", [128, 4, NBB])
    oTB = sqB[:, 2]
    ocB = s2tmp[:, 0]
    osB = sqB[:, 3]
    ogB = P.sb("ogB", [128, NBB], BF16)
    print("SBUF left after layer-B alloc:", P.sb_top - P.sb_off)
    blocksB = [(0, PCOL + i * NBB, NBB, CB, NBB // CB) for i in range(TP // NBB)] + \
              [(1, S1COL, 16, 16, 1), (2, S2COL, 16, 16, 1)]
    ENH = -float(np.exp(-0.5))

    load_pair_w(0)
    nblkB = 0
    for pr in range(8):
        W = Wp[pr % 2]
        WO = Wob[pr % 2]
        for g_ in range(4):
            P.tt(Wq[:, :, g_, :], W[:, :, g_, :], bc(vecs[:, g_, :], [128, KC, 128]), ALU.mult)
        P.tt(W[:], W[:], Wq[:], ALU.subtract)
        Wr = W
        spar = [0, 0, 0]
        cur_seq = -1
        for (seq, t0, NB, C, nch) in blocksB:
            nx = 2 * nch
            bpar = nblkB % 2
            nblkB += 1
            zsB, arT, ar1 = zsB2[bpar], arT2[bpar], ar12[bpar]
            if seq != cur_seq:
                cur_seq = seq
                spar[seq] = 0
                if seq == 0:
                    for hd in range(2):
                        P.cp(R(StB[0][0][hd][:]), zeros[0:64, 0:64], e="act")
                else:
                    P.dma("sp", stio[:].rearrange("v (h k) -> v h k", h=2),
                          wkv_s[seq - 1, 2 * pr:2 * pr + 2].rearrange("h v k -> v h k"), writes=[stio[:]])
                    for hd in range(2):
                        ps = P.ps("b1")
                        P.tr(ps[0:64, 0:64], stio[:, hd * 64:(hd + 1) * 64], ident[0:64, 0:64])
                        P.cp(R(StB[seq][0][hd][:]), ps[0:64, 0:64])
            for g_ in range(4):
                ps = P.ps("b1")
                for kc in range(KC):
                    P.mm(ps[:, 0:NB], Wr[:, kc, g_, :], hT[:, kc, t0:t0 + NB], start=(kc == 0), stop=False)
                    P.mm(ps[:, 0:NB], Wq[:, kc, g_, :], hT[:, kc, t0 - 1:t0 - 1 + NB], start=False, stop=(kc == KC - 1))
                if g_ < 3:
                    P.cp(rkvT[:, g_, 0:NB], ps[:, 0:NB], e="act")
                else:
                    P.act(zsB[:, 0:NB], ps[:, 0:NB], AF.Tanh, scale=0.5)
                    P.stt(zsB[:, 0:NB], zsB[:, 0:NB], 1.0, ps[:, 0:NB], ALU.add, ALU.mult)
            ps = P.ps("b1")
            P.mm(ps[:, 0:NB], w2b[:, pr * 128:(pr + 1) * 128], t1T[:, t0:t0 + NB])
            P.act(lwT[:, 0:NB], ps[:, 0:NB], AF.Tanh, scale=0.5, bias=hvec[:, 0, pr:pr + 1])
            P.ts(lwT[:, 0:NB], lwT[:, 0:NB], 0.5 * ENH, ALU.mult, 0.5 * ENH, ALU.add)
            ps = P.ps("b1")
            P.mm(ps[:, 0:NB], a2b[:, pr * 128:(pr + 1) * 128], a1T[:, t0:t0 + NB])
            P.act(aT[:, 0:NB], ps[:, 0:NB], AF.Tanh, scale=0.5, bias=hvec[:, 1, pr:pr + 1])
            P.ts(aT[:, 0:NB], aT[:, 0:NB], 0.5, ALU.mult, 0.5, ALU.add)
            rT = rkvT[:, 0, 0:NB]
            kT_ = rkvT[:, 1, 0:NB]
            vT_ = rkvT[:, 2, 0:NB]
            P.ts(kkT[:, 0:NB], kT_, vecs[:, V_KK, pr:pr + 1], ALU.mult)
            P.act(R(sqB[:, 0, 0:NB]), kkT[:, 0:NB], AF.Square)
            ps = P.ps("b1")
            P.mmr(ps[:, 0:NB], blk1[:, :], sqB[:, 0, 0:NB])
            P.act(tmpB[:, 1, 0:NB], ps[:, 0:NB], AF.Ln, bias=1e-6)
            P.act(tmpB[:, 1, 0:NB], tmpB[:, 1, 0:NB], AF.Exp, scale=-0.5)
            P.tt(kkT[:, 0:NB], kkT[:, 0:NB], tmpB[:, 1, 0:NB], ALU.mult)
            P.ts(tmpB[:, 2, 0:NB], aT[:, 0:NB], -1.0, ALU.add, vecs[:, V_KA, pr:pr + 1], ALU.mult)
            P.ts(tmpB[:, 2, 0:NB], tmpB[:, 2, 0:NB], 1.0, ALU.add)
            P.tt(k2T[:, 0:NB], kT_, tmpB[:, 2, 0:NB], ALU.mult)
            P.scan(cwv[:, 0:NB], rmask[:, 0:NB], lwT[:, 0:NB])
            P.act(eW[:, 0, 0:NB], cwv[:, 0:NB], AF.Exp)
            P.act(eW[:, 1, 0:NB], cwv[:, 0:NB], AF.Exp, scale=-1.0)
            P.tt(tmpB[:, 3, 0:NB], cwv[:, 0:NB], lwT[:, 0:NB], ALU.subtract)
            P.act(eW[:, 2, 0:NB], tmpB[:, 3, 0:NB], AF.Exp)
            P.stt(R(arT[:, 0, 0:NB]), kkT[:, 0:NB], -1.0, eW[:, 2, 0:NB], ALU.mult, ALU.mult)
            P.tt(R(arT[:, 1, 0:NB]), rT, eW[:, 0, 0:NB], ALU.mult)
            P.tt(tmpB[:, 0, 0:NB], kkT[:, 0:NB], aT[:, 0:NB], ALU.mult)
            P.tt(R(bkT[:, 0, 0:NB]), tmpB[:, 0, 0:NB], eW[:, 1, 0:NB], ALU.mult)
            P.tt(R(bkT[:, 1, 0:NB]), k2T[:, 0:NB], eW[:, 1, 0:NB], ALU.mult)
            ewc = eW[:, 0, 0:NB].rearrange("p (c i) -> p c i", c=nch)[:, :, C - 1]
            P.cp(Wc[:, 0:nch], ewc)
            bkv = bkT[:, :, 0:NB].rearrange("p a (c i) -> p a c i", c=nch)
            bhv = bkh[:, :, 0:NB].rearrange("p a (c i) -> p a c i", c=nch)
            for a_ in range(2):
                P.tt(bhv[:, a_], bkv[:, a_], bc(Wc[:, 0:nch], [128, nch, C]), ALU.mult)
            P.stt(R(sqB[:, 1, 0:NB]), rT, vecs[:, V_RK, pr:pr + 1], k2T[:, 0:NB], ALU.mult, ALU.mult)
            ps = P.ps("b1")
            P.mmr(ps[:, 0:NB], blk1[:, :], sqB[:, 1, 0:NB])
            P.tt(rkb[:, 0:NB], ps[:, 0:NB], vT_, ALU.mult)
            ps = P.ps("b1")
            P.mmr(ps[0:64, 0:2 * NB], Sh[:, :], arT[:, :, 0:NB])
            P.cp(R(ar1[:, :, 0:NB]), ps[0:64, 0:2 * NB].rearrange("p (a t) -> p a t", a=2), e="act")
            ps = P.ps("b1")
            P.mmr(ps[0:64, 0:2 * NB], Sh[:, :], bkT[:, :, 0:NB])
            P.cp(R(bk1[:, :, 0:NB]), ps[0:64, 0:2 * NB].rearrange("p (a t) -> p a t", a=2), e="act")
            ps = P.ps("b1")
            P.mm(ps[0:64, 0:nch], cst32[:, 128:192], Wc[:, 0:nch])
            P.cp(Wc1[:, 0:nch], ps[0:64, 0:nch])
            AR = [arT[0:64], ar1[:]]
            BK = [bkT[0:64], bk1[:]]
            WC = [Wc[0:64], Wc1[:]]
            for (src, dst) in ((vT_, vt), (bkh[:, 0, 0:NB], bht), (bkh[:, 1, 0:NB], kht)):
                ps = P.ps("b1")
                for c in range(nch):
                    P.tr(ps[0:C, c * 128:(c + 1) * 128], src[:, c * C:(c + 1) * C], ident[:, :])
                P.cp(R(dst[0:C, 0:nch, :]), ps[0:C, 0:nch * 128].rearrange("p (c d) -> p c d", c=nch), e="act")
            psN = P.ps("b1")
            for hd in range(2):
                hs = slice(hd * 64, (hd + 1) * 64)
                psA = P.ps("b1")
                psB_ = P.ps("b1")
                for c in range(nch):
                    csl = slice(c * C, (c + 1) * C)
                    x_ = hd * nch + c
                    P.mmr(psA[0:C, c * 2 * C:(c + 1) * 2 * C], BK[hd][:, 0, csl], AR[hd][:, :, csl])
                    P.mmr(psB_[0:C, c * 2 * C:(c + 1) * 2 * C], BK[hd][:, 1, csl], AR[hd][:, :, csl])
                    P.mmr(psN[0:C, x_ * C:(x_ + 1) * C], AR[hd][:, 0, csl], BK[hd][:, 0, csl])
                for (psx, dstx) in ((psA, XA), (psB_, XB)):
                    pv = psx[0:C, 0:nch * 2 * C].rearrange("p (c a i) -> p c a i", c=nch, a=2)
                    dv = dstx[0:C, hd * nch:(hd + 1) * nch, :].rearrange("p c (a i) -> p c a i", a=2)[:, :, :, 0:C]
                    mv = MXT[0:C, :].rearrange("p (a i) -> p a i", a=2)[:, :, 0:C].unsqueeze(1).broadcast_to([C, nch, 2, C])
                    P.tt(R(dv), pv, mv, ALU.mult)
            P.tt(R(MnB[0][0:C, 0:nx, 0:C]), psN[0:C, 0:nx * C].rearrange("p (x i) -> p x i", x=nx),
                 bcm(MsL[0:C, 0:C], [C, nx, C]), ALU.mult)
            gen_ = neumann(P, ident, XA[0:C, 0:nx, 0:C], None, MnB, MTTb, nx, C, pool="b1")
            while True:
                try:
                    next(gen_)
                except StopIteration as e_:
                    TTf = e_.value
                    break
            pso = P.psb[7]
            for c in range(nch):
                csl = slice(c * C, (c + 1) * C)
                Sc = StB[seq][spar[seq]]
                Sn = StB[seq][1 - spar[seq]]
                Rb = Rsb[c % 2]
                Ub = Usb[c % 2]
                ps = P.ps("b2")
                for hd in range(2):
                    hs = slice(hd * 64, (hd + 1) * 64)
                    x_ = hd * nch + c
                    P.mmr(ps[0:C, hs], AR[hd][:, 0, csl], Sc[hd][:, :], start=True, stop=False)
                    P.mmr(ps[0:C, hs], XB[0:C, x_, 0:C], vt[0:C, c, hs], start=False, stop=True)
                P.cp(R(Rb[0:C, :]), ps[0:C, 0:128], e="act")
                ps = P.ps("b2")
                for hd in range(2):
                    hs = slice(hd * 64, (hd + 1) * 64)
                    x_ = hd * nch + c
                    P.mmr(ps[0:C, hs], TTf[:, x_, :], Rb[0:C, hs])
                P.cp(R(Ub[0:C, :]), ps[0:C, 0:128], e="act")
                for hd in range(2):
                    hs = slice(hd * 64, (hd + 1) * 64)
                    x_ = hd * nch + c
                    oo = pso[hs, csl]
                    mmf = P.mmr if hd == 0 else P.mm
                    mmf(oo, Sc[hd][:, :], AR[hd][:, 1, csl], start=True, stop=False)
                    mmf(oo, Ub[0:C, hs], XA[0:C, x_, CB:CB + C], start=False, stop=False)
                    mmf(oo, vt[0:C, c, hs], XB[0:C, x_, CB:CB + C], start=False, stop=True)
                ps2 = P.ps("b2")
                for hd in range(2):
                    hs = slice(hd * 64, (hd + 1) * 64)
                    P.mmr(ps2[0:64, hs], bht[0:C, c, hs], Ub[0:C, hs], start=True, stop=False)
                    P.mmr(ps2[0:64, hs], kht[0:C, c, hs], vt[0:C, c, hs], start=False, stop=True)
                for hd in range(2):
                    hs = slice(hd * 64, (hd + 1) * 64)
                    P.stt(R(Sn[hd][:, :]), Sc[hd][:, :], WC[hd][:, c:c + 1], ps2[0:64, hs], ALU.mult, ALU.add)
                spar[seq] = 1 - spar[seq]
            P.cp(R(oTB[:, 0:NB]), pso[:, 0:NB], e="act")
            ps = P.ps("b2")
            P.mmr(ps[:, 0:NB], blk1[:, :], oTB[:, 0:NB])
            P.stt(ocB[:, 0:NB], ps[:, 0:NB], -1.0 / 64.0, oTB[:, 0:NB], ALU.mult, ALU.add)
            P.act(R(osB[:, 0:NB]), ocB[:, 0:NB], AF.Square)
            ps = P.ps("b2")
            P.mmr(ps[:, 0:NB], blk1[:, :], osB[:, 0:NB])
            P.act(s2tmp[:, 1, 0:NB], ps[:, 0:NB], AF.Ln, scale=1.0 / 64.0, bias=GN_EPS)
            P.act(s2tmp[:, 1, 0:NB], s2tmp[:, 1, 0:NB], AF.Exp, scale=-0.5)
            P.tt(ocB[:, 0:NB], ocB[:, 0:NB], s2tmp[:, 1, 0:NB], ALU.mult)
            P.ts(ocB[:, 0:NB], ocB[:, 0:NB], vecs[:, V_GW, pr:pr + 1], ALU.mult, vecs[:, V_GB, pr:pr + 1], ALU.add)
            P.tt(ocB[:, 0:NB], ocB[:, 0:NB], rkb[:, 0:NB], ALU.add)
            P.stt(ogB[:, 0:NB], ocB[:, 0:NB], 0.5, zsB[:, 0:NB], ALU.mult, ALU.mult)
            for tt0 in range(0, NB, 128):
                nt = min(128, NB - tt0)
                if seq == 0:
                    tile_i, prow = (t0 - PCOL + tt0) // 128, 0
                else:
                    tile_i, prow = 16, (seq - 1) * 32
                for nh in range(2):
                    ps = P.ps("b2")
                    P.mm(ps[prow:prow + nt, :], ogB[:, tt0:tt0 + nt], WO[:, nh * 512:(nh + 1) * 512])
                    xr = xres[tile_i][prow:prow + nt, nh * 512:(nh + 1) * 512]
                    P.tt(xr, xr, ps[prow:prow + nt, :], ALU.add)
            last = (t0 + NB == PCOL + TP) or seq > 0
            if last:
                Sf = StB[seq][spar[seq]]
                ps = P.ps("b2")
                for hd in range(2):
                    P.tr(ps[0:64, hd * 64:(hd + 1) * 64], Sf[hd][:, :], ident[0:64, 0:64])
                P.cp(stio[:, :], ps[0:64, 0:128])
                P.dma("sp", o_wkv[seq, 2 * pr:2 * pr + 2].rearrange("h v k -> v h k"),
                      stio[:].rearrange("v (h k) -> v h k", h=2), reads=[stio[:]], final=True)
        if pr + 1 < 8:
            load_pair_w(pr + 1)

    P.barrier()
    P.sb_off = blk_mark
    xn = [P.sb("xnF", [128, D])] * 2
    fnwb = P.sb("fnwb", [128, D])
    P.dma("sp", fnwb[:], final_norm_w.partition_broadcast(128), writes=[fnwb[:]])
    for i in range(NTILE):
        nt = tile_rows(i)
        xt = xres[i]
        xb = xn[0]
        sl = slice(i % 4, i % 4 + 1)
        P.act(xb[0:nt, :], xt[0:nt, :], AF.Square, accum_out=ssq[0:nt, sl])
        P.act(rstd[0:nt, sl], ssq[0:nt, sl], AF.Sqrt, scale=1.0 / D, bias=RMS_EPS)
        P.recip(rstd[0:nt, sl], rstd[0:nt, sl])
        P.stt(xb[0:nt, :], xt[0:nt, :], rstd[0:nt, sl], fnwb[0:nt, :], ALU.mult, ALU.mult)
        if i < 16:
            P.dma("sp", y_p[i * 128:(i + 1) * 128, :], xb[:, :], reads=[xb[:]], final=True)
        else:
            P.dma("sp", y_s[0:16, :], xb[0:16, :], reads=[xb[:]], final=True)
            P.dma("sp", y_s[16:32, :], xb[32:48, :], reads=[xb[:]], final=True)
    P.finish()
    return nc


_NC_CACHE = {}


def make_in_maps(inputs):
    g = lambda k: np.ascontiguousarray(np.asarray(inputs[k], dtype=np.float32))
    xp, xs = g("x_prompt"), g("x_sample")
    cc, sd, ss, sw = g("cache_conv_a"), g("state_delta_a"), g("state_shift_b"), g("state_wkv_b")
    shared = {
        "norm_w": g("norm_w"), "final_norm_w": g("final_norm_w"), "a_w_in": g("a_w_in")[0],
        "a_conv_w": g("a_conv_w")[0], "a_log": g("a_log")[0], "a_dt_bias": g("a_dt_bias")[0],
        "a_norm_w": g("a_norm_w")[0], "a_w_out": g("a_w_out")[0], "b_mu": g("b_mu")[0],
        "b_w_in": g("b_w_in")[0], "b_w0": g("b_w0")[0], "b_w_w1": g("b_w_w1")[0], "b_w_w2": g("b_w_w2")[0],
        "b_a0": g("b_a0")[0], "b_a_w1": g("b_a_w1")[0], "b_a_w2": g("b_a_w2")[0], "b_k_k": g("b_k_k")[0],
        "b_k_a": g("b_k_a")[0], "b_r_k": g("b_r_k")[0].reshape(-1), "b_gn_w": g("b_gn_w")[0],
        "b_gn_b": g("b_gn_b")[0], "b_w_out": g("b_w_out")[0],
    }
    maps = []
    for i in range(8):
        m = dict(shared)
        m["x_p"] = xp[i]
        m["x_s"] = np.ascontiguousarray(xs[2 * i:2 * i + 2].reshape(2 * TS, D))
        m["conv_s"] = np.ascontiguousarray(cc[0, 2 * i:2 * i + 2])
        m["delta_s"] = np.ascontiguousarray(sd[0, 2 * i:2 * i + 2])
        m["shift_s"] = np.ascontiguousarray(ss[0, 2 * i:2 * i + 2])
        m["wkv_s"] = np.ascontiguousarray(sw[0, 2 * i:2 * i + 2])
        maps.append(m)
    return maps


def kernel(**inputs):
    if "nc" not in _NC_CACHE:
        _NC_CACHE["nc"] = build()
    nc = _NC_CACHE["nc"]
    maps = make_in_maps(inputs)
    res = run_bass_kernel_spmd(nc, maps, core_ids=list(range(8)))
    R = res.results
    y_prompt = np.stack([R[i]["y_p"] for i in range(8)], 0)
    y_sample = np.concatenate([R[i]["y_s"].reshape(2, TS, D) for i in range(8)], 0)

    def pick(name, sl):
        return np.stack([R[i][name][sl] for i in range(8)], 0)[None] if isinstance(sl, int) else \
            np.concatenate([R[i][name][sl] for i in range(8)], 0)[None]

    p_conv, s_conv = pick("o_conv", 0), pick("o_conv", slice(1, 3))
    p_delta, s_delta = pick("o_delta", 0), pick("o_delta", slice(1, 3))
    p_shift, s_shift = pick("o_shift", 0), pick("o_shift", slice(1, 3))
    p_wkv, s_wkv = pick("o_wkv", 0), pick("o_wkv", slice(1, 3))
    return (y_prompt, y_sample, p_conv, p_delta, p_shift, p_wkv, s_conv, s_delta, s_shift, s_wkv)
```

```python
import numpy as np
import concourse.bass as bass
import concourse.mybir as mybir
from concourse.bass_utils import run_bass_kernel_spmd

F32 = mybir.dt.float32
BF16 = mybir.dt.bfloat16
AF = mybir.ActivationFunctionType
ALU = mybir.AluOpType
AX = mybir.AxisListType

D = 1024
KC = 8
TP = 2048
TS = 16
NTILE = 17
H_A = 8
RMS_EPS = 1e-6
GN_EPS = 64e-5
NEG = -30000.0


class Trk:
    __slots__ = ("lw", "rd", "dsem", "dcount")

    def __init__(self):
        self.lw = None
        self.rd = []
        self.dsem = None
        self.dcount = 0


def fsz(ap):
    n = 1
    for d in ap.shape[1:]:
        n *= d
    return n


class Prog:
    WINDOW = 4000
    LAT = 120.0
    EPS = 150.0

    def __init__(self, nc):
        self.nc = nc
        self.h = {"pe": nc.tensor, "act": nc.scalar, "dve": nc.vector, "pool": nc.gpsimd, "sp": nc.sync}
        self.sem = {k: nc.alloc_semaphore("sem_" + k) for k in self.h}
        self.cnt = {k: 0 for k in self.h}
        self.known = {k: {} for k in self.h}
        self.trk = {}
        self.ops = []
        self.nps = 0
        self.npool = {}
        self.psb = []
        self.nsem = 0
        self.sb_off = nc.sbuf_base
        self.sb_top = nc.sbuf_top
        self.seg_start = 0
        self.sel = {}

    def sb(self, name, shape, dt=F32):
        n = 1
        for d in shape[1:]:
            n *= d
        nbytes = n * (2 if dt == BF16 else 4)
        off = (self.sb_off + 63) // 64 * 64
        assert off + nbytes <= self.sb_top, "SBUF overflow at %s: need %d have %d" % (name, nbytes, self.sb_top - off)
        t = self.nc.alloc_sbuf_tensor_at(name, list(shape), dt, offset=off)
        self.sb_off = off + nbytes
        self.trk[t.name] = Trk()
        return t

    def barrier(self):
        self.ops.append(("fence", None, None, (), 0.0, False, None))
        for t in self._all_trk():
            t.lw = None
            t.rd = []

    def _all_trk(self):
        for t in self.trk.values():
            if isinstance(t, list):
                for x in t:
                    yield x
            else:
                yield t

    def init_psum(self):
        for i in range(8):
            t = self.nc.alloc_psum_tensor("psb%d" % i, [128, 512], F32)
            self.trk["psb%d" % i] = Trk()
            self.psb.append(t)

    POOLS = {0: (0, 1, 2, 3, 4, 5, 6), 2: (0, 1, 2, 3, 4, 5, 6),
             "a1a": (0, 1), "a1m": (2,), "a1b": (3, 4), "a2": (5, 6),
             "b1": (0, 1, 2, 3, 4), "b2": (5, 6)}

    def ps(self, pool=0):
        banks = self.POOLS[pool]
        n = self.npool.get(pool, 0)
        self.npool[pool] = n + 1
        return self.psb[banks[n % len(banks)]]

    def split(self, tensor, n):
        self.trk[tensor.name] = [Trk() for _ in range(n)]

    def only(self, **sel):
        prog = self

        class _Ctx:
            def __enter__(self_):
                self_.old = dict(prog.sel)
                prog.sel.update(sel)

            def __exit__(self_, *a):
                prog.sel = self_.old
        return _Ctx()

    def _tks(self, ap):
        t = self.trk[ap.tensor.name]
        if isinstance(t, list):
            idx = self.sel.get(ap.tensor.name.rsplit("_", 1)[0])
            return [t[i] for i in idx] if idx is not None else list(t)
        return [t]

    def _record(self, kind, e, payload, reads, writes, cost, final=False):
        rt = []
        for a in reads:
            for t in self._tks(a):
                if t not in rt:
                    rt.append(t)
        wt = []
        for a in writes:
            for t in self._tks(a):
                if t not in wt:
                    wt.append(t)
        i = len(self.ops)
        preds = set()
        for t in rt:
            if t.lw is not None:
                preds.add(t.lw)
        for t in wt:
            if t.lw is not None:
                preds.add(t.lw)
            preds.update(t.rd)
        preds.discard(i)
        for t in rt:
            t.rd.append(i)
        for t in wt:
            t.lw = i
            t.rd = []
        t0 = (wt + rt)[0] if kind == "dma" else None
        self.ops.append((kind, e, payload, tuple(preds), float(cost), final, t0))
        return i

    def op(self, e, fn, reads, writes, cost=None):
        if cost is None:
            n = fsz(writes[0]) if writes else 64
            cost = {"act": 200.0 + 0.85 * n, "dve": 110.0 + 1.05 * n, "pool": 260.0 + 1.0 * n, "pe": 150.0}[e]
        return self._record("op", e, fn, reads, writes, cost)

    def dma(self, q, out, in_, reads=(), writes=(), final=False, **kw):
        return self._record("dma", q, (out, in_, kw), reads, writes, 150.0 if q == "sp" else 1200.0, final)

    def _wait(self, e, key, semh, val):
        k = self.known[e]
        if k.get(key, 0) < val:
            self.h[e].wait_ge(semh, val)
            k[key] = val

    def _schedule(self, lo, hi):
        ops = self.ops
        n = hi - lo
        indeg = [0] * n
        succ = [[] for _ in range(n)]
        for i in range(lo, hi):
            ps_ = [p for p in ops[i][3] if p >= lo]
            indeg[i - lo] = len(ps_)
            for p in ps_:
                succ[p - lo].append(i)
        blev = [0.0] * n
        for k in range(n - 1, -1, -1):
            o = ops[lo + k]
            c = o[4] + (2500.0 if o[0] == "dma" else 0.0)
            m = 0.0
            for j in succ[k]:
                v = blev[j - lo] + self.LAT
                if v > m:
                    m = v
            blev[k] = c + m
        dready = [0.0] * n
        etime = {k: 0.0 for k in self.h}
        ready = {k: [] for k in self.h}
        for i in range(lo, hi):
            if indeg[i - lo] == 0:
                ready[ops[i][1]].append(i)
        order = []
        done = [False] * n
        minp = lo
        W = self.WINDOW
        EPS = self.EPS
        while len(order) < n:
            while minp < hi and done[minp - lo]:
                minp += 1
            lim = minp + W
            best = None
            for e, lst in ready.items():
                if not lst:
                    continue
                te = etime[e]
                cand = None
                for i in lst:
                    if i >= lim:
                        continue
                    dr = dready[i - lo]
                    st = te if dr <= te + EPS else dr
                    key = (st, -blev[i - lo], i)
                    if cand is None or key < cand:
                        cand = key
                if cand is not None and (best is None or cand < best[0]):
                    best = (cand, e)
            (st, _, i), e = best
            st = max(st, dready[i - lo], etime[e])
            ready[e].remove(i)
            kind = ops[i][0]
            cost = ops[i][4]
            etime[e] = st + cost
            f = st + cost + (2500.0 if kind == "dma" else 0.0)
            done[i - lo] = True
            order.append(i)
            for j in succ[i - lo]:
                v = f + self.LAT
                if v > dready[j - lo]:
                    dready[j - lo] = v
                indeg[j - lo] -= 1
                if indeg[j - lo] == 0:
                    ready[ops[j][1]].append(j)
        return order, max(etime.values())

    def finish(self):
        ops = self.ops
        bounds = [i for i, o in enumerate(ops) if o[0] == "fence"] + [len(ops)]
        needs_inc = [False] * len(ops)
        info = {}
        clock = {}
        finals = []
        lo = 0
        est_total = 0.0
        nwait = 0
        last_inc = {k: None for k in self.h}
        for b in bounds:
            order, est = self._schedule(lo, b)
            est_total += est
            pos = {i: k for k, i in enumerate(order)}
            kept = {}
            lastop = {}
            for i in order:
                kind, e = ops[i][0], ops[i][1]
                if kind == "op":
                    lastop[e] = i
                best = {}
                keep = []
                for p in ops[i][3]:
                    if p < lo:
                        continue
                    if ops[p][0] != "op":
                        keep.append(p)
                        continue
                    f = ops[p][1]
                    if f == "pe" and e == "pe":
                        continue
                    if f not in best or pos[p] > pos[best[f]]:
                        best[f] = p
                for p in best.values():
                    needs_inc[p] = True
                    keep.append(p)
                kept[i] = keep
            for i in lastop.values():
                needs_inc[i] = True
            for i in order:
                kind, e, payload, preds, cost, final, t0 = ops[i]
                kn = self.known[e]
                for p in sorted(kept[i], key=lambda x: pos[x]):
                    if p < lo:
                        continue
                    pi = info[p]
                    if pi[0] == "op":
                        f, c = pi[1], pi[2]
                        if f == "pe" and e == "pe":
                            continue
                        if kn.get(f, 0) < c:
                            self.h[e].wait_ge(self.sem[f], c)
                            nwait += 1
                            kn[f] = c
                            for g, v in clock[p].items():
                                if kn.get(g, 0) < v:
                                    kn[g] = v
                    else:
                        if kn.get(pi[3], 0) < pi[2]:
                            self.h[e].wait_ge(pi[1], pi[2])
                            nwait += 1
                            kn[pi[3]] = pi[2]
                if kind == "op":
                    ins = payload(self.h[e])
                    if needs_inc[i]:
                        self.cnt[e] += 1
                        ins.then_inc(self.sem[e], 1)
                        info[i] = ("op", e, self.cnt[e])
                        snap = {g: v for g, v in kn.items() if g in self.h}
                        snap[e] = self.cnt[e]
                        clock[i] = snap
                    else:
                        info[i] = ("op", e, self.cnt[e] + 1)
                        clock[i] = {}
                else:
                    out, in_, kw = payload
                    if t0.dsem is None:
                        t0.dsem = self.nc.alloc_semaphore("dsem%d" % self.nsem)
                        self.nsem += 1
                    ins = self.h[e].dma_start(out=out, in_=in_, **kw)
                    t0.dcount += 16
                    ins.then_inc(t0.dsem, 16)
                    info[i] = ("dma", t0.dsem, t0.dcount, "d%d" % id(t0))
                    if final:
                        finals.append(info[i])
            lo = b + 1
            if b < len(ops):
                for e in self.h:
                    for f in self.h:
                        if f != e and self.cnt[f] > 0:
                            self._wait(e, f, self.sem[f], self.cnt[f])
                    for t in self._all_trk():
                        if t.dsem is not None and t.dcount > 0:
                            self._wait(e, "d%d" % id(t), t.dsem, t.dcount)
        for (_, semh, c, key) in finals:
            self._wait("sp", key, semh, c)
        print("scheduler estimate: %.1f us, %d ops, %d waits, incs %s" % (est_total / 1e3, len(ops), nwait, dict(self.cnt)))

    def mmr(self, out, lhsT, rhs, start=True, stop=True):
        return self.mm(out, R(lhsT), R(rhs), start=start, stop=stop)

    def mm(self, out, lhsT, rhs, start=True, stop=True):
        passes = 4.0 if rhs.dtype == F32 else 1.0
        cost = 70.0 + passes * 0.42 * (fsz(rhs) + min(fsz(lhsT), 128))
        return self.op("pe", lambda h: h.matmul(out, lhsT=lhsT, rhs=rhs, start=start, stop=stop),
                       [lhsT, rhs], [out], cost=cost)

    def tr(self, out, in_, ident):
        return self.op("pe", lambda h: h.transpose(out, in_, ident), [in_, ident], [out], cost=160.0)

    def act(self, out, in_, func, e="act", **kw):
        rd = [in_] + [v for v in kw.values() if hasattr(v, "tensor")]
        wr = [out]
        if "accum_out" in kw:
            wr.append(kw["accum_out"])
            rd.remove(kw["accum_out"])
        return self.op("act", lambda h: h.activation(out=out, in_=in_, func=func, **kw), rd, wr)

    def tt(self, out, in0, in1, op, e="dve"):
        return self.op(e, lambda h: h.tensor_tensor(out=out, in0=in0, in1=in1, op=op), [in0, in1], [out])

    def ts(self, out, in0, s1, op0, s2=None, op1=None, e="dve"):
        rd = [in0] + [v for v in (s1, s2) if hasattr(v, "tensor")]
        if op1 is None:
            return self.op(e, lambda h: h.tensor_scalar(out=out, in0=in0, scalar1=s1, scalar2=None, op0=op0),
                           rd, [out])
        return self.op(e, lambda h: h.tensor_scalar(out=out, in0=in0, scalar1=s1, scalar2=s2, op0=op0, op1=op1),
                       rd, [out])

    def stt(self, out, in0, scalar, in1, op0, op1):
        rd = [in0, in1] + ([scalar] if hasattr(scalar, "tensor") else [])
        return self.op("dve", lambda h: h.scalar_tensor_tensor(out=out, in0=in0, scalar=scalar, in1=in1,
                                                                 op0=op0, op1=op1), rd, [out])

    def cp(self, out, in_, e="dve"):
        if e == "act":
            return self.act(out, in_, AF.Copy)
        return self.op(e, lambda h: h.tensor_copy(out=out, in_=in_), [in_], [out])

    def scan(self, out, d0, d1):
        return self.op("dve", lambda h: h.tensor_tensor_scan(out=out, data0=d0, data1=d1, initial=0.0,
                                                              op0=ALU.mult, op1=ALU.add),
                       [d0, d1], [out], cost=110.0 + 2.1 * fsz(out))

    def rsqrt_pool(self, out, in_, mhalf):
        return self.op("pool", lambda h: h.tensor_tensor(out=out, in0=in_, in1=mhalf, op=ALU.pow), [in_, mhalf], [out])

    def recip(self, out, in_):
        return self.op("dve", lambda h: h.reciprocal(out=out, in_=in_), [in_], [out], cost=110.0 + 3.0 * fsz(out))

    def memset(self, ap, val, e="pool"):
        return self.op(e, lambda h: h.memset(ap, val), [], [ap])

    def asel(self, out, in_, pattern, cmp, fill, base, cm):
        return self.op("pool", lambda h: h.affine_select(out=out, in_=in_, pattern=pattern, compare_op=cmp,
                                                          fill=fill, base=base, channel_multiplier=cm),
                       [in_], [out])


F32R = mybir.dt.float32r


def R(ap):
    return ap.bitcast(F32R)


def neumann(P, ident, MT0, M0, Mbuf, MTT, nx, C, pool=0):
    L = {128: 7, 64: 6, 16: 4}[C]
    G = 512 // (2 * C)

    def grp3(ps, n, w):
        return ps[0:C, 0:n * w].rearrange("p (x i) -> p x i", x=n)

    psa = P.ps(pool)
    psb = P.ps(pool)
    for x in range(nx):
        P.mm(psa[0:C, x * C:(x + 1) * C], R(MT0[:, x, :]), R(Mbuf[0][0:C, x, 0:C]))
        P.mm(psb[0:C, x * C:(x + 1) * C], R(Mbuf[0][0:C, x, 0:C]), R(MT0[:, x, :]))
    P.cp(R(Mbuf[1][0:C, 0:nx, 0:C]), grp3(psa, nx, C), e="act")
    P.cp(R(MTT[0][0:C, 0:nx, 0, 0:C]), grp3(psb, nx, C), e="act")
    P.tt(R(MTT[0][0:C, 0:nx, 1, 0:C]), MT0, bcm(ident[0:C, 0:C], [C, nx, C]), ALU.add)
    yield
    cm, ct = 1, 0
    for lev in range(2, L + 1):
        last = lev == L
        Mc = Mbuf[cm]
        cur = MTT[ct]
        nxt = MTT[1 - ct]
        if not last:
            psa = P.ps(pool)
            for x in range(nx):
                P.mm(psa[0:C, x * C:(x + 1) * C], R(cur[0:C, x, 0, 0:C]), R(Mc[0:C, x, 0:C]))
        for x0 in range(0, nx, G):
            n = min(G, nx - x0)
            psx = P.ps(pool)
            for j in range(n):
                x = x0 + j
                if last:
                    P.mm(psx[0:C, j * C:(j + 1) * C], R(Mc[0:C, x, 0:C]), R(cur[0:C, x, 1, 0:C]))
                else:
                    P.mm(psx[0:C, j * 2 * C:(j + 1) * 2 * C], R(Mc[0:C, x, 0:C]), R(cur[0:C, x, :, 0:C]))
            if last:
                P.tt(R(nxt[0:C, x0:x0 + n, 1, 0:C]), grp3(psx, n, C), cur[0:C, x0:x0 + n, 1, 0:C], ALU.add)
            else:
                pv = psx[0:C, 0:n * 2 * C].rearrange("p (x a i) -> p x a i", x=n, a=2)
                P.cp(R(nxt[0:C, x0:x0 + n, 0, 0:C]), pv[:, :, 0, :], e="act")
                P.tt(R(nxt[0:C, x0:x0 + n, 1, 0:C]), pv[:, :, 1, :], cur[0:C, x0:x0 + n, 1, 0:C], ALU.add)
        if not last:
            P.cp(R(Mbuf[1 - cm][0:C, 0:nx, 0:C]), grp3(psa, nx, C), e="act")
        cm = 1 - cm
        ct = 1 - ct
        yield
    return MTT[ct][0:C, 0:nx, 1, 0:C]


def bc(ap, shape):
    return ap.unsqueeze(len(ap.shape)).broadcast_to(list(shape))


def bcm(ap, shape):
    return ap.unsqueeze(1).broadcast_to(list(shape))


PCOL = 1
S1COL = TP + 1 + 1
S2COL = S1COL + 32
NTOK = S2COL + 16 + 1
NBA = 256
CA = 128
NCH = TP // CA + 2


def build(stop=None):
    nc = bass.Bass("TRN2", target_bir_lowering=False)
    P = Prog(nc)
    P.init_psum()

    def din(name, shape):
        return nc.dram_tensor(name, list(shape), F32, kind="ExternalInput").ap()

    def dout(name, shape):
        return nc.dram_tensor(name, list(shape), F32, kind="ExternalOutput").ap()

    x_p = din("x_p", [TP, D])
    x_s = din("x_s", [2 * TS, D])
    conv_s = din("conv_s", [2, 3, 4096])
    delta_s = din("delta_s", [2, 8, 128, 256])
    shift_s = din("shift_s", [2, D])
    wkv_s = din("wkv_s", [2, 16, 64, 64])
    norm_w = din("norm_w", [2, D])
    final_norm_w = din("final_norm_w", [D])
    a_w_in = din("a_w_in", [D, 6160])
    a_conv_w = din("a_conv_w", [4, 4096])
    a_log = din("a_log", [8])
    a_dt_bias = din("a_dt_bias", [8])
    a_norm_w = din("a_norm_w", [256])
    a_w_out = din("a_w_out", [2048, D])
    b_mu = din("b_mu", [6, D])
    b_w_in = din("b_w_in", [D, 4096])
    b_w0 = din("b_w0", [D])
    b_w_w1 = din("b_w_w1", [D, 64])
    b_w_w2 = din("b_w_w2", [64, D])
    b_a0 = din("b_a0", [D])
    b_a_w1 = din("b_a_w1", [D, 64])
    b_a_w2 = din("b_a_w2", [64, D])
    b_k_k = din("b_k_k", [D])
    b_k_a = din("b_k_a", [D])
    b_r_k = din("b_r_k", [D])
    b_gn_w = din("b_gn_w", [D])
    b_gn_b = din("b_gn_b", [D])
    b_w_out = din("b_w_out", [D, D])

    y_p = dout("y_p", [TP, D])
    y_s = dout("y_s", [2 * TS, D])
    o_conv = dout("o_conv", [3, 3, 4096])
    o_delta = dout("o_delta", [3, 8, 128, 256])
    o_shift = dout("o_shift", [3, D])
    o_wkv = dout("o_wkv", [3, 16, 64, 64])
    dbg = dout("dbg", [NTILE * 128, D]) if stop else None

    ident = P.sb("ident", [128, 128])
    ones = P.sb("ones", [128, 128])
    mones = P.sb("mones", [128, 128])
    zeros = P.sb("zeros", [128, 128])
    Utri = P.sb("Utri", [128, 128])
    NEGT = P.sb("NEGT", [128, 128])
    MsT = P.sb("MsT", [128, 128])
    P.memset(ones[:], 1.0)
    ones_r = P.sb("ones_r", [128, 128])
    P.cp(R(ones_r[:]), ones[:], e="act")
    P.memset(mones[:], -1.0)
    P.memset(zeros[:], 0.0)
    P.asel(ident[:], ones[:], [[-1, 128]], ALU.is_equal, 0.0, 0, 1)
    P.asel(Utri[:], ones[:, :], [[1, 128]], ALU.is_ge, 0.0, 0, -1)
    P.asel(NEGT[:], zeros[:, :], [[1, 128]], ALU.is_ge, NEG, 0, -1)
    P.asel(MsT[:], ones[:, :], [[1, 128]], ALU.is_gt, 0.0, 0, -1)

    xres = [P.sb("xres%d" % i, [128, D]) for i in range(NTILE)]
    nw = P.sb("nw", [128, 2, KC])
    fnw = P.sb("fnw", [128, KC])
    P.dma("sp", nw[:], norm_w.rearrange("l (k p) -> p l k", p=128), writes=[nw[:]], allow_slow_non_contiguous=True)
    P.dma("sp", fnw[:], final_norm_w.rearrange("(k p) -> p k", p=128), writes=[fnw[:]],
          allow_slow_non_contiguous=True)
    ssq = P.sb("ssq", [128, 4])
    rstd = P.sb("rstd", [128, 4])
    phase_mark = P.sb_off
    hT = P.sb("hT", [128, KC, NTOK], BF16)
    xn = [P.sb("xn0", [128, D])] * 2

    def tile_rows(i):
        return 128 if i < 16 else 48

    def tile_col(i):
        return PCOL + i * 128 if i < 16 else S1COL

    def norm_to_hT(layer, hT):
        for i in range(NTILE):
            nt = tile_rows(i)
            xt = xres[i]
            xb = xn[i % 2]
            sl = slice(i % 4, i % 4 + 1)
            P.act(xb[0:nt, :], xt[0:nt, :], AF.Square, accum_out=ssq[0:nt, sl])
            P.act(rstd[0:nt, sl], ssq[0:nt, sl], AF.Sqrt, scale=1.0 / D, bias=RMS_EPS)
            P.recip(rstd[0:nt, sl], rstd[0:nt, sl])
            P.ts(xb[0:nt, :], xt[0:nt, :], rstd[0:nt, sl], ALU.mult)
            c0 = tile_col(i)
            for half in range(2):
                ps = P.ps()
                for j in range(4):
                    kc = half * 4 + j
                    P.tr(ps[:, j * 128:j * 128 + nt], xb[0:nt, kc * 128:(kc + 1) * 128], ident[0:nt, 0:nt])
                pv = ps[:, :].rearrange("p (j t) -> p j t", j=4)[:, :, 0:nt]
                P.tt(hT[:, half * 4:half * 4 + 4, c0:c0 + nt], pv,
                     bc(nw[:, layer, half * 4:half * 4 + 4], [128, 4, nt]), ALU.mult)

    for i in range(NTILE):
        if i < 16:
            P.dma("sp", xres[i][:], x_p[i * 128:(i + 1) * 128, :], writes=[xres[i][:]])
        else:
            P.memset(xres[i][:], 0.0)
            P.dma("sp", xres[i][0:16, :], x_s[0:16, :], writes=[xres[i][:]])
            P.dma("sp", xres[i][32:48, :], x_s[16:32, :], writes=[xres[i][:]])
    P.memset(hT[:, :, 0:1], 0.0)
    norm_to_hT(0, hT)

    blocks = [(0, PCOL + i * NBA, NBA, CA, NBA // CA, i * (NBA // CA)) for i in range(TP // NBA)] + \
             [(1, S1COL, 16, 16, 1, NCH - 2), (2, S2COL, 16, 16, 1, NCH - 1)]

    cwt = P.sb("cwt", [32, 4, 128])
    cw = P.sb("cw", [128, 4, 32])
    P.dma("sp", cwt[:], a_conv_w.rearrange("t (g c) -> g t c", c=128), writes=[cwt[:]])
    ps = P.ps()
    for t in range(4):
        P.tr(ps[:, t * 32:(t + 1) * 32], cwt[:, t, :], ident[0:32, 0:32])
    P.cp(cw[:].rearrange("p t g -> p (t g)"), ps[:, 0:128])
    halo_all = P.sb("halo_all", [128, 2, 3, 32])
    hrow = P.sb("hrow", [96, 2, 128])
    for s in range(2):
        P.dma("sp", hrow[:, s, :], conv_s[s].rearrange("t (g c) -> (t g) c", c=128), writes=[hrow[:]])
    ps = P.ps()
    for s in range(2):
        P.tr(ps[:, s * 96:(s + 1) * 96], hrow[:, s, :], ident[0:96, 0:96])
    P.cp(halo_all[:].rearrange("p s t g -> p (s t g)"), ps[:, 0:192])
    fin_all = P.sb("fin_all", [128, 3, 3, 32])
    anw = P.sb("anw", [128, 2])
    P.dma("sp", anw[:], a_norm_w.rearrange("(h p) -> p h", p=128), writes=[anw[:]], allow_slow_non_contiguous=True)
    P.ts(anw[:], anw[:], 0.5, ALU.mult)

    wba = P.sb("wba", [128, KC, 16], BF16)
    P.dma("pool", wba[:], a_w_in.rearrange("(k p) c -> p k c", p=128)[:, :, 6144:6160], writes=[wba[:]])
    NCP = TP // CA
    BA = P.sb("BA", [CA, NCH, 16])
    P.memset(BA[:], 0.0)
    ps = P.ps()
    for c in range(NCP):
        for kc in range(KC):
            P.mm(ps[0:CA, c * 16:(c + 1) * 16], hT[:, kc, PCOL + c * CA:PCOL + (c + 1) * CA], wba[:, kc, :],
                 start=(kc == 0), stop=(kc == KC - 1))
    P.cp(BA[:, 0:NCP, :].rearrange("p c k -> p (c k)"), ps[0:CA, 0:NCP * 16])
    ps = P.ps()
    for s, sc in enumerate((S1COL, S2COL)):
        for kc in range(KC):
            P.mm(ps[0:16, s * 16:(s + 1) * 16], hT[:, kc, sc:sc + 16], wba[:, kc, :],
                 start=(kc == 0), stop=(kc == KC - 1))
    P.cp(BA[0:16, NCP:NCP + 2, :].rearrange("p c k -> p (c k)"), ps[0:16, 0:32])
    alg = P.sb("alg", [CA, 8])
    dtb = P.sb("dtb", [CA, 8])
    P.dma("sp", alg[:], a_log.partition_broadcast(CA), writes=[alg[:]])
    P.dma("sp", dtb[:], a_dt_bias.partition_broadcast(CA), writes=[dtb[:]])
    P.act(alg[:], alg[:], AF.Exp)
    P.ts(alg[:], alg[:], -1.0, ALU.mult)
    beta = P.sb("beta", [CA, NCH, 8])
    gg = P.sb("gg", [CA, NCH, 8])
    Gc = P.sb("Gc", [CA, NCH, 8])
    Glb = P.sb("Glb", [128, NCH, 8])
    gl = P.sb("gl", [128, NCH, 8])
    eG = P.sb("eG", [CA, NCH, 8])
    bG = P.sb("bG", [CA, NCH, 8])
    dte = P.sb("dte", [CA, NCH, 8])
    P.act(beta[:], BA[:, :, 0:8], AF.Sigmoid)
    P.tt(gg[:], BA[:, :, 8:16], bcm(dtb[:], [CA, NCH, 8]), ALU.add)
    P.act(gg[:], gg[:], AF.Exp)
    P.act(gg[:], gg[:], AF.Ln, bias=1.0)
    P.tt(gg[:], gg[:], bcm(alg[:], [CA, NCH, 8]), ALU.mult)
    psG = P.ps()
    psL = P.ps()
    g2 = gg[:].rearrange("p c k -> p (c k)")
    GP = NCP * 8
    P.mm(psG[0:CA, 0:GP], Utri[0:CA, 0:CA], g2[:, 0:GP])
    P.mm(psG[0:16, GP:GP + 16], Utri[0:16, 0:16], g2[0:16, GP:GP + 16])
    P.mm(psL[:, 0:GP], ones[0:CA, :], g2[:, 0:GP])
    P.mm(psL[:, GP:GP + 16], ones[0:16, :], g2[0:16, GP:GP + 16])
    P.memset(Gc[:], 0.0)
    P.cp(Gc[:, 0:NCP, :].rearrange("p c k -> p (c k)"), psG[0:CA, 0:GP])
    P.cp(Gc[0:16, NCP:NCP + 2, :].rearrange("p c k -> p (c k)"), psG[0:16, GP:GP + 16])
    P.cp(Glb[:].rearrange("p c k -> p (c k)"), psL[:, 0:GP + 16])
    P.act(gl[:], Glb[:], AF.Exp)
    P.act(eG[:], Gc[:], AF.Exp)
    P.tt(bG[:], beta[:], eG[:], ALU.mult)
    hbeta = eG
    P.ts(hbeta[:], beta[:], 0.5, ALU.mult)
    P.tt(dte[:], Glb[0:CA], Gc[:], ALU.subtract)
    P.act(dte[:], dte[:], AF.Exp)

    Wh = [P.sb("Wh0", [128, KC, 768], BF16)] * 2
    Wo = [P.sb("Wo0", [128, 2, D], BF16)] * 2
    w_in_v = a_w_in.rearrange("(k p) c -> p k c", p=128)

    def load_head_w(h):
        sl = h % 2
        for (c0, n, o) in ((h * 128, 128, 0), (1024 + h * 128, 128, 128), (2048 + h * 256, 256, 256),
                           (4096 + h * 256, 256, 512)):
            P.dma("pool", Wh[sl][:, :, o:o + n], w_in_v[:, :, c0:c0 + n], writes=[Wh[sl][:]])

    def load_head_wo(h):
        sl = h % 2
        P.dma("pool", Wo[sl][:], a_w_out[h * 256:(h + 1) * 256, :].rearrange("(hh p) c -> p hh c", p=128),
              writes=[Wo[sl][:]])

    NB_ = NBA
    NC_ = NBA // CA
    pre = P.sb("pre", [128, 4, NB_ + 3])
    acc = P.sb("acc", [128, 4, NB_])
    P.split(acc, 4)
    P.split(pre, 4)
    qkv = P.sb("qkv", [128, 4, NB_])
    zs2 = [P.sb("zs%d" % i, [128, 2, NB_]) for i in range(2)]
    sqr = P.sb("sqr", [128, 2, NB_])
    sq = sqr
    rq = acc[:, 2:4]
    oT = P.sb("oTp", [128, 2, NB_])
    osq = P.sb("osq2", [128, 2, NB_])
    qT = P.sb("qT", [128, NB_])
    kT = P.sb("kT", [128, NB_])
    kbT = P.sb("kbT", [128, NB_])
    qgT2 = [P.sb("qgT%d" % i, [128, NB_]) for i in range(2)]
    dg = P.sb("dg", [128, 2, NB_])
    eGb = P.sb("eGb", [128, NB_])
    betab = P.sb("betab", [128, NB_])
    kbeG2 = [P.sb("kbeG%d" % i, [CA, NC_, 128]) for i in range(2)]
    ktk2 = [P.sb("ktk%d" % i, [CA, NC_, 128]) for i in range(2)]
    vb2 = [P.sb("vb%d" % i, [CA, NC_, 256]) for i in range(2)]
    gU = P.sb("gU", [CA, NC_, CA])
    DT = P.sb("DT", [CA, NC_, CA])
    DTs = P.sb("DTs", [CA, NC_, CA])
    qkT2 = [P.sb("qkT%d" % i, [CA, NC_, CA]) for i in range(2)]
    Mn = [P.sb("Mn%d" % i, [CA, NC_, CA]) for i in range(2)]
    MT0a2 = [P.sb("MT0a%d" % i, [CA, NC_, CA]) for i in range(2)]
    MTTa = [P.sb("MTTa%d" % i, [CA, NC_, 2, CA]) for i in range(2)]
    u0 = P.sb("u0", [CA, NC_, 256])
    wkT = P.sb("wkT", [128, NB_])
    uu = [P.sb("uu%d" % i, [CA, 256]) for i in range(2)]
    Sp_ = [P.sb("Sp%d" % i, [128, 256]) for i in range(2)]
    Ss_ = [P.sb("Ss%d" % i, [128, 256]) for i in range(2)]
    Sst = [Sp_, Ss_, Ss_]
    ors = P.sb("ors", [128, NB_])
    og = P.sb("og", [128, 2, NB_], BF16)
    print("SBUF left after layer-A alloc:", P.sb_top - P.sb_off)

    DONE = object()

    def A_s1(it, h, blk):
        (seq, t0, NB, C, nch, c0) = blk
        par = it % 2
        W = Wh[0]
        ggrp = (h, 8 + h, 16 + 2 * h, 17 + 2 * h)
        zs, qgT, ktk, qkT = zs2[par], qgT2[par], ktk2[par], qkT2[par]
        kbeG, vb, MT0a = kbeG2[par], vb2[par], MT0a2[par]
        first_of_seq = (seq == 0 and t0 == PCOL) or seq > 0
        if seq == 0 and t0 == PCOL:
            load_head_w(h)
        if first_of_seq:
            if seq == 0:
                P.memset(pre[:, :, 0:3], 0.0)
            else:
                for gi, g in enumerate(ggrp):
                    with P.only(pre=[gi]):
                        P.cp(pre[:, gi, 0:3], halo_all[:, seq - 1, :, g], e="pool")
        for m in range(6):
            ps = P.ps("a1a")
            for kc in range(KC):
                P.mm(ps[:, 0:NB], W[:, kc, m * 128:(m + 1) * 128], hT[:, kc, t0:t0 + NB],
                     start=(kc == 0), stop=(kc == KC - 1))
            if m < 4:
                with P.only(pre=[m]):
                    P.cp(pre[:, m, 3:3 + NB], ps[:, 0:NB], e="act")
            else:
                P.act(zs[:, m - 4, 0:NB], ps[:, 0:NB], AF.Tanh, scale=0.5)
                P.stt(zs[:, m - 4, 0:NB], zs[:, m - 4, 0:NB], 1.0, ps[:, 0:NB], ALU.add, ALU.mult)
            yield
        for gi, g in enumerate(ggrp):
            with P.only(acc=[gi], pre=[gi]):
                P.act(acc[:, gi, 0:NB], pre[:, gi, 3:3 + NB], AF.Copy, scale=cw[:, 3, g:g + 1])
                for tap in (2, 1, 0):
                    P.stt(acc[:, gi, 0:NB], pre[:, gi, tap:tap + NB], cw[:, tap, g:g + 1], acc[:, gi, 0:NB],
                          ALU.mult, ALU.add)
            yield
        P.act(qkv[:, :, 0:NB], acc[:, :, 0:NB], AF.Tanh, scale=0.5)
        P.stt(qkv[:, :, 0:NB], qkv[:, :, 0:NB], 1.0, acc[:, :, 0:NB], ALU.add, ALU.mult)
        last = (t0 + NB == PCOL + TP) or seq > 0
        for gi, g in enumerate(ggrp):
            with P.only(pre=[gi]):
                if last:
                    P.cp(fin_all[:, seq, :, g], pre[:, gi, NB:NB + 3], e="pool")
                else:
                    P.cp(pre[:, gi, 0:3], pre[:, gi, NB:NB + 3], e="pool")
        yield
        P.act(R(sq[:, :, 0:NB]), qkv[:, 0:2, 0:NB], AF.Square)
        for j in range(2):
            ps = P.ps("a1m")
            P.mmr(ps[:, 0:NB], ones_r[:, :], sq[:, j, 0:NB])
            with P.only(acc=[2 + j]):
                if j == 0:
                    P.act(rq[:, j, 0:NB], ps[:, 0:NB], AF.Ln, scale=128.0, bias=512.0 * 1e-6)
                else:
                    P.act(rq[:, j, 0:NB], ps[:, 0:NB], AF.Ln, bias=4e-6)
        yield
        with P.only(acc=[2, 3]):
            P.act(rq[:, :, 0:NB], rq[:, :, 0:NB], AF.Exp, scale=-0.5)
        with P.only(acc=[2]):
            P.tt(R(qT[:, 0:NB]), qkv[:, 0, 0:NB], rq[:, 0, 0:NB], ALU.mult)
        with P.only(acc=[3]):
            P.tt(R(kT[:, 0:NB]), qkv[:, 1, 0:NB], rq[:, 1, 0:NB], ALU.mult)
        yield
        cs = slice(c0, c0 + nch)
        idb = bcm(ident[0:C, 0:C], [C, nch, C])
        dgv = dg[0:C, :, 0:NB].rearrange("p a (c i) -> p a c i", c=nch)
        P.tt(dgv[:, 0], idb, bc(Gc[0:C, cs, h], [C, nch, C]), ALU.mult)
        P.tt(dgv[:, 1], idb, bc(beta[0:C, cs, h], [C, nch, C]), ALU.mult)
        ps = P.ps("a1m")
        P.mm(ps[:, 0:NB], ones[0:C, :], dg[0:C, 0, 0:NB])
        P.act(eGb[:, 0:NB], ps[:, 0:NB], AF.Exp)
        ps = P.ps("a1m")
        P.mm(ps[:, 0:NB], ones[0:C, :], dg[0:C, 1, 0:NB])
        P.cp(betab[:, 0:NB], ps[:, 0:NB], e="act")
        yield
        P.tt(R(kbT[:, 0:NB]), kT[:, 0:NB], betab[:, 0:NB], ALU.mult)
        P.tt(R(qgT[:, 0:NB]), qT[:, 0:NB], eGb[:, 0:NB], ALU.mult)
        yield
        for c4 in range(0, nch, 4):
            n4 = min(4, nch - c4)
            ps = P.ps("a1m")
            for j in range(n4):
                c = c4 + j
                P.tr(ps[0:C, j * 128:(j + 1) * 128], kT[:, c * C:(c + 1) * C], ident[:, :])
            pv = ps[0:C, 0:n4 * 128].rearrange("p (j d) -> p j d", j=n4)
            P.tt(R(kbeG[0:C, c4:c4 + n4, :]), pv, bc(bG[0:C, c0 + c4:c0 + c4 + n4, h], [C, n4, 128]), ALU.mult)
            P.tt(R(ktk[0:C, c4:c4 + n4, :]), pv, bc(dte[0:C, c0 + c4:c0 + c4 + n4, h], [C, n4, 128]), ALU.mult)
            yield
        for c2 in range(0, nch, 2):
            n2 = min(2, nch - c2)
            ps = P.ps("a1m")
            for j in range(n2):
                c = c2 + j
                for half in range(2):
                    P.tr(ps[0:C, j * 256 + half * 128:j * 256 + (half + 1) * 128],
                         qkv[:, 2 + half, c * C:(c + 1) * C], ident[:, :])
            pv = ps[0:C, 0:n2 * 256].rearrange("p (j d) -> p j d", j=n2)
            P.tt(R(vb[0:C, c2:c2 + n2, :]), pv, bc(hbeta[0:C, c0 + c2:c0 + c2 + n2, h], [C, n2, 256]), ALU.mult)
            yield
        P.tt(gU[0:C, 0:nch, 0:C], bcm(Utri[0:C, 0:C], [C, nch, C]), bc(gg[0:C, cs, h], [C, nch, C]), ALU.mult)
        ps = P.ps("a1m")
        for c in range(nch):
            o = ps[0:C, c * C:(c + 1) * C]
            P.mm(o, ones[0:C, 0:C], gU[0:C, c, 0:C], start=True, stop=False)
            P.mm(o, gU[0:C, c, 0:C], mones[0:C, 0:C], start=False, stop=False)
            P.mm(o, ident[0:C, 0:C], NEGT[0:C, 0:C], start=False, stop=True)
        pv = ps[0:C, 0:nch * C].rearrange("p (c i) -> p c i", c=nch)
        P.act(DT[0:C, 0:nch, 0:C], pv, AF.Exp)
        P.tt(DTs[0:C, 0:nch, 0:C], DT[0:C, 0:nch, 0:C], bcm(MsT[0:C, 0:C], [C, nch, C]), ALU.mult, e="pool")
        yield
        ps = P.ps("a1m")
        for c in range(nch):
            P.mmr(ps[0:C, c * C:(c + 1) * C], kT[:, c * C:(c + 1) * C], kbT[:, c * C:(c + 1) * C])
        for c in range(nch):
            P.mmr(ps[0:C, 256 + c * C:256 + (c + 1) * C], kT[:, c * C:(c + 1) * C], qT[:, c * C:(c + 1) * C])
        pv = ps[0:C, 0:nch * C].rearrange("p (c i) -> p c i", c=nch)
        pv2 = ps[0:C, 256:256 + nch * C].rearrange("p (c i) -> p c i", c=nch)
        P.stt(R(MT0a[0:C, 0:nch, 0:C]), pv, -1.0, DTs[0:C, 0:nch, 0:C], ALU.mult, ALU.mult)
        P.tt(R(qkT[0:C, 0:nch, 0:C]), pv2, DT[0:C, 0:nch, 0:C], ALU.mult)
        yield
        ps = P.ps("a1m")
        for c in range(nch):
            P.tr(ps[0:C, c * C:(c + 1) * C], MT0a[0:C, c, 0:C], ident[0:C, 0:C])
        P.cp(R(Mn[0][0:C, 0:nch, 0:C]), ps[0:C, 0:nch * C].rearrange("p (c i) -> p c i", c=nch), e="act")
        yield
        TTf = yield from neumann(P, ident, MT0a[0:C, 0:nch, 0:C], None, Mn, MTTa, nch, C, pool="a1b")
        yield "DRAIN2"
        for c2 in range(0, nch, 2):
            n2 = min(2, nch - c2)
            ps = P.ps("a1b")
            for j in range(n2):
                P.mmr(ps[0:C, j * 256:(j + 1) * 256], TTf[:, c2 + j, :], vb[0:C, c2 + j, :])
            P.cp(u0[0:C, c2:c2 + n2, :], ps[0:C, 0:n2 * 256].rearrange("p (j d) -> p j d", j=n2), e="act")
        ps = P.ps("a1b")
        for c in range(nch):
            P.mmr(ps[:, c * C:(c + 1) * C], kbeG[0:C, c, :], TTf[:, c, :])
        P.cp(R(wkT[:, 0:NB]), ps[:, 0:NB], e="act")
        yield

    spar = [0, 0, 0]

    def A_s2(it, h, blk):
        (seq, t0, NB, C, nch, c0) = blk
        par = it % 2
        WO = Wo[0]
        zs, qgT, ktk, qkT = zs2[par], qgT2[par], ktk2[par], qkT2[par]
        first_of_seq = (seq == 0 and t0 == PCOL) or seq > 0
        if seq == 0 and t0 == PCOL:
            load_head_wo(h)
        if first_of_seq:
            spar[seq] = 0
            if seq == 0:
                P.cp(R(Sst[0][0][:]), zeros[:, 0:1].broadcast_to([128, 256]), e="act")
            else:
                P.dma("sp", Sst[seq][1][:], delta_s[seq - 1, h], writes=[Sst[seq][1][:]])
                P.cp(R(Sst[seq][0][:]), Sst[seq][1][:], e="act")
        pso = P.psb[7]
        for c in range(nch):
            Sc = Sst[seq][spar[seq]]
            Sn = Sst[seq][1 - spar[seq]]
            u = uu[c % 2]
            ps = P.ps("a2")
            P.mmr(ps[0:C, 0:256], wkT[:, c * C:(c + 1) * C], Sc[:, :])
            P.tt(R(u[0:C, :]), u0[0:C, c, :], ps[0:C, 0:256], ALU.subtract)
            yield
            ps2 = P.ps("a2")
            P.mmr(ps2[:, 0:256], ktk[0:C, c, :], u[0:C, :])
            P.stt(R(Sn[:, :]), Sc[:, :], gl[:, c0 + c, h:h + 1], ps2[:, 0:256], ALU.mult, ALU.add)
            for half in range(2):
                oo = pso[:, (half * nch + c) * C:(half * nch + c + 1) * C]
                P.mmr(oo, Sc[:, half * 128:(half + 1) * 128], qgT[:, c * C:(c + 1) * C], start=True, stop=False)
                P.mmr(oo, u[0:C, half * 128:(half + 1) * 128], qkT[0:C, c, 0:C], start=False, stop=True)
            spar[seq] = 1 - spar[seq]
            yield
        P.cp(oT[:, :, 0:NB], pso[:, 0:2 * NB].rearrange("p (a t) -> p a t", a=2), e="act")
        P.act(R(osq[:, :, 0:NB]), oT[:, :, 0:NB], AF.Square)
        ps = P.ps("a2")
        P.mmr(ps[:, 0:NB], ones_r[:, :], osq[:, 0, 0:NB], start=True, stop=False)
        P.mmr(ps[:, 0:NB], ones_r[:, :], osq[:, 1, 0:NB], start=False, stop=True)
        P.act(ors[:, 0:NB], ps[:, 0:NB], AF.Ln, scale=1.0 / 256.0, bias=RMS_EPS)
        yield
        P.act(ors[:, 0:NB], ors[:, 0:NB], AF.Exp, scale=-0.5)
        for half in range(2):
            P.stt(oT[:, half, 0:NB], oT[:, half, 0:NB], anw[:, half:half + 1], ors[:, 0:NB], ALU.mult, ALU.mult)
        P.tt(og[:, :, 0:NB], oT[:, :, 0:NB], zs[:, :, 0:NB], ALU.mult)
        yield
        for tt0 in range(0, NB, 128):
            nt = min(128, NB - tt0)
            if seq == 0:
                tile_i, prow = (t0 - PCOL + tt0) // 128, 0
            else:
                tile_i, prow = 16, (seq - 1) * 32
            for nh in range(2):
                ps = P.ps("a2")
                for half in range(2):
                    P.mm(ps[prow:prow + nt, :], og[:, half, tt0:tt0 + nt], WO[:, half, nh * 512:(nh + 1) * 512],
                         start=(half == 0), stop=(half == 1))
                xr = xres[tile_i][prow:prow + nt, nh * 512:(nh + 1) * 512]
                P.tt(xr, xr, ps[prow:prow + nt, :], ALU.add)
            yield
        last = (t0 + NB == PCOL + TP) or seq > 0
        if last:
            P.dma("sp", o_delta[seq, h], Sst[seq][spar[seq]][:], reads=[Sst[seq][spar[seq]][:]], final=True)

    def pipeline(items, s1, s2, ratio):
        g2 = None
        for it, item in enumerate(list(items) + [None]):
            g1 = s1(it, *item) if item is not None else None
            while g1 is not None or g2 is not None:
                if g2 is not None:
                    if next(g2, DONE) is DONE:
                        g2 = None
                if g1 is not None:
                    for _ in range(ratio if g2 is not None else 1000000):
                        r = next(g1, DONE)
                        if r is DONE:
                            g1 = None
                            break
                        if r == "DRAIN2":
                            while g2 is not None:
                                if next(g2, DONE) is DONE:
                                    g2 = None
            g2 = s2(it, *item) if item is not None else None

    pipeline([(h, blk) for h in range(H_A) for blk in blocks], A_s1, A_s2, 3)

    ps = P.ps()
    for s in range(3):
        P.tr(ps[0:96, s * 128:(s + 1) * 128], fin_all[:, s].rearrange("p t g -> p (t g)"), ident[:, :])
    for s in range(3):
        P.cp(acc[0:96, s, 0:128], ps[0:96, s * 128:(s + 1) * 128])
        P.dma("sp", o_conv[s].rearrange("t (g c) -> (t g) c", c=128), acc[0:96, s, 0:128], reads=[acc[:]], final=True)

    if stop == "A":
        for i in range(NTILE):
            nt = tile_rows(i)
            P.dma("sp", dbg[i * 128:i * 128 + nt, :], xres[i][0:nt, :], reads=[xres[i][:]], final=True)
        P.finish()
        return nc

    P.barrier()
    P.sb_off = phase_mark
    hT = P.sb("hT2", [128, KC, NTOK], BF16)
    shout = P.sb("shout", [128, 3, KC])
    P.memset(hT[:, :, 0:1], 0.0)
    mark_b0 = P.sb_off
    xn = [P.sb("xnB", [128, D])] * 2

    def norm_to_hT_B():
        for i in range(NTILE):
            nt = tile_rows(i)
            xt = xres[i]
            xb = xn[i % 2]
            sl = slice(i % 4, i % 4 + 1)
            P.act(xb[0:nt, :], xt[0:nt, :], AF.Square, accum_out=ssq[0:nt, sl])
            P.act(rstd[0:nt, sl], ssq[0:nt, sl], AF.Sqrt, scale=1.0 / D, bias=RMS_EPS)
            P.recip(rstd[0:nt, sl], rstd[0:nt, sl])
            P.ts(xb[0:nt, :], xt[0:nt, :], rstd[0:nt, sl], ALU.mult)
            c0 = tile_col(i)
            for half in range(2):
                ps = P.ps(2)
                for j in range(4):
                    kc = half * 4 + j
                    P.tr(ps[:, j * 128:j * 128 + nt], xb[0:nt, kc * 128:(kc + 1) * 128], ident[0:nt, 0:nt])
                pv4 = ps[:, :].rearrange("p (j t) -> p j t", j=4)
                pv = pv4[:, :, 0:nt]
                P.tt(hT[:, half * 4:half * 4 + 4, c0:c0 + nt], pv,
                     bc(nw[:, 1, half * 4:half * 4 + 4], [128, 4, nt]), ALU.mult)
                lastcols = {15: [(0, 127)], 16: [(1, 15), (2, 47)]}.get(i, [])
                for (sq_, col) in lastcols:
                    P.tt(shout[:, sq_, half * 4:half * 4 + 4], pv4[:, :, col], nw[:, 1, half * 4:half * 4 + 4], ALU.mult)

    norm_to_hT_B()
    P.barrier()
    P.sb_off = mark_b0
    for s_ in range(3):
        P.dma("sp", o_shift[s_].rearrange("(k p) -> p k", p=128), shout[:, s_, :], reads=[shout[:]], final=True,
              allow_slow_non_contiguous=True)
    shin = P.sb("shin", [128, 2, KC])
    P.dma("sp", shin[:], shift_s.rearrange("s (k p) -> p s k", p=128), writes=[shin[:]], allow_slow_non_contiguous=True)
    P.cp(hT[:, :, S1COL - 1], shin[:, 0, :])
    P.cp(hT[:, :, S2COL - 1], shin[:, 1, :])

    vecs = P.sb("vecs", [128, 13, KC])
    P.dma("sp", vecs[:, 0:6, :], b_mu.rearrange("g (k p) -> p g k", p=128), writes=[vecs[:]], allow_slow_non_contiguous=True)
    for vi, v_ in enumerate((b_w0, b_a0, b_k_k, b_k_a, b_r_k, b_gn_w, b_gn_b)):
        P.dma("sp", vecs[:, 6 + vi, :], v_.rearrange("(k p) -> p k", p=128), writes=[vecs[:]], allow_slow_non_contiguous=True)
    V_W0, V_A0, V_KK, V_KA, V_RK, V_GW, V_GB = range(6, 13)
    hvec = P.sb("hvec", [128, 2, KC])
    P.ts(hvec[:], vecs[:, 6:8, :], 0.5, ALU.mult)
    blk1 = P.sb("blk1", [128, 128])
    cst32 = P.sb("cst32", [128, 192])
    P.asel(cst32[:, 0:64], ones[:, 0:64], [[0, 64]], ALU.is_ge, 0.0, 63, -1)
    P.asel(cst32[:, 64:128], ones[:, 0:64], [[0, 64]], ALU.is_ge, 0.0, -64, 1)
    P.cp(R(blk1[:]), cst32[:, 0:128])
    CB = 128
    MXT = P.sb("MXT", [CB, 2 * CB])
    P.cp(MXT[:, 0:CB], MsT[0:CB, 0:CB], e="pool")
    P.cp(MXT[:, CB:2 * CB], Utri[0:CB, 0:CB], e="pool")
    MsL = P.sb("MsL", [CB, CB])
    Sh = P.sb("Sh", [128, 64])
    P.asel(cst32[:, 128:192], ones[:, 0:64], [[-1, 64]], ALU.is_equal, 0.0, -64, 1)
    P.cp(R(Sh[:]), cst32[:, 128:192])
    P.asel(MsL[:], ones[0:CB, 0:CB], [[-1, CB]], ALU.is_gt, 0.0, 0, 1)
    rmask = P.sb("rmask", [128, 256])
    P.memset(rmask[:], 1.0)
    for c in range(256 // CB):
        P.memset(rmask[:, c * CB:c * CB + 1], 0.0)

    NBB = 256
    t1T = P.sb("t1T", [64, NTOK], BF16)
    a1T = P.sb("a1T", [64, NTOK], BF16)
    w2b = P.sb("w2b", [64, D], BF16)
    a2b = P.sb("a2b", [64, D], BF16)
    blk_mark = P.sb_off
    lw1 = P.sb("lw1", [128, KC, 2, 64], BF16)
    lw1p = P.sb("lw1p", [128, KC, 2, 64], BF16)
    lw1pp = P.sb("lw1pp", [128, KC, 2, 64], BF16)
    P.dma("pool", lw1[:, :, 0, :], b_w_w1.rearrange("(k p) c -> p k c", p=128), writes=[lw1[:]])
    P.dma("pool", lw1[:, :, 1, :], b_a_w1.rearrange("(k p) c -> p k c", p=128), writes=[lw1[:]])
    for j in range(2):
        P.tt(lw1p[:, :, j, :], lw1[:, :, j, :], bc(vecs[:, 4 + j, :], [128, KC, 64]), ALU.mult)
    P.tt(lw1pp[:], lw1[:], lw1p[:], ALU.subtract)
    P.dma("pool", w2b[:], b_w_w2[:, :], writes=[w2b[:]])
    P.dma("pool", a2b[:], b_a_w2[:, :], writes=[a2b[:]])
    col_ranges = [(PCOL + i * 512, 512) for i in range(4)] + [(S1COL, 16), (S2COL, 16)]
    for (cc0, n) in col_ranges:
        for j, dst in enumerate((t1T, a1T)):
            ps = P.ps(2)
            for kc in range(KC):
                P.mm(ps[0:64, 0:n], lw1pp[:, kc, j, :], hT[:, kc, cc0:cc0 + n], start=(kc == 0), stop=False)
                P.mm(ps[0:64, 0:n], lw1p[:, kc, j, :], hT[:, kc, cc0 - 1:cc0 - 1 + n], start=False, stop=(kc == KC - 1))
            P.act(dst[:, cc0:cc0 + n], ps[0:64, 0:n], AF.Tanh if j == 0 else AF.Copy)

    P.barrier()
    P.sb_off = blk_mark
    Wp = [P.sb("Wp0", [128, KC, 4, 128], BF16)] * 2
    Wq = P.sb("Wq", [128, KC, 4, 128], BF16)
    Wob = [P.sb("Wob0", [128, D], BF16)] * 2
    b_in_v = b_w_in.rearrange("(k p) (g c) -> p k g c", p=128, g=4)

    def load_pair_w(pr):
        for g_ in range(4):
            P.dma("pool", Wp[pr % 2][:, :, g_, :], b_in_v[:, :, g_, pr * 128:(pr + 1) * 128], writes=[Wp[pr % 2][:]])
        P.dma("pool", Wob[pr % 2][:], b_w_out[pr * 128:(pr + 1) * 128, :], writes=[Wob[pr % 2][:]])

    NCB = NBB // CB
    NX = 2 * NCB
    rkvT = P.sb("rkvT", [128, 3, NBB])
    zsB2 = [P.sb("zsB%d" % i, [128, NBB]) for i in range(2)]
    s2tmp = P.sb("s2tmp", [128, 2, NBB])
    lwT = P.sb("lwT", [128, NBB])
    aT = P.sb("aT", [128, NBB])
    cwv = P.sb("cwv", [128, NBB])
    eW = P.sb("eW", [128, 3, NBB])
    tmpB = P.sb("tmpB", [128, 4, NBB])
    kkT = P.sb("kkT", [128, NBB])
    k2T = P.sb("k2T", [128, NBB])
    arT2 = [P.sb("arT%d" % i, [128, 2, NBB]) for i in range(2)]
    bkT = P.sb("bkT", [128, 2, NBB])
    bkh = P.sb("bkh", [128, 2, NBB])
    Wc = P.sb("Wc", [128, NCB])
    rkb = P.sb("rkb", [128, NBB])
    vt = P.sb("vt", [CB, NCB, 128])
    bht = P.sb("bht", [CB, NCB, 128])
    kht = P.sb("kht", [CB, NCB, 128])
    XA = P.sb("XA", [CB, NX, 2 * CB])
    XB = P.sb("XB", [CB, NX, 2 * CB])
    MnB = [P.sb("MnB%d" % i, [CB, NX, CB]) for i in range(2)]
    MTTb = [P.sb("MTTb%d" % i, [CB, NX, 2, CB]) for i in range(2)]
    Rsb = [P.sb("Rsb%d" % i, [CB, 128]) for i in range(2)]
    Usb = [P.sb("Usb%d" % i, [CB, 128]) for i in range(2)]
    StP = [[P.sb("StP%d_%d" % (i, hd), [64, 64]) for hd in range(2)] for i in range(2)]
    StS = [[P.sb("StS%d_%d" % (i, hd), [64, 64]) for hd in range(2)] for i in range(2)]
    StB = [StP, StS, StS]
    stio = P.sb("stio", [64, 128])
    ar12 = [P.sb("ar1_%d" % i, [64, 2, NBB]) for i in range(2)]
    bk1 = P.sb("bk1", [64, 2, NBB])
    Wc1 = P.sb("Wc1", [64, NCB])
    sqB = P.sb("sqB", [128, 4, NBB])
    oTB = sqB[:, 2]
    ocB = s2tmp[:, 0]
    osB = sqB[:, 3]
    ogB = P.sb("ogB", [128, NBB], BF16)
    print("SBUF left after layer-B alloc:", P.sb_top - P.sb_off)
    blocksB = [(0, PCOL + i * NBB, NBB, CB, NBB // CB) for i in range(TP // NBB)] + \
              [(1, S1COL, 16, 16, 1), (2, S2COL, 16, 16, 1)]
    ENH = -float(np.exp(-0.5))

    load_pair_w(0)
    nblkB = 0
    for pr in range(8):
        W = Wp[pr % 2]
        WO = Wob[pr % 2]
        for g_ in range(4):
            P.tt(Wq[:, :, g_, :], W[:, :, g_, :], bc(vecs[:, g_, :], [128, KC, 128]), ALU.mult)
        P.tt(W[:], W[:], Wq[:], ALU.subtract)
        Wr = W
        spar = [0, 0, 0]
        cur_seq = -1
        for (seq, t0, NB, C, nch) in blocksB:
            nx = 2 * nch
            bpar = nblkB % 2
            nblkB += 1
            zsB, arT, ar1 = zsB2[bpar], arT2[bpar], ar12[bpar]
            if seq != cur_seq:
                cur_seq = seq
                spar[seq] = 0
                if seq == 0:
                    for hd in range(2):
                        P.cp(R(StB[0][0][hd][:]), zeros[0:64, 0:64], e="act")
                else:
                    P.dma("sp", stio[:].rearrange("v (h k) -> v h k", h=2),
                          wkv_s[seq - 1, 2 * pr:2 * pr + 2].rearrange("h v k -> v h k"), writes=[stio[:]])
                    for hd in range(2):
                        ps = P.ps("b1")
                        P.tr(ps[0:64, 0:64], stio[:, hd * 64:(hd + 1) * 64], ident[0:64, 0:64])
                        P.cp(R(StB[seq][0][hd][:]), ps[0:64, 0:64])
            for g_ in range(4):
                ps = P.ps("b1")
                for kc in range(KC):
                    P.mm(ps[:, 0:NB], Wr[:, kc, g_, :], hT[:, kc, t0:t0 + NB], start=(kc == 0), stop=False)
                    P.mm(ps[:, 0:NB], Wq[:, kc, g_, :], hT[:, kc, t0 - 1:t0 - 1 + NB], start=False, stop=(kc == KC - 1))
                if g_ < 3:
                    P.cp(rkvT[:, g_, 0:NB], ps[:, 0:NB], e="act")
                else:
                    P.act(zsB[:, 0:NB], ps[:, 0:NB], AF.Tanh, scale=0.5)
                    P.stt(zsB[:, 0:NB], zsB[:, 0:NB], 1.0, ps[:, 0:NB], ALU.add, ALU.mult)
            ps = P.ps("b1")
            P.mm(ps[:, 0:NB], w2b[:, pr * 128:(pr + 1) * 128], t1T[:, t0:t0 + NB])
            P.act(lwT[:, 0:NB], ps[:, 0:NB], AF.Tanh, scale=0.5, bias=hvec[:, 0, pr:pr + 1])
            P.ts(lwT[:, 0:NB], lwT[:, 0:NB], 0.5 * ENH, ALU.mult, 0.5 * ENH, ALU.add)
            ps = P.ps("b1")
            P.mm(ps[:, 0:NB], a2b[:, pr * 128:(pr + 1) * 128], a1T[:, t0:t0 + NB])
            P.act(aT[:, 0:NB], ps[:, 0:NB], AF.Tanh, scale=0.5, bias=hvec[:, 1, pr:pr + 1])
            P.ts(aT[:, 0:NB], aT[:, 0:NB], 0.5, ALU.mult, 0.5, ALU.add)
            rT = rkvT[:, 0, 0:NB]
            kT_ = rkvT[:, 1, 0:NB]
            vT_ = rkvT[:, 2, 0:NB]
            P.ts(kkT[:, 0:NB], kT_, vecs[:, V_KK, pr:pr + 1], ALU.mult)
            P.act(R(sqB[:, 0, 0:NB]), kkT[:, 0:NB], AF.Square)
            ps = P.ps("b1")
            P.mmr(ps[:, 0:NB], blk1[:, :], sqB[:, 0, 0:NB])
            P.act(tmpB[:, 1, 0:NB], ps[:, 0:NB], AF.Ln, bias=1e-6)
            P.act(tmpB[:, 1, 0:NB], tmpB[:, 1, 0:NB], AF.Exp, scale=-0.5)
            P.tt(kkT[:, 0:NB], kkT[:, 0:NB], tmpB[:, 1, 0:NB], ALU.mult)
            P.ts(tmpB[:, 2, 0:NB], aT[:, 0:NB], -1.0, ALU.add, vecs[:, V_KA, pr:pr + 1], ALU.mult)
            P.ts(tmpB[:, 2, 0:NB], tmpB[:, 2, 0:NB], 1.0, ALU.add)
            P.tt(k2T[:, 0:NB], kT_, tmpB[:, 2, 0:NB], ALU.mult)
            P.scan(cwv[:, 0:NB], rmask[:, 0:NB], lwT[:, 0:NB])
            P.act(eW[:, 0, 0:NB], cwv[:, 0:NB], AF.Exp)
            P.act(eW[:, 1, 0:NB], cwv[:, 0:NB], AF.Exp, scale=-1.0)
            P.tt(tmpB[:, 3, 0:NB], cwv[:, 0:NB], lwT[:, 0:NB], ALU.subtract)
            P.act(eW[:, 2, 0:NB], tmpB[:, 3, 0:NB], AF.Exp)
            P.stt(R(arT[:, 0, 0:NB]), kkT[:, 0:NB], -1.0, eW[:, 2, 0:NB], ALU.mult, ALU.mult)
            P.tt(R(arT[:, 1, 0:NB]), rT, eW[:, 0, 0:NB], ALU.mult)
            P.tt(tmpB[:, 0, 0:NB], kkT[:, 0:NB], aT[:, 0:NB], ALU.mult)
            P.tt(R(bkT[:, 0, 0:NB]), tmpB[:, 0, 0:NB], eW[:, 1, 0:NB], ALU.mult)
            P.tt(R(bkT[:, 1, 0:NB]), k2T[:, 0:NB], eW[:, 1, 0:NB], ALU.mult)
            ewc = eW[:, 0, 0:NB].rearrange("p (c i) -> p c i", c=nch)[:, :, C - 1]
            P.cp(Wc[:, 0:nch], ewc)
            bkv = bkT[:, :, 0:NB].rearrange("p a (c i) -> p a c i", c=nch)
            bhv = bkh[:, :, 0:NB].rearrange("p a (c i) -> p a c i", c=nch)
            for a_ in range(2):
                P.tt(bhv[:, a_], bkv[:, a_], bc(Wc[:, 0:nch], [128, nch, C]), ALU.mult)
            P.stt(R(sqB[:, 1, 0:NB]), rT, vecs[:, V_RK, pr:pr + 1], k2T[:, 0:NB], ALU.mult, ALU.mult)
            ps = P.ps("b1")
            P.mmr(ps[:, 0:NB], blk1[:, :], sqB[:, 1, 0:NB])
            P.tt(rkb[:, 0:NB], ps[:, 0:NB], vT_, ALU.mult)
            ps = P.ps("b1")
            P.mmr(ps[0:64, 0:2 * NB], Sh[:, :], arT[:, :, 0:NB])
            P.cp(R(ar1[:, :, 0:NB]), ps[0:64, 0:2 * NB].rearrange("p (a t) -> p a t", a=2), e="act")
            ps = P.ps("b1")
            P.mmr(ps[0:64, 0:2 * NB], Sh[:, :], bkT[:, :, 0:NB])
            P.cp(R(bk1[:, :, 0:NB]), ps[0:64, 0:2 * NB].rearrange("p (a t) -> p a t", a=2), e="act")
            ps = P.ps("b1")
            P.mm(ps[0:64, 0:nch], cst32[:, 128:192], Wc[:, 0:nch])
            P.cp(Wc1[:, 0:nch], ps[0:64, 0:nch])
            AR = [arT[0:64], ar1[:]]
            BK = [bkT[0:64], bk1[:]]
            WC = [Wc[0:64], Wc1[:]]
            for (src, dst) in ((vT_, vt), (bkh[:, 0, 0:NB], bht), (bkh[:, 1, 0:NB], kht)):
                ps = P.ps("b1")
                for c in range(nch):
                    P.tr(ps[0:C, c * 128:(c + 1) * 128], src[:, c * C:(c + 1) * C], ident[:, :])
                P.cp(R(dst[0:C, 0:nch, :]), ps[0:C, 0:nch * 128].rearrange("p (c d) -> p c d", c=nch), e="act")
            psN = P.ps("b1")
            for hd in range(2):
                hs = slice(hd * 64, (hd + 1) * 64)
                psA = P.ps("b1")
                psB_ = P.ps("b1")
                for c in range(nch):
                    csl = slice(c * C, (c + 1) * C)
                    x_ = hd * nch + c
                    P.mmr(psA[0:C, c * 2 * C:(c + 1) * 2 * C], BK[hd][:, 0, csl], AR[hd][:, :, csl])
                    P.mmr(psB_[0:C, c * 2 * C:(c + 1) * 2 * C], BK[hd][:, 1, csl], AR[hd][:, :, csl])
                    P.mmr(psN[0:C, x_ * C:(x_ + 1) * C], AR[hd][:, 0, csl], BK[hd][:, 0, csl])
                for (psx, dstx) in ((psA, XA), (psB_, XB)):
                    pv = psx[0:C, 0:nch * 2 * C].rearrange("p (c a i) -> p c a i", c=nch, a=2)
                    dv = dstx[0:C, hd * nch:(hd + 1) * nch, :].rearrange("p c (a i) -> p c a i", a=2)[:, :, :, 0:C]
                    mv = MXT[0:C, :].rearrange("p (a i) -> p a i", a=2)[:, :, 0:C].unsqueeze(1).broadcast_to([C, nch, 2, C])
                    P.tt(R(dv), pv, mv, ALU.mult)
            P.tt(R(MnB[0][0:C, 0:nx, 0:C]), psN[0:C, 0:nx * C].rearrange("p (x i) -> p x i", x=nx),
                 bcm(MsL[0:C, 0:C], [C, nx, C]), ALU.mult)
            gen_ = neumann(P, ident, XA[0:C, 0:nx, 0:C], None, MnB, MTTb, nx, C, pool="b1")
            while True:
                try:
                    next(gen_)
                except StopIteration as e_:
                    TTf = e_.value
                    break
            pso = P.psb[7]
            for c in range(nch):
                csl = slice(c * C, (c + 1) * C)
                Sc = StB[seq][spar[seq]]
                Sn = StB[seq][1 - spar[seq]]
                Rb = Rsb[c % 2]
                Ub = Usb[c % 2]
                ps = P.ps("b2")
                for hd in range(2):
                    hs = slice(hd * 64, (hd + 1) * 64)
                    x_ = hd * nch + c
                    P.mmr(ps[0:C, hs], AR[hd][:, 0, csl], Sc[hd][:, :], start=True, stop=False)
                    P.mmr(ps[0:C, hs], XB[0:C, x_, 0:C], vt[0:C, c, hs], start=False, stop=True)
                P.cp(R(Rb[0:C, :]), ps[0:C, 0:128], e="act")
                ps = P.ps("b2")
                for hd in range(2):
                    hs = slice(hd * 64, (hd + 1) * 64)
                    x_ = hd * nch + c
                    P.mmr(ps[0:C, hs], TTf[:, x_, :], Rb[0:C, hs])
                P.cp(R(Ub[0:C, :]), ps[0:C, 0:128], e="act")
                for hd in range(2):
                    hs = slice(hd * 64, (hd + 1) * 64)
                    x_ = hd * nch + c
                    oo = pso[hs, csl]
                    mmf = P.mmr if hd == 0 else P.mm
                    mmf(oo, Sc[hd][:, :], AR[hd][:, 1, csl], start=True, stop=False)
                    mmf(oo, Ub[0:C, hs], XA[0:C, x_, CB:CB + C], start=False, stop=False)
                    mmf(oo, vt[0:C, c, hs], XB[0:C, x_, CB:CB + C], start=False, stop=True)
                ps2 = P.ps("b2")
                for hd in range(2):
                    hs = slice(hd * 64, (hd + 1) * 64)
                    P.mmr(ps2[0:64, hs], bht[0:C, c, hs], Ub[0:C, hs], start=True, stop=False)
                    P.mmr(ps2[0:64, hs], kht[0:C, c, hs], vt[0:C, c, hs], start=False, stop=True)
                for hd in range(2):
                    hs = slice(hd * 64, (hd + 1) * 64)
                    P.stt(R(Sn[hd][:, :]), Sc[hd][:, :], WC[hd][:, c:c + 1], ps2[0:64, hs], ALU.mult, ALU.add)
                spar[seq] = 1 - spar[seq]
            P.cp(R(oTB[:, 0:NB]), pso[:, 0:NB], e="act")
            ps = P.ps("b2")
            P.mmr(ps[:, 0:NB], blk1[:, :], oTB[:, 0:NB])
            P.stt(ocB[:, 0:NB], ps[:, 0:NB], -1.0 / 64.0, oTB[:, 0:NB], ALU.mult, ALU.add)
            P.act(R(osB[:, 0:NB]), ocB[:, 0:NB], AF.Square)
            ps = P.ps("b2")
            P.mmr(ps[:, 0:NB], blk1[:, :], osB[:, 0:NB])
            P.act(s2tmp[:, 1, 0:NB], ps[:, 0:NB], AF.Ln, scale=1.0 / 64.0, bias=GN_EPS)
            P.act(s2tmp[:, 1, 0:NB], s2tmp[:, 1, 0:NB], AF.Exp, scale=-0.5)
            P.tt(ocB[:, 0:NB], ocB[:, 0:NB], s2tmp[:, 1, 0:NB], ALU.mult)
            P.ts(ocB[:, 0:NB], ocB[:, 0:NB], vecs[:, V_GW, pr:pr + 1], ALU.mult, vecs[:, V_GB, pr:pr + 1], ALU.add)
            P.tt(ocB[:, 0:NB], ocB[:, 0:NB], rkb[:, 0:NB], ALU.add)
            P.stt(ogB[:, 0:NB], ocB[:, 0:NB], 0.5, zsB[:, 0:NB], ALU.mult, ALU.mult)
            for tt0 in range(0, NB, 128):
                nt = min(128, NB - tt0)
                if seq == 0:
                    tile_i, prow = (t0 - PCOL + tt0) // 128, 0
                else:
                    tile_i, prow = 16, (seq - 1) * 32
                for nh in range(2):
                    ps = P.ps("b2")
                    P.mm(ps[prow:prow + nt, :], ogB[:, tt0:tt0 + nt], WO[:, nh * 512:(nh + 1) * 512])
                    xr = xres[tile_i][prow:prow + nt, nh * 512:(nh + 1) * 512]
                    P.tt(xr, xr, ps[prow:prow + nt, :], ALU.add)
            last = (t0 + NB == PCOL + TP) or seq > 0
            if last:
                Sf = StB[seq][spar[seq]]
                ps = P.ps("b2")
                for hd in range(2):
                    P.tr(ps[0:64, hd * 64:(hd + 1) * 64], Sf[hd][:, :], ident[0:64, 0:64])
                P.cp(stio[:, :], ps[0:64, 0:128])
                P.dma("sp", o_wkv[seq, 2 * pr:2 * pr + 2].rearrange("h v k -> v h k"),
                      stio[:].rearrange("v (h k) -> v h k", h=2), reads=[stio[:]], final=True)
        if pr + 1 < 8:
            load_pair_w(pr + 1)

    P.barrier()
    P.sb_off = blk_mark
    xn = [P.sb("xnF", [128, D])] * 2
    fnwb = P.sb("fnwb", [128, D])
    P.dma("sp", fnwb[:], final_norm_w.partition_broadcast(128), writes=[fnwb[:]])
    for i in range(NTILE):
        nt = tile_rows(i)
        xt = xres[i]
        xb = xn[0]
        sl = slice(i % 4, i % 4 + 1)
        P.act(xb[0:nt, :], xt[0:nt, :], AF.Square, accum_out=ssq[0:nt, sl])
        P.act(rstd[0:nt, sl], ssq[0:nt, sl], AF.Sqrt, scale=1.0 / D, bias=RMS_EPS)
        P.recip(rstd[0:nt, sl], rstd[0:nt, sl])
        P.stt(xb[0:nt, :], xt[0:nt, :], rstd[0:nt, sl], fnwb[0:nt, :], ALU.mult, ALU.mult)
        if i < 16:
            P.dma("sp", y_p[i * 128:(i + 1) * 128, :], xb[:, :], reads=[xb[:]], final=True)
        else:
            P.dma("sp", y_s[0:16, :], xb[0:16, :], reads=[xb[:]], final=True)
            P.dma("sp", y_s[16:32, :], xb[32:48, :], reads=[xb[:]], final=True)
    P.finish()
    return nc


_NC_CACHE = {}


def make_in_maps(inputs):
    g = lambda k: np.ascontiguousarray(np.asarray(inputs[k], dtype=np.float32))
    xp, xs = g("x_prompt"), g("x_sample")
    cc, sd, ss, sw = g("cache_conv_a"), g("state_delta_a"), g("state_shift_b"), g("state_wkv_b")
    shared = {
        "norm_w": g("norm_w"), "final_norm_w": g("final_norm_w"), "a_w_in": g("a_w_in")[0],
        "a_conv_w": g("a_conv_w")[0], "a_log": g("a_log")[0], "a_dt_bias": g("a_dt_bias")[0],
        "a_norm_w": g("a_norm_w")[0], "a_w_out": g("a_w_out")[0], "b_mu": g("b_mu")[0],
        "b_w_in": g("b_w_in")[0], "b_w0": g("b_w0")[0], "b_w_w1": g("b_w_w1")[0], "b_w_w2": g("b_w_w2")[0],
        "b_a0": g("b_a0")[0], "b_a_w1": g("b_a_w1")[0], "b_a_w2": g("b_a_w2")[0], "b_k_k": g("b_k_k")[0],
        "b_k_a": g("b_k_a")[0], "b_r_k": g("b_r_k")[0].reshape(-1), "b_gn_w": g("b_gn_w")[0],
        "b_gn_b": g("b_gn_b")[0], "b_w_out": g("b_w_out")[0],
    }
    maps = []
    for i in range(8):
        m = dict(shared)
        m["x_p"] = xp[i]
        m["x_s"] = np.ascontiguousarray(xs[2 * i:2 * i + 2].reshape(2 * TS, D))
        m["conv_s"] = np.ascontiguousarray(cc[0, 2 * i:2 * i + 2])
        m["delta_s"] = np.ascontiguousarray(sd[0, 2 * i:2 * i + 2])
        m["shift_s"] = np.ascontiguousarray(ss[0, 2 * i:2 * i + 2])
        m["wkv_s"] = np.ascontiguousarray(sw[0, 2 * i:2 * i + 2])
        maps.append(m)
    return maps


def kernel(**inputs):
    if "nc" not in _NC_CACHE:
        _NC_CACHE["nc"] = build()
    nc = _NC_CACHE["nc"]
    maps = make_in_maps(inputs)
    res = run_bass_kernel_spmd(nc, maps, core_ids=list(range(8)))
    R = res.results
    y_prompt = np.stack([R[i]["y_p"] for i in range(8)], 0)
    y_sample = np.concatenate([R[i]["y_s"].reshape(2, TS, D) for i in range(8)], 0)

    def pick(name, sl):
        return np.stack([R[i][name][sl] for i in range(8)], 0)[None] if isinstance(sl, int) else \
            np.concatenate([R[i][name][sl] for i in range(8)], 0)[None]

    p_conv, s_conv = pick("o_conv", 0), pick("o_conv", slice(1, 3))
    p_delta, s_delta = pick("o_delta", 0), pick("o_delta", slice(1, 3))
    p_shift, s_shift = pick("o_shift", 0), pick("o_shift", slice(1, 3))
    p_wkv, s_wkv = pick("o_wkv", 0), pick("o_wkv", slice(1, 3))
    return (y_prompt, y_sample, p_conv, p_delta, p_shift, p_wkv, s_conv, s_delta, s_shift, s_wkv)
```

```python
import numpy as np
import concourse.bass as bass
import concourse.mybir as mybir
from concourse.bass_utils import run_bass_kernel_spmd

F32 = mybir.dt.float32
BF16 = mybir.dt.bfloat16
AF = mybir.ActivationFunctionType
ALU = mybir.AluOpType
AX = mybir.AxisListType

D = 1024
KC = 8
TP = 2048
TS = 16
NTILE = 17
H_A = 8
RMS_EPS = 1e-6
GN_EPS = 64e-5
NEG = -30000.0


class Trk:
    __slots__ = ("lw", "rd", "dsem", "dcount")

    def __init__(self):
        self.lw = None
        self.rd = []
        self.dsem = None
        self.dcount = 0


def fsz(ap):
    n = 1
    for d in ap.shape[1:]:
        n *= d
    return n


class Prog:
    WINDOW = 4000
    LAT = 350.0
    LAT_SAME = 100.0
    PE_SWITCH = 0.0
    EPS = 150.0

    def __init__(self, nc):
        self.nc = nc
        self.h = {"pe": nc.tensor, "act": nc.scalar, "dve": nc.vector, "pool": nc.gpsimd, "sp": nc.sync}
        self.sem = {k: nc.alloc_semaphore("sem_" + k) for k in self.h}
        self.cnt = {k: 0 for k in self.h}
        self.known = {k: {} for k in self.h}
        self.trk = {}
        self.ops = []
        self.nps = 0
        self.npool = {}
        self.psb = []
        self.nsem = 0
        self.sb_off = nc.sbuf_base
        self.sb_top = nc.sbuf_top
        self.seg_start = 0
        self.sel = {}
        self.pecls = {}

    def sb(self, name, shape, dt=F32):
        n = 1
        for d in shape[1:]:
            n *= d
        nbytes = n * (2 if dt == BF16 else 4)
        off = (self.sb_off + 63) // 64 * 64
        assert off + nbytes <= self.sb_top, "SBUF overflow at %s: need %d have %d" % (name, nbytes, self.sb_top - off)
        t = self.nc.alloc_sbuf_tensor_at(name, list(shape), dt, offset=off)
        self.sb_off = off + nbytes
        self.trk[t.name] = Trk()
        return t

    def barrier(self):
        self.ops.append(("fence", None, None, (), 0.0, False, None))
        for t in self._all_trk():
            t.lw = None
            t.rd = []

    def _all_trk(self):
        for t in self.trk.values():
            if isinstance(t, list):
                for x in t:
                    yield x
            else:
                yield t

    def init_psum(self):
        for i in range(8):
            t = self.nc.alloc_psum_tensor("psb%d" % i, [128, 512], F32)
            self.trk["psb%d" % i] = Trk()
            self.psb.append(t)

    POOLS = {0: (0, 1, 2, 3, 4, 5, 6), 2: (0, 1, 2, 3, 4, 5, 6),
             "a1a": (0, 1), "a1m": (2,), "a1b": (3, 4), "a2": (5, 6),
             "b1": (0, 1, 2, 3, 4), "b2": (5, 6)}

    def ps(self, pool=0):
        banks = self.POOLS[pool]
        n = self.npool.get(pool, 0)
        self.npool[pool] = n + 1
        return self.psb[banks[n % len(banks)]]

    def split(self, tensor, n):
        self.trk[tensor.name] = [Trk() for _ in range(n)]

    def only(self, **sel):
        prog = self

        class _Ctx:
            def __enter__(self_):
                self_.old = dict(prog.sel)
                prog.sel.update(sel)

            def __exit__(self_, *a):
                prog.sel = self_.old
        return _Ctx()

    def _tks(self, ap):
        t = self.trk[ap.tensor.name]
        if isinstance(t, list):
            idx = self.sel.get(ap.tensor.name.rsplit("_", 1)[0])
            return [t[i] for i in idx] if idx is not None else list(t)
        return [t]

    def _record(self, kind, e, payload, reads, writes, cost, final=False):
        rt = []
        for a in reads:
            for t in self._tks(a):
                if t not in rt:
                    rt.append(t)
        wt = []
        for a in writes:
            for t in self._tks(a):
                if t not in wt:
                    wt.append(t)
        i = len(self.ops)
        preds = set()
        for t in rt:
            if t.lw is not None:
                preds.add(t.lw)
        for t in wt:
            if t.lw is not None:
                preds.add(t.lw)
            preds.update(t.rd)
        preds.discard(i)
        for t in rt:
            t.rd.append(i)
        for t in wt:
            t.lw = i
            t.rd = []
        t0 = (wt + rt)[0] if kind == "dma" else None
        self.ops.append((kind, e, payload, tuple(preds), float(cost), final, t0))
        return i

    def op(self, e, fn, reads, writes, cost=None):
        if cost is None:
            n = fsz(writes[0]) if writes else 64
            cost = {"act": 200.0 + 0.85 * n, "dve": 110.0 + 1.05 * n, "pool": 260.0 + 1.0 * n, "pe": 150.0}[e]
        return self._record("op", e, fn, reads, writes, cost)

    def dma(self, q, out, in_, reads=(), writes=(), final=False, **kw):
        return self._record("dma", q, (out, in_, kw), reads, writes, 150.0 if q == "sp" else 1200.0, final)

    def _wait(self, e, key, semh, val):
        k = self.known[e]
        if k.get(key, 0) < val:
            self.h[e].wait_ge(semh, val)
            k[key] = val

    def _schedule(self, lo, hi):
        ops = self.ops
        n = hi - lo
        indeg = [0] * n
        succ = [[] for _ in range(n)]
        for i in range(lo, hi):
            ps_ = [p for p in ops[i][3] if p >= lo]
            indeg[i - lo] = len(ps_)
            for p in ps_:
                succ[p - lo].append(i)
        blev = [0.0] * n
        for k in range(n - 1, -1, -1):
            o = ops[lo + k]
            c = o[4] + (2500.0 if o[0] == "dma" else 0.0)
            m = 0.0
            for j in succ[k]:
                v = blev[j - lo] + self.LAT
                if v > m:
                    m = v
            blev[k] = c + m
        dready = [0.0] * n
        etime = {k: 0.0 for k in self.h}
        ready = {k: [] for k in self.h}
        for i in range(lo, hi):
            if indeg[i - lo] == 0:
                ready[ops[i][1]].append(i)
        order = []
        done = [False] * n
        lastcls = None
        minp = lo
        W = self.WINDOW
        EPS = self.EPS
        while len(order) < n:
            while minp < hi and done[minp - lo]:
                minp += 1
            lim = minp + W
            best = None
            for e, lst in ready.items():
                if not lst:
                    continue
                te = etime[e]
                cand = None
                for i in lst:
                    if i >= lim:
                        continue
                    dr = dready[i - lo]
                    st = te if dr <= te + EPS else dr
                    if e == "pe" and self.pecls.get(i) != lastcls:
                        st += self.PE_SWITCH
                    key = (st, -blev[i - lo], i)
                    if cand is None or key < cand:
                        cand = key
                if cand is not None and (best is None or cand < best[0]):
                    best = (cand, e)
            (st, _, i), e = best
            st = max(st, dready[i - lo], etime[e])
            ready[e].remove(i)
            kind = ops[i][0]
            cost = ops[i][4]
            if e == "pe":
                cl = self.pecls.get(i)
                if cl != lastcls:
                    cost += self.PE_SWITCH
                lastcls = cl
            etime[e] = st + cost
            f = st + cost + (2500.0 if kind == "dma" else 0.0)
            done[i - lo] = True
            order.append(i)
            for j in succ[i - lo]:
                ej = ops[j][1]
                v = f + (0.0 if (e == "pe" and ej == "pe") else (self.LAT_SAME if ej == e else self.LAT))
                if v > dready[j - lo]:
                    dready[j - lo] = v
                indeg[j - lo] -= 1
                if indeg[j - lo] == 0:
                    ready[ops[j][1]].append(j)
        return order, max(etime.values())

    def finish(self):
        ops = self.ops
        bounds = [i for i, o in enumerate(ops) if o[0] == "fence"] + [len(ops)]
        needs_inc = [False] * len(ops)
        info = {}
        clock = {}
        finals = []
        lo = 0
        est_total = 0.0
        nwait = 0
        last_inc = {k: None for k in self.h}
        for b in bounds:
            order, est = self._schedule(lo, b)
            est_total += est
            pos = {i: k for k, i in enumerate(order)}
            kept = {}
            lastop = {}
            for i in order:
                kind, e = ops[i][0], ops[i][1]
                if kind == "op":
                    lastop[e] = i
                best = {}
                keep = []
                for p in ops[i][3]:
                    if p < lo:
                        continue
                    if ops[p][0] != "op":
                        keep.append(p)
                        continue
                    f = ops[p][1]
                    if f == "pe" and e == "pe":
                        continue
                    if f not in best or pos[p] > pos[best[f]]:
                        best[f] = p
                for p in best.values():
                    needs_inc[p] = True
                    keep.append(p)
                kept[i] = keep
            for i in lastop.values():
                needs_inc[i] = True
            for i in order:
                kind, e, payload, preds, cost, final, t0 = ops[i]
                kn = self.known[e]
                for p in sorted(kept[i], key=lambda x: pos[x]):
                    if p < lo:
                        continue
                    pi = info[p]
                    if pi[0] == "op":
                        f, c = pi[1], pi[2]
                        if f == "pe" and e == "pe":
                            continue
                        if kn.get(f, 0) < c:
                            self.h[e].wait_ge(self.sem[f], c)
                            nwait += 1
                            kn[f] = c
                            for g, v in clock[p].items():
                                if kn.get(g, 0) < v:
                                    kn[g] = v
                    else:
                        if kn.get(pi[3], 0) < pi[2]:
                            self.h[e].wait_ge(pi[1], pi[2])
                            nwait += 1
                            kn[pi[3]] = pi[2]
                if kind == "op":
                    ins = payload(self.h[e])
                    if needs_inc[i]:
                        self.cnt[e] += 1
                        ins.then_inc(self.sem[e], 1)
                        info[i] = ("op", e, self.cnt[e])
                        snap = {g: v for g, v in kn.items() if g in self.h}
                        snap[e] = self.cnt[e]
                        clock[i] = snap
                    else:
                        info[i] = ("op", e, self.cnt[e] + 1)
                        clock[i] = {}
                else:
                    out, in_, kw = payload
                    if t0.dsem is None:
                        t0.dsem = self.nc.alloc_semaphore("dsem%d" % self.nsem)
                        self.nsem += 1
                    ins = self.h[e].dma_start(out=out, in_=in_, **kw)
                    t0.dcount += 16
                    ins.then_inc(t0.dsem, 16)
                    info[i] = ("dma", t0.dsem, t0.dcount, "d%d" % id(t0))
                    if final:
                        finals.append(info[i])
            lo = b + 1
            if b < len(ops):
                for e in self.h:
                    for f in self.h:
                        if f != e and self.cnt[f] > 0:
                            self._wait(e, f, self.sem[f], self.cnt[f])
                    for t in self._all_trk():
                        if t.dsem is not None and t.dcount > 0:
                            self._wait(e, "d%d" % id(t), t.dsem, t.dcount)
        fmax = {}
        for (_, semh, c, key) in finals:
            if key not in fmax or c > fmax[key][1]:
                fmax[key] = (semh, c)
        for key, (semh, c) in fmax.items():
            self._wait("sp", key, semh, c)
        print("scheduler estimate: %.1f us, %d ops, %d waits, incs %s" % (est_total / 1e3, len(ops), nwait, dict(self.cnt)))

    def mmr(self, out, lhsT, rhs, start=True, stop=True):
        return self.mm(out, R(lhsT), R(rhs), start=start, stop=stop)

    def mm(self, out, lhsT, rhs, start=True, stop=True):
        passes = 4.0 if rhs.dtype == F32 else 1.0
        cost = 70.0 + passes * 0.42 * (fsz(rhs) + min(fsz(lhsT), 128))
        i = self.op("pe", lambda h: h.matmul(out, lhsT=lhsT, rhs=rhs, start=start, stop=stop),
                    [lhsT, rhs], [out], cost=cost)
        self.pecls[i] = str(rhs.dtype)
        return i

    def tr(self, out, in_, ident):
        i = self.op("pe", lambda h: h.transpose(out, in_, ident), [in_, ident], [out], cost=160.0)
        self.pecls[i] = "tr"
        return i

    def act(self, out, in_, func, e="act", **kw):
        rd = [in_] + [v for v in kw.values() if hasattr(v, "tensor")]
        wr = [out]
        if "accum_out" in kw:
            wr.append(kw["accum_out"])
            rd.remove(kw["accum_out"])
        return self.op("act", lambda h: h.activation(out=out, in_=in_, func=func, **kw), rd, wr)

    def tt(self, out, in0, in1, op, e="dve"):
        return self.op(e, lambda h: h.tensor_tensor(out=out, in0=in0, in1=in1, op=op), [in0, in1], [out])

    def ts(self, out, in0, s1, op0, s2=None, op1=None, e="dve"):
        rd = [in0] + [v for v in (s1, s2) if hasattr(v, "tensor")]
        if op1 is None:
            return self.op(e, lambda h: h.tensor_scalar(out=out, in0=in0, scalar1=s1, scalar2=None, op0=op0),
                           rd, [out])
        return self.op(e, lambda h: h.tensor_scalar(out=out, in0=in0, scalar1=s1, scalar2=s2, op0=op0, op1=op1),
                       rd, [out])

    def stt(self, out, in0, scalar, in1, op0, op1):
        rd = [in0, in1] + ([scalar] if hasattr(scalar, "tensor") else [])
        return self.op("dve", lambda h: h.scalar_tensor_tensor(out=out, in0=in0, scalar=scalar, in1=in1,
                                                                 op0=op0, op1=op1), rd, [out])

    def cp(self, out, in_, e="dve"):
        if e == "act":
            return self.act(out, in_, AF.Copy)
        return self.op(e, lambda h: h.tensor_copy(out=out, in_=in_), [in_], [out])

    def scan(self, out, d0, d1):
        return self.op("dve", lambda h: h.tensor_tensor_scan(out=out, data0=d0, data1=d1, initial=0.0,
                                                              op0=ALU.mult, op1=ALU.add),
                       [d0, d1], [out], cost=110.0 + 2.1 * fsz(out))

    def rsqrt_pool(self, out, in_, mhalf):
        return self.op("pool", lambda h: h.tensor_tensor(out=out, in0=in_, in1=mhalf, op=ALU.pow), [in_, mhalf], [out])

    def recip(self, out, in_):
        return self.op("dve", lambda h: h.reciprocal(out=out, in_=in_), [in_], [out], cost=110.0 + 3.0 * fsz(out))

    def memset(self, ap, val, e="pool"):
        return self.op(e, lambda h: h.memset(ap, val), [], [ap])

    def asel(self, out, in_, pattern, cmp, fill, base, cm):
        return self.op("pool", lambda h: h.affine_select(out=out, in_=in_, pattern=pattern, compare_op=cmp,
                                                          fill=fill, base=base, channel_multiplier=cm),
                       [in_], [out])


F32R = mybir.dt.float32r


def R(ap):
    return ap.bitcast(F32R)


def neumann(P, ident, MT0, M0, Mbuf, MTT, nx, C, pool=0):
    L = {128: 7, 64: 6, 16: 4}[C]
    G = 512 // (2 * C)

    def grp3(ps, n, w):
        return ps[0:C, 0:n * w].rearrange("p (x i) -> p x i", x=n)

    psa = P.ps(pool)
    psb = P.ps(pool)
    for x in range(nx):
        P.mm(psa[0:C, x * C:(x + 1) * C], R(MT0[:, x, :]), R(Mbuf[0][0:C, x, 0:C]))
        P.mm(psb[0:C, x * C:(x + 1) * C], R(Mbuf[0][0:C, x, 0:C]), R(MT0[:, x, :]))
    P.cp(R(Mbuf[1][0:C, 0:nx, 0:C]), grp3(psa, nx, C), e="act")
    P.cp(R(MTT[0][0:C, 0:nx, 0, 0:C]), grp3(psb, nx, C), e="act")
    P.tt(R(MTT[0][0:C, 0:nx, 1, 0:C]), MT0, bcm(ident[0:C, 0:C], [C, nx, C]), ALU.add)
    yield
    cm, ct = 1, 0
    for lev in range(2, L + 1):
        last = lev == L
        Mc = Mbuf[cm]
        cur = MTT[ct]
        nxt = MTT[1 - ct]
        if not last:
            psa = P.ps(pool)
            for x in range(nx):
                P.mm(psa[0:C, x * C:(x + 1) * C], R(cur[0:C, x, 0, 0:C]), R(Mc[0:C, x, 0:C]))
        for x0 in range(0, nx, G):
            n = min(G, nx - x0)
            psx = P.ps(pool)
            for j in range(n):
                x = x0 + j
                if last:
                    P.mm(psx[0:C, j * C:(j + 1) * C], R(Mc[0:C, x, 0:C]), R(cur[0:C, x, 1, 0:C]))
                else:
                    P.mm(psx[0:C, j * 2 * C:(j + 1) * 2 * C], R(Mc[0:C, x, 0:C]), R(cur[0:C, x, :, 0:C]))
            if last:
                P.tt(R(nxt[0:C, x0:x0 + n, 1, 0:C]), grp3(psx, n, C), cur[0:C, x0:x0 + n, 1, 0:C], ALU.add)
            else:
                pv = psx[0:C, 0:n * 2 * C].rearrange("p (x a i) -> p x a i", x=n, a=2)
                P.cp(R(nxt[0:C, x0:x0 + n, 0, 0:C]), pv[:, :, 0, :], e="act")
                P.tt(R(nxt[0:C, x0:x0 + n, 1, 0:C]), pv[:, :, 1, :], cur[0:C, x0:x0 + n, 1, 0:C], ALU.add)
        if not last:
            P.cp(R(Mbuf[1 - cm][0:C, 0:nx, 0:C]), grp3(psa, nx, C), e="act")
        cm = 1 - cm
        ct = 1 - ct
        yield
    return MTT[ct][0:C, 0:nx, 1, 0:C]


def bc(ap, shape):
    return ap.unsqueeze(len(ap.shape)).broadcast_to(list(shape))


def bcm(ap, shape):
    return ap.unsqueeze(1).broadcast_to(list(shape))


PCOL = 1
S1COL = TP + 1 + 1
S2COL = S1COL + 32
NTOK = S2COL + 16 + 1
NBA = 256
CA = 128
NCH = TP // CA + 2


def build(stop=None):
    nc = bass.Bass("TRN2", target_bir_lowering=False)
    P = Prog(nc)
    P.init_psum()

    def din(name, shape):
        return nc.dram_tensor(name, list(shape), F32, kind="ExternalInput").ap()

    def dout(name, shape):
        return nc.dram_tensor(name, list(shape), F32, kind="ExternalOutput").ap()

    x_p = din("x_p", [TP, D])
    x_s = din("x_s", [2 * TS, D])
    conv_s = din("conv_s", [2, 3, 4096])
    delta_s = din("delta_s", [2, 8, 128, 256])
    shift_s = din("shift_s", [2, D])
    wkv_s = din("wkv_s", [2, 16, 64, 64])
    norm_w = din("norm_w", [2, D])
    final_norm_w = din("final_norm_w", [D])
    a_w_in = din("a_w_in", [D, 6160])
    a_conv_w = din("a_conv_w", [4, 4096])
    a_log = din("a_log", [8])
    a_dt_bias = din("a_dt_bias", [8])
    a_norm_w = din("a_norm_w", [256])
    a_w_out = din("a_w_out", [2048, D])
    b_mu = din("b_mu", [6, D])
    b_w_in = din("b_w_in", [D, 4096])
    b_w0 = din("b_w0", [D])
    b_w_w1 = din("b_w_w1", [D, 64])
    b_w_w2 = din("b_w_w2", [64, D])
    b_a0 = din("b_a0", [D])
    b_a_w1 = din("b_a_w1", [D, 64])
    b_a_w2 = din("b_a_w2", [64, D])
    b_k_k = din("b_k_k", [D])
    b_k_a = din("b_k_a", [D])
    b_r_k = din("b_r_k", [D])
    b_gn_w = din("b_gn_w", [D])
    b_gn_b = din("b_gn_b", [D])
    b_w_out = din("b_w_out", [D, D])

    y_p = dout("y_p", [TP, D])
    y_s = dout("y_s", [2 * TS, D])
    o_conv = dout("o_conv", [3, 3, 4096])
    o_delta = dout("o_delta", [3, 8, 128, 256])
    o_shift = dout("o_shift", [3, D])
    o_wkv = dout("o_wkv", [3, 16, 64, 64])
    dbg = dout("dbg", [NTILE * 128, D]) if stop else None

    ident = P.sb("ident", [128, 128])
    ones = P.sb("ones", [128, 128])
    mones = P.sb("mones", [128, 128])
    zeros = P.sb("zeros", [128, 128])
    Utri = P.sb("Utri", [128, 128])
    NEGT = P.sb("NEGT", [128, 128])
    MsT = P.sb("MsT", [128, 128])
    P.memset(ones[:], 1.0)
    ones_r = P.sb("ones_r", [128, 128])
    P.cp(R(ones_r[:]), ones[:], e="act")
    P.memset(mones[:], -1.0)
    P.memset(zeros[:], 0.0)
    P.asel(ident[:], ones[:], [[-1, 128]], ALU.is_equal, 0.0, 0, 1)
    P.asel(Utri[:], ones[:, :], [[1, 128]], ALU.is_ge, 0.0, 0, -1)
    P.asel(NEGT[:], zeros[:, :], [[1, 128]], ALU.is_ge, NEG, 0, -1)
    P.asel(MsT[:], ones[:, :], [[1, 128]], ALU.is_gt, 0.0, 0, -1)

    xres = [P.sb("xres%d" % i, [128, D]) for i in range(NTILE)]
    nw = P.sb("nw", [128, 2, KC])
    fnw = P.sb("fnw", [128, KC])
    P.dma("sp", nw[:], norm_w.rearrange("l (k p) -> p l k", p=128), writes=[nw[:]], allow_slow_non_contiguous=True)
    P.dma("sp", fnw[:], final_norm_w.rearrange("(k p) -> p k", p=128), writes=[fnw[:]],
          allow_slow_non_contiguous=True)
    ssq = P.sb("ssq", [128, 4])
    rstd = P.sb("rstd", [128, 4])
    phase_mark = P.sb_off
    hT = P.sb("hT", [128, KC, NTOK], BF16)
    xn = [P.sb("xn0", [128, D])] * 2

    def tile_rows(i):
        return 128 if i < 16 else 48

    def tile_col(i):
        return PCOL + i * 128 if i < 16 else S1COL

    def norm_to_hT(layer, hT):
        for i in range(NTILE):
            nt = tile_rows(i)
            xt = xres[i]
            xb = xn[i % 2]
            sl = slice(i % 4, i % 4 + 1)
            P.act(xb[0:nt, :], xt[0:nt, :], AF.Square, accum_out=ssq[0:nt, sl])
            P.act(rstd[0:nt, sl], ssq[0:nt, sl], AF.Sqrt, scale=1.0 / D, bias=RMS_EPS)
            P.recip(rstd[0:nt, sl], rstd[0:nt, sl])
            P.ts(xb[0:nt, :], xt[0:nt, :], rstd[0:nt, sl], ALU.mult)
            c0 = tile_col(i)
            for half in range(2):
                ps = P.ps()
                for j in range(4):
                    kc = half * 4 + j
                    P.tr(ps[:, j * 128:j * 128 + nt], xb[0:nt, kc * 128:(kc + 1) * 128], ident[0:nt, 0:nt])
                pv = ps[:, :].rearrange("p (j t) -> p j t", j=4)[:, :, 0:nt]
                P.tt(hT[:, half * 4:half * 4 + 4, c0:c0 + nt], pv,
                     bc(nw[:, layer, half * 4:half * 4 + 4], [128, 4, nt]), ALU.mult)

    for i in range(NTILE):
        if i < 16:
            P.dma("sp", xres[i][:], x_p[i * 128:(i + 1) * 128, :], writes=[xres[i][:]])
        else:
            P.memset(xres[i][:], 0.0)
            P.dma("sp", xres[i][0:16, :], x_s[0:16, :], writes=[xres[i][:]])
            P.dma("sp", xres[i][32:48, :], x_s[16:32, :], writes=[xres[i][:]])
    P.memset(hT[:, :, 0:1], 0.0)
    norm_to_hT(0, hT)

    blocks = [(0, PCOL + i * NBA, NBA, CA, NBA // CA, i * (NBA // CA)) for i in range(TP // NBA)] + \
             [(1, S1COL, 16, 16, 1, NCH - 2), (2, S2COL, 16, 16, 1, NCH - 1)]

    cwt = P.sb("cwt", [32, 4, 128])
    cw = P.sb("cw", [128, 4, 32])
    P.dma("sp", cwt[:], a_conv_w.rearrange("t (g c) -> g t c", c=128), writes=[cwt[:]])
    ps = P.ps()
    for t in range(4):
        P.tr(ps[:, t * 32:(t + 1) * 32], cwt[:, t, :], ident[0:32, 0:32])
    P.cp(cw[:].rearrange("p t g -> p (t g)"), ps[:, 0:128])
    halo_all = P.sb("halo_all", [128, 2, 3, 32])
    hrow = P.sb("hrow", [96, 2, 128])
    for s in range(2):
        P.dma("sp", hrow[:, s, :], conv_s[s].rearrange("t (g c) -> (t g) c", c=128), writes=[hrow[:]])
    ps = P.ps()
    for s in range(2):
        P.tr(ps[:, s * 96:(s + 1) * 96], hrow[:, s, :], ident[0:96, 0:96])
    P.cp(halo_all[:].rearrange("p s t g -> p (s t g)"), ps[:, 0:192])
    fin_all = P.sb("fin_all", [128, 3, 3, 32])
    anw = P.sb("anw", [128, 2])
    P.dma("sp", anw[:], a_norm_w.rearrange("(h p) -> p h", p=128), writes=[anw[:]], allow_slow_non_contiguous=True)
    P.ts(anw[:], anw[:], 0.5, ALU.mult)

    wba = P.sb("wba", [128, KC, 16], BF16)
    P.dma("pool", wba[:], a_w_in.rearrange("(k p) c -> p k c", p=128)[:, :, 6144:6160], writes=[wba[:]])
    NCP = TP // CA
    BA = P.sb("BA", [CA, NCH, 16])
    P.memset(BA[:], 0.0)
    ps = P.ps()
    for c in range(NCP):
        for kc in range(KC):
            P.mm(ps[0:CA, c * 16:(c + 1) * 16], hT[:, kc, PCOL + c * CA:PCOL + (c + 1) * CA], wba[:, kc, :],
                 start=(kc == 0), stop=(kc == KC - 1))
    P.cp(BA[:, 0:NCP, :].rearrange("p c k -> p (c k)"), ps[0:CA, 0:NCP * 16])
    ps = P.ps()
    for s, sc in enumerate((S1COL, S2COL)):
        for kc in range(KC):
            P.mm(ps[0:16, s * 16:(s + 1) * 16], hT[:, kc, sc:sc + 16], wba[:, kc, :],
                 start=(kc == 0), stop=(kc == KC - 1))
    P.cp(BA[0:16, NCP:NCP + 2, :].rearrange("p c k -> p (c k)"), ps[0:16, 0:32])
    alg = P.sb("alg", [CA, 8])
    dtb = P.sb("dtb", [CA, 8])
    P.dma("sp", alg[:], a_log.partition_broadcast(CA), writes=[alg[:]])
    P.dma("sp", dtb[:], a_dt_bias.partition_broadcast(CA), writes=[dtb[:]])
    P.act(alg[:], alg[:], AF.Exp)
    P.ts(alg[:], alg[:], -1.0, ALU.mult)
    beta = P.sb("beta", [CA, NCH, 8])
    gg = P.sb("gg", [CA, NCH, 8])
    Gc = P.sb("Gc", [CA, NCH, 8])
    Glb = P.sb("Glb", [128, NCH, 8])
    gl = P.sb("gl", [128, NCH, 8])
    eG = P.sb("eG", [CA, NCH, 8])
    bG = P.sb("bG", [CA, NCH, 8])
    dte = P.sb("dte", [CA, NCH, 8])
    P.act(beta[:], BA[:, :, 0:8], AF.Sigmoid)
    P.tt(gg[:], BA[:, :, 8:16], bcm(dtb[:], [CA, NCH, 8]), ALU.add)
    P.act(gg[:], gg[:], AF.Exp)
    P.act(gg[:], gg[:], AF.Ln, bias=1.0)
    P.tt(gg[:], gg[:], bcm(alg[:], [CA, NCH, 8]), ALU.mult)
    psG = P.ps()
    psL = P.ps()
    g2 = gg[:].rearrange("p c k -> p (c k)")
    GP = NCP * 8
    P.mm(psG[0:CA, 0:GP], Utri[0:CA, 0:CA], g2[:, 0:GP])
    P.mm(psG[0:16, GP:GP + 16], Utri[0:16, 0:16], g2[0:16, GP:GP + 16])
    P.mm(psL[:, 0:GP], ones[0:CA, :], g2[:, 0:GP])
    P.mm(psL[:, GP:GP + 16], ones[0:16, :], g2[0:16, GP:GP + 16])
    P.memset(Gc[:], 0.0)
    P.cp(Gc[:, 0:NCP, :].rearrange("p c k -> p (c k)"), psG[0:CA, 0:GP])
    P.cp(Gc[0:16, NCP:NCP + 2, :].rearrange("p c k -> p (c k)"), psG[0:16, GP:GP + 16])
    P.cp(Glb[:].rearrange("p c k -> p (c k)"), psL[:, 0:GP + 16])
    P.act(gl[:], Glb[:], AF.Exp)
    P.act(eG[:], Gc[:], AF.Exp)
    P.tt(bG[:], beta[:], eG[:], ALU.mult)
    hbeta = eG
    P.ts(hbeta[:], beta[:], 0.5, ALU.mult)
    P.tt(dte[:], Glb[0:CA], Gc[:], ALU.subtract)
    P.act(dte[:], dte[:], AF.Exp)

    Wh = [P.sb("Wh0", [128, KC, 768], BF16)] * 2
    Wo = [P.sb("Wo0", [128, 2, D], BF16)] * 2
    w_in_v = a_w_in.rearrange("(k p) c -> p k c", p=128)

    def load_head_w(h):
        sl = h % 2
        for (c0, n, o) in ((h * 128, 128, 0), (1024 + h * 128, 128, 128), (2048 + h * 256, 256, 256),
                           (4096 + h * 256, 256, 512)):
            P.dma("pool", Wh[sl][:, :, o:o + n], w_in_v[:, :, c0:c0 + n], writes=[Wh[sl][:]])

    def load_head_wo(h):
        sl = h % 2
        P.dma("pool", Wo[sl][:], a_w_out[h * 256:(h + 1) * 256, :].rearrange("(hh p) c -> p hh c", p=128),
              writes=[Wo[sl][:]])

    NB_ = NBA
    NC_ = NBA // CA
    pre = P.sb("pre", [128, 4, NB_ + 3])
    acc = P.sb("acc", [128, 4, NB_])
    P.split(acc, 4)
    P.split(pre, 4)
    qkv = P.sb("qkv", [128, 4, NB_])
    zs2 = [P.sb("zs%d" % i, [128, 2, NB_]) for i in range(2)]
    sqr = P.sb("sqr", [128, 2, NB_])
    sq = sqr
    rq = acc[:, 2:4]
    oT = P.sb("oTp", [128, 2, NB_])
    osq = P.sb("osq2", [128, 2, NB_])
    qT = P.sb("qT", [128, NB_])
    kT = P.sb("kT", [128, NB_])
    kbT = P.sb("kbT", [128, NB_])
    qgT2 = [P.sb("qgT%d" % i, [128, NB_]) for i in range(2)]
    dg = P.sb("dg", [128, 2, NB_])
    eGb = P.sb("eGb", [128, NB_])
    betab = P.sb("betab", [128, NB_])
    kbeG2 = [P.sb("kbeG%d" % i, [CA, NC_, 128]) for i in range(2)]
    ktk2 = [P.sb("ktk%d" % i, [CA, NC_, 128]) for i in range(2)]
    vb2 = [P.sb("vb%d" % i, [CA, NC_, 256]) for i in range(2)]
    gU = P.sb("gU", [CA, NC_, CA])
    DT = P.sb("DT", [CA, NC_, CA])
    DTs = P.sb("DTs", [CA, NC_, CA])
    qkT2 = [P.sb("qkT%d" % i, [CA, NC_, CA]) for i in range(2)]
    Mn = [P.sb("Mn%d" % i, [CA, NC_, CA]) for i in range(2)]
    MT0a2 = [P.sb("MT0a%d" % i, [CA, NC_, CA]) for i in range(2)]
    MTTa = [P.sb("MTTa%d" % i, [CA, NC_, 2, CA]) for i in range(2)]
    u0 = P.sb("u0", [CA, NC_, 256])
    wkT = P.sb("wkT", [128, NB_])
    uu = [P.sb("uu%d" % i, [CA, 256]) for i in range(2)]
    Sp_ = [P.sb("Sp%d" % i, [128, 256]) for i in range(2)]
    Ss_ = [P.sb("Ss%d" % i, [128, 256]) for i in range(2)]
    Sst = [Sp_, Ss_, Ss_]
    ors = P.sb("ors", [128, NB_])
    og = P.sb("og", [128, 2, NB_], BF16)
    print("SBUF left after layer-A alloc:", P.sb_top - P.sb_off)

    DONE = object()

    def A_s1(it, h, blk):
        (seq, t0, NB, C, nch, c0) = blk
        par = it % 2
        W = Wh[0]
        ggrp = (h, 8 + h, 16 + 2 * h, 17 + 2 * h)
        zs, qgT, ktk, qkT = zs2[par], qgT2[par], ktk2[par], qkT2[par]
        kbeG, vb, MT0a = kbeG2[par], vb2[par], MT0a2[par]
        first_of_seq = (seq == 0 and t0 == PCOL) or seq > 0
        if seq == 0 and t0 == PCOL:
            load_head_w(h)
        if first_of_seq:
            if seq == 0:
                P.memset(pre[:, :, 0:3], 0.0)
            else:
                for gi, g in enumerate(ggrp):
                    with P.only(pre=[gi]):
                        P.cp(pre[:, gi, 0:3], halo_all[:, seq - 1, :, g], e="pool")
        for m in range(6):
            ps = P.ps("a1a")
            for kc in range(KC):
                P.mm(ps[:, 0:NB], W[:, kc, m * 128:(m + 1) * 128], hT[:, kc, t0:t0 + NB],
                     start=(kc == 0), stop=(kc == KC - 1))
            if m < 4:
                with P.only(pre=[m]):
                    P.cp(pre[:, m, 3:3 + NB], ps[:, 0:NB], e="act")
            else:
                P.act(zs[:, m - 4, 0:NB], ps[:, 0:NB], AF.Tanh, scale=0.5)
                P.stt(zs[:, m - 4, 0:NB], zs[:, m - 4, 0:NB], 1.0, ps[:, 0:NB], ALU.add, ALU.mult)
            yield
        for gi, g in enumerate(ggrp):
            with P.only(acc=[gi], pre=[gi]):
                P.act(acc[:, gi, 0:NB], pre[:, gi, 3:3 + NB], AF.Copy, scale=cw[:, 3, g:g + 1])
                for tap in (2, 1, 0):
                    P.stt(acc[:, gi, 0:NB], pre[:, gi, tap:tap + NB], cw[:, tap, g:g + 1], acc[:, gi, 0:NB],
                          ALU.mult, ALU.add)
            yield
        P.act(qkv[:, :, 0:NB], acc[:, :, 0:NB], AF.Tanh, scale=0.5)
        P.stt(qkv[:, :, 0:NB], qkv[:, :, 0:NB], 1.0, acc[:, :, 0:NB], ALU.add, ALU.mult)
        last = (t0 + NB == PCOL + TP) or seq > 0
        for gi, g in enumerate(ggrp):
            with P.only(pre=[gi]):
                if last:
                    P.cp(fin_all[:, seq, :, g], pre[:, gi, NB:NB + 3], e="pool")
                else:
                    P.cp(pre[:, gi, 0:3], pre[:, gi, NB:NB + 3], e="pool")
        yield
        P.act(R(sq[:, :, 0:NB]), qkv[:, 0:2, 0:NB], AF.Square)
        for j in range(2):
            ps = P.ps("a1m")
            P.mmr(ps[:, 0:NB], ones_r[:, :], sq[:, j, 0:NB])
            with P.only(acc=[2 + j]):
                if j == 0:
                    P.act(rq[:, j, 0:NB], ps[:, 0:NB], AF.Ln, scale=128.0, bias=512.0 * 1e-6)
                else:
                    P.act(rq[:, j, 0:NB], ps[:, 0:NB], AF.Ln, bias=4e-6)
        yield
        with P.only(acc=[2, 3]):
            P.act(rq[:, :, 0:NB], rq[:, :, 0:NB], AF.Exp, scale=-0.5)
        with P.only(acc=[2]):
            P.tt(R(qT[:, 0:NB]), qkv[:, 0, 0:NB], rq[:, 0, 0:NB], ALU.mult)
        with P.only(acc=[3]):
            P.tt(R(kT[:, 0:NB]), qkv[:, 1, 0:NB], rq[:, 1, 0:NB], ALU.mult)
        yield
        cs = slice(c0, c0 + nch)
        idb = bcm(ident[0:C, 0:C], [C, nch, C])
        dgv = dg[0:C, :, 0:NB].rearrange("p a (c i) -> p a c i", c=nch)
        P.tt(dgv[:, 0], idb, bc(Gc[0:C, cs, h], [C, nch, C]), ALU.mult)
        P.tt(dgv[:, 1], idb, bc(beta[0:C, cs, h], [C, nch, C]), ALU.mult)
        ps = P.ps("a1m")
        P.mm(ps[:, 0:NB], ones[0:C, :], dg[0:C, 0, 0:NB])
        P.act(eGb[:, 0:NB], ps[:, 0:NB], AF.Exp)
        ps = P.ps("a1m")
        P.mm(ps[:, 0:NB], ones[0:C, :], dg[0:C, 1, 0:NB])
        P.cp(betab[:, 0:NB], ps[:, 0:NB], e="act")
        yield
        P.tt(R(kbT[:, 0:NB]), kT[:, 0:NB], betab[:, 0:NB], ALU.mult)
        P.tt(R(qgT[:, 0:NB]), qT[:, 0:NB], eGb[:, 0:NB], ALU.mult)
        yield
        for c4 in range(0, nch, 4):
            n4 = min(4, nch - c4)
            ps = P.ps("a1m")
            for j in range(n4):
                c = c4 + j
                P.tr(ps[0:C, j * 128:(j + 1) * 128], kT[:, c * C:(c + 1) * C], ident[:, :])
            pv = ps[0:C, 0:n4 * 128].rearrange("p (j d) -> p j d", j=n4)
            P.tt(R(kbeG[0:C, c4:c4 + n4, :]), pv, bc(bG[0:C, c0 + c4:c0 + c4 + n4, h], [C, n4, 128]), ALU.mult)
            P.tt(R(ktk[0:C, c4:c4 + n4, :]), pv, bc(dte[0:C, c0 + c4:c0 + c4 + n4, h], [C, n4, 128]), ALU.mult)
            yield
        for c2 in range(0, nch, 2):
            n2 = min(2, nch - c2)
            ps = P.ps("a1m")
            for j in range(n2):
                c = c2 + j
                for half in range(2):
                    P.tr(ps[0:C, j * 256 + half * 128:j * 256 + (half + 1) * 128],
                         qkv[:, 2 + half, c * C:(c + 1) * C], ident[:, :])
            pv = ps[0:C, 0:n2 * 256].rearrange("p (j d) -> p j d", j=n2)
            P.tt(R(vb[0:C, c2:c2 + n2, :]), pv, bc(hbeta[0:C, c0 + c2:c0 + c2 + n2, h], [C, n2, 256]), ALU.mult)
            yield
        P.tt(gU[0:C, 0:nch, 0:C], bcm(Utri[0:C, 0:C], [C, nch, C]), bc(gg[0:C, cs, h], [C, nch, C]), ALU.mult)
        ps = P.ps("a1m")
        for c in range(nch):
            o = ps[0:C, c * C:(c + 1) * C]
            P.mm(o, ones[0:C, 0:C], gU[0:C, c, 0:C], start=True, stop=False)
            P.mm(o, gU[0:C, c, 0:C], mones[0:C, 0:C], start=False, stop=False)
            P.mm(o, ident[0:C, 0:C], NEGT[0:C, 0:C], start=False, stop=True)
        pv = ps[0:C, 0:nch * C].rearrange("p (c i) -> p c i", c=nch)
        P.act(DT[0:C, 0:nch, 0:C], pv, AF.Exp)
        P.tt(DTs[0:C, 0:nch, 0:C], DT[0:C, 0:nch, 0:C], bcm(MsT[0:C, 0:C], [C, nch, C]), ALU.mult, e="pool")
        yield
        ps = P.ps("a1m")
        for c in range(nch):
            P.mmr(ps[0:C, c * C:(c + 1) * C], kT[:, c * C:(c + 1) * C], kbT[:, c * C:(c + 1) * C])
        for c in range(nch):
            P.mmr(ps[0:C, 256 + c * C:256 + (c + 1) * C], kT[:, c * C:(c + 1) * C], qT[:, c * C:(c + 1) * C])
        pv = ps[0:C, 0:nch * C].rearrange("p (c i) -> p c i", c=nch)
        pv2 = ps[0:C, 256:256 + nch * C].rearrange("p (c i) -> p c i", c=nch)
        P.stt(R(MT0a[0:C, 0:nch, 0:C]), pv, -1.0, DTs[0:C, 0:nch, 0:C], ALU.mult, ALU.mult)
        P.tt(R(qkT[0:C, 0:nch, 0:C]), pv2, DT[0:C, 0:nch, 0:C], ALU.mult)
        yield
        ps = P.ps("a1m")
        for c in range(nch):
            P.tr(ps[0:C, c * C:(c + 1) * C], MT0a[0:C, c, 0:C], ident[0:C, 0:C])
        P.cp(R(Mn[0][0:C, 0:nch, 0:C]), ps[0:C, 0:nch * C].rearrange("p (c i) -> p c i", c=nch), e="act")
        yield
        TTf = yield from neumann(P, ident, MT0a[0:C, 0:nch, 0:C], None, Mn, MTTa, nch, C, pool="a1b")
        yield "DRAIN2"
        for c2 in range(0, nch, 2):
            n2 = min(2, nch - c2)
            ps = P.ps("a1b")
            for j in range(n2):
                P.mmr(ps[0:C, j * 256:(j + 1) * 256], TTf[:, c2 + j, :], vb[0:C, c2 + j, :])
            P.cp(u0[0:C, c2:c2 + n2, :], ps[0:C, 0:n2 * 256].rearrange("p (j d) -> p j d", j=n2), e="act")
        ps = P.ps("a1b")
        for c in range(nch):
            P.mmr(ps[:, c * C:(c + 1) * C], kbeG[0:C, c, :], TTf[:, c, :])
        P.cp(R(wkT[:, 0:NB]), ps[:, 0:NB], e="act")
        yield

    spar = [0, 0, 0]

    def A_s2(it, h, blk):
        (seq, t0, NB, C, nch, c0) = blk
        par = it % 2
        WO = Wo[0]
        zs, qgT, ktk, qkT = zs2[par], qgT2[par], ktk2[par], qkT2[par]
        first_of_seq = (seq == 0 and t0 == PCOL) or seq > 0
        if seq == 0 and t0 == PCOL:
            load_head_wo(h)
        if first_of_seq:
            spar[seq] = 0
            if seq == 0:
                P.cp(R(Sst[0][0][:]), zeros[:, 0:1].broadcast_to([128, 256]), e="act")
            else:
                P.dma("sp", Sst[seq][1][:], delta_s[seq - 1, h], writes=[Sst[seq][1][:]])
                P.cp(R(Sst[seq][0][:]), Sst[seq][1][:], e="act")
        pso = P.psb[7]
        for c in range(nch):
            Sc = Sst[seq][spar[seq]]
            Sn = Sst[seq][1 - spar[seq]]
            u = uu[c % 2]
            ps = P.ps("a2")
            P.mmr(ps[0:C, 0:256], wkT[:, c * C:(c + 1) * C], Sc[:, :])
            P.tt(R(u[0:C, :]), u0[0:C, c, :], ps[0:C, 0:256], ALU.subtract)
            yield
            ps2 = P.ps("a2")
            P.mmr(ps2[:, 0:256], ktk[0:C, c, :], u[0:C, :])
            P.stt(R(Sn[:, :]), Sc[:, :], gl[:, c0 + c, h:h + 1], ps2[:, 0:256], ALU.mult, ALU.add)
            for half in range(2):
                oo = pso[:, (half * nch + c) * C:(half * nch + c + 1) * C]
                P.mmr(oo, Sc[:, half * 128:(half + 1) * 128], qgT[:, c * C:(c + 1) * C], start=True, stop=False)
                P.mmr(oo, u[0:C, half * 128:(half + 1) * 128], qkT[0:C, c, 0:C], start=False, stop=True)
            spar[seq] = 1 - spar[seq]
            yield
        P.cp(oT[:, :, 0:NB], pso[:, 0:2 * NB].rearrange("p (a t) -> p a t", a=2), e="act")
        P.act(R(osq[:, :, 0:NB]), oT[:, :, 0:NB], AF.Square)
        ps = P.ps("a2")
        P.mmr(ps[:, 0:NB], ones_r[:, :], osq[:, 0, 0:NB], start=True, stop=False)
        P.mmr(ps[:, 0:NB], ones_r[:, :], osq[:, 1, 0:NB], start=False, stop=True)
        P.act(ors[:, 0:NB], ps[:, 0:NB], AF.Ln, scale=1.0 / 256.0, bias=RMS_EPS)
        yield
        P.act(ors[:, 0:NB], ors[:, 0:NB], AF.Exp, scale=-0.5)
        for half in range(2):
            P.stt(oT[:, half, 0:NB], oT[:, half, 0:NB], anw[:, half:half + 1], ors[:, 0:NB], ALU.mult, ALU.mult)
        P.tt(og[:, :, 0:NB], oT[:, :, 0:NB], zs[:, :, 0:NB], ALU.mult)
        yield
        for tt0 in range(0, NB, 128):
            nt = min(128, NB - tt0)
            if seq == 0:
                tile_i, prow = (t0 - PCOL + tt0) // 128, 0
            else:
                tile_i, prow = 16, (seq - 1) * 32
            for nh in range(2):
                ps = P.ps("a2")
                for half in range(2):
                    P.mm(ps[prow:prow + nt, :], og[:, half, tt0:tt0 + nt], WO[:, half, nh * 512:(nh + 1) * 512],
                         start=(half == 0), stop=(half == 1))
                xr = xres[tile_i][prow:prow + nt, nh * 512:(nh + 1) * 512]
                P.tt(xr, xr, ps[prow:prow + nt, :], ALU.add)
            yield
        last = (t0 + NB == PCOL + TP) or seq > 0
        if last:
            P.dma("sp", o_delta[seq, h], Sst[seq][spar[seq]][:], reads=[Sst[seq][spar[seq]][:]], final=True)

    def pipeline(items, s1, s2, ratio):
        g2 = None
        for it, item in enumerate(list(items) + [None]):
            g1 = s1(it, *item) if item is not None else None
            while g1 is not None or g2 is not None:
                if g2 is not None:
                    if next(g2, DONE) is DONE:
                        g2 = None
                if g1 is not None:
                    for _ in range(ratio if g2 is not None else 1000000):
                        r = next(g1, DONE)
                        if r is DONE:
                            g1 = None
                            break
                        if r == "DRAIN2":
                            while g2 is not None:
                                if next(g2, DONE) is DONE:
                                    g2 = None
            g2 = s2(it, *item) if item is not None else None

    pipeline([(h, blk) for h in range(H_A) for blk in blocks], A_s1, A_s2, 3)

    ps = P.ps()
    for s in range(3):
        P.tr(ps[0:96, s * 128:(s + 1) * 128], fin_all[:, s].rearrange("p t g -> p (t g)"), ident[:, :])
    for s in range(3):
        P.cp(acc[0:96, s, 0:128], ps[0:96, s * 128:(s + 1) * 128])
        P.dma("sp", o_conv[s].rearrange("t (g c) -> (t g) c", c=128), acc[0:96, s, 0:128], reads=[acc[:]], final=True)

    if stop == "A":
        for i in range(NTILE):
            nt = tile_rows(i)
            P.dma("sp", dbg[i * 128:i * 128 + nt, :], xres[i][0:nt, :], reads=[xres[i][:]], final=True)
        P.finish()
        return nc

    P.barrier()
    P.sb_off = phase_mark
    hT = P.sb("hT2", [128, KC, NTOK], BF16)
    shout = P.sb("shout", [128, 3, KC])
    P.memset(hT[:, :, 0:1], 0.0)
    mark_b0 = P.sb_off
    xn = [P.sb("xnB", [128, D])] * 2

    def norm_to_hT_B():
        for i in range(NTILE):
            nt = tile_rows(i)
            xt = xres[i]
            xb = xn[i % 2]
            sl = slice(i % 4, i % 4 + 1)
            P.act(xb[0:nt, :], xt[0:nt, :], AF.Square, accum_out=ssq[0:nt, sl])
            P.act(rstd[0:nt, sl], ssq[0:nt, sl], AF.Sqrt, scale=1.0 / D, bias=RMS_EPS)
            P.recip(rstd[0:nt, sl], rstd[0:nt, sl])
            P.ts(xb[0:nt, :], xt[0:nt, :], rstd[0:nt, sl], ALU.mult)
            c0 = tile_col(i)
            for half in range(2):
                ps = P.ps(2)
                for j in range(4):
                    kc = half * 4 + j
                    P.tr(ps[:, j * 128:j * 128 + nt], xb[0:nt, kc * 128:(kc + 1) * 128], ident[0:nt, 0:nt])
                pv4 = ps[:, :].rearrange("p (j t) -> p j t", j=4)
                pv = pv4[:, :, 0:nt]
                P.tt(hT[:, half * 4:half * 4 + 4, c0:c0 + nt], pv,
                     bc(nw[:, 1, half * 4:half * 4 + 4], [128, 4, nt]), ALU.mult)
                lastcols = {15: [(0, 127)], 16: [(1, 15), (2, 47)]}.get(i, [])
                for (sq_, col) in lastcols:
                    P.tt(shout[:, sq_, half * 4:half * 4 + 4], pv4[:, :, col], nw[:, 1, half * 4:half * 4 + 4], ALU.mult)

    norm_to_hT_B()
    P.barrier()
    P.sb_off = mark_b0
    for s_ in range(3):
        P.dma("sp", o_shift[s_].rearrange("(k p) -> p k", p=128), shout[:, s_, :], reads=[shout[:]], final=True,
              allow_slow_non_contiguous=True)
    shin = P.sb("shin", [128, 2, KC])
    P.dma("sp", shin[:], shift_s.rearrange("s (k p) -> p s k", p=128), writes=[shin[:]], allow_slow_non_contiguous=True)
    P.cp(hT[:, :, S1COL - 1], shin[:, 0, :])
    P.cp(hT[:, :, S2COL - 1], shin[:, 1, :])

    vecs = P.sb("vecs", [128, 13, KC])
    P.dma("sp", vecs[:, 0:6, :], b_mu.rearrange("g (k p) -> p g k", p=128), writes=[vecs[:]], allow_slow_non_contiguous=True)
    for vi, v_ in enumerate((b_w0, b_a0, b_k_k, b_k_a, b_r_k, b_gn_w, b_gn_b)):
        P.dma("sp", vecs[:, 6 + vi, :], v_.rearrange("(k p) -> p k", p=128), writes=[vecs[:]], allow_slow_non_contiguous=True)
    V_W0, V_A0, V_KK, V_KA, V_RK, V_GW, V_GB = range(6, 13)
    hvec = P.sb("hvec", [128, 2, KC])
    P.ts(hvec[:], vecs[:, 6:8, :], 0.5, ALU.mult)
    blk1 = P.sb("blk1", [128, 128])
    cst32 = P.sb("cst32", [128, 192])
    P.asel(cst32[:, 0:64], ones[:, 0:64], [[0, 64]], ALU.is_ge, 0.0, 63, -1)
    P.asel(cst32[:, 64:128], ones[:, 0:64], [[0, 64]], ALU.is_ge, 0.0, -64, 1)
    P.cp(R(blk1[:]), cst32[:, 0:128])
    CB = 128
    MXT = P.sb("MXT", [CB, 2 * CB])
    P.cp(MXT[:, 0:CB], MsT[0:CB, 0:CB], e="pool")
    P.cp(MXT[:, CB:2 * CB], Utri[0:CB, 0:CB], e="pool")
    MsL = P.sb("MsL", [CB, CB])
    Sh = P.sb("Sh", [128, 64])
    P.asel(cst32[:, 128:192], ones[:, 0:64], [[-1, 64]], ALU.is_equal, 0.0, -64, 1)
    P.cp(R(Sh[:]), cst32[:, 128:192])
    P.asel(MsL[:], ones[0:CB, 0:CB], [[-1, CB]], ALU.is_gt, 0.0, 0, 1)
    rmask = P.sb("rmask", [128, 256])
    P.memset(rmask[:], 1.0)
    for c in range(256 // CB):
        P.memset(rmask[:, c * CB:c * CB + 1], 0.0)

    NBB = 256
    t1T = P.sb("t1T", [64, NTOK], BF16)
    a1T = P.sb("a1T", [64, NTOK], BF16)
    w2b = P.sb("w2b", [64, D], BF16)
    a2b = P.sb("a2b", [64, D], BF16)
    blk_mark = P.sb_off
    lw1 = P.sb("lw1", [128, KC, 2, 64], BF16)
    lw1p = P.sb("lw1p", [128, KC, 2, 64], BF16)
    lw1pp = P.sb("lw1pp", [128, KC, 2, 64], BF16)
    P.dma("pool", lw1[:, :, 0, :], b_w_w1.rearrange("(k p) c -> p k c", p=128), writes=[lw1[:]])
    P.dma("pool", lw1[:, :, 1, :], b_a_w1.rearrange("(k p) c -> p k c", p=128), writes=[lw1[:]])
    for j in range(2):
        P.tt(lw1p[:, :, j, :], lw1[:, :, j, :], bc(vecs[:, 4 + j, :], [128, KC, 64]), ALU.mult)
    P.tt(lw1pp[:], lw1[:], lw1p[:], ALU.subtract)
    P.dma("pool", w2b[:], b_w_w2[:, :], writes=[w2b[:]])
    P.dma("pool", a2b[:], b_a_w2[:, :], writes=[a2b[:]])
    col_ranges = [(PCOL + i * 512, 512) for i in range(4)] + [(S1COL, 16), (S2COL, 16)]
    for (cc0, n) in col_ranges:
        for j, dst in enumerate((t1T, a1T)):
            ps = P.ps(2)
            for kc in range(KC):
                P.mm(ps[0:64, 0:n], lw1pp[:, kc, j, :], hT[:, kc, cc0:cc0 + n], start=(kc == 0), stop=False)
                P.mm(ps[0:64, 0:n], lw1p[:, kc, j, :], hT[:, kc, cc0 - 1:cc0 - 1 + n], start=False, stop=(kc == KC - 1))
            P.act(dst[:, cc0:cc0 + n], ps[0:64, 0:n], AF.Tanh if j == 0 else AF.Copy)

    P.barrier()
    P.sb_off = blk_mark
    Wp = [P.sb("Wp0", [128, KC, 4, 128], BF16)] * 2
    Wq = P.sb("Wq", [128, KC, 4, 128], BF16)
    Wob = [P.sb("Wob0", [128, D], BF16)] * 2
    b_in_v = b_w_in.rearrange("(k p) (g c) -> p k g c", p=128, g=4)

    def load_pair_w(pr):
        for g_ in range(4):
            P.dma("pool", Wp[pr % 2][:, :, g_, :], b_in_v[:, :, g_, pr * 128:(pr + 1) * 128], writes=[Wp[pr % 2][:]])
        P.dma("pool", Wob[pr % 2][:], b_w_out[pr * 128:(pr + 1) * 128, :], writes=[Wob[pr % 2][:]])

    NCB = NBB // CB
    NX = 2 * NCB
    rkvT = P.sb("rkvT", [128, 3, NBB])
    zsB2 = [P.sb("zsB%d" % i, [128, NBB]) for i in range(2)]
    s2tmp = P.sb("s2tmp", [128, 2, NBB])
    lwT = P.sb("lwT", [128, NBB])
    aT = P.sb("aT", [128, NBB])
    cwv = P.sb("cwv", [128, NBB])
    eW = P.sb("eW", [128, 3, NBB])
    tmpB = P.sb("tmpB", [128, 4, NBB])
    kkT = P.sb("kkT", [128, NBB])
    k2T = P.sb("k2T", [128, NBB])
    arT2 = [P.sb("arT%d" % i, [128, 2, NBB]) for i in range(2)]
    bkT = P.sb("bkT", [128, 2, NBB])
    bkh = P.sb("bkh", [128, 2, NBB])
    Wc = P.sb("Wc", [128, NCB])
    rkb = P.sb("rkb", [128, NBB])
    vt = P.sb("vt", [CB, NCB, 128])
    bht = P.sb("bht", [CB, NCB, 128])
    kht = P.sb("kht", [CB, NCB, 128])
    XA = P.sb("XA", [CB, NX, 2 * CB])
    XB = P.sb("XB", [CB, NX, 2 * CB])
    MnB = [P.sb("MnB%d" % i, [CB, NX, CB]) for i in range(2)]
    MTTb = [P.sb("MTTb%d" % i, [CB, NX, 2, CB]) for i in range(2)]
    Rsb = [P.sb("Rsb%d" % i, [CB, 128]) for i in range(2)]
    Usb = [P.sb("Usb%d" % i, [CB, 128]) for i in range(2)]
    StP = [[P.sb("StP%d_%d" % (i, hd), [64, 64]) for hd in range(2)] for i in range(2)]
    StS = [[P.sb("StS%d_%d" % (i, hd), [64, 64]) for hd in range(2)] for i in range(2)]
    StB = [StP, StS, StS]
    stio = P.sb("stio", [64, 128])
    ar12 = [P.sb("ar1_%d" % i, [64, 2, NBB]) for i in range(2)]
    bk1 = P.sb("bk1", [64, 2, NBB])
    Wc1 = P.sb("Wc1", [64, NCB])
    sqB = P.sb("sqB", [128, 4, NBB])
    oTB = sqB[:, 2]
    ocB = s2tmp[:, 0]
    osB = sqB[:, 3]
    ogB = P.sb("ogB", [128, NBB], BF16)
    print("SBUF left after layer-B alloc:", P.sb_top - P.sb_off)
    blocksB = [(0, PCOL + i * NBB, NBB, CB, NBB // CB) for i in range(TP // NBB)] + \
              [(1, S1COL, 16, 16, 1), (2, S2COL, 16, 16, 1)]
    ENH = -float(np.exp(-0.5))

    load_pair_w(0)
    nblkB = 0
    for pr in range(8):
        W = Wp[pr % 2]
        WO = Wob[pr % 2]
        for g_ in range(4):
            P.tt(Wq[:, :, g_, :], W[:, :, g_, :], bc(vecs[:, g_, :], [128, KC, 128]), ALU.mult)
        P.tt(W[:], W[:], Wq[:], ALU.subtract)
        Wr = W
        spar = [0, 0, 0]
        cur_seq = -1
        for (seq, t0, NB, C, nch) in blocksB:
            nx = 2 * nch
            bpar = nblkB % 2
            nblkB += 1
            zsB, arT, ar1 = zsB2[bpar], arT2[bpar], ar12[bpar]
            if seq != cur_seq:
                cur_seq = seq
                spar[seq] = 0
                if seq == 0:
                    for hd in range(2):
                        P.cp(R(StB[0][0][hd][:]), zeros[0:64, 0:64], e="act")
                else:
                    P.dma("sp", stio[:].rearrange("v (h k) -> v h k", h=2),
                          wkv_s[seq - 1, 2 * pr:2 * pr + 2].rearrange("h v k -> v h k"), writes=[stio[:]])
                    for hd in range(2):
                        ps = P.ps("b1")
                        P.tr(ps[0:64, 0:64], stio[:, hd * 64:(hd + 1) * 64], ident[0:64, 0:64])
                        P.cp(R(StB[seq][0][hd][:]), ps[0:64, 0:64])
            for g_ in range(4):
                ps = P.ps("b1")
                for kc in range(KC):
                    P.mm(ps[:, 0:NB], Wr[:, kc, g_, :], hT[:, kc, t0:t0 + NB], start=(kc == 0), stop=False)
                    P.mm(ps[:, 0:NB], Wq[:, kc, g_, :], hT[:, kc, t0 - 1:t0 - 1 + NB], start=False, stop=(kc == KC - 1))
                if g_ < 3:
                    P.cp(rkvT[:, g_, 0:NB], ps[:, 0:NB], e="act")
                else:
                    P.act(zsB[:, 0:NB], ps[:, 0:NB], AF.Tanh, scale=0.5)
                    P.stt(zsB[:, 0:NB], zsB[:, 0:NB], 1.0, ps[:, 0:NB], ALU.add, ALU.mult)
            ps = P.ps("b1")
            P.mm(ps[:, 0:NB], w2b[:, pr * 128:(pr + 1) * 128], t1T[:, t0:t0 + NB])
            P.act(lwT[:, 0:NB], ps[:, 0:NB], AF.Tanh, scale=0.5, bias=hvec[:, 0, pr:pr + 1])
            P.ts(lwT[:, 0:NB], lwT[:, 0:NB], 0.5 * ENH, ALU.mult, 0.5 * ENH, ALU.add)
            ps = P.ps("b1")
            P.mm(ps[:, 0:NB], a2b[:, pr * 128:(pr + 1) * 128], a1T[:, t0:t0 + NB])
            P.act(aT[:, 0:NB], ps[:, 0:NB], AF.Tanh, scale=0.5, bias=hvec[:, 1, pr:pr + 1])
            P.ts(aT[:, 0:NB], aT[:, 0:NB], 0.5, ALU.mult, 0.5, ALU.add)
            rT = rkvT[:, 0, 0:NB]
            kT_ = rkvT[:, 1, 0:NB]
            vT_ = rkvT[:, 2, 0:NB]
            P.ts(kkT[:, 0:NB], kT_, vecs[:, V_KK, pr:pr + 1], ALU.mult)
            P.act(R(sqB[:, 0, 0:NB]), kkT[:, 0:NB], AF.Square)
            ps = P.ps("b1")
            P.mmr(ps[:, 0:NB], blk1[:, :], sqB[:, 0, 0:NB])
            P.act(tmpB[:, 1, 0:NB], ps[:, 0:NB], AF.Ln, bias=1e-6)
            P.act(tmpB[:, 1, 0:NB], tmpB[:, 1, 0:NB], AF.Exp, scale=-0.5)
            P.tt(kkT[:, 0:NB], kkT[:, 0:NB], tmpB[:, 1, 0:NB], ALU.mult)
            P.ts(tmpB[:, 2, 0:NB], aT[:, 0:NB], -1.0, ALU.add, vecs[:, V_KA, pr:pr + 1], ALU.mult)
            P.ts(tmpB[:, 2, 0:NB], tmpB[:, 2, 0:NB], 1.0, ALU.add)
            P.tt(k2T[:, 0:NB], kT_, tmpB[:, 2, 0:NB], ALU.mult)
            P.scan(cwv[:, 0:NB], rmask[:, 0:NB], lwT[:, 0:NB])
            P.act(eW[:, 0, 0:NB], cwv[:, 0:NB], AF.Exp)
            P.act(eW[:, 1, 0:NB], cwv[:, 0:NB], AF.Exp, scale=-1.0)
            P.tt(tmpB[:, 3, 0:NB], cwv[:, 0:NB], lwT[:, 0:NB], ALU.subtract)
            P.act(eW[:, 2, 0:NB], tmpB[:, 3, 0:NB], AF.Exp)
            P.stt(R(arT[:, 0, 0:NB]), kkT[:, 0:NB], -1.0, eW[:, 2, 0:NB], ALU.mult, ALU.mult)
            P.tt(R(arT[:, 1, 0:NB]), rT, eW[:, 0, 0:NB], ALU.mult)
            P.tt(tmpB[:, 0, 0:NB], kkT[:, 0:NB], aT[:, 0:NB], ALU.mult)
            P.tt(R(bkT[:, 0, 0:NB]), tmpB[:, 0, 0:NB], eW[:, 1, 0:NB], ALU.mult)
            P.tt(R(bkT[:, 1, 0:NB]), k2T[:, 0:NB], eW[:, 1, 0:NB], ALU.mult)
            ewc = eW[:, 0, 0:NB].rearrange("p (c i) -> p c i", c=nch)[:, :, C - 1]
            P.cp(Wc[:, 0:nch], ewc)
            bkv = bkT[:, :, 0:NB].rearrange("p a (c i) -> p a c i", c=nch)
            bhv = bkh[:, :, 0:NB].rearrange("p a (c i) -> p a c i", c=nch)
            for a_ in range(2):
                P.tt(bhv[:, a_], bkv[:, a_], bc(Wc[:, 0:nch], [128, nch, C]), ALU.mult)
            P.stt(R(sqB[:, 1, 0:NB]), rT, vecs[:, V_RK, pr:pr + 1], k2T[:, 0:NB], ALU.mult, ALU.mult)
            ps = P.ps("b1")
            P.mmr(ps[:, 0:NB], blk1[:, :], sqB[:, 1, 0:NB])
            P.tt(rkb[:, 0:NB], ps[:, 0:NB], vT_, ALU.mult)
            ps = P.ps("b1")
            P.mmr(ps[0:64, 0:2 * NB], Sh[:, :], arT[:, :, 0:NB])
            P.cp(R(ar1[:, :, 0:NB]), ps[0:64, 0:2 * NB].rearrange("p (a t) -> p a t", a=2), e="act")
            ps = P.ps("b1")
            P.mmr(ps[0:64, 0:2 * NB], Sh[:, :], bkT[:, :, 0:NB])
            P.cp(R(bk1[:, :, 0:NB]), ps[0:64, 0:2 * NB].rearrange("p (a t) -> p a t", a=2), e="act")
            ps = P.ps("b1")
            P.mm(ps[0:64, 0:nch], cst32[:, 128:192], Wc[:, 0:nch])
            P.cp(Wc1[:, 0:nch], ps[0:64, 0:nch])
            AR = [arT[0:64], ar1[:]]
            BK = [bkT[0:64], bk1[:]]
            WC = [Wc[0:64], Wc1[:]]
            for (src, dst) in ((vT_, vt), (bkh[:, 0, 0:NB], bht), (bkh[:, 1, 0:NB], kht)):
                ps = P.ps("b1")
                for c in range(nch):
                    P.tr(ps[0:C, c * 128:(c + 1) * 128], src[:, c * C:(c + 1) * C], ident[:, :])
                P.cp(R(dst[0:C, 0:nch, :]), ps[0:C, 0:nch * 128].rearrange("p (c d) -> p c d", c=nch), e="act")
            psN = P.ps("b1")
            for hd in range(2):
                hs = slice(hd * 64, (hd + 1) * 64)
                psA = P.ps("b1")
                psB_ = P.ps("b1")
                for c in range(nch):
                    csl = slice(c * C, (c + 1) * C)
                    x_ = hd * nch + c
                    P.mmr(psA[0:C, c * 2 * C:(c + 1) * 2 * C], BK[hd][:, 0, csl], AR[hd][:, :, csl])
                    P.mmr(psB_[0:C, c * 2 * C:(c + 1) * 2 * C], BK[hd][:, 1, csl], AR[hd][:, :, csl])
                    P.mmr(psN[0:C, x_ * C:(x_ + 1) * C], AR[hd][:, 0, csl], BK[hd][:, 0, csl])
                for (psx, dstx) in ((psA, XA), (psB_, XB)):
                    pv = psx[0:C, 0:nch * 2 * C].rearrange("p (c a i) -> p c a i", c=nch, a=2)
                    dv = dstx[0:C, hd * nch:(hd + 1) * nch, :].rearrange("p c (a i) -> p c a i", a=2)[:, :, :, 0:C]
                    mv = MXT[0:C, :].rearrange("p (a i) -> p a i", a=2)[:, :, 0:C].unsqueeze(1).broadcast_to([C, nch, 2, C])
                    P.tt(R(dv), pv, mv, ALU.mult)
            P.tt(R(MnB[0][0:C, 0:nx, 0:C]), psN[0:C, 0:nx * C].rearrange("p (x i) -> p x i", x=nx),
                 bcm(MsL[0:C, 0:C], [C, nx, C]), ALU.mult)
            gen_ = neumann(P, ident, XA[0:C, 0:nx, 0:C], None, MnB, MTTb, nx, C, pool="b1")
            while True:
                try:
                    next(gen_)
                except StopIteration as e_:
                    TTf = e_.value
                    break
            pso = P.psb[7]
            for c in range(nch):
                csl = slice(c * C, (c + 1) * C)
                Sc = StB[seq][spar[seq]]
                Sn = StB[seq][1 - spar[seq]]
                Rb = Rsb[c % 2]
                Ub = Usb[c % 2]
                ps = P.ps("b2")
                for hd in range(2):
                    hs = slice(hd * 64, (hd + 1) * 64)
                    x_ = hd * nch + c
                    P.mmr(ps[0:C, hs], AR[hd][:, 0, csl], Sc[hd][:, :], start=True, stop=False)
                    P.mmr(ps[0:C, hs], XB[0:C, x_, 0:C], vt[0:C, c, hs], start=False, stop=True)
                P.cp(R(Rb[0:C, :]), ps[0:C, 0:128], e="act")
                ps = P.ps("b2")
                for hd in range(2):
                    hs = slice(hd * 64, (hd + 1) * 64)
                    x_ = hd * nch + c
                    P.mmr(ps[0:C, hs], TTf[:, x_, :], Rb[0:C, hs])
                P.cp(R(Ub[0:C, :]), ps[0:C, 0:128], e="act")
                for hd in range(2):
                    hs = slice(hd * 64, (hd + 1) * 64)
                    x_ = hd * nch + c
                    oo = pso[hs, csl]
                    mmf = P.mmr if hd == 0 else P.mm
                    mmf(oo, Sc[hd][:, :], AR[hd][:, 1, csl], start=True, stop=False)
                    mmf(oo, Ub[0:C, hs], XA[0:C, x_, CB:CB + C], start=False, stop=False)
                    mmf(oo, vt[0:C, c, hs], XB[0:C, x_, CB:CB + C], start=False, stop=True)
                ps2 = P.ps("b2")
                for hd in range(2):
                    hs = slice(hd * 64, (hd + 1) * 64)
                    P.mmr(ps2[0:64, hs], bht[0:C, c, hs], Ub[0:C, hs], start=True, stop=False)
                    P.mmr(ps2[0:64, hs], kht[0:C, c, hs], vt[0:C, c, hs], start=False, stop=True)
                for hd in range(2):
                    hs = slice(hd * 64, (hd + 1) * 64)
                    P.stt(R(Sn[hd][:, :]), Sc[hd][:, :], WC[hd][:, c:c + 1], ps2[0:64, hs], ALU.mult, ALU.add)
                spar[seq] = 1 - spar[seq]
            P.cp(R(oTB[:, 0:NB]), pso[:, 0:NB], e="act")
            ps = P.ps("b2")
            P.mmr(ps[:, 0:NB], blk1[:, :], oTB[:, 0:NB])
            P.stt(ocB[:, 0:NB], ps[:, 0:NB], -1.0 / 64.0, oTB[:, 0:NB], ALU.mult, ALU.add)
            P.act(R(osB[:, 0:NB]), ocB[:, 0:NB], AF.Square)
            ps = P.ps("b2")
            P.mmr(ps[:, 0:NB], blk1[:, :], osB[:, 0:NB])
            P.act(s2tmp[:, 1, 0:NB], ps[:, 0:NB], AF.Ln, scale=1.0 / 64.0, bias=GN_EPS)
            P.act(s2tmp[:, 1, 0:NB], s2tmp[:, 1, 0:NB], AF.Exp, scale=-0.5)
            P.tt(ocB[:, 0:NB], ocB[:, 0:NB], s2tmp[:, 1, 0:NB], ALU.mult)
            P.ts(ocB[:, 0:NB], ocB[:, 0:NB], vecs[:, V_GW, pr:pr + 1], ALU.mult, vecs[:, V_GB, pr:pr + 1], ALU.add)
            P.tt(ocB[:, 0:NB], ocB[:, 0:NB], rkb[:, 0:NB], ALU.add)
            P.stt(ogB[:, 0:NB], ocB[:, 0:NB], 0.5, zsB[:, 0:NB], ALU.mult, ALU.mult)
            for tt0 in range(0, NB, 128):
                nt = min(128, NB - tt0)
                if seq == 0:
                    tile_i, prow = (t0 - PCOL + tt0) // 128, 0
                else:
                    tile_i, prow = 16, (seq - 1) * 32
                for nh in range(2):
                    ps = P.ps("b2")
                    P.mm(ps[prow:prow + nt, :], ogB[:, tt0:tt0 + nt], WO[:, nh * 512:(nh + 1) * 512])
                    xr = xres[tile_i][prow:prow + nt, nh * 512:(nh + 1) * 512]
                    P.tt(xr, xr, ps[prow:prow + nt, :], ALU.add)
            last = (t0 + NB == PCOL + TP) or seq > 0
            if last:
                Sf = StB[seq][spar[seq]]
                ps = P.ps("b2")
                for hd in range(2):
                    P.tr(ps[0:64, hd * 64:(hd + 1) * 64], Sf[hd][:, :], ident[0:64, 0:64])
                P.cp(stio[:, :], ps[0:64, 0:128])
                P.dma("sp", o_wkv[seq, 2 * pr:2 * pr + 2].rearrange("h v k -> v h k"),
                      stio[:].rearrange("v (h k) -> v h k", h=2), reads=[stio[:]], final=True)
        if pr + 1 < 8:
            load_pair_w(pr + 1)

    P.barrier()
    P.sb_off = blk_mark
    xn = [P.sb("xnF", [128, D])] * 2
    fnwb = P.sb("fnwb", [128, D])
    P.dma("sp", fnwb[:], final_norm_w.partition_broadcast(128), writes=[fnwb[:]])
    for i in range(NTILE):
        nt = tile_rows(i)
        xt = xres[i]
        xb = xn[0]
        sl = slice(i % 4, i % 4 + 1)
        P.act(xb[0:nt, :], xt[0:nt, :], AF.Square, accum_out=ssq[0:nt, sl])
        P.act(rstd[0:nt, sl], ssq[0:nt, sl], AF.Sqrt, scale=1.0 / D, bias=RMS_EPS)
        P.recip(rstd[0:nt, sl], rstd[0:nt, sl])
        P.stt(xb[0:nt, :], xt[0:nt, :], rstd[0:nt, sl], fnwb[0:nt, :], ALU.mult, ALU.mult)
        if i < 16:
            P.dma("sp", y_p[i * 128:(i + 1) * 128, :], xb[:, :], reads=[xb[:]], final=True)
        else:
            P.dma("sp", y_s[0:16, :], xb[0:16, :], reads=[xb[:]], final=True)
            P.dma("sp", y_s[16:32, :], xb[32:48, :], reads=[xb[:]], final=True)
    P.finish()
    return nc


_NC_CACHE = {}


def make_in_maps(inputs):
    g = lambda k: np.ascontiguousarray(np.asarray(inputs[k], dtype=np.float32))
    xp, xs = g("x_prompt"), g("x_sample")
    cc, sd, ss, sw = g("cache_conv_a"), g("state_delta_a"), g("state_shift_b"), g("state_wkv_b")
    shared = {
        "norm_w": g("norm_w"), "final_norm_w": g("final_norm_w"), "a_w_in": g("a_w_in")[0],
        "a_conv_w": g("a_conv_w")[0], "a_log": g("a_log")[0], "a_dt_bias": g("a_dt_bias")[0],
        "a_norm_w": g("a_norm_w")[0], "a_w_out": g("a_w_out")[0], "b_mu": g("b_mu")[0],
        "b_w_in": g("b_w_in")[0], "b_w0": g("b_w0")[0], "b_w_w1": g("b_w_w1")[0], "b_w_w2": g("b_w_w2")[0],
        "b_a0": g("b_a0")[0], "b_a_w1": g("b_a_w1")[0], "b_a_w2": g("b_a_w2")[0], "b_k_k": g("b_k_k")[0],
        "b_k_a": g("b_k_a")[0], "b_r_k": g("b_r_k")[0].reshape(-1), "b_gn_w": g("b_gn_w")[0],
        "b_gn_b": g("b_gn_b")[0], "b_w_out": g("b_w_out")[0],
    }
    maps = []
    for i in range(8):
        m = dict(shared)
        m["x_p"] = xp[i]
        m["x_s"] = np.ascontiguousarray(xs[2 * i:2 * i + 2].reshape(2 * TS, D))
        m["conv_s"] = np.ascontiguousarray(cc[0, 2 * i:2 * i + 2])
        m["delta_s"] = np.ascontiguousarray(sd[0, 2 * i:2 * i + 2])
        m["shift_s"] = np.ascontiguousarray(ss[0, 2 * i:2 * i + 2])
        m["wkv_s"] = np.ascontiguousarray(sw[0, 2 * i:2 * i + 2])
        maps.append(m)
    return maps


def kernel(**inputs):
    if "nc" not in _NC_CACHE:
        _NC_CACHE["nc"] = build()
    nc = _NC_CACHE["nc"]
    maps = make_in_maps(inputs)
    res = run_bass_kernel_spmd(nc, maps, core_ids=list(range(8)))
    R = res.results
    y_prompt = np.stack([R[i]["y_p"] for i in range(8)], 0)
    y_sample = np.concatenate([R[i]["y_s"].reshape(2, TS, D) for i in range(8)], 0)

    def pick(name, sl):
        return np.stack([R[i][name][sl] for i in range(8)], 0)[None] if isinstance(sl, int) else \
            np.concatenate([R[i][name][sl] for i in range(8)], 0)[None]

    p_conv, s_conv = pick("o_conv", 0), pick("o_conv", slice(1, 3))
    p_delta, s_delta = pick("o_delta", 0), pick("o_delta", slice(1, 3))
    p_shift, s_shift = pick("o_shift", 0), pick("o_shift", slice(1, 3))
    p_wkv, s_wkv = pick("o_wkv", 0), pick("o_wkv", slice(1, 3))
    return (y_prompt, y_sample, p_conv, p_delta, p_shift, p_wkv, s_conv, s_delta, s_shift, s_wkv)
```

```python
import numpy as np
import concourse.bass as bass
import concourse.mybir as mybir
from concourse.bass_utils import run_bass_kernel_spmd

F32 = mybir.dt.float32
BF16 = mybir.dt.bfloat16
AF = mybir.ActivationFunctionType
ALU = mybir.AluOpType
AX = mybir.AxisListType

D = 1024
KC = 8
TP = 2048
TS = 16
NTILE = 17
H_A = 8
RMS_EPS = 1e-6
GN_EPS = 64e-5
NEG = -30000.0


class Trk:
    __slots__ = ("lw", "rd", "dsem", "dcount")

    def __init__(self):
        self.lw = None
        self.rd = []
        self.dsem = None
        self.dcount = 0


def fsz(ap):
    n = 1
    for d in ap.shape[1:]:
        n *= d
    return n


class Prog:
    WINDOW = 4000
    LAT = 350.0
    LAT_SAME = 100.0
    PE_SWITCH = 0.0
    EPS = 150.0

    def __init__(self, nc):
        self.nc = nc
        self.h = {"pe": nc.tensor, "act": nc.scalar, "dve": nc.vector, "pool": nc.gpsimd, "sp": nc.sync}
        self.sem = {k: nc.alloc_semaphore("sem_" + k) for k in self.h}
        self.cnt = {k: 0 for k in self.h}
        self.known = {k: {} for k in self.h}
        self.trk = {}
        self.ops = []
        self.nps = 0
        self.npool = {}
        self.psb = []
        self.nsem = 0
        self.sb_off = nc.sbuf_base
        self.sb_top = nc.sbuf_top
        self.seg_start = 0
        self.sel = {}
        self.pecls = {}

    def sb(self, name, shape, dt=F32):
        n = 1
        for d in shape[1:]:
            n *= d
        nbytes = n * (2 if dt == BF16 else 4)
        off = (self.sb_off + 63) // 64 * 64
        assert off + nbytes <= self.sb_top, "SBUF overflow at %s: need %d have %d" % (name, nbytes, self.sb_top - off)
        t = self.nc.alloc_sbuf_tensor_at(name, list(shape), dt, offset=off)
        self.sb_off = off + nbytes
        self.trk[t.name] = Trk()
        return t

    def barrier(self):
        self.ops.append(("fence", None, None, (), 0.0, False, None))
        for t in self._all_trk():
            t.lw = None
            t.rd = []

    def _all_trk(self):
        for t in self.trk.values():
            if isinstance(t, list):
                for x in t:
                    yield x
            else:
                yield t

    def init_psum(self):
        for i in range(8):
            t = self.nc.alloc_psum_tensor("psb%d" % i, [128, 512], F32)
            self.trk["psb%d" % i] = Trk()
            self.psb.append(t)

    POOLS = {0: (0, 1, 2, 3, 4, 5, 6), 2: (0, 1, 2, 3, 4, 5, 6),
             "a1a": (0, 1), "a1m": (2,), "a1b": (3, 4), "a2": (5, 6),
             "b1": (0, 1, 2, 3, 4), "b2": (5, 6)}

    def ps(self, pool=0):
        banks = self.POOLS[pool]
        n = self.npool.get(pool, 0)
        self.npool[pool] = n + 1
        return self.psb[banks[n % len(banks)]]

    def split(self, tensor, n):
        self.trk[tensor.name] = [Trk() for _ in range(n)]

    def only(self, **sel):
        prog = self

        class _Ctx:
            def __enter__(self_):
                self_.old = dict(prog.sel)
                prog.sel.update(sel)

            def __exit__(self_, *a):
                prog.sel = self_.old
        return _Ctx()

    def _tks(self, ap):
        t = self.trk[ap.tensor.name]
        if isinstance(t, list):
            idx = self.sel.get(ap.tensor.name.rsplit("_", 1)[0])
            if idx is not None:
                return [t[i] for i in idx]
            try:
                pat = ap.ap
                F = 1
                for d in ap.tensor.shape[1:]:
                    F *= d
                pstride = pat[0][0]
                off = int(ap.offset)
                if pstride != F:
                    return list(t)
                f0 = off % F
                ext = 1
                for st, cnt in pat[1:]:
                    ext += (cnt - 1) * abs(st)
                gsz = F // len(t)
                g0 = f0 // gsz
                g1 = (f0 + ext - 1) // gsz
                if g0 < 0 or g1 >= len(t):
                    return list(t)
                return [t[i] for i in range(g0, g1 + 1)]
            except Exception:
                return list(t)
        return [t]

    def _record(self, kind, e, payload, reads, writes, cost, final=False):
        rt = []
        for a in reads:
            for t in self._tks(a):
                if t not in rt:
                    rt.append(t)
        wt = []
        for a in writes:
            for t in self._tks(a):
                if t not in wt:
                    wt.append(t)
        i = len(self.ops)
        preds = set()
        for t in rt:
            if t.lw is not None:
                preds.add(t.lw)
        for t in wt:
            if t.lw is not None:
                preds.add(t.lw)
            preds.update(t.rd)
        preds.discard(i)
        for t in rt:
            t.rd.append(i)
        for t in wt:
            t.lw = i
            t.rd = []
        t0 = (wt + rt)[0] if kind == "dma" else None
        self.ops.append((kind, e, payload, tuple(preds), float(cost), final, t0))
        return i

    def op(self, e, fn, reads, writes, cost=None):
        if cost is None:
            n = fsz(writes[0]) if writes else 64
            cost = {"act": 200.0 + 0.85 * n, "dve": 110.0 + 1.05 * n, "pool": 260.0 + 1.0 * n, "pe": 150.0}[e]
        return self._record("op", e, fn, reads, writes, cost)

    def dma(self, q, out, in_, reads=(), writes=(), final=False, **kw):
        return self._record("dma", q, (out, in_, kw), reads, writes, 150.0 if q == "sp" else 1200.0, final)

    def _wait(self, e, key, semh, val):
        k = self.known[e]
        if k.get(key, 0) < val:
            self.h[e].wait_ge(semh, val)
            k[key] = val

    def _schedule(self, lo, hi):
        ops = self.ops
        n = hi - lo
        indeg = [0] * n
        succ = [[] for _ in range(n)]
        for i in range(lo, hi):
            ps_ = [p for p in ops[i][3] if p >= lo]
            indeg[i - lo] = len(ps_)
            for p in ps_:
                succ[p - lo].append(i)
        blev = [0.0] * n
        for k in range(n - 1, -1, -1):
            o = ops[lo + k]
            c = o[4] + (2500.0 if o[0] == "dma" else 0.0)
            m = 0.0
            for j in succ[k]:
                v = blev[j - lo] + self.LAT
                if v > m:
                    m = v
            blev[k] = c + m
        dready = [0.0] * n
        etime = {k: 0.0 for k in self.h}
        ready = {k: [] for k in self.h}
        for i in range(lo, hi):
            if indeg[i - lo] == 0:
                ready[ops[i][1]].append(i)
        order = []
        done = [False] * n
        lastcls = None
        minp = lo
        W = self.WINDOW
        EPS = self.EPS
        while len(order) < n:
            while minp < hi and done[minp - lo]:
                minp += 1
            lim = minp + W
            best = None
            for e, lst in ready.items():
                if not lst:
                    continue
                te = etime[e]
                cand = None
                for i in lst:
                    if i >= lim:
                        continue
                    dr = dready[i - lo]
                    st = te if dr <= te + EPS else dr
                    if e == "pe" and self.pecls.get(i) != lastcls:
                        st += self.PE_SWITCH
                    key = (st, -blev[i - lo], i)
                    if cand is None or key < cand:
                        cand = key
                if cand is not None and (best is None or cand < best[0]):
                    best = (cand, e)
            (st, _, i), e = best
            st = max(st, dready[i - lo], etime[e])
            ready[e].remove(i)
            kind = ops[i][0]
            cost = ops[i][4]
            if e == "pe":
                cl = self.pecls.get(i)
                if cl != lastcls:
                    cost += self.PE_SWITCH
                lastcls = cl
            etime[e] = st + cost
            f = st + cost + (2500.0 if kind == "dma" else 0.0)
            done[i - lo] = True
            order.append(i)
            for j in succ[i - lo]:
                ej = ops[j][1]
                v = f + (0.0 if (e == "pe" and ej == "pe") else (self.LAT_SAME if ej == e else self.LAT))
                if v > dready[j - lo]:
                    dready[j - lo] = v
                indeg[j - lo] -= 1
                if indeg[j - lo] == 0:
                    ready[ops[j][1]].append(j)
        return order, max(etime.values())

    def finish(self):
        ops = self.ops
        bounds = [i for i, o in enumerate(ops) if o[0] == "fence"] + [len(ops)]
        needs_inc = [False] * len(ops)
        info = {}
        clock = {}
        finals = []
        lo = 0
        est_total = 0.0
        nwait = 0
        last_inc = {k: None for k in self.h}
        for b in bounds:
            order, est = self._schedule(lo, b)
            est_total += est
            pos = {i: k for k, i in enumerate(order)}
            kept = {}
            lastop = {}
            for i in order:
                kind, e = ops[i][0], ops[i][1]
                if kind == "op":
                    lastop[e] = i
                best = {}
                keep = []
                for p in ops[i][3]:
                    if p < lo:
                        continue
                    if ops[p][0] != "op":
                        keep.append(p)
                        continue
                    f = ops[p][1]
                    if f == "pe" and e == "pe":
                        continue
                    if f not in best or pos[p] > pos[best[f]]:
                        best[f] = p
                for p in best.values():
                    needs_inc[p] = True
                    keep.append(p)
                kept[i] = keep
            for i in lastop.values():
                needs_inc[i] = True
            for i in order:
                kind, e, payload, preds, cost, final, t0 = ops[i]
                kn = self.known[e]
                for p in sorted(kept[i], key=lambda x: pos[x]):
                    if p < lo:
                        continue
                    pi = info[p]
                    if pi[0] == "op":
                        f, c = pi[1], pi[2]
                        if f == "pe" and e == "pe":
                            continue
                        if kn.get(f, 0) < c:
                            self.h[e].wait_ge(self.sem[f], c)
                            nwait += 1
                            kn[f] = c
                            for g, v in clock[p].items():
                                if kn.get(g, 0) < v:
                                    kn[g] = v
                    else:
                        if kn.get(pi[3], 0) < pi[2]:
                            self.h[e].wait_ge(pi[1], pi[2])
                            nwait += 1
                            kn[pi[3]] = pi[2]
                if kind == "op":
                    ins = payload(self.h[e])
                    if needs_inc[i]:
                        self.cnt[e] += 1
                        ins.then_inc(self.sem[e], 1)
                        info[i] = ("op", e, self.cnt[e])
                        snap = {g: v for g, v in kn.items() if g in self.h}
                        snap[e] = self.cnt[e]
                        clock[i] = snap
                    else:
                        info[i] = ("op", e, self.cnt[e] + 1)
                        clock[i] = {}
                else:
                    out, in_, kw = payload
                    if t0.dsem is None:
                        t0.dsem = self.nc.alloc_semaphore("dsem%d" % self.nsem)
                        self.nsem += 1
                    ins = self.h[e].dma_start(out=out, in_=in_, **kw)
                    t0.dcount += 16
                    ins.then_inc(t0.dsem, 16)
                    info[i] = ("dma", t0.dsem, t0.dcount, "d%d" % id(t0))
                    if final:
                        finals.append(info[i])
            lo = b + 1
            if b < len(ops):
                for e in self.h:
                    for f in self.h:
                        if f != e and self.cnt[f] > 0:
                            self._wait(e, f, self.sem[f], self.cnt[f])
                    for t in self._all_trk():
                        if t.dsem is not None and t.dcount > 0:
                            self._wait(e, "d%d" % id(t), t.dsem, t.dcount)
        fmax = {}
        for (_, semh, c, key) in finals:
            if key not in fmax or c > fmax[key][1]:
                fmax[key] = (semh, c)
        for key, (semh, c) in fmax.items():
            self._wait("sp", key, semh, c)
        print("scheduler estimate: %.1f us, %d ops, %d waits, incs %s" % (est_total / 1e3, len(ops), nwait, dict(self.cnt)))

    def mmr(self, out, lhsT, rhs, start=True, stop=True):
        return self.mm(out, R(lhsT), R(rhs), start=start, stop=stop)

    def mm(self, out, lhsT, rhs, start=True, stop=True):
        passes = 4.0 if rhs.dtype == F32 else 1.0
        cost = 70.0 + passes * 0.42 * (fsz(rhs) + min(fsz(lhsT), 128))
        i = self.op("pe", lambda h: h.matmul(out, lhsT=lhsT, rhs=rhs, start=start, stop=stop),
                    [lhsT, rhs], [out], cost=cost)
        self.pecls[i] = str(rhs.dtype)
        return i

    def tr(self, out, in_, ident):
        i = self.op("pe", lambda h: h.transpose(out, in_, ident), [in_, ident], [out], cost=160.0)
        self.pecls[i] = "tr"
        return i

    def act(self, out, in_, func, e="act", **kw):
        rd = [in_] + [v for v in kw.values() if hasattr(v, "tensor")]
        wr = [out]
        if "accum_out" in kw:
            wr.append(kw["accum_out"])
            rd.remove(kw["accum_out"])
        return self.op("act", lambda h: h.activation(out=out, in_=in_, func=func, **kw), rd, wr)

    def tt(self, out, in0, in1, op, e="dve"):
        return self.op(e, lambda h: h.tensor_tensor(out=out, in0=in0, in1=in1, op=op), [in0, in1], [out])

    def ts(self, out, in0, s1, op0, s2=None, op1=None, e="dve"):
        rd = [in0] + [v for v in (s1, s2) if hasattr(v, "tensor")]
        if op1 is None:
            return self.op(e, lambda h: h.tensor_scalar(out=out, in0=in0, scalar1=s1, scalar2=None, op0=op0),
                           rd, [out])
        return self.op(e, lambda h: h.tensor_scalar(out=out, in0=in0, scalar1=s1, scalar2=s2, op0=op0, op1=op1),
                       rd, [out])

    def stt(self, out, in0, scalar, in1, op0, op1):
        rd = [in0, in1] + ([scalar] if hasattr(scalar, "tensor") else [])
        return self.op("dve", lambda h: h.scalar_tensor_tensor(out=out, in0=in0, scalar=scalar, in1=in1,
                                                                 op0=op0, op1=op1), rd, [out])

    def cp(self, out, in_, e="dve"):
        if e == "act":
            return self.act(out, in_, AF.Copy)
        return self.op(e, lambda h: h.tensor_copy(out=out, in_=in_), [in_], [out])

    def scan(self, out, d0, d1):
        return self.op("dve", lambda h: h.tensor_tensor_scan(out=out, data0=d0, data1=d1, initial=0.0,
                                                              op0=ALU.mult, op1=ALU.add),
                       [d0, d1], [out], cost=110.0 + 2.1 * fsz(out))

    def rsqrt_pool(self, out, in_, mhalf):
        return self.op("pool", lambda h: h.tensor_tensor(out=out, in0=in_, in1=mhalf, op=ALU.pow), [in_, mhalf], [out])

    def recip(self, out, in_):
        return self.op("dve", lambda h: h.reciprocal(out=out, in_=in_), [in_], [out], cost=110.0 + 3.0 * fsz(out))

    def memset(self, ap, val, e="pool"):
        return self.op(e, lambda h: h.memset(ap, val), [], [ap])

    def asel(self, out, in_, pattern, cmp, fill, base, cm):
        return self.op("pool", lambda h: h.affine_select(out=out, in_=in_, pattern=pattern, compare_op=cmp,
                                                          fill=fill, base=base, channel_multiplier=cm),
                       [in_], [out])


F32R = mybir.dt.float32r


def R(ap):
    return ap.bitcast(F32R)


def neumann(P, ident, MT0, M0, Mbuf, MTT, nx, C, pool=0):
    L = {128: 7, 64: 6, 16: 4}[C]
    G = 512 // (2 * C)
    ngrp = (nx + G - 1) // G
    names = [m.name.rsplit("_", 1)[0] for m in MTT]
    split = ngrp > 1 and all(isinstance(P.trk[m.name], list) for m in MTT)

    def grp_only(g):
        if not split:
            return P.only()
        return P.only(**{nm: [g] for nm in names})

    def grp3(ps, n, w):
        return ps[0:C, 0:n * w].rearrange("p (x i) -> p x i", x=n)

    psa = P.ps(pool)
    psb = P.ps(pool)
    for x in range(nx):
        P.mm(psa[0:C, x * C:(x + 1) * C], R(MT0[:, x, :]), R(Mbuf[0][0:C, x, 0:C]))
        P.mm(psb[0:C, x * C:(x + 1) * C], R(Mbuf[0][0:C, x, 0:C]), R(MT0[:, x, :]))
    P.cp(R(Mbuf[1][0:C, 0:nx, 0:C]), grp3(psa, nx, C), e="act")
    P.cp(R(MTT[0][0:C, 0:nx, 0, 0:C]), grp3(psb, nx, C), e="act")
    P.tt(R(MTT[0][0:C, 0:nx, 1, 0:C]), MT0, bcm(ident[0:C, 0:C], [C, nx, C]), ALU.add)
    yield
    cm, ct = 1, 0
    for lev in range(2, L + 1):
        last = lev == L
        Mc = Mbuf[cm]
        cur = MTT[ct]
        nxt = MTT[1 - ct]
        if not last:
            psa = P.ps(pool)
            for x in range(nx):
                P.mm(psa[0:C, x * C:(x + 1) * C], R(cur[0:C, x, 0, 0:C]), R(Mc[0:C, x, 0:C]))
        for x0 in range(0, nx, G):
            n = min(G, nx - x0)
            psx = P.ps(pool)
            with grp_only(x0 // G):
                for j in range(n):
                    x = x0 + j
                    if last:
                        P.mm(psx[0:C, j * C:(j + 1) * C], R(Mc[0:C, x, 0:C]), R(cur[0:C, x, 1, 0:C]))
                    else:
                        P.mm(psx[0:C, j * 2 * C:(j + 1) * 2 * C], R(Mc[0:C, x, 0:C]), R(cur[0:C, x, :, 0:C]))
                if last:
                    P.tt(R(nxt[0:C, x0:x0 + n, 1, 0:C]), grp3(psx, n, C), cur[0:C, x0:x0 + n, 1, 0:C], ALU.add)
                else:
                    pv = psx[0:C, 0:n * 2 * C].rearrange("p (x a i) -> p x a i", x=n, a=2)
                    P.cp(R(nxt[0:C, x0:x0 + n, 0, 0:C]), pv[:, :, 0, :], e="act")
                    P.tt(R(nxt[0:C, x0:x0 + n, 1, 0:C]), pv[:, :, 1, :], cur[0:C, x0:x0 + n, 1, 0:C], ALU.add)
        if not last:
            P.cp(R(Mbuf[1 - cm][0:C, 0:nx, 0:C]), grp3(psa, nx, C), e="act")
        cm = 1 - cm
        ct = 1 - ct
        yield
    return MTT[ct][0:C, 0:nx, 1, 0:C]


def bc(ap, shape):
    return ap.unsqueeze(len(ap.shape)).broadcast_to(list(shape))


def bcm(ap, shape):
    return ap.unsqueeze(1).broadcast_to(list(shape))


PCOL = 1
S1COL = TP + 1 + 1
S2COL = S1COL + 32
NTOK = S2COL + 16 + 1
NBA = 256
CA = 128
NCH = TP // CA + 2


def build(stop=None):
    nc = bass.Bass("TRN2", target_bir_lowering=False)
    P = Prog(nc)
    P.init_psum()

    def din(name, shape):
        return nc.dram_tensor(name, list(shape), F32, kind="ExternalInput").ap()

    def dout(name, shape):
        return nc.dram_tensor(name, list(shape), F32, kind="ExternalOutput").ap()

    x_p = din("x_p", [TP, D])
    x_s = din("x_s", [2 * TS, D])
    conv_s = din("conv_s", [2, 3, 4096])
    delta_s = din("delta_s", [2, 8, 128, 256])
    shift_s = din("shift_s", [2, D])
    wkv_s = din("wkv_s", [2, 16, 64, 64])
    norm_w = din("norm_w", [2, D])
    final_norm_w = din("final_norm_w", [D])
    a_w_in = din("a_w_in", [D, 6160])
    a_conv_w = din("a_conv_w", [4, 4096])
    a_log = din("a_log", [8])
    a_dt_bias = din("a_dt_bias", [8])
    a_norm_w = din("a_norm_w", [256])
    a_w_out = din("a_w_out", [2048, D])
    b_mu = din("b_mu", [6, D])
    b_w_in = din("b_w_in", [D, 4096])
    b_w0 = din("b_w0", [D])
    b_w_w1 = din("b_w_w1", [D, 64])
    b_w_w2 = din("b_w_w2", [64, D])
    b_a0 = din("b_a0", [D])
    b_a_w1 = din("b_a_w1", [D, 64])
    b_a_w2 = din("b_a_w2", [64, D])
    b_k_k = din("b_k_k", [D])
    b_k_a = din("b_k_a", [D])
    b_r_k = din("b_r_k", [D])
    b_gn_w = din("b_gn_w", [D])
    b_gn_b = din("b_gn_b", [D])
    b_w_out = din("b_w_out", [D, D])

    y_p = dout("y_p", [TP, D])
    y_s = dout("y_s", [2 * TS, D])
    o_conv = dout("o_conv", [3, 3, 4096])
    o_delta = dout("o_delta", [3, 8, 128, 256])
    o_shift = dout("o_shift", [3, D])
    o_wkv = dout("o_wkv", [3, 16, 64, 64])
    dbg = dout("dbg", [NTILE * 128, D]) if stop else None

    ident = P.sb("ident", [128, 128])
    ones = P.sb("ones", [128, 128])
    mones = P.sb("mones", [128, 128])
    zeros = P.sb("zeros", [128, 128])
    Utri = P.sb("Utri", [128, 128])
    NEGT = P.sb("NEGT", [128, 128])
    MsT = P.sb("MsT", [128, 128])
    P.memset(ones[:], 1.0)
    ones_r = P.sb("ones_r", [128, 128])
    P.cp(R(ones_r[:]), ones[:], e="act")
    P.memset(mones[:], -1.0)
    P.memset(zeros[:], 0.0)
    P.asel(ident[:], ones[:], [[-1, 128]], ALU.is_equal, 0.0, 0, 1)
    P.asel(Utri[:], ones[:, :], [[1, 128]], ALU.is_ge, 0.0, 0, -1)
    P.asel(NEGT[:], zeros[:, :], [[1, 128]], ALU.is_ge, NEG, 0, -1)
    P.asel(MsT[:], ones[:, :], [[1, 128]], ALU.is_gt, 0.0, 0, -1)

    xres = [P.sb("xres%d" % i, [128, D]) for i in range(NTILE)]
    nw = P.sb("nw", [128, 2, KC])
    fnw = P.sb("fnw", [128, KC])
    P.dma("sp", nw[:], norm_w.rearrange("l (k p) -> p l k", p=128), writes=[nw[:]], allow_slow_non_contiguous=True)
    P.dma("sp", fnw[:], final_norm_w.rearrange("(k p) -> p k", p=128), writes=[fnw[:]],
          allow_slow_non_contiguous=True)
    ssq = P.sb("ssq", [128, 4])
    rstd = P.sb("rstd", [128, 4])
    phase_mark = P.sb_off
    hT = P.sb("hT", [128, KC, NTOK], BF16)
    xn = [P.sb("xn0", [128, D])] * 2

    def tile_rows(i):
        return 128 if i < 16 else 48

    def tile_col(i):
        return PCOL + i * 128 if i < 16 else S1COL

    def norm_to_hT(layer, hT):
        for i in range(NTILE):
            nt = tile_rows(i)
            xt = xres[i]
            xb = xn[i % 2]
            sl = slice(i % 4, i % 4 + 1)
            P.act(xb[0:nt, :], xt[0:nt, :], AF.Square, accum_out=ssq[0:nt, sl])
            P.act(rstd[0:nt, sl], ssq[0:nt, sl], AF.Sqrt, scale=1.0 / D, bias=RMS_EPS)
            P.recip(rstd[0:nt, sl], rstd[0:nt, sl])
            P.ts(xb[0:nt, :], xt[0:nt, :], rstd[0:nt, sl], ALU.mult)
            c0 = tile_col(i)
            for half in range(2):
                ps = P.ps()
                for j in range(4):
                    kc = half * 4 + j
                    P.tr(ps[:, j * 128:j * 128 + nt], xb[0:nt, kc * 128:(kc + 1) * 128], ident[0:nt, 0:nt])
                pv = ps[:, :].rearrange("p (j t) -> p j t", j=4)[:, :, 0:nt]
                P.tt(hT[:, half * 4:half * 4 + 4, c0:c0 + nt], pv,
                     bc(nw[:, layer, half * 4:half * 4 + 4], [128, 4, nt]), ALU.mult)

    for i in range(NTILE):
        if i < 16:
            P.dma("sp", xres[i][:], x_p[i * 128:(i + 1) * 128, :], writes=[xres[i][:]])
        else:
            P.memset(xres[i][:], 0.0)
            P.dma("sp", xres[i][0:16, :], x_s[0:16, :], writes=[xres[i][:]])
            P.dma("sp", xres[i][32:48, :], x_s[16:32, :], writes=[xres[i][:]])
    P.memset(hT[:, :, 0:1], 0.0)
    norm_to_hT(0, hT)

    blocks = [(0, PCOL + i * NBA, NBA, CA, NBA // CA, i * (NBA // CA)) for i in range(TP // NBA)] + \
             [(1, S1COL, 16, 16, 1, NCH - 2), (2, S2COL, 16, 16, 1, NCH - 1)]

    cwt = P.sb("cwt", [32, 4, 128])
    cw = P.sb("cw", [128, 4, 32])
    P.dma("sp", cwt[:], a_conv_w.rearrange("t (g c) -> g t c", c=128), writes=[cwt[:]])
    ps = P.ps()
    for t in range(4):
        P.tr(ps[:, t * 32:(t + 1) * 32], cwt[:, t, :], ident[0:32, 0:32])
    P.cp(cw[:].rearrange("p t g -> p (t g)"), ps[:, 0:128])
    halo_all = P.sb("halo_all", [128, 2, 3, 32])
    hrow = P.sb("hrow", [96, 2, 128])
    for s in range(2):
        P.dma("sp", hrow[:, s, :], conv_s[s].rearrange("t (g c) -> (t g) c", c=128), writes=[hrow[:]])
    ps = P.ps()
    for s in range(2):
        P.tr(ps[:, s * 96:(s + 1) * 96], hrow[:, s, :], ident[0:96, 0:96])
    P.cp(halo_all[:].rearrange("p s t g -> p (s t g)"), ps[:, 0:192])
    fin_all = P.sb("fin_all", [128, 3, 3, 32])
    anw = P.sb("anw", [128, 2])
    P.dma("sp", anw[:], a_norm_w.rearrange("(h p) -> p h", p=128), writes=[anw[:]], allow_slow_non_contiguous=True)
    P.ts(anw[:], anw[:], 0.5, ALU.mult)

    wba = P.sb("wba", [128, KC, 16], BF16)
    P.dma("pool", wba[:], a_w_in.rearrange("(k p) c -> p k c", p=128)[:, :, 6144:6160], writes=[wba[:]])
    NCP = TP // CA
    BA = P.sb("BA", [CA, NCH, 16])
    P.memset(BA[:], 0.0)
    ps = P.ps()
    for c in range(NCP):
        for kc in range(KC):
            P.mm(ps[0:CA, c * 16:(c + 1) * 16], hT[:, kc, PCOL + c * CA:PCOL + (c + 1) * CA], wba[:, kc, :],
                 start=(kc == 0), stop=(kc == KC - 1))
    P.cp(BA[:, 0:NCP, :].rearrange("p c k -> p (c k)"), ps[0:CA, 0:NCP * 16])
    ps = P.ps()
    for s, sc in enumerate((S1COL, S2COL)):
        for kc in range(KC):
            P.mm(ps[0:16, s * 16:(s + 1) * 16], hT[:, kc, sc:sc + 16], wba[:, kc, :],
                 start=(kc == 0), stop=(kc == KC - 1))
    P.cp(BA[0:16, NCP:NCP + 2, :].rearrange("p c k -> p (c k)"), ps[0:16, 0:32])
    alg = P.sb("alg", [CA, 8])
    dtb = P.sb("dtb", [CA, 8])
    P.dma("sp", alg[:], a_log.partition_broadcast(CA), writes=[alg[:]])
    P.dma("sp", dtb[:], a_dt_bias.partition_broadcast(CA), writes=[dtb[:]])
    P.act(alg[:], alg[:], AF.Exp)
    P.ts(alg[:], alg[:], -1.0, ALU.mult)
    beta = P.sb("beta", [CA, NCH, 8])
    gg = P.sb("gg", [CA, NCH, 8])
    Gc = P.sb("Gc", [CA, NCH, 8])
    Glb = P.sb("Glb", [128, NCH, 8])
    gl = P.sb("gl", [128, NCH, 8])
    eG = P.sb("eG", [CA, NCH, 8])
    bG = P.sb("bG", [CA, NCH, 8])
    dte = P.sb("dte", [CA, NCH, 8])
    P.act(beta[:], BA[:, :, 0:8], AF.Sigmoid)
    P.tt(gg[:], BA[:, :, 8:16], bcm(dtb[:], [CA, NCH, 8]), ALU.add)
    P.act(gg[:], gg[:], AF.Exp)
    P.act(gg[:], gg[:], AF.Ln, bias=1.0)
    P.tt(gg[:], gg[:], bcm(alg[:], [CA, NCH, 8]), ALU.mult)
    psG = P.ps()
    psL = P.ps()
    g2 = gg[:].rearrange("p c k -> p (c k)")
    GP = NCP * 8
    P.mm(psG[0:CA, 0:GP], Utri[0:CA, 0:CA], g2[:, 0:GP])
    P.mm(psG[0:16, GP:GP + 16], Utri[0:16, 0:16], g2[0:16, GP:GP + 16])
    P.mm(psL[:, 0:GP], ones[0:CA, :], g2[:, 0:GP])
    P.mm(psL[:, GP:GP + 16], ones[0:16, :], g2[0:16, GP:GP + 16])
    P.memset(Gc[:], 0.0)
    P.cp(Gc[:, 0:NCP, :].rearrange("p c k -> p (c k)"), psG[0:CA, 0:GP])
    P.cp(Gc[0:16, NCP:NCP + 2, :].rearrange("p c k -> p (c k)"), psG[0:16, GP:GP + 16])
    P.cp(Glb[:].rearrange("p c k -> p (c k)"), psL[:, 0:GP + 16])
    P.act(gl[:], Glb[:], AF.Exp)
    P.act(eG[:], Gc[:], AF.Exp)
    P.tt(bG[:], beta[:], eG[:], ALU.mult)
    hbeta = eG
    P.ts(hbeta[:], beta[:], 0.5, ALU.mult)
    P.tt(dte[:], Glb[0:CA], Gc[:], ALU.subtract)
    P.act(dte[:], dte[:], AF.Exp)

    Wh = [P.sb("Wh0", [128, KC, 768], BF16)] * 2
    Wo = [P.sb("Wo0", [128, 2, D], BF16)] * 2
    w_in_v = a_w_in.rearrange("(k p) c -> p k c", p=128)

    def load_head_w(h):
        sl = h % 2
        for (c0, n, o) in ((h * 128, 128, 0), (1024 + h * 128, 128, 128), (2048 + h * 256, 256, 256),
                           (4096 + h * 256, 256, 512)):
            P.dma("pool", Wh[sl][:, :, o:o + n], w_in_v[:, :, c0:c0 + n], writes=[Wh[sl][:]])

    def load_head_wo(h):
        sl = h % 2
        P.dma("pool", Wo[sl][:], a_w_out[h * 256:(h + 1) * 256, :].rearrange("(hh p) c -> p hh c", p=128),
              writes=[Wo[sl][:]])

    NB_ = NBA
    NC_ = NBA // CA
    pre = P.sb("pre", [128, 4, NB_ + 3])
    acc = P.sb("acc", [128, 4, NB_])
    P.split(acc, 4)
    P.split(pre, 4)
    qkv = P.sb("qkv", [128, 4, NB_])
    zs2 = [P.sb("zs%d" % i, [128, 2, NB_]) for i in range(2)]
    sqr = P.sb("sqr", [128, 2, NB_])
    sq = sqr
    rq = acc[:, 2:4]
    oT = P.sb("oTp", [128, 2, NB_])
    osq = P.sb("osq2", [128, 2, NB_])
    qT = P.sb("qT", [128, NB_])
    kT = P.sb("kT", [128, NB_])
    kbT = P.sb("kbT", [128, NB_])
    qgT2 = [P.sb("qgT%d" % i, [128, NB_]) for i in range(2)]
    dg = P.sb("dg", [128, 2, NB_])
    eGb = P.sb("eGb", [128, NB_])
    betab = P.sb("betab", [128, NB_])
    kbeG2 = [P.sb("kbeG%d" % i, [CA, NC_, 128]) for i in range(2)]
    ktk2 = [P.sb("ktk%d" % i, [CA, NC_, 128]) for i in range(2)]
    vb2 = [P.sb("vb%d" % i, [CA, NC_, 256]) for i in range(2)]
    gU = P.sb("gU", [CA, NC_, CA])
    DT = P.sb("DT", [CA, NC_, CA])
    DTs = P.sb("DTs", [CA, NC_, CA])
    qkT2 = [P.sb("qkT%d" % i, [CA, NC_, CA]) for i in range(2)]
    Mn = [P.sb("Mn%d" % i, [CA, NC_, CA]) for i in range(2)]
    MT0a2 = [P.sb("MT0a%d" % i, [CA, NC_, CA]) for i in range(2)]
    MTTa = [P.sb("MTTa%d" % i, [CA, NC_, 2, CA]) for i in range(2)]
    u0 = P.sb("u0", [CA, NC_, 256])
    wkT = P.sb("wkT", [128, NB_])
    uu = [P.sb("uu%d" % i, [CA, 256]) for i in range(2)]
    Sp_ = [P.sb("Sp%d" % i, [128, 256]) for i in range(2)]
    Ss_ = [P.sb("Ss%d" % i, [128, 256]) for i in range(2)]
    Sst = [Sp_, Ss_, Ss_]
    ors = P.sb("ors", [128, NB_])
    og = P.sb("og", [128, 2, NB_], BF16)
    print("SBUF left after layer-A alloc:", P.sb_top - P.sb_off)
    for t_, n_ in [(qkv, 4), (oT, 2), (osq, 2), (sqr, 2), (dg, 2), (gU, NC_), (DT, NC_), (DTs, NC_), (u0, NC_),
                   (og, 2)] + [(x_, 2) for x_ in zs2] + [(x_, NC_) for x_ in kbeG2 + ktk2 + vb2 + qkT2 + Mn + MT0a2 + MTTa]:
        P.split(t_, n_)

    DONE = object()

    def A_s1(it, h, blk):
        (seq, t0, NB, C, nch, c0) = blk
        par = it % 2
        W = Wh[0]
        ggrp = (h, 8 + h, 16 + 2 * h, 17 + 2 * h)
        zs, qgT, ktk, qkT = zs2[par], qgT2[par], ktk2[par], qkT2[par]
        kbeG, vb, MT0a = kbeG2[par], vb2[par], MT0a2[par]
        first_of_seq = (seq == 0 and t0 == PCOL) or seq > 0
        if seq == 0 and t0 == PCOL:
            load_head_w(h)
        if first_of_seq:
            if seq == 0:
                P.memset(pre[:, :, 0:3], 0.0)
            else:
                for gi, g in enumerate(ggrp):
                    with P.only(pre=[gi]):
                        P.cp(pre[:, gi, 0:3], halo_all[:, seq - 1, :, g], e="pool")
        for m in range(6):
            ps = P.ps("a1a")
            for kc in range(KC):
                P.mm(ps[:, 0:NB], W[:, kc, m * 128:(m + 1) * 128], hT[:, kc, t0:t0 + NB],
                     start=(kc == 0), stop=(kc == KC - 1))
            if m < 4:
                with P.only(pre=[m]):
                    P.cp(pre[:, m, 3:3 + NB], ps[:, 0:NB], e="act")
            else:
                P.act(zs[:, m - 4, 0:NB], ps[:, 0:NB], AF.Tanh, scale=0.5)
                P.stt(zs[:, m - 4, 0:NB], zs[:, m - 4, 0:NB], 1.0, ps[:, 0:NB], ALU.add, ALU.mult)
            yield
        for gi, g in enumerate(ggrp):
            with P.only(acc=[gi], pre=[gi]):
                P.act(acc[:, gi, 0:NB], pre[:, gi, 3:3 + NB], AF.Copy, scale=cw[:, 3, g:g + 1])
                for tap in (2, 1, 0):
                    P.stt(acc[:, gi, 0:NB], pre[:, gi, tap:tap + NB], cw[:, tap, g:g + 1], acc[:, gi, 0:NB],
                          ALU.mult, ALU.add)
            yield
        P.act(qkv[:, :, 0:NB], acc[:, :, 0:NB], AF.Tanh, scale=0.5)
        P.stt(qkv[:, :, 0:NB], qkv[:, :, 0:NB], 1.0, acc[:, :, 0:NB], ALU.add, ALU.mult)
        last = (t0 + NB == PCOL + TP) or seq > 0
        for gi, g in enumerate(ggrp):
            with P.only(pre=[gi]):
                if last:
                    P.cp(fin_all[:, seq, :, g], pre[:, gi, NB:NB + 3], e="pool")
                else:
                    P.cp(pre[:, gi, 0:3], pre[:, gi, NB:NB + 3], e="pool")
        yield
        P.act(R(sq[:, :, 0:NB]), qkv[:, 0:2, 0:NB], AF.Square)
        for j in range(2):
            ps = P.ps("a1m")
            P.mmr(ps[:, 0:NB], ones_r[:, :], sq[:, j, 0:NB])
            with P.only(acc=[2 + j]):
                if j == 0:
                    P.act(rq[:, j, 0:NB], ps[:, 0:NB], AF.Ln, scale=128.0, bias=512.0 * 1e-6)
                else:
                    P.act(rq[:, j, 0:NB], ps[:, 0:NB], AF.Ln, bias=4e-6)
        yield
        with P.only(acc=[2, 3]):
            P.act(rq[:, :, 0:NB], rq[:, :, 0:NB], AF.Exp, scale=-0.5)
        with P.only(acc=[2]):
            P.tt(R(qT[:, 0:NB]), qkv[:, 0, 0:NB], rq[:, 0, 0:NB], ALU.mult)
        with P.only(acc=[3]):
            P.tt(R(kT[:, 0:NB]), qkv[:, 1, 0:NB], rq[:, 1, 0:NB], ALU.mult)
        yield
        cs = slice(c0, c0 + nch)
        idb = bcm(ident[0:C, 0:C], [C, nch, C])
        dgv = dg[0:C, :, 0:NB].rearrange("p a (c i) -> p a c i", c=nch)
        P.tt(dgv[:, 0], idb, bc(Gc[0:C, cs, h], [C, nch, C]), ALU.mult)
        P.tt(dgv[:, 1], idb, bc(beta[0:C, cs, h], [C, nch, C]), ALU.mult)
        ps = P.ps("a1m")
        P.mm(ps[:, 0:NB], ones[0:C, :], dg[0:C, 0, 0:NB])
        P.act(eGb[:, 0:NB], ps[:, 0:NB], AF.Exp)
        ps = P.ps("a1m")
        P.mm(ps[:, 0:NB], ones[0:C, :], dg[0:C, 1, 0:NB])
        P.cp(betab[:, 0:NB], ps[:, 0:NB], e="act")
        yield
        P.tt(R(kbT[:, 0:NB]), kT[:, 0:NB], betab[:, 0:NB], ALU.mult)
        P.tt(R(qgT[:, 0:NB]), qT[:, 0:NB], eGb[:, 0:NB], ALU.mult)
        yield
        for c4 in range(0, nch, 4):
            n4 = min(4, nch - c4)
            ps = P.ps("a1m")
            for j in range(n4):
                c = c4 + j
                P.tr(ps[0:C, j * 128:(j + 1) * 128], kT[:, c * C:(c + 1) * C], ident[:, :])
            pv = ps[0:C, 0:n4 * 128].rearrange("p (j d) -> p j d", j=n4)
            P.tt(R(kbeG[0:C, c4:c4 + n4, :]), pv, bc(bG[0:C, c0 + c4:c0 + c4 + n4, h], [C, n4, 128]), ALU.mult)
            P.tt(R(ktk[0:C, c4:c4 + n4, :]), pv, bc(dte[0:C, c0 + c4:c0 + c4 + n4, h], [C, n4, 128]), ALU.mult)
            yield
        for c2 in range(0, nch, 2):
            n2 = min(2, nch - c2)
            ps = P.ps("a1m")
            for j in range(n2):
                c = c2 + j
                for half in range(2):
                    P.tr(ps[0:C, j * 256 + half * 128:j * 256 + (half + 1) * 128],
                         qkv[:, 2 + half, c * C:(c + 1) * C], ident[:, :])
            pv = ps[0:C, 0:n2 * 256].rearrange("p (j d) -> p j d", j=n2)
            P.tt(R(vb[0:C, c2:c2 + n2, :]), pv, bc(hbeta[0:C, c0 + c2:c0 + c2 + n2, h], [C, n2, 256]), ALU.mult)
            yield
        P.tt(gU[0:C, 0:nch, 0:C], bcm(Utri[0:C, 0:C], [C, nch, C]), bc(gg[0:C, cs, h], [C, nch, C]), ALU.mult)
        ps = P.ps("a1m")
        for c in range(nch):
            o = ps[0:C, c * C:(c + 1) * C]
            P.mm(o, ones[0:C, 0:C], gU[0:C, c, 0:C], start=True, stop=False)
            P.mm(o, gU[0:C, c, 0:C], mones[0:C, 0:C], start=False, stop=False)
            P.mm(o, ident[0:C, 0:C], NEGT[0:C, 0:C], start=False, stop=True)
        pv = ps[0:C, 0:nch * C].rearrange("p (c i) -> p c i", c=nch)
        P.act(DT[0:C, 0:nch, 0:C], pv, AF.Exp)
        P.tt(DTs[0:C, 0:nch, 0:C], DT[0:C, 0:nch, 0:C], bcm(MsT[0:C, 0:C], [C, nch, C]), ALU.mult, e="pool")
        yield
        ps = P.ps("a1m")
        for c in range(nch):
            P.mmr(ps[0:C, c * C:(c + 1) * C], kT[:, c * C:(c + 1) * C], kbT[:, c * C:(c + 1) * C])
        for c in range(nch):
            P.mmr(ps[0:C, 256 + c * C:256 + (c + 1) * C], kT[:, c * C:(c + 1) * C], qT[:, c * C:(c + 1) * C])
        pv = ps[0:C, 0:nch * C].rearrange("p (c i) -> p c i", c=nch)
        pv2 = ps[0:C, 256:256 + nch * C].rearrange("p (c i) -> p c i", c=nch)
        P.stt(R(MT0a[0:C, 0:nch, 0:C]), pv, -1.0, DTs[0:C, 0:nch, 0:C], ALU.mult, ALU.mult)
        P.tt(R(qkT[0:C, 0:nch, 0:C]), pv2, DT[0:C, 0:nch, 0:C], ALU.mult)
        yield
        ps = P.ps("a1m")
        for c in range(nch):
            P.tr(ps[0:C, c * C:(c + 1) * C], MT0a[0:C, c, 0:C], ident[0:C, 0:C])
        P.cp(R(Mn[0][0:C, 0:nch, 0:C]), ps[0:C, 0:nch * C].rearrange("p (c i) -> p c i", c=nch), e="act")
        yield
        TTf = yield from neumann(P, ident, MT0a[0:C, 0:nch, 0:C], None, Mn, MTTa, nch, C, pool="a1b")
        yield "DRAIN2"
        for c2 in range(0, nch, 2):
            n2 = min(2, nch - c2)
            ps = P.ps("a1b")
            for j in range(n2):
                P.mmr(ps[0:C, j * 256:(j + 1) * 256], TTf[:, c2 + j, :], vb[0:C, c2 + j, :])
            P.cp(u0[0:C, c2:c2 + n2, :], ps[0:C, 0:n2 * 256].rearrange("p (j d) -> p j d", j=n2), e="act")
        ps = P.ps("a1b")
        for c in range(nch):
            P.mmr(ps[:, c * C:(c + 1) * C], kbeG[0:C, c, :], TTf[:, c, :])
        P.cp(R(wkT[:, 0:NB]), ps[:, 0:NB], e="act")
        yield

    spar = [0, 0, 0]

    def A_s2(it, h, blk):
        (seq, t0, NB, C, nch, c0) = blk
        par = it % 2
        WO = Wo[0]
        zs, qgT, ktk, qkT = zs2[par], qgT2[par], ktk2[par], qkT2[par]
        first_of_seq = (seq == 0 and t0 == PCOL) or seq > 0
        if seq == 0 and t0 == PCOL:
            load_head_wo(h)
        if first_of_seq:
            spar[seq] = 0
            if seq == 0:
                P.cp(R(Sst[0][0][:]), zeros[:, 0:1].broadcast_to([128, 256]), e="act")
            else:
                P.dma("sp", Sst[seq][1][:], delta_s[seq - 1, h], writes=[Sst[seq][1][:]])
                P.cp(R(Sst[seq][0][:]), Sst[seq][1][:], e="act")
        pso = P.psb[7]
        for c in range(nch):
            Sc = Sst[seq][spar[seq]]
            Sn = Sst[seq][1 - spar[seq]]
            u = uu[c % 2]
            ps = P.ps("a2")
            P.mmr(ps[0:C, 0:256], wkT[:, c * C:(c + 1) * C], Sc[:, :])
            P.tt(R(u[0:C, :]), u0[0:C, c, :], ps[0:C, 0:256], ALU.subtract)
            yield
            ps2 = P.ps("a2")
            P.mmr(ps2[:, 0:256], ktk[0:C, c, :], u[0:C, :])
            P.stt(R(Sn[:, :]), Sc[:, :], gl[:, c0 + c, h:h + 1], ps2[:, 0:256], ALU.mult, ALU.add)
            for half in range(2):
                oo = pso[:, (half * nch + c) * C:(half * nch + c + 1) * C]
                P.mmr(oo, Sc[:, half * 128:(half + 1) * 128], qgT[:, c * C:(c + 1) * C], start=True, stop=False)
                P.mmr(oo, u[0:C, half * 128:(half + 1) * 128], qkT[0:C, c, 0:C], start=False, stop=True)
            spar[seq] = 1 - spar[seq]
            yield
        P.cp(oT[:, :, 0:NB], pso[:, 0:2 * NB].rearrange("p (a t) -> p a t", a=2), e="act")
        P.act(R(osq[:, :, 0:NB]), oT[:, :, 0:NB], AF.Square)
        ps = P.ps("a2")
        P.mmr(ps[:, 0:NB], ones_r[:, :], osq[:, 0, 0:NB], start=True, stop=False)
        P.mmr(ps[:, 0:NB], ones_r[:, :], osq[:, 1, 0:NB], start=False, stop=True)
        P.act(ors[:, 0:NB], ps[:, 0:NB], AF.Ln, scale=1.0 / 256.0, bias=RMS_EPS)
        yield
        P.act(ors[:, 0:NB], ors[:, 0:NB], AF.Exp, scale=-0.5)
        for half in range(2):
            P.stt(oT[:, half, 0:NB], oT[:, half, 0:NB], anw[:, half:half + 1], ors[:, 0:NB], ALU.mult, ALU.mult)
        P.tt(og[:, :, 0:NB], oT[:, :, 0:NB], zs[:, :, 0:NB], ALU.mult)
        yield
        for tt0 in range(0, NB, 128):
            nt = min(128, NB - tt0)
            if seq == 0:
                tile_i, prow = (t0 - PCOL + tt0) // 128, 0
            else:
                tile_i, prow = 16, (seq - 1) * 32
            for nh in range(2):
                ps = P.ps("a2")
                for half in range(2):
                    P.mm(ps[prow:prow + nt, :], og[:, half, tt0:tt0 + nt], WO[:, half, nh * 512:(nh + 1) * 512],
                         start=(half == 0), stop=(half == 1))
                xr = xres[tile_i][prow:prow + nt, nh * 512:(nh + 1) * 512]
                P.tt(xr, xr, ps[prow:prow + nt, :], ALU.add)
            yield
        last = (t0 + NB == PCOL + TP) or seq > 0
        if last:
            P.dma("sp", o_delta[seq, h], Sst[seq][spar[seq]][:], reads=[Sst[seq][spar[seq]][:]], final=True)

    def pipeline(items, s1, s2, ratio):
        g2 = None
        for it, item in enumerate(list(items) + [None]):
            g1 = s1(it, *item) if item is not None else None
            while g1 is not None or g2 is not None:
                if g2 is not None:
                    if next(g2, DONE) is DONE:
                        g2 = None
                if g1 is not None:
                    for _ in range(ratio if g2 is not None else 1000000):
                        r = next(g1, DONE)
                        if r is DONE:
                            g1 = None
                            break
                        if r == "DRAIN2":
                            while g2 is not None:
                                if next(g2, DONE) is DONE:
                                    g2 = None
            g2 = s2(it, *item) if item is not None else None

    pipeline([(h, blk) for h in range(H_A) for blk in blocks], A_s1, A_s2, 3)

    ps = P.ps()
    for s in range(3):
        P.tr(ps[0:96, s * 128:(s + 1) * 128], fin_all[:, s].rearrange("p t g -> p (t g)"), ident[:, :])
    for s in range(3):
        P.cp(acc[0:96, s, 0:128], ps[0:96, s * 128:(s + 1) * 128])
        P.dma("sp", o_conv[s].rearrange("t (g c) -> (t g) c", c=128), acc[0:96, s, 0:128], reads=[acc[:]], final=True)

    if stop == "A":
        for i in range(NTILE):
            nt = tile_rows(i)
            P.dma("sp", dbg[i * 128:i * 128 + nt, :], xres[i][0:nt, :], reads=[xres[i][:]], final=True)
        P.finish()
        return nc

    P.barrier()
    P.sb_off = phase_mark
    hT = P.sb("hT2", [128, KC, NTOK], BF16)
    shout = P.sb("shout", [128, 3, KC])
    P.memset(hT[:, :, 0:1], 0.0)
    mark_b0 = P.sb_off
    xn = [P.sb("xnB", [128, D])] * 2

    def norm_to_hT_B():
        for i in range(NTILE):
            nt = tile_rows(i)
            xt = xres[i]
            xb = xn[i % 2]
            sl = slice(i % 4, i % 4 + 1)
            P.act(xb[0:nt, :], xt[0:nt, :], AF.Square, accum_out=ssq[0:nt, sl])
            P.act(rstd[0:nt, sl], ssq[0:nt, sl], AF.Sqrt, scale=1.0 / D, bias=RMS_EPS)
            P.recip(rstd[0:nt, sl], rstd[0:nt, sl])
            P.ts(xb[0:nt, :], xt[0:nt, :], rstd[0:nt, sl], ALU.mult)
            c0 = tile_col(i)
            for half in range(2):
                ps = P.ps(2)
                for j in range(4):
                    kc = half * 4 + j
                    P.tr(ps[:, j * 128:j * 128 + nt], xb[0:nt, kc * 128:(kc + 1) * 128], ident[0:nt, 0:nt])
                pv4 = ps[:, :].rearrange("p (j t) -> p j t", j=4)
                pv = pv4[:, :, 0:nt]
                P.tt(hT[:, half * 4:half * 4 + 4, c0:c0 + nt], pv,
                     bc(nw[:, 1, half * 4:half * 4 + 4], [128, 4, nt]), ALU.mult)
                lastcols = {15: [(0, 127)], 16: [(1, 15), (2, 47)]}.get(i, [])
                for (sq_, col) in lastcols:
                    P.tt(shout[:, sq_, half * 4:half * 4 + 4], pv4[:, :, col], nw[:, 1, half * 4:half * 4 + 4], ALU.mult)

    norm_to_hT_B()
    P.barrier()
    P.sb_off = mark_b0
    for s_ in range(3):
        P.dma("sp", o_shift[s_].rearrange("(k p) -> p k", p=128), shout[:, s_, :], reads=[shout[:]], final=True,
              allow_slow_non_contiguous=True)
    shin = P.sb("shin", [128, 2, KC])
    P.dma("sp", shin[:], shift_s.rearrange("s (k p) -> p s k", p=128), writes=[shin[:]], allow_slow_non_contiguous=True)
    P.cp(hT[:, :, S1COL - 1], shin[:, 0, :])
    P.cp(hT[:, :, S2COL - 1], shin[:, 1, :])

    vecs = P.sb("vecs", [128, 13, KC])
    P.dma("sp", vecs[:, 0:6, :], b_mu.rearrange("g (k p) -> p g k", p=128), writes=[vecs[:]], allow_slow_non_contiguous=True)
    for vi, v_ in enumerate((b_w0, b_a0, b_k_k, b_k_a, b_r_k, b_gn_w, b_gn_b)):
        P.dma("sp", vecs[:, 6 + vi, :], v_.rearrange("(k p) -> p k", p=128), writes=[vecs[:]], allow_slow_non_contiguous=True)
    V_W0, V_A0, V_KK, V_KA, V_RK, V_GW, V_GB = range(6, 13)
    hvec = P.sb("hvec", [128, 2, KC])
    P.ts(hvec[:], vecs[:, 6:8, :], 0.5, ALU.mult)
    blk1 = P.sb("blk1", [128, 128])
    cst32 = P.sb("cst32", [128, 192])
    P.asel(cst32[:, 0:64], ones[:, 0:64], [[0, 64]], ALU.is_ge, 0.0, 63, -1)
    P.asel(cst32[:, 64:128], ones[:, 0:64], [[0, 64]], ALU.is_ge, 0.0, -64, 1)
    P.cp(R(blk1[:]), cst32[:, 0:128])
    CB = 128
    MXT = P.sb("MXT", [CB, 2 * CB])
    P.cp(MXT[:, 0:CB], MsT[0:CB, 0:CB], e="pool")
    P.cp(MXT[:, CB:2 * CB], Utri[0:CB, 0:CB], e="pool")
    MsL = P.sb("MsL", [CB, CB])
    Sh = P.sb("Sh", [128, 64])
    P.asel(cst32[:, 128:192], ones[:, 0:64], [[-1, 64]], ALU.is_equal, 0.0, -64, 1)
    P.cp(R(Sh[:]), cst32[:, 128:192])
    P.asel(MsL[:], ones[0:CB, 0:CB], [[-1, CB]], ALU.is_gt, 0.0, 0, 1)
    rmask = P.sb("rmask", [128, 256])
    P.memset(rmask[:], 1.0)
    for c in range(256 // CB):
        P.memset(rmask[:, c * CB:c * CB + 1], 0.0)

    NBB = 256
    t1T = P.sb("t1T", [64, NTOK], BF16)
    a1T = P.sb("a1T", [64, NTOK], BF16)
    w2b = P.sb("w2b", [64, D], BF16)
    a2b = P.sb("a2b", [64, D], BF16)
    blk_mark = P.sb_off
    lw1 = P.sb("lw1", [128, KC, 2, 64], BF16)
    lw1p = P.sb("lw1p", [128, KC, 2, 64], BF16)
    lw1pp = P.sb("lw1pp", [128, KC, 2, 64], BF16)
    P.dma("pool", lw1[:, :, 0, :], b_w_w1.rearrange("(k p) c -> p k c", p=128), writes=[lw1[:]])
    P.dma("pool", lw1[:, :, 1, :], b_a_w1.rearrange("(k p) c -> p k c", p=128), writes=[lw1[:]])
    for j in range(2):
        P.tt(lw1p[:, :, j, :], lw1[:, :, j, :], bc(vecs[:, 4 + j, :], [128, KC, 64]), ALU.mult)
    P.tt(lw1pp[:], lw1[:], lw1p[:], ALU.subtract)
    P.dma("pool", w2b[:], b_w_w2[:, :], writes=[w2b[:]])
    P.dma("pool", a2b[:], b_a_w2[:, :], writes=[a2b[:]])
    col_ranges = [(PCOL + i * 512, 512) for i in range(4)] + [(S1COL, 16), (S2COL, 16)]
    for (cc0, n) in col_ranges:
        for j, dst in enumerate((t1T, a1T)):
            ps = P.ps(2)
            for kc in range(KC):
                P.mm(ps[0:64, 0:n], lw1pp[:, kc, j, :], hT[:, kc, cc0:cc0 + n], start=(kc == 0), stop=False)
                P.mm(ps[0:64, 0:n], lw1p[:, kc, j, :], hT[:, kc, cc0 - 1:cc0 - 1 + n], start=False, stop=(kc == KC - 1))
            P.act(dst[:, cc0:cc0 + n], ps[0:64, 0:n], AF.Tanh if j == 0 else AF.Copy)

    P.barrier()
    P.sb_off = blk_mark
    Wp = [P.sb("Wp0", [128, KC, 4, 128], BF16)] * 2
    Wq = P.sb("Wq", [128, KC, 4, 128], BF16)
    Wob = [P.sb("Wob0", [128, D], BF16)] * 2
    b_in_v = b_w_in.rearrange("(k p) (g c) -> p k g c", p=128, g=4)

    def load_pair_w(pr):
        for g_ in range(4):
            P.dma("pool", Wp[pr % 2][:, :, g_, :], b_in_v[:, :, g_, pr * 128:(pr + 1) * 128], writes=[Wp[pr % 2][:]])
        P.dma("pool", Wob[pr % 2][:], b_w_out[pr * 128:(pr + 1) * 128, :], writes=[Wob[pr % 2][:]])

    NCB = NBB // CB
    NX = 2 * NCB
    rkvT = P.sb("rkvT", [128, 3, NBB])
    zsB2 = [P.sb("zsB%d" % i, [128, NBB]) for i in range(2)]
    s2tmp = P.sb("s2tmp", [128, 2, NBB])
    lwT = P.sb("lwT", [128, NBB])
    aT = P.sb("aT", [128, NBB])
    cwv = P.sb("cwv", [128, NBB])
    eW = P.sb("eW", [128, 3, NBB])
    tmpB = P.sb("tmpB", [128, 4, NBB])
    kkT = P.sb("kkT", [128, NBB])
    k2T = P.sb("k2T", [128, NBB])
    arT2 = [P.sb("arT%d" % i, [128, 2, NBB]) for i in range(2)]
    bkT = P.sb("bkT", [128, 2, NBB])
    bkh = P.sb("bkh", [128, 2, NBB])
    Wc = P.sb("Wc", [128, NCB])
    rkb = P.sb("rkb", [128, NBB])
    vt = P.sb("vt", [CB, NCB, 128])
    bht = P.sb("bht", [CB, NCB, 128])
    kht = P.sb("kht", [CB, NCB, 128])
    XA = P.sb("XA", [CB, NX, 2 * CB])
    XB = P.sb("XB", [CB, NX, 2 * CB])
    MnB = [P.sb("MnB%d" % i, [CB, NX, CB]) for i in range(2)]
    MTTb = [P.sb("MTTb%d" % i, [CB, NX, 2, CB]) for i in range(2)]
    for m_ in MTTb:
        P.split(m_, 2)
    Rsb = [P.sb("Rsb%d" % i, [CB, 128]) for i in range(2)]
    Usb = [P.sb("Usb%d" % i, [CB, 128]) for i in range(2)]
    StP = [[P.sb("StP%d_%d" % (i, hd), [64, 64]) for hd in range(2)] for i in range(2)]
    StS = [[P.sb("StS%d_%d" % (i, hd), [64, 64]) for hd in range(2)] for i in range(2)]
    StB = [StP, StS, StS]
    stio = P.sb("stio", [64, 128])
    ar12 = [P.sb("ar1_%d" % i, [64, 2, NBB]) for i in range(2)]
    bk1 = P.sb("bk1", [64, 2, NBB])
    Wc1 = P.sb("Wc1", [64, NCB])
    sqB = P.sb("sqB", [128, 4, NBB])
    oTB = sqB[:, 2]
    ocB = s2tmp[:, 0]
    osB = sqB[:, 3]
    ogB = P.sb("ogB", [128, NBB], BF16)
    print("SBUF left after layer-B alloc:", P.sb_top - P.sb_off)
    for t_, n_ in [(rkvT, 3), (eW, 3), (tmpB, 4), (sqB, 4), (bkT, 2), (bkh, 2), (XA, NX), (XB, NX), (bk1, 2),
                   (vt, NCB), (bht, NCB), (kht, NCB), (s2tmp, 2)] + [(x_, 2) for x_ in arT2 + ar12] + \
            [(x_, NX) for x_ in MnB]:
        P.split(t_, n_)
    blocksB = [(0, PCOL + i * NBB, NBB, CB, NBB // CB) for i in range(TP // NBB)] + \
              [(1, S1COL, 16, 16, 1), (2, S2COL, 16, 16, 1)]
    ENH = -float(np.exp(-0.5))

    load_pair_w(0)
    nblkB = 0
    for pr in range(8):
        W = Wp[pr % 2]
        WO = Wob[pr % 2]
        for g_ in range(4):
            P.tt(Wq[:, :, g_, :], W[:, :, g_, :], bc(vecs[:, g_, :], [128, KC, 128]), ALU.mult)
        P.tt(W[:], W[:], Wq[:], ALU.subtract)
        Wr = W
        spar = [0, 0, 0]
        cur_seq = -1
        for (seq, t0, NB, C, nch) in blocksB:
            nx = 2 * nch
            bpar = nblkB % 2
            nblkB += 1
            zsB, arT, ar1 = zsB2[bpar], arT2[bpar], ar12[bpar]
            if seq != cur_seq:
                cur_seq = seq
                spar[seq] = 0
                if seq == 0:
                    for hd in range(2):
                        P.cp(R(StB[0][0][hd][:]), zeros[0:64, 0:64], e="act")
                else:
                    P.dma("sp", stio[:].rearrange("v (h k) -> v h k", h=2),
                          wkv_s[seq - 1, 2 * pr:2 * pr + 2].rearrange("h v k -> v h k"), writes=[stio[:]])
                    for hd in range(2):
                        ps = P.ps("b1")
                        P.tr(ps[0:64, 0:64], stio[:, hd * 64:(hd + 1) * 64], ident[0:64, 0:64])
                        P.cp(R(StB[seq][0][hd][:]), ps[0:64, 0:64])
            for g_ in range(4):
                ps = P.ps("b1")
                for kc in range(KC):
                    P.mm(ps[:, 0:NB], Wr[:, kc, g_, :], hT[:, kc, t0:t0 + NB], start=(kc == 0), stop=False)
                    P.mm(ps[:, 0:NB], Wq[:, kc, g_, :], hT[:, kc, t0 - 1:t0 - 1 + NB], start=False, stop=(kc == KC - 1))
                if g_ < 3:
                    P.cp(rkvT[:, g_, 0:NB], ps[:, 0:NB], e="act")
                else:
                    P.act(zsB[:, 0:NB], ps[:, 0:NB], AF.Tanh, scale=0.5)
                    P.stt(zsB[:, 0:NB], zsB[:, 0:NB], 1.0, ps[:, 0:NB], ALU.add, ALU.mult)
            ps = P.ps("b1")
            P.mm(ps[:, 0:NB], w2b[:, pr * 128:(pr + 1) * 128], t1T[:, t0:t0 + NB])
            P.act(lwT[:, 0:NB], ps[:, 0:NB], AF.Tanh, scale=0.5, bias=hvec[:, 0, pr:pr + 1])
            P.ts(lwT[:, 0:NB], lwT[:, 0:NB], 0.5 * ENH, ALU.mult, 0.5 * ENH, ALU.add)
            ps = P.ps("b1")
            P.mm(ps[:, 0:NB], a2b[:, pr * 128:(pr + 1) * 128], a1T[:, t0:t0 + NB])
            P.act(aT[:, 0:NB], ps[:, 0:NB], AF.Tanh, scale=0.5, bias=hvec[:, 1, pr:pr + 1])
            P.ts(aT[:, 0:NB], aT[:, 0:NB], 0.5, ALU.mult, 0.5, ALU.add)
            rT = rkvT[:, 0, 0:NB]
            kT_ = rkvT[:, 1, 0:NB]
            vT_ = rkvT[:, 2, 0:NB]
            P.ts(kkT[:, 0:NB], kT_, vecs[:, V_KK, pr:pr + 1], ALU.mult)
            P.act(R(sqB[:, 0, 0:NB]), kT_, AF.Square, scale=vecs[:, V_KK, pr:pr + 1])
            ps = P.ps("b1")
            P.mmr(ps[:, 0:NB], blk1[:, :], sqB[:, 0, 0:NB])
            P.act(tmpB[:, 1, 0:NB], ps[:, 0:NB], AF.Ln, bias=1e-6)
            P.act(tmpB[:, 1, 0:NB], tmpB[:, 1, 0:NB], AF.Exp, scale=-0.5)
            P.tt(kkT[:, 0:NB], kkT[:, 0:NB], tmpB[:, 1, 0:NB], ALU.mult)
            P.ts(tmpB[:, 2, 0:NB], aT[:, 0:NB], -1.0, ALU.add, vecs[:, V_KA, pr:pr + 1], ALU.mult)
            P.ts(tmpB[:, 2, 0:NB], tmpB[:, 2, 0:NB], 1.0, ALU.add)
            P.tt(k2T[:, 0:NB], kT_, tmpB[:, 2, 0:NB], ALU.mult)
            P.scan(cwv[:, 0:NB], rmask[:, 0:NB], lwT[:, 0:NB])
            P.act(eW[:, 0, 0:NB], cwv[:, 0:NB], AF.Exp)
            P.act(eW[:, 1, 0:NB], cwv[:, 0:NB], AF.Exp, scale=-1.0)
            P.tt(tmpB[:, 3, 0:NB], cwv[:, 0:NB], lwT[:, 0:NB], ALU.subtract)
            P.act(eW[:, 2, 0:NB], tmpB[:, 3, 0:NB], AF.Exp)
            P.stt(R(arT[:, 0, 0:NB]), kkT[:, 0:NB], -1.0, eW[:, 2, 0:NB], ALU.mult, ALU.mult)
            P.tt(R(arT[:, 1, 0:NB]), rT, eW[:, 0, 0:NB], ALU.mult)
            P.tt(tmpB[:, 0, 0:NB], kkT[:, 0:NB], aT[:, 0:NB], ALU.mult)
            P.tt(R(bkT[:, 0, 0:NB]), tmpB[:, 0, 0:NB], eW[:, 1, 0:NB], ALU.mult)
            P.tt(R(bkT[:, 1, 0:NB]), k2T[:, 0:NB], eW[:, 1, 0:NB], ALU.mult)
            ewc = eW[:, 0, 0:NB].rearrange("p (c i) -> p c i", c=nch)[:, :, C - 1]
            P.cp(Wc[:, 0:nch], ewc)
            bkv = bkT[:, :, 0:NB].rearrange("p a (c i) -> p a c i", c=nch)
            bhv = bkh[:, :, 0:NB].rearrange("p a (c i) -> p a c i", c=nch)
            for a_ in range(2):
                P.tt(bhv[:, a_], bkv[:, a_], bc(Wc[:, 0:nch], [128, nch, C]), ALU.mult)
            P.stt(R(sqB[:, 1, 0:NB]), rT, vecs[:, V_RK, pr:pr + 1], k2T[:, 0:NB], ALU.mult, ALU.mult)
            ps = P.ps("b1")
            P.mmr(ps[:, 0:NB], blk1[:, :], sqB[:, 1, 0:NB])
            P.tt(rkb[:, 0:NB], ps[:, 0:NB], vT_, ALU.mult)
            ps = P.ps("b1")
            P.mmr(ps[0:64, 0:2 * NB], Sh[:, :], arT[:, :, 0:NB])
            P.cp(R(ar1[:, :, 0:NB]), ps[0:64, 0:2 * NB].rearrange("p (a t) -> p a t", a=2), e="act")
            ps = P.ps("b1")
            P.mmr(ps[0:64, 0:2 * NB], Sh[:, :], bkT[:, :, 0:NB])
            P.cp(R(bk1[:, :, 0:NB]), ps[0:64, 0:2 * NB].rearrange("p (a t) -> p a t", a=2), e="act")
            ps = P.ps("b1")
            P.mm(ps[0:64, 0:nch], cst32[:, 128:192], Wc[:, 0:nch])
            P.cp(Wc1[:, 0:nch], ps[0:64, 0:nch])
            AR = [arT[0:64], ar1[:]]
            BK = [bkT[0:64], bk1[:]]
            WC = [Wc[0:64], Wc1[:]]
            for (src, dst) in ((vT_, vt), (bkh[:, 0, 0:NB], bht), (bkh[:, 1, 0:NB], kht)):
                ps = P.ps("b1")
                for c in range(nch):
                    P.tr(ps[0:C, c * 128:(c + 1) * 128], src[:, c * C:(c + 1) * C], ident[:, :])
                P.cp(R(dst[0:C, 0:nch, :]), ps[0:C, 0:nch * 128].rearrange("p (c d) -> p c d", c=nch), e="act")
            psN = P.ps("b1")
            for hd in range(2):
                hs = slice(hd * 64, (hd + 1) * 64)
                psA = P.ps("b1")
                psB_ = P.ps("b1")
                for c in range(nch):
                    csl = slice(c * C, (c + 1) * C)
                    x_ = hd * nch + c
                    P.mmr(psA[0:C, c * 2 * C:(c + 1) * 2 * C], BK[hd][:, 0, csl], AR[hd][:, :, csl])
                    P.mmr(psB_[0:C, c * 2 * C:(c + 1) * 2 * C], BK[hd][:, 1, csl], AR[hd][:, :, csl])
                    P.mmr(psN[0:C, x_ * C:(x_ + 1) * C], AR[hd][:, 0, csl], BK[hd][:, 0, csl])
                for (psx, dstx) in ((psA, XA), (psB_, XB)):
                    pv = psx[0:C, 0:nch * 2 * C].rearrange("p (c a i) -> p c a i", c=nch, a=2)
                    dv = dstx[0:C, hd * nch:(hd + 1) * nch, :].rearrange("p c (a i) -> p c a i", a=2)[:, :, :, 0:C]
                    mv = MXT[0:C, :].rearrange("p (a i) -> p a i", a=2)[:, :, 0:C].unsqueeze(1).broadcast_to([C, nch, 2, C])
                    P.tt(R(dv), pv, mv, ALU.mult)
            P.tt(R(MnB[0][0:C, 0:nx, 0:C]), psN[0:C, 0:nx * C].rearrange("p (x i) -> p x i", x=nx),
                 bcm(MsL[0:C, 0:C], [C, nx, C]), ALU.mult)
            gen_ = neumann(P, ident, XA[0:C, 0:nx, 0:C], None, MnB, MTTb, nx, C, pool="b1")
            while True:
                try:
                    next(gen_)
                except StopIteration as e_:
                    TTf = e_.value
                    break
            pso = P.psb[7]
            for c in range(nch):
                csl = slice(c * C, (c + 1) * C)
                Sc = StB[seq][spar[seq]]
                Sn = StB[seq][1 - spar[seq]]
                Rb = Rsb[c % 2]
                Ub = Usb[c % 2]
                ps = P.ps("b2")
                for hd in range(2):
                    hs = slice(hd * 64, (hd + 1) * 64)
                    x_ = hd * nch + c
                    P.mmr(ps[0:C, hs], AR[hd][:, 0, csl], Sc[hd][:, :], start=True, stop=False)
                    P.mmr(ps[0:C, hs], XB[0:C, x_, 0:C], vt[0:C, c, hs], start=False, stop=True)
                P.cp(R(Rb[0:C, :]), ps[0:C, 0:128], e="act")
                ps = P.ps("b2")
                for hd in range(2):
                    hs = slice(hd * 64, (hd + 1) * 64)
                    x_ = hd * nch + c
                    P.mmr(ps[0:C, hs], TTf[:, x_, :], Rb[0:C, hs])
                P.cp(R(Ub[0:C, :]), ps[0:C, 0:128], e="act")
                for hd in range(2):
                    hs = slice(hd * 64, (hd + 1) * 64)
                    x_ = hd * nch + c
                    oo = pso[hs, csl]
                    mmf = P.mmr if hd == 0 else P.mm
                    mmf(oo, Sc[hd][:, :], AR[hd][:, 1, csl], start=True, stop=False)
                    mmf(oo, Ub[0:C, hs], XA[0:C, x_, CB:CB + C], start=False, stop=False)
                    mmf(oo, vt[0:C, c, hs], XB[0:C, x_, CB:CB + C], start=False, stop=True)
                ps2 = P.ps("b2")
                for hd in range(2):
                    hs = slice(hd * 64, (hd + 1) * 64)
                    P.mmr(ps2[0:64, hs], bht[0:C, c, hs], Ub[0:C, hs], start=True, stop=False)
                    P.mmr(ps2[0:64, hs], kht[0:C, c, hs], vt[0:C, c, hs], start=False, stop=True)
                for hd in range(2):
                    hs = slice(hd * 64, (hd + 1) * 64)
                    P.stt(R(Sn[hd][:, :]), Sc[hd][:, :], WC[hd][:, c:c + 1], ps2[0:64, hs], ALU.mult, ALU.add)
                spar[seq] = 1 - spar[seq]
            P.cp(R(oTB[:, 0:NB]), pso[:, 0:NB], e="act")
            ps = P.ps("b2")
            P.mmr(ps[:, 0:NB], blk1[:, :], oTB[:, 0:NB])
            P.stt(ocB[:, 0:NB], ps[:, 0:NB], -1.0 / 64.0, oTB[:, 0:NB], ALU.mult, ALU.add)
            P.act(R(osB[:, 0:NB]), ocB[:, 0:NB], AF.Square)
            ps = P.ps("b2")
            P.mmr(ps[:, 0:NB], blk1[:, :], osB[:, 0:NB])
            P.act(s2tmp[:, 1, 0:NB], ps[:, 0:NB], AF.Ln, scale=1.0 / 64.0, bias=GN_EPS)
            P.act(s2tmp[:, 1, 0:NB], s2tmp[:, 1, 0:NB], AF.Exp, scale=-0.5)
            P.tt(ocB[:, 0:NB], ocB[:, 0:NB], s2tmp[:, 1, 0:NB], ALU.mult)
            P.ts(ocB[:, 0:NB], ocB[:, 0:NB], vecs[:, V_GW, pr:pr + 1], ALU.mult, vecs[:, V_GB, pr:pr + 1], ALU.add)
            P.tt(ocB[:, 0:NB], ocB[:, 0:NB], rkb[:, 0:NB], ALU.add)
            P.stt(ogB[:, 0:NB], ocB[:, 0:NB], 0.5, zsB[:, 0:NB], ALU.mult, ALU.mult)
            for tt0 in range(0, NB, 128):
                nt = min(128, NB - tt0)
                if seq == 0:
                    tile_i, prow = (t0 - PCOL + tt0) // 128, 0
                else:
                    tile_i, prow = 16, (seq - 1) * 32
                for nh in range(2):
                    ps = P.ps("b2")
                    P.mm(ps[prow:prow + nt, :], ogB[:, tt0:tt0 + nt], WO[:, nh * 512:(nh + 1) * 512])
                    xr = xres[tile_i][prow:prow + nt, nh * 512:(nh + 1) * 512]
                    P.tt(xr, xr, ps[prow:prow + nt, :], ALU.add)
            last = (t0 + NB == PCOL + TP) or seq > 0
            if last:
                Sf = StB[seq][spar[seq]]
                ps = P.ps("b2")
                for hd in range(2):
                    P.tr(ps[0:64, hd * 64:(hd + 1) * 64], Sf[hd][:, :], ident[0:64, 0:64])
                P.cp(stio[:, :], ps[0:64, 0:128])
                P.dma("sp", o_wkv[seq, 2 * pr:2 * pr + 2].rearrange("h v k -> v h k"),
                      stio[:].rearrange("v (h k) -> v h k", h=2), reads=[stio[:]], final=True)
        if pr + 1 < 8:
            load_pair_w(pr + 1)

    P.barrier()
    P.sb_off = blk_mark
    xn = [P.sb("xnF", [128, D])] * 2
    fnwb = P.sb("fnwb", [128, D])
    P.dma("sp", fnwb[:], final_norm_w.partition_broadcast(128), writes=[fnwb[:]])
    for i in range(NTILE):
        nt = tile_rows(i)
        xt = xres[i]
        xb = xn[0]
        sl = slice(i % 4, i % 4 + 1)
        P.act(xb[0:nt, :], xt[0:nt, :], AF.Square, accum_out=ssq[0:nt, sl])
        P.act(rstd[0:nt, sl], ssq[0:nt, sl], AF.Sqrt, scale=1.0 / D, bias=RMS_EPS)
        P.recip(rstd[0:nt, sl], rstd[0:nt, sl])
        P.stt(xb[0:nt, :], xt[0:nt, :], rstd[0:nt, sl], fnwb[0:nt, :], ALU.mult, ALU.mult)
        if i < 16:
            P.dma("sp", y_p[i * 128:(i + 1) * 128, :], xb[:, :], reads=[xb[:]], final=True)
        else:
            P.dma("sp", y_s[0:16, :], xb[0:16, :], reads=[xb[:]], final=True)
            P.dma("sp", y_s[16:32, :], xb[32:48, :], reads=[xb[:]], final=True)
    P.finish()
    return nc


_NC_CACHE = {}


def make_in_maps(inputs):
    g = lambda k: np.ascontiguousarray(np.asarray(inputs[k], dtype=np.float32))
    xp, xs = g("x_prompt"), g("x_sample")
    cc, sd, ss, sw = g("cache_conv_a"), g("state_delta_a"), g("state_shift_b"), g("state_wkv_b")
    shared = {
        "norm_w": g("norm_w"), "final_norm_w": g("final_norm_w"), "a_w_in": g("a_w_in")[0],
        "a_conv_w": g("a_conv_w")[0], "a_log": g("a_log")[0], "a_dt_bias": g("a_dt_bias")[0],
        "a_norm_w": g("a_norm_w")[0], "a_w_out": g("a_w_out")[0], "b_mu": g("b_mu")[0],
        "b_w_in": g("b_w_in")[0], "b_w0": g("b_w0")[0], "b_w_w1": g("b_w_w1")[0], "b_w_w2": g("b_w_w2")[0],
        "b_a0": g("b_a0")[0], "b_a_w1": g("b_a_w1")[0], "b_a_w2": g("b_a_w2")[0], "b_k_k": g("b_k_k")[0],
        "b_k_a": g("b_k_a")[0], "b_r_k": g("b_r_k")[0].reshape(-1), "b_gn_w": g("b_gn_w")[0],
        "b_gn_b": g("b_gn_b")[0], "b_w_out": g("b_w_out")[0],
    }
    maps = []
    for i in range(8):
        m = dict(shared)
        m["x_p"] = xp[i]
        m["x_s"] = np.ascontiguousarray(xs[2 * i:2 * i + 2].reshape(2 * TS, D))
        m["conv_s"] = np.ascontiguousarray(cc[0, 2 * i:2 * i + 2])
        m["delta_s"] = np.ascontiguousarray(sd[0, 2 * i:2 * i + 2])
        m["shift_s"] = np.ascontiguousarray(ss[0, 2 * i:2 * i + 2])
        m["wkv_s"] = np.ascontiguousarray(sw[0, 2 * i:2 * i + 2])
        maps.append(m)
    return maps


def kernel(**inputs):
    if "nc" not in _NC_CACHE:
        _NC_CACHE["nc"] = build()
    nc = _NC_CACHE["nc"]
    maps = make_in_maps(inputs)
    res = run_bass_kernel_spmd(nc, maps, core_ids=list(range(8)))
    R = res.results
    y_prompt = np.stack([R[i]["y_p"] for i in range(8)], 0)
    y_sample = np.concatenate([R[i]["y_s"].reshape(2, TS, D) for i in range(8)], 0)

    def pick(name, sl):
        return np.stack([R[i][name][sl] for i in range(8)], 0)[None] if isinstance(sl, int) else \
            np.concatenate([R[i][name][sl] for i in range(8)], 0)[None]

    p_conv, s_conv = pick("o_conv", 0), pick("o_conv", slice(1, 3))
    p_delta, s_delta = pick("o_delta", 0), pick("o_delta", slice(1, 3))
    p_shift, s_shift = pick("o_shift", 0), pick("o_shift", slice(1, 3))
    p_wkv, s_wkv = pick("o_wkv", 0), pick("o_wkv", slice(1, 3))
    return (y_prompt, y_sample, p_conv, p_delta, p_shift, p_wkv, s_conv, s_delta, s_shift, s_wkv)
```

```python
import numpy as np
import concourse.bass as bass
import concourse.mybir as mybir
from concourse.bass_utils import run_bass_kernel_spmd

F32 = mybir.dt.float32
BF16 = mybir.dt.bfloat16
AF = mybir.ActivationFunctionType
ALU = mybir.AluOpType
AX = mybir.AxisListType

D = 1024
KC = 8
TP = 2048
TS = 16
NTILE = 17
H_A = 8
RMS_EPS = 1e-6
GN_EPS = 64e-5
NEG = -30000.0


class Trk:
    __slots__ = ("lw", "rd", "dsem", "dcount")

    def __init__(self):
        self.lw = None
        self.rd = []
        self.dsem = None
        self.dcount = 0


def fsz(ap):
    n = 1
    for d in ap.shape[1:]:
        n *= d
    return n


class Prog:
    WINDOW = 4000
    LAT = 700.0
    LAT_SAME = 200.0
    PE_SWITCH = 0.0
    EPS = 150.0

    def __init__(self, nc):
        self.nc = nc
        self.h = {"pe": nc.tensor, "act": nc.scalar, "dve": nc.vector, "pool": nc.gpsimd, "sp": nc.sync}
        self.sem = {k: nc.alloc_semaphore("sem_" + k) for k in self.h}
        self.cnt = {k: 0 for k in self.h}
        self.known = {k: {} for k in self.h}
        self.trk = {}
        self.ops = []
        self.nps = 0
        self.npool = {}
        self.psb = []
        self.nsem = 0
        self.sb_off = nc.sbuf_base
        self.sb_top = nc.sbuf_top
        self.seg_start = 0
        self.sel = {}
        self.pecls = {}

    def sb(self, name, shape, dt=F32):
        n = 1
        for d in shape[1:]:
            n *= d
        nbytes = n * (2 if dt == BF16 else 4)
        off = (self.sb_off + 63) // 64 * 64
        assert off + nbytes <= self.sb_top, "SBUF overflow at %s: need %d have %d" % (name, nbytes, self.sb_top - off)
        t = self.nc.alloc_sbuf_tensor_at(name, list(shape), dt, offset=off)
        self.sb_off = off + nbytes
        self.trk[t.name] = Trk()
        return t

    def barrier(self):
        self.ops.append(("fence", None, None, (), 0.0, False, None))
        for t in self._all_trk():
            t.lw = None
            t.rd = []

    def _all_trk(self):
        for t in self.trk.values():
            if isinstance(t, list):
                for x in t:
                    yield x
            else:
                yield t

    def init_psum(self):
        for i in range(8):
            t = self.nc.alloc_psum_tensor("psb%d" % i, [128, 512], F32)
            self.trk["psb%d" % i] = Trk()
            self.psb.append(t)

    POOLS = {0: (0, 1, 2, 3, 4, 5, 6), 2: (0, 1, 2, 3, 4, 5, 6),
             "a1a": (0, 1), "a1m": (2,), "a1b": (3, 4), "a2": (5, 6),
             "b1": (0, 1, 2, 3, 4), "b2": (5, 6)}

    def ps(self, pool=0):
        banks = self.POOLS[pool]
        n = self.npool.get(pool, 0)
        self.npool[pool] = n + 1
        return self.psb[banks[n % len(banks)]]

    def split(self, tensor, n):
        self.trk[tensor.name] = [Trk() for _ in range(n)]

    def only(self, **sel):
        prog = self

        class _Ctx:
            def __enter__(self_):
                self_.old = dict(prog.sel)
                prog.sel.update(sel)

            def __exit__(self_, *a):
                prog.sel = self_.old
        return _Ctx()

    def _tks(self, ap):
        t = self.trk[ap.tensor.name]
        if isinstance(t, list):
            idx = self.sel.get(ap.tensor.name.rsplit("_", 1)[0])
            if idx is not None:
                return [t[i] for i in idx]
            try:
                pat = ap.ap
                F = 1
                for d in ap.tensor.shape[1:]:
                    F *= d
                pstride = pat[0][0]
                off = int(ap.offset)
                if pstride != F:
                    return list(t)
                f0 = off % F
                ext = 1
                for st, cnt in pat[1:]:
                    ext += (cnt - 1) * abs(st)
                gsz = F // len(t)
                g0 = f0 // gsz
                g1 = (f0 + ext - 1) // gsz
                if g0 < 0 or g1 >= len(t):
                    return list(t)
                return [t[i] for i in range(g0, g1 + 1)]
            except Exception:
                return list(t)
        return [t]

    def _record(self, kind, e, payload, reads, writes, cost, final=False):
        rt = []
        for a in reads:
            for t in self._tks(a):
                if t not in rt:
                    rt.append(t)
        wt = []
        for a in writes:
            for t in self._tks(a):
                if t not in wt:
                    wt.append(t)
        i = len(self.ops)
        preds = set()
        for t in rt:
            if t.lw is not None:
                preds.add(t.lw)
        for t in wt:
            if t.lw is not None:
                preds.add(t.lw)
            preds.update(t.rd)
        preds.discard(i)
        for t in rt:
            t.rd.append(i)
        for t in wt:
            t.lw = i
            t.rd = []
        t0 = (wt + rt)[0] if kind == "dma" else None
        self.ops.append((kind, e, payload, tuple(preds), float(cost), final, t0))
        return i

    def op(self, e, fn, reads, writes, cost=None):
        if cost is None:
            n = fsz(writes[0]) if writes else 64
            cost = {"act": 200.0 + 0.85 * n, "dve": 110.0 + 1.05 * n, "pool": 260.0 + 1.0 * n, "pe": 150.0}[e]
        return self._record("op", e, fn, reads, writes, cost)

    def dma(self, q, out, in_, reads=(), writes=(), final=False, **kw):
        return self._record("dma", q, (out, in_, kw), reads, writes, 150.0 if q == "sp" else 1200.0, final)

    def _wait(self, e, key, semh, val):
        k = self.known[e]
        if k.get(key, 0) < val:
            self.h[e].wait_ge(semh, val)
            k[key] = val

    def _schedule(self, lo, hi):
        ops = self.ops
        n = hi - lo
        indeg = [0] * n
        succ = [[] for _ in range(n)]
        for i in range(lo, hi):
            ps_ = [p for p in ops[i][3] if p >= lo]
            indeg[i - lo] = len(ps_)
            for p in ps_:
                succ[p - lo].append(i)
        blev = [0.0] * n
        for k in range(n - 1, -1, -1):
            o = ops[lo + k]
            c = o[4] + (2500.0 if o[0] == "dma" else 0.0)
            m = 0.0
            for j in succ[k]:
                v = blev[j - lo] + self.LAT
                if v > m:
                    m = v
            blev[k] = c + m
        dready = [0.0] * n
        etime = {k: 0.0 for k in self.h}
        ready = {k: [] for k in self.h}
        for i in range(lo, hi):
            if indeg[i - lo] == 0:
                ready[ops[i][1]].append(i)
        order = []
        done = [False] * n
        lastcls = None
        minp = lo
        W = self.WINDOW
        EPS = self.EPS
        while len(order) < n:
            while minp < hi and done[minp - lo]:
                minp += 1
            lim = minp + W
            best = None
            for e, lst in ready.items():
                if not lst:
                    continue
                te = etime[e]
                cand = None
                for i in lst:
                    if i >= lim:
                        continue
                    dr = dready[i - lo]
                    st = te if dr <= te + EPS else dr
                    if e == "pe" and self.pecls.get(i) != lastcls:
                        st += self.PE_SWITCH
                    key = (st, -blev[i - lo], i)
                    if cand is None or key < cand:
                        cand = key
                if cand is not None and (best is None or cand < best[0]):
                    best = (cand, e)
            (st, _, i), e = best
            st = max(st, dready[i - lo], etime[e])
            ready[e].remove(i)
            kind = ops[i][0]
            cost = ops[i][4]
            if e == "pe":
                cl = self.pecls.get(i)
                if cl != lastcls:
                    cost += self.PE_SWITCH
                lastcls = cl
            etime[e] = st + cost
            f = st + cost + (2500.0 if kind == "dma" else 0.0)
            done[i - lo] = True
            order.append(i)
            for j in succ[i - lo]:
                ej = ops[j][1]
                v = f + (0.0 if (e == "pe" and ej == "pe") else (self.LAT_SAME if ej == e else self.LAT))
                if v > dready[j - lo]:
                    dready[j - lo] = v
                indeg[j - lo] -= 1
                if indeg[j - lo] == 0:
                    ready[ops[j][1]].append(j)
        return order, max(etime.values())

    def finish(self):
        ops = self.ops
        bounds = [i for i, o in enumerate(ops) if o[0] == "fence"] + [len(ops)]
        needs_inc = [False] * len(ops)
        info = {}
        clock = {}
        finals = []
        lo = 0
        est_total = 0.0
        nwait = 0
        last_inc = {k: None for k in self.h}
        for b in bounds:
            order, est = self._schedule(lo, b)
            est_total += est
            pos = {i: k for k, i in enumerate(order)}
            kept = {}
            lastop = {}
            for i in order:
                kind, e = ops[i][0], ops[i][1]
                if kind == "op":
                    lastop[e] = i
                best = {}
                keep = []
                for p in ops[i][3]:
                    if p < lo:
                        continue
                    if ops[p][0] != "op":
                        keep.append(p)
                        continue
                    f = ops[p][1]
                    if f == "pe" and e == "pe":
                        continue
                    if f not in best or pos[p] > pos[best[f]]:
                        best[f] = p
                for p in best.values():
                    needs_inc[p] = True
                    keep.append(p)
                kept[i] = keep
            for i in lastop.values():
                needs_inc[i] = True
            for i in order:
                kind, e, payload, preds, cost, final, t0 = ops[i]
                kn = self.known[e]
                for p in sorted(kept[i], key=lambda x: pos[x]):
                    if p < lo:
                        continue
                    pi = info[p]
                    if pi[0] == "op":
                        f, c = pi[1], pi[2]
                        if f == "pe" and e == "pe":
                            continue
                        if kn.get(f, 0) < c:
                            self.h[e].wait_ge(self.sem[f], c)
                            nwait += 1
                            kn[f] = c
                            for g, v in clock[p].items():
                                if kn.get(g, 0) < v:
                                    kn[g] = v
                    else:
                        if kn.get(pi[3], 0) < pi[2]:
                            self.h[e].wait_ge(pi[1], pi[2])
                            nwait += 1
                            kn[pi[3]] = pi[2]
                if kind == "op":
                    ins = payload(self.h[e])
                    if needs_inc[i]:
                        self.cnt[e] += 1
                        ins.then_inc(self.sem[e], 1)
                        info[i] = ("op", e, self.cnt[e])
                        snap = {g: v for g, v in kn.items() if g in self.h}
                        snap[e] = self.cnt[e]
                        clock[i] = snap
                    else:
                        info[i] = ("op", e, self.cnt[e] + 1)
                        clock[i] = {}
                else:
                    out, in_, kw = payload
                    if t0.dsem is None:
                        t0.dsem = self.nc.alloc_semaphore("dsem%d" % self.nsem)
                        self.nsem += 1
                    ins = self.h[e].dma_start(out=out, in_=in_, **kw)
                    t0.dcount += 16
                    ins.then_inc(t0.dsem, 16)
                    info[i] = ("dma", t0.dsem, t0.dcount, "d%d" % id(t0))
                    if final:
                        finals.append(info[i])
            lo = b + 1
            if b < len(ops):
                for e in self.h:
                    for f in self.h:
                        if f != e and self.cnt[f] > 0:
                            self._wait(e, f, self.sem[f], self.cnt[f])
                    for t in self._all_trk():
                        if t.dsem is not None and t.dcount > 0:
                            self._wait(e, "d%d" % id(t), t.dsem, t.dcount)
        fmax = {}
        for (_, semh, c, key) in finals:
            if key not in fmax or c > fmax[key][1]:
                fmax[key] = (semh, c)
        for key, (semh, c) in fmax.items():
            self._wait("sp", key, semh, c)
        print("scheduler estimate: %.1f us, %d ops, %d waits, incs %s" % (est_total / 1e3, len(ops), nwait, dict(self.cnt)))

    def mmr(self, out, lhsT, rhs, start=True, stop=True):
        return self.mm(out, R(lhsT), R(rhs), start=start, stop=stop)

    def mm(self, out, lhsT, rhs, start=True, stop=True):
        passes = 4.0 if rhs.dtype == F32 else 1.0
        cost = 70.0 + passes * 0.42 * (fsz(rhs) + min(fsz(lhsT), 128))
        i = self.op("pe", lambda h: h.matmul(out, lhsT=lhsT, rhs=rhs, start=start, stop=stop),
                    [lhsT, rhs], [out], cost=cost)
        self.pecls[i] = str(rhs.dtype)
        return i

    def tr(self, out, in_, ident):
        i = self.op("pe", lambda h: h.transpose(out, in_, ident), [in_, ident], [out], cost=160.0)
        self.pecls[i] = "tr"
        return i

    def act(self, out, in_, func, e="act", **kw):
        rd = [in_] + [v for v in kw.values() if hasattr(v, "tensor")]
        wr = [out]
        if "accum_out" in kw:
            wr.append(kw["accum_out"])
            rd.remove(kw["accum_out"])
        return self.op("act", lambda h: h.activation(out=out, in_=in_, func=func, **kw), rd, wr)

    def tt(self, out, in0, in1, op, e="dve"):
        return self.op(e, lambda h: h.tensor_tensor(out=out, in0=in0, in1=in1, op=op), [in0, in1], [out])

    def ts(self, out, in0, s1, op0, s2=None, op1=None, e="dve"):
        rd = [in0] + [v for v in (s1, s2) if hasattr(v, "tensor")]
        if op1 is None:
            return self.op(e, lambda h: h.tensor_scalar(out=out, in0=in0, scalar1=s1, scalar2=None, op0=op0),
                           rd, [out])
        return self.op(e, lambda h: h.tensor_scalar(out=out, in0=in0, scalar1=s1, scalar2=s2, op0=op0, op1=op1),
                       rd, [out])

    def stt(self, out, in0, scalar, in1, op0, op1):
        rd = [in0, in1] + ([scalar] if hasattr(scalar, "tensor") else [])
        return self.op("dve", lambda h: h.scalar_tensor_tensor(out=out, in0=in0, scalar=scalar, in1=in1,
                                                                 op0=op0, op1=op1), rd, [out])

    def cp(self, out, in_, e="dve"):
        if e == "act":
            return self.act(out, in_, AF.Copy)
        return self.op(e, lambda h: h.tensor_copy(out=out, in_=in_), [in_], [out])

    def scan(self, out, d0, d1):
        return self.op("dve", lambda h: h.tensor_tensor_scan(out=out, data0=d0, data1=d1, initial=0.0,
                                                              op0=ALU.mult, op1=ALU.add),
                       [d0, d1], [out], cost=110.0 + 2.1 * fsz(out))

    def rsqrt_pool(self, out, in_, mhalf):
        return self.op("pool", lambda h: h.tensor_tensor(out=out, in0=in_, in1=mhalf, op=ALU.pow), [in_, mhalf], [out])

    def recip(self, out, in_):
        return self.op("dve", lambda h: h.reciprocal(out=out, in_=in_), [in_], [out], cost=110.0 + 3.0 * fsz(out))

    def memset(self, ap, val, e="pool"):
        return self.op(e, lambda h: h.memset(ap, val), [], [ap])

    def asel(self, out, in_, pattern, cmp, fill, base, cm):
        return self.op("pool", lambda h: h.affine_select(out=out, in_=in_, pattern=pattern, compare_op=cmp,
                                                          fill=fill, base=base, channel_multiplier=cm),
                       [in_], [out])


F32R = mybir.dt.float32r


def R(ap):
    return ap.bitcast(F32R)


def neumann(P, ident, MT0, M0, Mbuf, MTT, nx, C, pool=0):
    L = {128: 7, 64: 6, 16: 4}[C]
    G = 512 // (2 * C)
    ngrp = (nx + G - 1) // G
    names = [m.name.rsplit("_", 1)[0] for m in MTT]
    split = ngrp > 1 and all(isinstance(P.trk[m.name], list) for m in MTT)

    def grp_only(g):
        if not split:
            return P.only()
        return P.only(**{nm: [g] for nm in names})

    def grp3(ps, n, w):
        return ps[0:C, 0:n * w].rearrange("p (x i) -> p x i", x=n)

    psa = P.ps(pool)
    psb = P.ps(pool)
    for x in range(nx):
        P.mm(psa[0:C, x * C:(x + 1) * C], R(MT0[:, x, :]), R(Mbuf[0][0:C, x, 0:C]))
        P.mm(psb[0:C, x * C:(x + 1) * C], R(Mbuf[0][0:C, x, 0:C]), R(MT0[:, x, :]))
    P.cp(R(Mbuf[1][0:C, 0:nx, 0:C]), grp3(psa, nx, C), e="act")
    P.cp(R(MTT[0][0:C, 0:nx, 0, 0:C]), grp3(psb, nx, C), e="act")
    P.tt(R(MTT[0][0:C, 0:nx, 1, 0:C]), MT0, bcm(ident[0:C, 0:C], [C, nx, C]), ALU.add)
    yield
    cm, ct = 1, 0
    for lev in range(2, L + 1):
        last = lev == L
        Mc = Mbuf[cm]
        cur = MTT[ct]
        nxt = MTT[1 - ct]
        if not last:
            psa = P.ps(pool)
            for x in range(nx):
                P.mm(psa[0:C, x * C:(x + 1) * C], R(cur[0:C, x, 0, 0:C]), R(Mc[0:C, x, 0:C]))
        for x0 in range(0, nx, G):
            n = min(G, nx - x0)
            psx = P.ps(pool)
            with grp_only(x0 // G):
                for j in range(n):
                    x = x0 + j
                    if last:
                        P.mm(psx[0:C, j * C:(j + 1) * C], R(Mc[0:C, x, 0:C]), R(cur[0:C, x, 1, 0:C]))
                    else:
                        P.mm(psx[0:C, j * 2 * C:(j + 1) * 2 * C], R(Mc[0:C, x, 0:C]), R(cur[0:C, x, :, 0:C]))
                if last:
                    P.tt(R(nxt[0:C, x0:x0 + n, 1, 0:C]), grp3(psx, n, C), cur[0:C, x0:x0 + n, 1, 0:C], ALU.add)
                else:
                    pv = psx[0:C, 0:n * 2 * C].rearrange("p (x a i) -> p x a i", x=n, a=2)
                    P.cp(R(nxt[0:C, x0:x0 + n, 0, 0:C]), pv[:, :, 0, :], e="act")
                    P.tt(R(nxt[0:C, x0:x0 + n, 1, 0:C]), pv[:, :, 1, :], cur[0:C, x0:x0 + n, 1, 0:C], ALU.add)
        if not last:
            P.cp(R(Mbuf[1 - cm][0:C, 0:nx, 0:C]), grp3(psa, nx, C), e="act")
        cm = 1 - cm
        ct = 1 - ct
        yield
    return MTT[ct][0:C, 0:nx, 1, 0:C]


def bc(ap, shape):
    return ap.unsqueeze(len(ap.shape)).broadcast_to(list(shape))


def bcm(ap, shape):
    return ap.unsqueeze(1).broadcast_to(list(shape))


PCOL = 1
S1COL = TP + 1 + 1
S2COL = S1COL + 32
NTOK = S2COL + 16 + 1
NBA = 256
CA = 128
NCH = TP // CA + 2


def build(stop=None):
    nc = bass.Bass("TRN2", target_bir_lowering=False)
    P = Prog(nc)
    P.init_psum()

    def din(name, shape):
        return nc.dram_tensor(name, list(shape), F32, kind="ExternalInput").ap()

    def dout(name, shape):
        return nc.dram_tensor(name, list(shape), F32, kind="ExternalOutput").ap()

    x_p = din("x_p", [TP, D])
    x_s = din("x_s", [2 * TS, D])
    conv_s = din("conv_s", [2, 3, 4096])
    delta_s = din("delta_s", [2, 8, 128, 256])
    shift_s = din("shift_s", [2, D])
    wkv_s = din("wkv_s", [2, 16, 64, 64])
    norm_w = din("norm_w", [2, D])
    final_norm_w = din("final_norm_w", [D])
    a_w_in = din("a_w_in", [D, 6160])
    a_conv_w = din("a_conv_w", [4, 4096])
    a_log = din("a_log", [8])
    a_dt_bias = din("a_dt_bias", [8])
    a_norm_w = din("a_norm_w", [256])
    a_w_out = din("a_w_out", [2048, D])
    b_mu = din("b_mu", [6, D])
    b_w_in = din("b_w_in", [D, 4096])
    b_w0 = din("b_w0", [D])
    b_w_w1 = din("b_w_w1", [D, 64])
    b_w_w2 = din("b_w_w2", [64, D])
    b_a0 = din("b_a0", [D])
    b_a_w1 = din("b_a_w1", [D, 64])
    b_a_w2 = din("b_a_w2", [64, D])
    b_k_k = din("b_k_k", [D])
    b_k_a = din("b_k_a", [D])
    b_r_k = din("b_r_k", [D])
    b_gn_w = din("b_gn_w", [D])
    b_gn_b = din("b_gn_b", [D])
    b_w_out = din("b_w_out", [D, D])

    y_p = dout("y_p", [TP, D])
    y_s = dout("y_s", [2 * TS, D])
    o_conv = dout("o_conv", [3, 3, 4096])
    o_delta = dout("o_delta", [3, 8, 128, 256])
    o_shift = dout("o_shift", [3, D])
    o_wkv = dout("o_wkv", [3, 16, 64, 64])
    dbg = dout("dbg", [NTILE * 128, D]) if stop else None

    ident = P.sb("ident", [128, 128])
    ones = P.sb("ones", [128, 128])
    mones = P.sb("mones", [128, 128])
    zeros = P.sb("zeros", [128, 128])
    Utri = P.sb("Utri", [128, 128])
    NEGT = P.sb("NEGT", [128, 128])
    MsT = P.sb("MsT", [128, 128])
    P.memset(ones[:], 1.0)
    ones_r = P.sb("ones_r", [128, 128])
    P.cp(R(ones_r[:]), ones[:], e="act")
    P.memset(mones[:], -1.0)
    P.memset(zeros[:], 0.0)
    P.asel(ident[:], ones[:], [[-1, 128]], ALU.is_equal, 0.0, 0, 1)
    P.asel(Utri[:], ones[:, :], [[1, 128]], ALU.is_ge, 0.0, 0, -1)
    P.asel(NEGT[:], zeros[:, :], [[1, 128]], ALU.is_ge, NEG, 0, -1)
    P.asel(MsT[:], ones[:, :], [[1, 128]], ALU.is_gt, 0.0, 0, -1)

    xres = [P.sb("xres%d" % i, [128, D]) for i in range(NTILE)]
    nw = P.sb("nw", [128, 2, KC])
    fnw = P.sb("fnw", [128, KC])
    P.dma("sp", nw[:], norm_w.rearrange("l (k p) -> p l k", p=128), writes=[nw[:]], allow_slow_non_contiguous=True)
    P.dma("sp", fnw[:], final_norm_w.rearrange("(k p) -> p k", p=128), writes=[fnw[:]],
          allow_slow_non_contiguous=True)
    ssq = P.sb("ssq", [128, 4])
    rstd = P.sb("rstd", [128, 4])
    phase_mark = P.sb_off
    hT = P.sb("hT", [128, KC, NTOK], BF16)
    xn = [P.sb("xn0", [128, D])] * 2

    def tile_rows(i):
        return 128 if i < 16 else 48

    def tile_col(i):
        return PCOL + i * 128 if i < 16 else S1COL

    def norm_to_hT(layer, hT):
        for i in range(NTILE):
            nt = tile_rows(i)
            xt = xres[i]
            xb = xn[i % 2]
            sl = slice(i % 4, i % 4 + 1)
            P.act(xb[0:nt, :], xt[0:nt, :], AF.Square, accum_out=ssq[0:nt, sl])
            P.act(rstd[0:nt, sl], ssq[0:nt, sl], AF.Sqrt, scale=1.0 / D, bias=RMS_EPS)
            P.recip(rstd[0:nt, sl], rstd[0:nt, sl])
            P.ts(xb[0:nt, :], xt[0:nt, :], rstd[0:nt, sl], ALU.mult)
            c0 = tile_col(i)
            for half in range(2):
                ps = P.ps()
                for j in range(4):
                    kc = half * 4 + j
                    P.tr(ps[:, j * 128:j * 128 + nt], xb[0:nt, kc * 128:(kc + 1) * 128], ident[0:nt, 0:nt])
                pv = ps[:, :].rearrange("p (j t) -> p j t", j=4)[:, :, 0:nt]
                P.tt(hT[:, half * 4:half * 4 + 4, c0:c0 + nt], pv,
                     bc(nw[:, layer, half * 4:half * 4 + 4], [128, 4, nt]), ALU.mult)

    for i in range(NTILE):
        if i < 16:
            P.dma("sp", xres[i][:], x_p[i * 128:(i + 1) * 128, :], writes=[xres[i][:]])
        else:
            P.memset(xres[i][:], 0.0)
            P.dma("sp", xres[i][0:16, :], x_s[0:16, :], writes=[xres[i][:]])
            P.dma("sp", xres[i][32:48, :], x_s[16:32, :], writes=[xres[i][:]])
    P.memset(hT[:, :, 0:1], 0.0)
    norm_to_hT(0, hT)

    blocks = [(0, PCOL + i * NBA, NBA, CA, NBA // CA, i * (NBA // CA)) for i in range(TP // NBA)] + \
             [(1, S1COL, 16, 16, 1, NCH - 2), (2, S2COL, 16, 16, 1, NCH - 1)]

    cwt = P.sb("cwt", [32, 4, 128])
    cw = P.sb("cw", [128, 4, 32])
    P.dma("sp", cwt[:], a_conv_w.rearrange("t (g c) -> g t c", c=128), writes=[cwt[:]])
    ps = P.ps()
    for t in range(4):
        P.tr(ps[:, t * 32:(t + 1) * 32], cwt[:, t, :], ident[0:32, 0:32])
    P.cp(cw[:].rearrange("p t g -> p (t g)"), ps[:, 0:128])
    halo_all = P.sb("halo_all", [128, 2, 3, 32])
    hrow = P.sb("hrow", [96, 2, 128])
    for s in range(2):
        P.dma("sp", hrow[:, s, :], conv_s[s].rearrange("t (g c) -> (t g) c", c=128), writes=[hrow[:]])
    ps = P.ps()
    for s in range(2):
        P.tr(ps[:, s * 96:(s + 1) * 96], hrow[:, s, :], ident[0:96, 0:96])
    P.cp(halo_all[:].rearrange("p s t g -> p (s t g)"), ps[:, 0:192])
    fin_all = P.sb("fin_all", [128, 3, 3, 32])
    anw = P.sb("anw", [128, 2])
    P.dma("sp", anw[:], a_norm_w.rearrange("(h p) -> p h", p=128), writes=[anw[:]], allow_slow_non_contiguous=True)
    P.ts(anw[:], anw[:], 0.5, ALU.mult)

    wba = P.sb("wba", [128, KC, 16], BF16)
    P.dma("pool", wba[:], a_w_in.rearrange("(k p) c -> p k c", p=128)[:, :, 6144:6160], writes=[wba[:]])
    NCP = TP // CA
    BA = P.sb("BA", [CA, NCH, 16])
    P.memset(BA[:], 0.0)
    ps = P.ps()
    for c in range(NCP):
        for kc in range(KC):
            P.mm(ps[0:CA, c * 16:(c + 1) * 16], hT[:, kc, PCOL + c * CA:PCOL + (c + 1) * CA], wba[:, kc, :],
                 start=(kc == 0), stop=(kc == KC - 1))
    P.cp(BA[:, 0:NCP, :].rearrange("p c k -> p (c k)"), ps[0:CA, 0:NCP * 16])
    ps = P.ps()
    for s, sc in enumerate((S1COL, S2COL)):
        for kc in range(KC):
            P.mm(ps[0:16, s * 16:(s + 1) * 16], hT[:, kc, sc:sc + 16], wba[:, kc, :],
                 start=(kc == 0), stop=(kc == KC - 1))
    P.cp(BA[0:16, NCP:NCP + 2, :].rearrange("p c k -> p (c k)"), ps[0:16, 0:32])
    alg = P.sb("alg", [CA, 8])
    dtb = P.sb("dtb", [CA, 8])
    P.dma("sp", alg[:], a_log.partition_broadcast(CA), writes=[alg[:]])
    P.dma("sp", dtb[:], a_dt_bias.partition_broadcast(CA), writes=[dtb[:]])
    P.act(alg[:], alg[:], AF.Exp)
    P.ts(alg[:], alg[:], -1.0, ALU.mult)
    beta = P.sb("beta", [CA, NCH, 8])
    gg = P.sb("gg", [CA, NCH, 8])
    Gc = P.sb("Gc", [CA, NCH, 8])
    Glb = P.sb("Glb", [128, NCH, 8])
    gl = P.sb("gl", [128, NCH, 8])
    eG = P.sb("eG", [CA, NCH, 8])
    bG = P.sb("bG", [CA, NCH, 8])
    dte = P.sb("dte", [CA, NCH, 8])
    P.act(beta[:], BA[:, :, 0:8], AF.Sigmoid)
    P.tt(gg[:], BA[:, :, 8:16], bcm(dtb[:], [CA, NCH, 8]), ALU.add)
    P.act(gg[:], gg[:], AF.Exp)
    P.act(gg[:], gg[:], AF.Ln, bias=1.0)
    P.tt(gg[:], gg[:], bcm(alg[:], [CA, NCH, 8]), ALU.mult)
    psG = P.ps()
    psL = P.ps()
    g2 = gg[:].rearrange("p c k -> p (c k)")
    GP = NCP * 8
    P.mm(psG[0:CA, 0:GP], Utri[0:CA, 0:CA], g2[:, 0:GP])
    P.mm(psG[0:16, GP:GP + 16], Utri[0:16, 0:16], g2[0:16, GP:GP + 16])
    P.mm(psL[:, 0:GP], ones[0:CA, :], g2[:, 0:GP])
    P.mm(psL[:, GP:GP + 16], ones[0:16, :], g2[0:16, GP:GP + 16])
    P.memset(Gc[:], 0.0)
    P.cp(Gc[:, 0:NCP, :].rearrange("p c k -> p (c k)"), psG[0:CA, 0:GP])
    P.cp(Gc[0:16, NCP:NCP + 2, :].rearrange("p c k -> p (c k)"), psG[0:16, GP:GP + 16])
    P.cp(Glb[:].rearrange("p c k -> p (c k)"), psL[:, 0:GP + 16])
    P.act(gl[:], Glb[:], AF.Exp)
    P.act(eG[:], Gc[:], AF.Exp)
    P.tt(bG[:], beta[:], eG[:], ALU.mult)
    hbeta = eG
    P.ts(hbeta[:], beta[:], 0.5, ALU.mult)
    P.tt(dte[:], Glb[0:CA], Gc[:], ALU.subtract)
    P.act(dte[:], dte[:], AF.Exp)

    Wh = [P.sb("Wh0", [128, KC, 768], BF16)] * 2
    Wo = [P.sb("Wo0", [128, 2, D], BF16)] * 2
    w_in_v = a_w_in.rearrange("(k p) c -> p k c", p=128)

    def load_head_w(h):
        sl = h % 2
        for (c0, n, o) in ((h * 128, 128, 0), (1024 + h * 128, 128, 128), (2048 + h * 256, 256, 256),
                           (4096 + h * 256, 256, 512)):
            P.dma("pool", Wh[sl][:, :, o:o + n], w_in_v[:, :, c0:c0 + n], writes=[Wh[sl][:]])

    def load_head_wo(h):
        sl = h % 2
        P.dma("pool", Wo[sl][:], a_w_out[h * 256:(h + 1) * 256, :].rearrange("(hh p) c -> p hh c", p=128),
              writes=[Wo[sl][:]])

    NB_ = NBA
    NC_ = NBA // CA
    pre = P.sb("pre", [128, 4, NB_ + 3])
    acc = P.sb("acc", [128, 4, NB_])
    P.split(acc, 4)
    P.split(pre, 4)
    qkv = P.sb("qkv", [128, 4, NB_])
    zs2 = [P.sb("zs%d" % i, [128, 2, NB_]) for i in range(2)]
    sqr = P.sb("sqr", [128, 2, NB_])
    sq = sqr
    rq = acc[:, 2:4]
    oT = P.sb("oTp", [128, 2, NB_])
    osq = P.sb("osq2", [128, 2, NB_])
    qT = P.sb("qT", [128, NB_])
    kT = P.sb("kT", [128, NB_])
    kbT = P.sb("kbT", [128, NB_])
    qgT2 = [P.sb("qgT%d" % i, [128, NB_]) for i in range(2)]
    dg = P.sb("dg", [128, 2, NB_])
    eGb = P.sb("eGb", [128, NB_])
    betab = P.sb("betab", [128, NB_])
    kbeG2 = [P.sb("kbeG%d" % i, [CA, NC_, 128]) for i in range(2)]
    ktk2 = [P.sb("ktk%d" % i, [CA, NC_, 128]) for i in range(2)]
    vb2 = [P.sb("vb%d" % i, [CA, NC_, 256]) for i in range(2)]
    gU = P.sb("gU", [CA, NC_, CA])
    DT = P.sb("DT", [CA, NC_, CA])
    DTs = P.sb("DTs", [CA, NC_, CA])
    qkT2 = [P.sb("qkT%d" % i, [CA, NC_, CA]) for i in range(2)]
    Mn = [P.sb("Mn%d" % i, [CA, NC_, CA]) for i in range(2)]
    MT0a2 = [P.sb("MT0a%d" % i, [CA, NC_, CA]) for i in range(2)]
    MTTa = [P.sb("MTTa%d" % i, [CA, NC_, 2, CA]) for i in range(2)]
    u0 = P.sb("u0", [CA, NC_, 256])
    wkT = P.sb("wkT", [128, NB_])
    uu = [P.sb("uu%d" % i, [CA, 256]) for i in range(2)]
    Sp_ = [P.sb("Sp%d" % i, [128, 256]) for i in range(2)]
    Ss_ = [P.sb("Ss%d" % i, [128, 256]) for i in range(2)]
    Sst = [Sp_, Ss_, Ss_]
    ors = P.sb("ors", [128, NB_])
    og = P.sb("og", [128, 2, NB_], BF16)
    print("SBUF left after layer-A alloc:", P.sb_top - P.sb_off)
    for t_, n_ in [(qkv, 4), (oT, 2), (osq, 2), (sqr, 2), (dg, 2), (gU, NC_), (DT, NC_), (DTs, NC_), (u0, NC_),
                   (og, 2)] + [(x_, 2) for x_ in zs2] + [(x_, NC_) for x_ in kbeG2 + ktk2 + vb2 + qkT2 + Mn + MT0a2 + MTTa]:
        P.split(t_, n_)

    DONE = object()

    def A_s1(it, h, blk):
        (seq, t0, NB, C, nch, c0) = blk
        par = it % 2
        W = Wh[0]
        ggrp = (h, 8 + h, 16 + 2 * h, 17 + 2 * h)
        zs, qgT, ktk, qkT = zs2[par], qgT2[par], ktk2[par], qkT2[par]
        kbeG, vb, MT0a = kbeG2[par], vb2[par], MT0a2[par]
        first_of_seq = (seq == 0 and t0 == PCOL) or seq > 0
        if seq == 0 and t0 == PCOL:
            load_head_w(h)
        if first_of_seq:
            if seq == 0:
                P.memset(pre[:, :, 0:3], 0.0)
            else:
                for gi, g in enumerate(ggrp):
                    with P.only(pre=[gi]):
                        P.cp(pre[:, gi, 0:3], halo_all[:, seq - 1, :, g], e="pool")
        for m in range(6):
            ps = P.ps("a1a")
            for kc in range(KC):
                P.mm(ps[:, 0:NB], W[:, kc, m * 128:(m + 1) * 128], hT[:, kc, t0:t0 + NB],
                     start=(kc == 0), stop=(kc == KC - 1))
            if m < 4:
                with P.only(pre=[m]):
                    P.cp(pre[:, m, 3:3 + NB], ps[:, 0:NB], e="act")
            else:
                P.act(zs[:, m - 4, 0:NB], ps[:, 0:NB], AF.Tanh, scale=0.5)
                P.stt(zs[:, m - 4, 0:NB], zs[:, m - 4, 0:NB], 1.0, ps[:, 0:NB], ALU.add, ALU.mult)
            yield
        for gi, g in enumerate(ggrp):
            with P.only(acc=[gi], pre=[gi]):
                P.act(acc[:, gi, 0:NB], pre[:, gi, 3:3 + NB], AF.Copy, scale=cw[:, 3, g:g + 1])
                for tap in (2, 1, 0):
                    P.stt(acc[:, gi, 0:NB], pre[:, gi, tap:tap + NB], cw[:, tap, g:g + 1], acc[:, gi, 0:NB],
                          ALU.mult, ALU.add)
            yield
        P.act(qkv[:, :, 0:NB], acc[:, :, 0:NB], AF.Tanh, scale=0.5)
        P.stt(qkv[:, :, 0:NB], qkv[:, :, 0:NB], 1.0, acc[:, :, 0:NB], ALU.add, ALU.mult)
        last = (t0 + NB == PCOL + TP) or seq > 0
        for gi, g in enumerate(ggrp):
            with P.only(pre=[gi]):
                if last:
                    P.cp(fin_all[:, seq, :, g], pre[:, gi, NB:NB + 3], e="pool")
                else:
                    P.cp(pre[:, gi, 0:3], pre[:, gi, NB:NB + 3], e="pool")
        yield
        P.act(R(sq[:, :, 0:NB]), qkv[:, 0:2, 0:NB], AF.Square)
        for j in range(2):
            ps = P.ps("a1m")
            P.mmr(ps[:, 0:NB], ones_r[:, :], sq[:, j, 0:NB])
            with P.only(acc=[2 + j]):
                if j == 0:
                    P.act(rq[:, j, 0:NB], ps[:, 0:NB], AF.Ln, scale=128.0, bias=512.0 * 1e-6)
                else:
                    P.act(rq[:, j, 0:NB], ps[:, 0:NB], AF.Ln, bias=4e-6)
        yield
        with P.only(acc=[2, 3]):
            P.act(rq[:, :, 0:NB], rq[:, :, 0:NB], AF.Exp, scale=-0.5)
        with P.only(acc=[2]):
            P.tt(R(qT[:, 0:NB]), qkv[:, 0, 0:NB], rq[:, 0, 0:NB], ALU.mult)
        with P.only(acc=[3]):
            P.tt(R(kT[:, 0:NB]), qkv[:, 1, 0:NB], rq[:, 1, 0:NB], ALU.mult)
        yield
        cs = slice(c0, c0 + nch)
        idb = bcm(ident[0:C, 0:C], [C, nch, C])
        dgv = dg[0:C, :, 0:NB].rearrange("p a (c i) -> p a c i", c=nch)
        P.tt(dgv[:, 0], idb, bc(Gc[0:C, cs, h], [C, nch, C]), ALU.mult)
        P.tt(dgv[:, 1], idb, bc(beta[0:C, cs, h], [C, nch, C]), ALU.mult)
        ps = P.ps("a1m")
        P.mm(ps[:, 0:NB], ones[0:C, :], dg[0:C, 0, 0:NB])
        P.act(eGb[:, 0:NB], ps[:, 0:NB], AF.Exp)
        ps = P.ps("a1m")
        P.mm(ps[:, 0:NB], ones[0:C, :], dg[0:C, 1, 0:NB])
        P.cp(betab[:, 0:NB], ps[:, 0:NB], e="act")
        yield
        P.tt(R(kbT[:, 0:NB]), kT[:, 0:NB], betab[:, 0:NB], ALU.mult)
        P.tt(R(qgT[:, 0:NB]), qT[:, 0:NB], eGb[:, 0:NB], ALU.mult)
        yield
        for c4 in range(0, nch, 4):
            n4 = min(4, nch - c4)
            ps = P.ps("a1m")
            for j in range(n4):
                c = c4 + j
                P.tr(ps[0:C, j * 128:(j + 1) * 128], kT[:, c * C:(c + 1) * C], ident[:, :])
            pv = ps[0:C, 0:n4 * 128].rearrange("p (j d) -> p j d", j=n4)
            P.tt(R(kbeG[0:C, c4:c4 + n4, :]), pv, bc(bG[0:C, c0 + c4:c0 + c4 + n4, h], [C, n4, 128]), ALU.mult)
            P.tt(R(ktk[0:C, c4:c4 + n4, :]), pv, bc(dte[0:C, c0 + c4:c0 + c4 + n4, h], [C, n4, 128]), ALU.mult)
            yield
        for c2 in range(0, nch, 2):
            n2 = min(2, nch - c2)
            ps = P.ps("a1m")
            for j in range(n2):
                c = c2 + j
                for half in range(2):
                    P.tr(ps[0:C, j * 256 + half * 128:j * 256 + (half + 1) * 128],
                         qkv[:, 2 + half, c * C:(c + 1) * C], ident[:, :])
            pv = ps[0:C, 0:n2 * 256].rearrange("p (j d) -> p j d", j=n2)
            P.tt(R(vb[0:C, c2:c2 + n2, :]), pv, bc(hbeta[0:C, c0 + c2:c0 + c2 + n2, h], [C, n2, 256]), ALU.mult)
            yield
        P.tt(gU[0:C, 0:nch, 0:C], bcm(Utri[0:C, 0:C], [C, nch, C]), bc(gg[0:C, cs, h], [C, nch, C]), ALU.mult)
        ps = P.ps("a1m")
        for c in range(nch):
            o = ps[0:C, c * C:(c + 1) * C]
            P.mm(o, ones[0:C, 0:C], gU[0:C, c, 0:C], start=True, stop=False)
            P.mm(o, gU[0:C, c, 0:C], mones[0:C, 0:C], start=False, stop=False)
            P.mm(o, ident[0:C, 0:C], NEGT[0:C, 0:C], start=False, stop=True)
        pv = ps[0:C, 0:nch * C].rearrange("p (c i) -> p c i", c=nch)
        P.act(DT[0:C, 0:nch, 0:C], pv, AF.Exp)
        P.tt(DTs[0:C, 0:nch, 0:C], DT[0:C, 0:nch, 0:C], bcm(MsT[0:C, 0:C], [C, nch, C]), ALU.mult, e="pool")
        yield
        ps = P.ps("a1m")
        for c in range(nch):
            P.mmr(ps[0:C, c * C:(c + 1) * C], kT[:, c * C:(c + 1) * C], kbT[:, c * C:(c + 1) * C])
        for c in range(nch):
            P.mmr(ps[0:C, 256 + c * C:256 + (c + 1) * C], kT[:, c * C:(c + 1) * C], qT[:, c * C:(c + 1) * C])
        pv = ps[0:C, 0:nch * C].rearrange("p (c i) -> p c i", c=nch)
        pv2 = ps[0:C, 256:256 + nch * C].rearrange("p (c i) -> p c i", c=nch)
        P.stt(R(MT0a[0:C, 0:nch, 0:C]), pv, -1.0, DTs[0:C, 0:nch, 0:C], ALU.mult, ALU.mult)
        P.tt(R(qkT[0:C, 0:nch, 0:C]), pv2, DT[0:C, 0:nch, 0:C], ALU.mult)
        yield
        ps = P.ps("a1m")
        for c in range(nch):
            P.tr(ps[0:C, c * C:(c + 1) * C], MT0a[0:C, c, 0:C], ident[0:C, 0:C])
        P.cp(R(Mn[0][0:C, 0:nch, 0:C]), ps[0:C, 0:nch * C].rearrange("p (c i) -> p c i", c=nch), e="act")
        yield
        TTf = yield from neumann(P, ident, MT0a[0:C, 0:nch, 0:C], None, Mn, MTTa, nch, C, pool="a1b")
        yield "DRAIN2"
        for c2 in range(0, nch, 2):
            n2 = min(2, nch - c2)
            ps = P.ps("a1b")
            for j in range(n2):
                P.mmr(ps[0:C, j * 256:(j + 1) * 256], TTf[:, c2 + j, :], vb[0:C, c2 + j, :])
            P.cp(u0[0:C, c2:c2 + n2, :], ps[0:C, 0:n2 * 256].rearrange("p (j d) -> p j d", j=n2), e="act")
        ps = P.ps("a1b")
        for c in range(nch):
            P.mmr(ps[:, c * C:(c + 1) * C], kbeG[0:C, c, :], TTf[:, c, :])
        P.cp(R(wkT[:, 0:NB]), ps[:, 0:NB], e="act")
        yield

    spar = [0, 0, 0]

    def A_s2(it, h, blk):
        (seq, t0, NB, C, nch, c0) = blk
        par = it % 2
        WO = Wo[0]
        zs, qgT, ktk, qkT = zs2[par], qgT2[par], ktk2[par], qkT2[par]
        first_of_seq = (seq == 0 and t0 == PCOL) or seq > 0
        if seq == 0 and t0 == PCOL:
            load_head_wo(h)
        if first_of_seq:
            spar[seq] = 0
            if seq == 0:
                P.cp(R(Sst[0][0][:]), zeros[:, 0:1].broadcast_to([128, 256]), e="act")
            else:
                P.dma("sp", Sst[seq][1][:], delta_s[seq - 1, h], writes=[Sst[seq][1][:]])
                P.cp(R(Sst[seq][0][:]), Sst[seq][1][:], e="act")
        pso = P.psb[7]
        for c in range(nch):
            Sc = Sst[seq][spar[seq]]
            Sn = Sst[seq][1 - spar[seq]]
            u = uu[c % 2]
            ps = P.ps("a2")
            P.mmr(ps[0:C, 0:256], wkT[:, c * C:(c + 1) * C], Sc[:, :])
            P.tt(R(u[0:C, :]), u0[0:C, c, :], ps[0:C, 0:256], ALU.subtract)
            yield
            ps2 = P.ps("a2")
            P.mmr(ps2[:, 0:256], ktk[0:C, c, :], u[0:C, :])
            P.stt(R(Sn[:, :]), Sc[:, :], gl[:, c0 + c, h:h + 1], ps2[:, 0:256], ALU.mult, ALU.add)
            for half in range(2):
                oo = pso[:, (half * nch + c) * C:(half * nch + c + 1) * C]
                P.mmr(oo, Sc[:, half * 128:(half + 1) * 128], qgT[:, c * C:(c + 1) * C], start=True, stop=False)
                P.mmr(oo, u[0:C, half * 128:(half + 1) * 128], qkT[0:C, c, 0:C], start=False, stop=True)
            spar[seq] = 1 - spar[seq]
            yield
        P.cp(oT[:, :, 0:NB], pso[:, 0:2 * NB].rearrange("p (a t) -> p a t", a=2), e="act")
        P.act(R(osq[:, :, 0:NB]), oT[:, :, 0:NB], AF.Square)
        ps = P.ps("a2")
        P.mmr(ps[:, 0:NB], ones_r[:, :], osq[:, 0, 0:NB], start=True, stop=False)
        P.mmr(ps[:, 0:NB], ones_r[:, :], osq[:, 1, 0:NB], start=False, stop=True)
        P.act(ors[:, 0:NB], ps[:, 0:NB], AF.Ln, scale=1.0 / 256.0, bias=RMS_EPS)
        yield
        P.act(ors[:, 0:NB], ors[:, 0:NB], AF.Exp, scale=-0.5)
        for half in range(2):
            P.stt(oT[:, half, 0:NB], oT[:, half, 0:NB], anw[:, half:half + 1], ors[:, 0:NB], ALU.mult, ALU.mult)
        P.tt(og[:, :, 0:NB], oT[:, :, 0:NB], zs[:, :, 0:NB], ALU.mult)
        yield
        for tt0 in range(0, NB, 128):
            nt = min(128, NB - tt0)
            if seq == 0:
                tile_i, prow = (t0 - PCOL + tt0) // 128, 0
            else:
                tile_i, prow = 16, (seq - 1) * 32
            for nh in range(2):
                ps = P.ps("a2")
                for half in range(2):
                    P.mm(ps[prow:prow + nt, :], og[:, half, tt0:tt0 + nt], WO[:, half, nh * 512:(nh + 1) * 512],
                         start=(half == 0), stop=(half == 1))
                xr = xres[tile_i][prow:prow + nt, nh * 512:(nh + 1) * 512]
                P.tt(xr, xr, ps[prow:prow + nt, :], ALU.add)
            yield
        last = (t0 + NB == PCOL + TP) or seq > 0
        if last:
            P.dma("sp", o_delta[seq, h], Sst[seq][spar[seq]][:], reads=[Sst[seq][spar[seq]][:]], final=True)

    def pipeline(items, s1, s2, ratio):
        g2 = None
        for it, item in enumerate(list(items) + [None]):
            g1 = s1(it, *item) if item is not None else None
            while g1 is not None or g2 is not None:
                if g2 is not None:
                    if next(g2, DONE) is DONE:
                        g2 = None
                if g1 is not None:
                    for _ in range(ratio if g2 is not None else 1000000):
                        r = next(g1, DONE)
                        if r is DONE:
                            g1 = None
                            break
                        if r == "DRAIN2":
                            while g2 is not None:
                                if next(g2, DONE) is DONE:
                                    g2 = None
            g2 = s2(it, *item) if item is not None else None

    pipeline([(h, blk) for h in range(H_A) for blk in blocks], A_s1, A_s2, 3)

    ps = P.ps()
    for s in range(3):
        P.tr(ps[0:96, s * 128:(s + 1) * 128], fin_all[:, s].rearrange("p t g -> p (t g)"), ident[:, :])
    for s in range(3):
        P.cp(acc[0:96, s, 0:128], ps[0:96, s * 128:(s + 1) * 128])
        P.dma("sp", o_conv[s].rearrange("t (g c) -> (t g) c", c=128), acc[0:96, s, 0:128], reads=[acc[:]], final=True)

    if stop == "A":
        for i in range(NTILE):
            nt = tile_rows(i)
            P.dma("sp", dbg[i * 128:i * 128 + nt, :], xres[i][0:nt, :], reads=[xres[i][:]], final=True)
        P.finish()
        return nc

    P.barrier()
    P.sb_off = phase_mark
    hT = P.sb("hT2", [128, KC, NTOK], BF16)
    shout = P.sb("shout", [128, 3, KC])
    P.memset(hT[:, :, 0:1], 0.0)
    mark_b0 = P.sb_off
    xn = [P.sb("xnB", [128, D])] * 2

    def norm_to_hT_B():
        for i in range(NTILE):
            nt = tile_rows(i)
            xt = xres[i]
            xb = xn[i % 2]
            sl = slice(i % 4, i % 4 + 1)
            P.act(xb[0:nt, :], xt[0:nt, :], AF.Square, accum_out=ssq[0:nt, sl])
            P.act(rstd[0:nt, sl], ssq[0:nt, sl], AF.Sqrt, scale=1.0 / D, bias=RMS_EPS)
            P.recip(rstd[0:nt, sl], rstd[0:nt, sl])
            P.ts(xb[0:nt, :], xt[0:nt, :], rstd[0:nt, sl], ALU.mult)
            c0 = tile_col(i)
            for half in range(2):
                ps = P.ps(2)
                for j in range(4):
                    kc = half * 4 + j
                    P.tr(ps[:, j * 128:j * 128 + nt], xb[0:nt, kc * 128:(kc + 1) * 128], ident[0:nt, 0:nt])
                pv4 = ps[:, :].rearrange("p (j t) -> p j t", j=4)
                pv = pv4[:, :, 0:nt]
                P.tt(hT[:, half * 4:half * 4 + 4, c0:c0 + nt], pv,
                     bc(nw[:, 1, half * 4:half * 4 + 4], [128, 4, nt]), ALU.mult)
                lastcols = {15: [(0, 127)], 16: [(1, 15), (2, 47)]}.get(i, [])
                for (sq_, col) in lastcols:
                    P.tt(shout[:, sq_, half * 4:half * 4 + 4], pv4[:, :, col], nw[:, 1, half * 4:half * 4 + 4], ALU.mult)

    norm_to_hT_B()
    P.barrier()
    P.sb_off = mark_b0
    for s_ in range(3):
        P.dma("sp", o_shift[s_].rearrange("(k p) -> p k", p=128), shout[:, s_, :], reads=[shout[:]], final=True,
              allow_slow_non_contiguous=True)
    shin = P.sb("shin", [128, 2, KC])
    P.dma("sp", shin[:], shift_s.rearrange("s (k p) -> p s k", p=128), writes=[shin[:]], allow_slow_non_contiguous=True)
    P.cp(hT[:, :, S1COL - 1], shin[:, 0, :])
    P.cp(hT[:, :, S2COL - 1], shin[:, 1, :])

    vecs = P.sb("vecs", [128, 13, KC])
    P.dma("sp", vecs[:, 0:6, :], b_mu.rearrange("g (k p) -> p g k", p=128), writes=[vecs[:]], allow_slow_non_contiguous=True)
    for vi, v_ in enumerate((b_w0, b_a0, b_k_k, b_k_a, b_r_k, b_gn_w, b_gn_b)):
        P.dma("sp", vecs[:, 6 + vi, :], v_.rearrange("(k p) -> p k", p=128), writes=[vecs[:]], allow_slow_non_contiguous=True)
    V_W0, V_A0, V_KK, V_KA, V_RK, V_GW, V_GB = range(6, 13)
    hvec = P.sb("hvec", [128, 2, KC])
    P.ts(hvec[:], vecs[:, 6:8, :], 0.5, ALU.mult)
    blk1 = P.sb("blk1", [128, 128])
    cst32 = P.sb("cst32", [128, 192])
    P.asel(cst32[:, 0:64], ones[:, 0:64], [[0, 64]], ALU.is_ge, 0.0, 63, -1)
    P.asel(cst32[:, 64:128], ones[:, 0:64], [[0, 64]], ALU.is_ge, 0.0, -64, 1)
    P.cp(R(blk1[:]), cst32[:, 0:128])
    CB = 128
    MXT = P.sb("MXT", [CB, 2 * CB])
    P.cp(MXT[:, 0:CB], MsT[0:CB, 0:CB], e="pool")
    P.cp(MXT[:, CB:2 * CB], Utri[0:CB, 0:CB], e="pool")
    MsL = P.sb("MsL", [CB, CB])
    Sh = P.sb("Sh", [128, 64])
    P.asel(cst32[:, 128:192], ones[:, 0:64], [[-1, 64]], ALU.is_equal, 0.0, -64, 1)
    P.cp(R(Sh[:]), cst32[:, 128:192])
    P.asel(MsL[:], ones[0:CB, 0:CB], [[-1, CB]], ALU.is_gt, 0.0, 0, 1)
    rmask = P.sb("rmask", [128, 256])
    P.memset(rmask[:], 1.0)
    for c in range(256 // CB):
        P.memset(rmask[:, c * CB:c * CB + 1], 0.0)

    NBB = 256
    t1T = P.sb("t1T", [64, NTOK], BF16)
    a1T = P.sb("a1T", [64, NTOK], BF16)
    w2b = P.sb("w2b", [64, D], BF16)
    a2b = P.sb("a2b", [64, D], BF16)
    blk_mark = P.sb_off
    lw1 = P.sb("lw1", [128, KC, 2, 64], BF16)
    lw1p = P.sb("lw1p", [128, KC, 2, 64], BF16)
    lw1pp = P.sb("lw1pp", [128, KC, 2, 64], BF16)
    P.dma("pool", lw1[:, :, 0, :], b_w_w1.rearrange("(k p) c -> p k c", p=128), writes=[lw1[:]])
    P.dma("pool", lw1[:, :, 1, :], b_a_w1.rearrange("(k p) c -> p k c", p=128), writes=[lw1[:]])
    for j in range(2):
        P.tt(lw1p[:, :, j, :], lw1[:, :, j, :], bc(vecs[:, 4 + j, :], [128, KC, 64]), ALU.mult)
    P.tt(lw1pp[:], lw1[:], lw1p[:], ALU.subtract)
    P.dma("pool", w2b[:], b_w_w2[:, :], writes=[w2b[:]])
    P.dma("pool", a2b[:], b_a_w2[:, :], writes=[a2b[:]])
    col_ranges = [(PCOL + i * 512, 512) for i in range(4)] + [(S1COL, 16), (S2COL, 16)]
    for (cc0, n) in col_ranges:
        for j, dst in enumerate((t1T, a1T)):
            ps = P.ps(2)
            for kc in range(KC):
                P.mm(ps[0:64, 0:n], lw1pp[:, kc, j, :], hT[:, kc, cc0:cc0 + n], start=(kc == 0), stop=False)
                P.mm(ps[0:64, 0:n], lw1p[:, kc, j, :], hT[:, kc, cc0 - 1:cc0 - 1 + n], start=False, stop=(kc == KC - 1))
            P.act(dst[:, cc0:cc0 + n], ps[0:64, 0:n], AF.Tanh if j == 0 else AF.Copy)

    P.barrier()
    P.sb_off = blk_mark
    Wp = [P.sb("Wp0", [128, KC, 4, 128], BF16)] * 2
    Wq = P.sb("Wq", [128, KC, 4, 128], BF16)
    Wob = [P.sb("Wob0", [128, D], BF16)] * 2
    b_in_v = b_w_in.rearrange("(k p) (g c) -> p k g c", p=128, g=4)

    def load_pair_w(pr):
        for g_ in range(4):
            P.dma("pool", Wp[pr % 2][:, :, g_, :], b_in_v[:, :, g_, pr * 128:(pr + 1) * 128], writes=[Wp[pr % 2][:]])
        P.dma("pool", Wob[pr % 2][:], b_w_out[pr * 128:(pr + 1) * 128, :], writes=[Wob[pr % 2][:]])

    NCB = NBB // CB
    NX = 2 * NCB
    rkvT = P.sb("rkvT", [128, 3, NBB])
    zsB2 = [P.sb("zsB%d" % i, [128, NBB]) for i in range(2)]
    s2tmp = P.sb("s2tmp", [128, 2, NBB])
    lwT = P.sb("lwT", [128, NBB])
    aT = P.sb("aT", [128, NBB])
    cwv = P.sb("cwv", [128, NBB])
    eW = P.sb("eW", [128, 3, NBB])
    tmpB = P.sb("tmpB", [128, 4, NBB])
    kkT = P.sb("kkT", [128, NBB])
    k2T = P.sb("k2T", [128, NBB])
    arT2 = [P.sb("arT%d" % i, [128, 2, NBB]) for i in range(2)]
    bkT = P.sb("bkT", [128, 2, NBB])
    bkh = P.sb("bkh", [128, 2, NBB])
    Wc = P.sb("Wc", [128, NCB])
    rkb = P.sb("rkb", [128, NBB])
    vt = P.sb("vt", [CB, NCB, 128])
    bht = P.sb("bht", [CB, NCB, 128])
    kht = P.sb("kht", [CB, NCB, 128])
    XA = P.sb("XA", [CB, NX, 2 * CB])
    XB = P.sb("XB", [CB, NX, 2 * CB])
    MnB = [P.sb("MnB%d" % i, [CB, NX, CB]) for i in range(2)]
    MTTb = [P.sb("MTTb%d" % i, [CB, NX, 2, CB]) for i in range(2)]
    for m_ in MTTb:
        P.split(m_, 2)
    Rsb = [P.sb("Rsb%d" % i, [CB, 128]) for i in range(2)]
    Usb = [P.sb("Usb%d" % i, [CB, 128]) for i in range(2)]
    StP = [[P.sb("StP%d_%d" % (i, hd), [64, 64]) for hd in range(2)] for i in range(2)]
    StS = [[P.sb("StS%d_%d" % (i, hd), [64, 64]) for hd in range(2)] for i in range(2)]
    StB = [StP, StS, StS]
    stio = P.sb("stio", [64, 128])
    ar12 = [P.sb("ar1_%d" % i, [64, 2, NBB]) for i in range(2)]
    bk1 = P.sb("bk1", [64, 2, NBB])
    Wc1 = P.sb("Wc1", [64, NCB])
    sqB = P.sb("sqB", [128, 4, NBB])
    oTB = sqB[:, 2]
    ocB = s2tmp[:, 0]
    osB = sqB[:, 3]
    ogB = P.sb("ogB", [128, NBB], BF16)
    print("SBUF left after layer-B alloc:", P.sb_top - P.sb_off)
    for t_, n_ in [(rkvT, 3), (eW, 3), (tmpB, 4), (sqB, 4), (bkT, 2), (bkh, 2), (XA, NX), (XB, NX), (bk1, 2),
                   (vt, NCB), (bht, NCB), (kht, NCB), (s2tmp, 2)] + [(x_, 2) for x_ in arT2 + ar12] + \
            [(x_, NX) for x_ in MnB]:
        P.split(t_, n_)
    blocksB = [(0, PCOL + i * NBB, NBB, CB, NBB // CB) for i in range(TP // NBB)] + \
              [(1, S1COL, 16, 16, 1), (2, S2COL, 16, 16, 1)]
    ENH = -float(np.exp(-0.5))

    load_pair_w(0)
    nblkB = 0
    for pr in range(8):
        W = Wp[pr % 2]
        WO = Wob[pr % 2]
        for g_ in range(4):
            P.tt(Wq[:, :, g_, :], W[:, :, g_, :], bc(vecs[:, g_, :], [128, KC, 128]), ALU.mult)
        P.tt(W[:], W[:], Wq[:], ALU.subtract)
        Wr = W
        spar = [0, 0, 0]
        cur_seq = -1
        for (seq, t0, NB, C, nch) in blocksB:
            nx = 2 * nch
            bpar = nblkB % 2
            nblkB += 1
            zsB, arT, ar1 = zsB2[bpar], arT2[bpar], ar12[bpar]
            if seq != cur_seq:
                cur_seq = seq
                spar[seq] = 0
                if seq == 0:
                    for hd in range(2):
                        P.cp(R(StB[0][0][hd][:]), zeros[0:64, 0:64], e="act")
                else:
                    P.dma("sp", stio[:].rearrange("v (h k) -> v h k", h=2),
                          wkv_s[seq - 1, 2 * pr:2 * pr + 2].rearrange("h v k -> v h k"), writes=[stio[:]])
                    for hd in range(2):
                        ps = P.ps("b1")
                        P.tr(ps[0:64, 0:64], stio[:, hd * 64:(hd + 1) * 64], ident[0:64, 0:64])
                        P.cp(R(StB[seq][0][hd][:]), ps[0:64, 0:64])
            for g_ in range(4):
                ps = P.ps("b1")
                for kc in range(KC):
                    P.mm(ps[:, 0:NB], Wr[:, kc, g_, :], hT[:, kc, t0:t0 + NB], start=(kc == 0), stop=False)
                    P.mm(ps[:, 0:NB], Wq[:, kc, g_, :], hT[:, kc, t0 - 1:t0 - 1 + NB], start=False, stop=(kc == KC - 1))
                if g_ < 3:
                    P.cp(rkvT[:, g_, 0:NB], ps[:, 0:NB], e="act")
                else:
                    P.act(zsB[:, 0:NB], ps[:, 0:NB], AF.Tanh, scale=0.5)
                    P.stt(zsB[:, 0:NB], zsB[:, 0:NB], 1.0, ps[:, 0:NB], ALU.add, ALU.mult)
            ps = P.ps("b1")
            P.mm(ps[:, 0:NB], w2b[:, pr * 128:(pr + 1) * 128], t1T[:, t0:t0 + NB])
            P.act(lwT[:, 0:NB], ps[:, 0:NB], AF.Tanh, scale=0.5, bias=hvec[:, 0, pr:pr + 1])
            P.ts(lwT[:, 0:NB], lwT[:, 0:NB], 0.5 * ENH, ALU.mult, 0.5 * ENH, ALU.add)
            ps = P.ps("b1")
            P.mm(ps[:, 0:NB], a2b[:, pr * 128:(pr + 1) * 128], a1T[:, t0:t0 + NB])
            P.act(aT[:, 0:NB], ps[:, 0:NB], AF.Tanh, scale=0.5, bias=hvec[:, 1, pr:pr + 1])
            P.ts(aT[:, 0:NB], aT[:, 0:NB], 0.5, ALU.mult, 0.5, ALU.add)
            rT = rkvT[:, 0, 0:NB]
            kT_ = rkvT[:, 1, 0:NB]
            vT_ = rkvT[:, 2, 0:NB]
            P.ts(kkT[:, 0:NB], kT_, vecs[:, V_KK, pr:pr + 1], ALU.mult)
            P.act(R(sqB[:, 0, 0:NB]), kT_, AF.Square, scale=vecs[:, V_KK, pr:pr + 1])
            ps = P.ps("b1")
            P.mmr(ps[:, 0:NB], blk1[:, :], sqB[:, 0, 0:NB])
            P.act(tmpB[:, 1, 0:NB], ps[:, 0:NB], AF.Ln, bias=1e-6)
            P.act(tmpB[:, 1, 0:NB], tmpB[:, 1, 0:NB], AF.Exp, scale=-0.5)
            P.tt(kkT[:, 0:NB], kkT[:, 0:NB], tmpB[:, 1, 0:NB], ALU.mult)
            P.ts(tmpB[:, 2, 0:NB], aT[:, 0:NB], -1.0, ALU.add, vecs[:, V_KA, pr:pr + 1], ALU.mult)
            P.ts(tmpB[:, 2, 0:NB], tmpB[:, 2, 0:NB], 1.0, ALU.add)
            P.tt(k2T[:, 0:NB], kT_, tmpB[:, 2, 0:NB], ALU.mult)
            P.scan(cwv[:, 0:NB], rmask[:, 0:NB], lwT[:, 0:NB])
            P.act(eW[:, 0, 0:NB], cwv[:, 0:NB], AF.Exp)
            P.act(eW[:, 1, 0:NB], cwv[:, 0:NB], AF.Exp, scale=-1.0)
            P.tt(tmpB[:, 3, 0:NB], cwv[:, 0:NB], lwT[:, 0:NB], ALU.subtract)
            P.act(eW[:, 2, 0:NB], tmpB[:, 3, 0:NB], AF.Exp)
            P.stt(R(arT[:, 0, 0:NB]), kkT[:, 0:NB], -1.0, eW[:, 2, 0:NB], ALU.mult, ALU.mult)
            P.tt(R(arT[:, 1, 0:NB]), rT, eW[:, 0, 0:NB], ALU.mult)
            P.tt(tmpB[:, 0, 0:NB], kkT[:, 0:NB], aT[:, 0:NB], ALU.mult)
            P.tt(R(bkT[:, 0, 0:NB]), tmpB[:, 0, 0:NB], eW[:, 1, 0:NB], ALU.mult)
            P.tt(R(bkT[:, 1, 0:NB]), k2T[:, 0:NB], eW[:, 1, 0:NB], ALU.mult)
            ewc = eW[:, 0, 0:NB].rearrange("p (c i) -> p c i", c=nch)[:, :, C - 1]
            P.cp(Wc[:, 0:nch], ewc)
            bkv = bkT[:, :, 0:NB].rearrange("p a (c i) -> p a c i", c=nch)
            bhv = bkh[:, :, 0:NB].rearrange("p a (c i) -> p a c i", c=nch)
            for a_ in range(2):
                P.tt(bhv[:, a_], bkv[:, a_], bc(Wc[:, 0:nch], [128, nch, C]), ALU.mult)
            P.stt(R(sqB[:, 1, 0:NB]), rT, vecs[:, V_RK, pr:pr + 1], k2T[:, 0:NB], ALU.mult, ALU.mult)
            ps = P.ps("b1")
            P.mmr(ps[:, 0:NB], blk1[:, :], sqB[:, 1, 0:NB])
            P.tt(rkb[:, 0:NB], ps[:, 0:NB], vT_, ALU.mult)
            ps = P.ps("b1")
            P.mmr(ps[0:64, 0:2 * NB], Sh[:, :], arT[:, :, 0:NB])
            P.cp(R(ar1[:, :, 0:NB]), ps[0:64, 0:2 * NB].rearrange("p (a t) -> p a t", a=2), e="act")
            ps = P.ps("b1")
            P.mmr(ps[0:64, 0:2 * NB], Sh[:, :], bkT[:, :, 0:NB])
            P.cp(R(bk1[:, :, 0:NB]), ps[0:64, 0:2 * NB].rearrange("p (a t) -> p a t", a=2), e="act")
            ps = P.ps("b1")
            P.mm(ps[0:64, 0:nch], cst32[:, 128:192], Wc[:, 0:nch])
            P.cp(Wc1[:, 0:nch], ps[0:64, 0:nch])
            AR = [arT[0:64], ar1[:]]
            BK = [bkT[0:64], bk1[:]]
            WC = [Wc[0:64], Wc1[:]]
            for (src, dst) in ((vT_, vt), (bkh[:, 0, 0:NB], bht), (bkh[:, 1, 0:NB], kht)):
                ps = P.ps("b1")
                for c in range(nch):
                    P.tr(ps[0:C, c * 128:(c + 1) * 128], src[:, c * C:(c + 1) * C], ident[:, :])
                P.cp(R(dst[0:C, 0:nch, :]), ps[0:C, 0:nch * 128].rearrange("p (c d) -> p c d", c=nch), e="act")
            psN = P.ps("b1")
            for hd in range(2):
                hs = slice(hd * 64, (hd + 1) * 64)
                psA = P.ps("b1")
                psB_ = P.ps("b1")
                for c in range(nch):
                    csl = slice(c * C, (c + 1) * C)
                    x_ = hd * nch + c
                    P.mmr(psA[0:C, c * 2 * C:(c + 1) * 2 * C], BK[hd][:, 0, csl], AR[hd][:, :, csl])
                    P.mmr(psB_[0:C, c * 2 * C:(c + 1) * 2 * C], BK[hd][:, 1, csl], AR[hd][:, :, csl])
                    P.mmr(psN[0:C, x_ * C:(x_ + 1) * C], AR[hd][:, 0, csl], BK[hd][:, 0, csl])
                for (psx, dstx) in ((psA, XA), (psB_, XB)):
                    pv = psx[0:C, 0:nch * 2 * C].rearrange("p (c a i) -> p c a i", c=nch, a=2)
                    dv = dstx[0:C, hd * nch:(hd + 1) * nch, :].rearrange("p c (a i) -> p c a i", a=2)[:, :, :, 0:C]
                    mv = MXT[0:C, :].rearrange("p (a i) -> p a i", a=2)[:, :, 0:C].unsqueeze(1).broadcast_to([C, nch, 2, C])
                    P.tt(R(dv), pv, mv, ALU.mult)
            P.tt(R(MnB[0][0:C, 0:nx, 0:C]), psN[0:C, 0:nx * C].rearrange("p (x i) -> p x i", x=nx),
                 bcm(MsL[0:C, 0:C], [C, nx, C]), ALU.mult)
            gen_ = neumann(P, ident, XA[0:C, 0:nx, 0:C], None, MnB, MTTb, nx, C, pool="b1")
            while True:
                try:
                    next(gen_)
                except StopIteration as e_:
                    TTf = e_.value
                    break
            pso = P.psb[7]
            for c in range(nch):
                csl = slice(c * C, (c + 1) * C)
                Sc = StB[seq][spar[seq]]
                Sn = StB[seq][1 - spar[seq]]
                Rb = Rsb[c % 2]
                Ub = Usb[c % 2]
                ps = P.ps("b2")
                for hd in range(2):
                    hs = slice(hd * 64, (hd + 1) * 64)
                    x_ = hd * nch + c
                    P.mmr(ps[0:C, hs], AR[hd][:, 0, csl], Sc[hd][:, :], start=True, stop=False)
                    P.mmr(ps[0:C, hs], XB[0:C, x_, 0:C], vt[0:C, c, hs], start=False, stop=True)
                P.cp(R(Rb[0:C, :]), ps[0:C, 0:128], e="act")
                ps = P.ps("b2")
                for hd in range(2):
                    hs = slice(hd * 64, (hd + 1) * 64)
                    x_ = hd * nch + c
                    P.mmr(ps[0:C, hs], TTf[:, x_, :], Rb[0:C, hs])
                P.cp(R(Ub[0:C, :]), ps[0:C, 0:128], e="act")
                for hd in range(2):
                    hs = slice(hd * 64, (hd + 1) * 64)
                    x_ = hd * nch + c
                    oo = pso[hs, csl]
                    mmf = P.mmr if hd == 0 else P.mm
                    mmf(oo, Sc[hd][:, :], AR[hd][:, 1, csl], start=True, stop=False)
                    mmf(oo, Ub[0:C, hs], XA[0:C, x_, CB:CB + C], start=False, stop=False)
                    mmf(oo, vt[0:C, c, hs], XB[0:C, x_, CB:CB + C], start=False, stop=True)
                ps2 = P.ps("b2")
                for hd in range(2):
                    hs = slice(hd * 64, (hd + 1) * 64)
                    P.mmr(ps2[0:64, hs], bht[0:C, c, hs], Ub[0:C, hs], start=True, stop=False)
                    P.mmr(ps2[0:64, hs], kht[0:C, c, hs], vt[0:C, c, hs], start=False, stop=True)
                for hd in range(2):
                    hs = slice(hd * 64, (hd + 1) * 64)
                    P.stt(R(Sn[hd][:, :]), Sc[hd][:, :], WC[hd][:, c:c + 1], ps2[0:64, hs], ALU.mult, ALU.add)
                spar[seq] = 1 - spar[seq]
            P.cp(R(oTB[:, 0:NB]), pso[:, 0:NB], e="act")
            ps = P.ps("b2")
            P.mmr(ps[:, 0:NB], blk1[:, :], oTB[:, 0:NB])
            P.stt(ocB[:, 0:NB], ps[:, 0:NB], -1.0 / 64.0, oTB[:, 0:NB], ALU.mult, ALU.add)
            P.act(R(osB[:, 0:NB]), ocB[:, 0:NB], AF.Square)
            ps = P.ps("b2")
            P.mmr(ps[:, 0:NB], blk1[:, :], osB[:, 0:NB])
            P.act(s2tmp[:, 1, 0:NB], ps[:, 0:NB], AF.Ln, scale=1.0 / 64.0, bias=GN_EPS)
            P.act(s2tmp[:, 1, 0:NB], s2tmp[:, 1, 0:NB], AF.Exp, scale=-0.5)
            P.tt(ocB[:, 0:NB], ocB[:, 0:NB], s2tmp[:, 1, 0:NB], ALU.mult)
            P.ts(ocB[:, 0:NB], ocB[:, 0:NB], vecs[:, V_GW, pr:pr + 1], ALU.mult, vecs[:, V_GB, pr:pr + 1], ALU.add)
            P.tt(ocB[:, 0:NB], ocB[:, 0:NB], rkb[:, 0:NB], ALU.add)
            P.stt(ogB[:, 0:NB], ocB[:, 0:NB], 0.5, zsB[:, 0:NB], ALU.mult, ALU.mult)
            for tt0 in range(0, NB, 128):
                nt = min(128, NB - tt0)
                if seq == 0:
                    tile_i, prow = (t0 - PCOL + tt0) // 128, 0
                else:
                    tile_i, prow = 16, (seq - 1) * 32
                for nh in range(2):
                    ps = P.ps("b2")
                    P.mm(ps[prow:prow + nt, :], ogB[:, tt0:tt0 + nt], WO[:, nh * 512:(nh + 1) * 512])
                    xr = xres[tile_i][prow:prow + nt, nh * 512:(nh + 1) * 512]
                    P.tt(xr, xr, ps[prow:prow + nt, :], ALU.add)
            last = (t0 + NB == PCOL + TP) or seq > 0
            if last:
                Sf = StB[seq][spar[seq]]
                ps = P.ps("b2")
                for hd in range(2):
                    P.tr(ps[0:64, hd * 64:(hd + 1) * 64], Sf[hd][:, :], ident[0:64, 0:64])
                P.cp(stio[:, :], ps[0:64, 0:128])
                P.dma("sp", o_wkv[seq, 2 * pr:2 * pr + 2].rearrange("h v k -> v h k"),
                      stio[:].rearrange("v (h k) -> v h k", h=2), reads=[stio[:]], final=True)
        if pr + 1 < 8:
            load_pair_w(pr + 1)

    P.barrier()
    P.sb_off = blk_mark
    xn = [P.sb("xnF", [128, D])] * 2
    fnwb = P.sb("fnwb", [128, D])
    P.dma("sp", fnwb[:], final_norm_w.partition_broadcast(128), writes=[fnwb[:]])
    for i in range(NTILE):
        nt = tile_rows(i)
        xt = xres[i]
        xb = xn[0]
        sl = slice(i % 4, i % 4 + 1)
        P.act(xb[0:nt, :], xt[0:nt, :], AF.Square, accum_out=ssq[0:nt, sl])
        P.act(rstd[0:nt, sl], ssq[0:nt, sl], AF.Sqrt, scale=1.0 / D, bias=RMS_EPS)
        P.recip(rstd[0:nt, sl], rstd[0:nt, sl])
        P.stt(xb[0:nt, :], xt[0:nt, :], rstd[0:nt, sl], fnwb[0:nt, :], ALU.mult, ALU.mult)
        if i < 16:
            P.dma("sp", y_p[i * 128:(i + 1) * 128, :], xb[:, :], reads=[xb[:]], final=True)
        else:
            P.dma("sp", y_s[0:16, :], xb[0:16, :], reads=[xb[:]], final=True)
            P.dma("sp", y_s[16:32, :], xb[32:48, :], reads=[xb[:]], final=True)
    P.finish()
    return nc


_NC_CACHE = {}


def make_in_maps(inputs):
    g = lambda k: np.ascontiguousarray(np.asarray(inputs[k], dtype=np.float32))
    xp, xs = g("x_prompt"), g("x_sample")
    cc, sd, ss, sw = g("cache_conv_a"), g("state_delta_a"), g("state_shift_b"), g("state_wkv_b")
    shared = {
        "norm_w": g("norm_w"), "final_norm_w": g("final_norm_w"), "a_w_in": g("a_w_in")[0],
        "a_conv_w": g("a_conv_w")[0], "a_log": g("a_log")[0], "a_dt_bias": g("a_dt_bias")[0],
        "a_norm_w": g("a_norm_w")[0], "a_w_out": g("a_w_out")[0], "b_mu": g("b_mu")[0],
        "b_w_in": g("b_w_in")[0], "b_w0": g("b_w0")[0], "b_w_w1": g("b_w_w1")[0], "b_w_w2": g("b_w_w2")[0],
        "b_a0": g("b_a0")[0], "b_a_w1": g("b_a_w1")[0], "b_a_w2": g("b_a_w2")[0], "b_k_k": g("b_k_k")[0],
        "b_k_a": g("b_k_a")[0], "b_r_k": g("b_r_k")[0].reshape(-1), "b_gn_w": g("b_gn_w")[0],
        "b_gn_b": g("b_gn_b")[0], "b_w_out": g("b_w_out")[0],
    }
    maps = []
    for i in range(8):
        m = dict(shared)
        m["x_p"] = xp[i]
        m["x_s"] = np.ascontiguousarray(xs[2 * i:2 * i + 2].reshape(2 * TS, D))
        m["conv_s"] = np.ascontiguousarray(cc[0, 2 * i:2 * i + 2])
        m["delta_s"] = np.ascontiguousarray(sd[0, 2 * i:2 * i + 2])
        m["shift_s"] = np.ascontiguousarray(ss[0, 2 * i:2 * i + 2])
        m["wkv_s"] = np.ascontiguousarray(sw[0, 2 * i:2 * i + 2])
        maps.append(m)
    return maps


def kernel(**inputs):
    if "nc" not in _NC_CACHE:
        _NC_CACHE["nc"] = build()
    nc = _NC_CACHE["nc"]
    maps = make_in_maps(inputs)
    res = run_bass_kernel_spmd(nc, maps, core_ids=list(range(8)))
    R = res.results
    y_prompt = np.stack([R[i]["y_p"] for i in range(8)], 0)
    y_sample = np.concatenate([R[i]["y_s"].reshape(2, TS, D) for i in range(8)], 0)

    def pick(name, sl):
        return np.stack([R[i][name][sl] for i in range(8)], 0)[None] if isinstance(sl, int) else \
            np.concatenate([R[i][name][sl] for i in range(8)], 0)[None]

    p_conv, s_conv = pick("o_conv", 0), pick("o_conv", slice(1, 3))
    p_delta, s_delta = pick("o_delta", 0), pick("o_delta", slice(1, 3))
    p_shift, s_shift = pick("o_shift", 0), pick("o_shift", slice(1, 3))
    p_wkv, s_wkv = pick("o_wkv", 0), pick("o_wkv", slice(1, 3))
    return (y_prompt, y_sample, p_conv, p_delta, p_shift, p_wkv, s_conv, s_delta, s_shift, s_wkv)
```

```python
import numpy as np
import concourse.bass as bass
import concourse.mybir as mybir
from concourse.bass_utils import run_bass_kernel_spmd

F32 = mybir.dt.float32
BF16 = mybir.dt.bfloat16
AF = mybir.ActivationFunctionType
ALU = mybir.AluOpType
AX = mybir.AxisListType

D = 1024
KC = 8
TP = 2048
TS = 16
NTILE = 17
H_A = 8
RMS_EPS = 1e-6
GN_EPS = 64e-5
NEG = -30000.0


class Trk:
    __slots__ = ("lw", "rd", "dsem", "dcount")

    def __init__(self):
        self.lw = None
        self.rd = []
        self.dsem = None
        self.dcount = 0


def fsz(ap):
    n = 1
    for d in ap.shape[1:]:
        n *= d
    return n


class Prog:
    WINDOW = 4000
    LAT = 700.0
    LAT_SAME = 200.0
    PE_SWITCH = 0.0
    EPS = 150.0

    def __init__(self, nc):
        self.nc = nc
        self.h = {"pe": nc.tensor, "act": nc.scalar, "dve": nc.vector, "pool": nc.gpsimd, "sp": nc.sync}
        self.sem = {k: nc.alloc_semaphore("sem_" + k) for k in self.h}
        self.cnt = {k: 0 for k in self.h}
        self.known = {k: {} for k in self.h}
        self.trk = {}
        self.ops = []
        self.nps = 0
        self.npool = {}
        self.psb = []
        self.nsem = 0
        self.sb_off = nc.sbuf_base
        self.sb_top = nc.sbuf_top
        self.seg_start = 0
        self.sel = {}
        self.pecls = {}

    def sb(self, name, shape, dt=F32):
        n = 1
        for d in shape[1:]:
            n *= d
        nbytes = n * (2 if dt == BF16 else 4)
        off = (self.sb_off + 63) // 64 * 64
        assert off + nbytes <= self.sb_top, "SBUF overflow at %s: need %d have %d" % (name, nbytes, self.sb_top - off)
        t = self.nc.alloc_sbuf_tensor_at(name, list(shape), dt, offset=off)
        self.sb_off = off + nbytes
        self.trk[t.name] = Trk()
        return t

    def barrier(self):
        self.ops.append(("fence", None, None, (), 0.0, False, None))
        for t in self._all_trk():
            t.lw = None
            t.rd = []

    def _all_trk(self):
        for t in self.trk.values():
            if isinstance(t, list):
                for x in t:
                    yield x
            else:
                yield t

    def init_psum(self):
        for i in range(8):
            t = self.nc.alloc_psum_tensor("psb%d" % i, [128, 512], F32)
            self.trk["psb%d" % i] = Trk()
            self.psb.append(t)

    POOLS = {0: (0, 1, 2, 3, 4, 5, 6), 2: (0, 1, 2, 3, 4, 5, 6),
             "a1a": (0, 1), "a1m": (2,), "a1b": (3, 4), "a2": (5, 6),
             "b1": (0, 1, 2, 3, 4), "b2": (5, 6)}

    def ps(self, pool=0):
        banks = self.POOLS[pool]
        n = self.npool.get(pool, 0)
        self.npool[pool] = n + 1
        return self.psb[banks[n % len(banks)]]

    def split(self, tensor, n):
        self.trk[tensor.name] = [Trk() for _ in range(n)]

    def only(self, **sel):
        prog = self

        class _Ctx:
            def __enter__(self_):
                self_.old = dict(prog.sel)
                prog.sel.update(sel)

            def __exit__(self_, *a):
                prog.sel = self_.old
        return _Ctx()

    def _tks(self, ap):
        t = self.trk[ap.tensor.name]
        if isinstance(t, list):
            idx = self.sel.get(ap.tensor.name.rsplit("_", 1)[0])
            if idx is not None:
                return [t[i] for i in idx]
            try:
                pat = ap.ap
                F = 1
                for d in ap.tensor.shape[1:]:
                    F *= d
                pstride = pat[0][0]
                off = int(ap.offset)
                if pstride != F:
                    return list(t)
                f0 = off % F
                ext = 1
                for st, cnt in pat[1:]:
                    ext += (cnt - 1) * abs(st)
                gsz = F // len(t)
                g0 = f0 // gsz
                g1 = (f0 + ext - 1) // gsz
                if g0 < 0 or g1 >= len(t):
                    return list(t)
                return [t[i] for i in range(g0, g1 + 1)]
            except Exception:
                return list(t)
        return [t]

    def _record(self, kind, e, payload, reads, writes, cost, final=False):
        rt = []
        for a in reads:
            for t in self._tks(a):
                if t not in rt:
                    rt.append(t)
        wt = []
        for a in writes:
            for t in self._tks(a):
                if t not in wt:
                    wt.append(t)
        i = len(self.ops)
        preds = set()
        for t in rt:
            if t.lw is not None:
                preds.add(t.lw)
        for t in wt:
            if t.lw is not None:
                preds.add(t.lw)
            preds.update(t.rd)
        preds.discard(i)
        for t in rt:
            t.rd.append(i)
        for t in wt:
            t.lw = i
            t.rd = []
        t0 = (wt + rt)[0] if kind == "dma" else None
        self.ops.append((kind, e, payload, tuple(preds), float(cost), final, t0))
        return i

    def op(self, e, fn, reads, writes, cost=None):
        if cost is None:
            n = fsz(writes[0]) if writes else 64
            cost = {"act": 200.0 + 0.85 * n, "dve": 110.0 + 1.05 * n, "pool": 260.0 + 1.0 * n, "pe": 150.0}[e]
        return self._record("op", e, fn, reads, writes, cost)

    def dma(self, q, out, in_, reads=(), writes=(), final=False, **kw):
        return self._record("dma", q, (out, in_, kw), reads, writes, 150.0 if q == "sp" else 1200.0, final)

    def _wait(self, e, key, semh, val):
        k = self.known[e]
        if k.get(key, 0) < val:
            self.h[e].wait_ge(semh, val)
            k[key] = val

    def _schedule(self, lo, hi):
        ops = self.ops
        n = hi - lo
        indeg = [0] * n
        succ = [[] for _ in range(n)]
        for i in range(lo, hi):
            ps_ = [p for p in ops[i][3] if p >= lo]
            indeg[i - lo] = len(ps_)
            for p in ps_:
                succ[p - lo].append(i)
        blev = [0.0] * n
        for k in range(n - 1, -1, -1):
            o = ops[lo + k]
            c = o[4] + (2500.0 if o[0] == "dma" else 0.0)
            m = 0.0
            for j in succ[k]:
                v = blev[j - lo] + self.LAT
                if v > m:
                    m = v
            blev[k] = c + m
        dready = [0.0] * n
        etime = {k: 0.0 for k in self.h}
        ready = {k: [] for k in self.h}
        for i in range(lo, hi):
            if indeg[i - lo] == 0:
                ready[ops[i][1]].append(i)
        order = []
        done = [False] * n
        lastcls = None
        minp = lo
        W = self.WINDOW
        EPS = self.EPS
        while len(order) < n:
            while minp < hi and done[minp - lo]:
                minp += 1
            lim = minp + W
            best = None
            for e, lst in ready.items():
                if not lst:
                    continue
                te = etime[e]
                cand = None
                for i in lst:
                    if i >= lim:
                        continue
                    dr = dready[i - lo]
                    st = te if dr <= te + EPS else dr
                    if e == "pe" and self.pecls.get(i) != lastcls:
                        st += self.PE_SWITCH
                    key = (st, -blev[i - lo], i)
                    if cand is None or key < cand:
                        cand = key
                if cand is not None and (best is None or cand < best[0]):
                    best = (cand, e)
            (st, _, i), e = best
            st = max(st, dready[i - lo], etime[e])
            ready[e].remove(i)
            kind = ops[i][0]
            cost = ops[i][4]
            if e == "pe":
                cl = self.pecls.get(i)
                if cl != lastcls:
                    cost += self.PE_SWITCH
                lastcls = cl
            etime[e] = st + cost
            f = st + cost + (2500.0 if kind == "dma" else 0.0)
            done[i - lo] = True
            order.append(i)
            for j in succ[i - lo]:
                ej = ops[j][1]
                v = f + (0.0 if (e == "pe" and ej == "pe") else (self.LAT_SAME if ej == e else self.LAT))
                if v > dready[j - lo]:
                    dready[j - lo] = v
                indeg[j - lo] -= 1
                if indeg[j - lo] == 0:
                    ready[ops[j][1]].append(j)
        return order, max(etime.values())

    def finish(self):
        ops = self.ops
        bounds = [i for i, o in enumerate(ops) if o[0] == "fence"] + [len(ops)]
        needs_inc = [False] * len(ops)
        info = {}
        clock = {}
        finals = []
        lo = 0
        est_total = 0.0
        nwait = 0
        last_inc = {k: None for k in self.h}
        for b in bounds:
            order, est = self._schedule(lo, b)
            est_total += est
            pos = {i: k for k, i in enumerate(order)}
            kept = {}
            lastop = {}
            for i in order:
                kind, e = ops[i][0], ops[i][1]
                if kind == "op":
                    lastop[e] = i
                best = {}
                keep = []
                for p in ops[i][3]:
                    if p < lo:
                        continue
                    if ops[p][0] != "op":
                        keep.append(p)
                        continue
                    f = ops[p][1]
                    if f == "pe" and e == "pe":
                        continue
                    if f not in best or pos[p] > pos[best[f]]:
                        best[f] = p
                for p in best.values():
                    needs_inc[p] = True
                    keep.append(p)
                kept[i] = keep
            for i in lastop.values():
                needs_inc[i] = True
            for i in order:
                kind, e, payload, preds, cost, final, t0 = ops[i]
                kn = self.known[e]
                for p in sorted(kept[i], key=lambda x: pos[x]):
                    if p < lo:
                        continue
                    pi = info[p]
                    if pi[0] == "op":
                        f, c = pi[1], pi[2]
                        if f == "pe" and e == "pe":
                            continue
                        if kn.get(f, 0) < c:
                            self.h[e].wait_ge(self.sem[f], c)
                            nwait += 1
                            kn[f] = c
                            for g, v in clock[p].items():
                                if kn.get(g, 0) < v:
                                    kn[g] = v
                    else:
                        if kn.get(pi[3], 0) < pi[2]:
                            self.h[e].wait_ge(pi[1], pi[2])
                            nwait += 1
                            kn[pi[3]] = pi[2]
                if kind == "op":
                    ins = payload(self.h[e])
                    if needs_inc[i]:
                        self.cnt[e] += 1
                        ins.then_inc(self.sem[e], 1)
                        info[i] = ("op", e, self.cnt[e])
                        snap = {g: v for g, v in kn.items() if g in self.h}
                        snap[e] = self.cnt[e]
                        clock[i] = snap
                    else:
                        info[i] = ("op", e, self.cnt[e] + 1)
                        clock[i] = {}
                else:
                    out, in_, kw = payload
                    if t0.dsem is None:
                        t0.dsem = self.nc.alloc_semaphore("dsem%d" % self.nsem)
                        self.nsem += 1
                    ins = self.h[e].dma_start(out=out, in_=in_, **kw)
                    t0.dcount += 16
                    ins.then_inc(t0.dsem, 16)
                    info[i] = ("dma", t0.dsem, t0.dcount, "d%d" % id(t0))
                    if final:
                        finals.append(info[i])
            lo = b + 1
            if b < len(ops):
                for e in self.h:
                    for f in self.h:
                        if f != e and self.cnt[f] > 0:
                            self._wait(e, f, self.sem[f], self.cnt[f])
                    for t in self._all_trk():
                        if t.dsem is not None and t.dcount > 0:
                            self._wait(e, "d%d" % id(t), t.dsem, t.dcount)
        fmax = {}
        for (_, semh, c, key) in finals:
            if key not in fmax or c > fmax[key][1]:
                fmax[key] = (semh, c)
        for key, (semh, c) in fmax.items():
            self._wait("sp", key, semh, c)
        print("scheduler estimate: %.1f us, %d ops, %d waits, incs %s" % (est_total / 1e3, len(ops), nwait, dict(self.cnt)))

    def mmr(self, out, lhsT, rhs, start=True, stop=True):
        return self.mm(out, R(lhsT), R(rhs), start=start, stop=stop)

    def mm(self, out, lhsT, rhs, start=True, stop=True):
        passes = 4.0 if rhs.dtype == F32 else 1.0
        cost = 70.0 + passes * 0.42 * (fsz(rhs) + min(fsz(lhsT), 128))
        i = self.op("pe", lambda h: h.matmul(out, lhsT=lhsT, rhs=rhs, start=start, stop=stop),
                    [lhsT, rhs], [out], cost=cost)
        self.pecls[i] = str(rhs.dtype)
        return i

    def tr(self, out, in_, ident):
        i = self.op("pe", lambda h: h.transpose(out, in_, ident), [in_, ident], [out], cost=160.0)
        self.pecls[i] = "tr"
        return i

    def act(self, out, in_, func, e="act", **kw):
        rd = [in_] + [v for v in kw.values() if hasattr(v, "tensor")]
        wr = [out]
        if "accum_out" in kw:
            wr.append(kw["accum_out"])
            rd.remove(kw["accum_out"])
        return self.op("act", lambda h: h.activation(out=out, in_=in_, func=func, **kw), rd, wr)

    def tt(self, out, in0, in1, op, e="dve"):
        return self.op(e, lambda h: h.tensor_tensor(out=out, in0=in0, in1=in1, op=op), [in0, in1], [out])

    def ts(self, out, in0, s1, op0, s2=None, op1=None, e="dve"):
        rd = [in0] + [v for v in (s1, s2) if hasattr(v, "tensor")]
        if op1 is None:
            return self.op(e, lambda h: h.tensor_scalar(out=out, in0=in0, scalar1=s1, scalar2=None, op0=op0),
                           rd, [out])
        return self.op(e, lambda h: h.tensor_scalar(out=out, in0=in0, scalar1=s1, scalar2=s2, op0=op0, op1=op1),
                       rd, [out])

    def stt(self, out, in0, scalar, in1, op0, op1):
        rd = [in0, in1] + ([scalar] if hasattr(scalar, "tensor") else [])
        return self.op("dve", lambda h: h.scalar_tensor_tensor(out=out, in0=in0, scalar=scalar, in1=in1,
                                                                 op0=op0, op1=op1), rd, [out])

    def cp(self, out, in_, e="dve"):
        if e == "act":
            return self.act(out, in_, AF.Copy)
        return self.op(e, lambda h: h.tensor_copy(out=out, in_=in_), [in_], [out])

    def scan(self, out, d0, d1):
        return self.op("dve", lambda h: h.tensor_tensor_scan(out=out, data0=d0, data1=d1, initial=0.0,
                                                              op0=ALU.mult, op1=ALU.add),
                       [d0, d1], [out], cost=110.0 + 2.1 * fsz(out))

    def rsqrt_pool(self, out, in_, mhalf):
        return self.op("pool", lambda h: h.tensor_tensor(out=out, in0=in_, in1=mhalf, op=ALU.pow), [in_, mhalf], [out])

    def recip(self, out, in_):
        return self.op("dve", lambda h: h.reciprocal(out=out, in_=in_), [in_], [out], cost=110.0 + 3.0 * fsz(out))

    def memset(self, ap, val, e="pool"):
        return self.op(e, lambda h: h.memset(ap, val), [], [ap])

    def asel(self, out, in_, pattern, cmp, fill, base, cm):
        return self.op("pool", lambda h: h.affine_select(out=out, in_=in_, pattern=pattern, compare_op=cmp,
                                                          fill=fill, base=base, channel_multiplier=cm),
                       [in_], [out])


F32R = mybir.dt.float32r


def R(ap):
    return ap.bitcast(F32R)


def neumann(P, ident, MT0, M0, Mbuf, MTT, nx, C, pool=0):
    L = {128: 7, 64: 6, 16: 4}[C]
    G = 512 // (2 * C)
    ngrp = (nx + G - 1) // G
    names = [m.name.rsplit("_", 1)[0] for m in MTT]
    split = ngrp > 1 and all(isinstance(P.trk[m.name], list) for m in MTT)

    def grp_only(g):
        if not split:
            return P.only()
        return P.only(**{nm: [g] for nm in names})

    def grp3(ps, n, w):
        return ps[0:C, 0:n * w].rearrange("p (x i) -> p x i", x=n)

    psa = P.ps(pool)
    psb = P.ps(pool)
    for x in range(nx):
        P.mm(psa[0:C, x * C:(x + 1) * C], R(MT0[:, x, :]), R(Mbuf[0][0:C, x, 0:C]))
        P.mm(psb[0:C, x * C:(x + 1) * C], R(Mbuf[0][0:C, x, 0:C]), R(MT0[:, x, :]))
    P.cp(R(Mbuf[1][0:C, 0:nx, 0:C]), grp3(psa, nx, C), e="act")
    P.cp(R(MTT[0][0:C, 0:nx, 0, 0:C]), grp3(psb, nx, C), e="act")
    P.tt(R(MTT[0][0:C, 0:nx, 1, 0:C]), MT0, bcm(ident[0:C, 0:C], [C, nx, C]), ALU.add)
    yield
    cm, ct = 1, 0
    for lev in range(2, L + 1):
        last = lev == L
        Mc = Mbuf[cm]
        cur = MTT[ct]
        nxt = MTT[1 - ct]
        if not last:
            psa = P.ps(pool)
            for x in range(nx):
                P.mm(psa[0:C, x * C:(x + 1) * C], R(cur[0:C, x, 0, 0:C]), R(Mc[0:C, x, 0:C]))
        for x0 in range(0, nx, G):
            n = min(G, nx - x0)
            psx = P.ps(pool)
            with grp_only(x0 // G):
                for j in range(n):
                    x = x0 + j
                    if last:
                        P.mm(psx[0:C, j * C:(j + 1) * C], R(Mc[0:C, x, 0:C]), R(cur[0:C, x, 1, 0:C]))
                    else:
                        P.mm(psx[0:C, j * 2 * C:(j + 1) * 2 * C], R(Mc[0:C, x, 0:C]), R(cur[0:C, x, :, 0:C]))
                if last:
                    P.tt(R(nxt[0:C, x0:x0 + n, 1, 0:C]), grp3(psx, n, C), cur[0:C, x0:x0 + n, 1, 0:C], ALU.add)
                else:
                    pv = psx[0:C, 0:n * 2 * C].rearrange("p (x a i) -> p x a i", x=n, a=2)
                    P.cp(R(nxt[0:C, x0:x0 + n, 0, 0:C]), pv[:, :, 0, :], e="act")
                    P.tt(R(nxt[0:C, x0:x0 + n, 1, 0:C]), pv[:, :, 1, :], cur[0:C, x0:x0 + n, 1, 0:C], ALU.add)
        if not last:
            P.cp(R(Mbuf[1 - cm][0:C, 0:nx, 0:C]), grp3(psa, nx, C), e="act")
        cm = 1 - cm
        ct = 1 - ct
        yield
    return MTT[ct][0:C, 0:nx, 1, 0:C]


def bc(ap, shape):
    return ap.unsqueeze(len(ap.shape)).broadcast_to(list(shape))


def bcm(ap, shape):
    return ap.unsqueeze(1).broadcast_to(list(shape))


PCOL = 1
S1COL = TP + 1 + 1
S2COL = S1COL + 32
NTOK = S2COL + 16 + 1
NBA = 256
CA = 128
NCH = TP // CA + 2


def build(stop=None):
    nc = bass.Bass("TRN2", target_bir_lowering=False)
    P = Prog(nc)
    P.init_psum()

    def din(name, shape):
        return nc.dram_tensor(name, list(shape), F32, kind="ExternalInput").ap()

    def dout(name, shape):
        return nc.dram_tensor(name, list(shape), F32, kind="ExternalOutput").ap()

    x_p = din("x_p", [TP, D])
    x_s = din("x_s", [2 * TS, D])
    conv_s = din("conv_s", [2, 3, 4096])
    delta_s = din("delta_s", [2, 8, 128, 256])
    shift_s = din("shift_s", [2, D])
    wkv_s = din("wkv_s", [2, 16, 64, 64])
    norm_w = din("norm_w", [2, D])
    final_norm_w = din("final_norm_w", [D])
    a_w_in = din("a_w_in", [D, 6160])
    a_conv_w = din("a_conv_w", [4, 4096])
    a_log = din("a_log", [8])
    a_dt_bias = din("a_dt_bias", [8])
    a_norm_w = din("a_norm_w", [256])
    a_w_out = din("a_w_out", [2048, D])
    b_mu = din("b_mu", [6, D])
    b_w_in = din("b_w_in", [D, 4096])
    b_w0 = din("b_w0", [D])
    b_w_w1 = din("b_w_w1", [D, 64])
    b_w_w2 = din("b_w_w2", [64, D])
    b_a0 = din("b_a0", [D])
    b_a_w1 = din("b_a_w1", [D, 64])
    b_a_w2 = din("b_a_w2", [64, D])
    b_k_k = din("b_k_k", [D])
    b_k_a = din("b_k_a", [D])
    b_r_k = din("b_r_k", [D])
    b_gn_w = din("b_gn_w", [D])
    b_gn_b = din("b_gn_b", [D])
    b_w_out = din("b_w_out", [D, D])

    y_p = dout("y_p", [TP, D])
    y_s = dout("y_s", [2 * TS, D])
    o_conv = dout("o_conv", [3, 3, 4096])
    o_delta = dout("o_delta", [3, 8, 128, 256])
    o_shift = dout("o_shift", [3, D])
    o_wkv = dout("o_wkv", [3, 16, 64, 64])
    dbg = dout("dbg", [NTILE * 128, D]) if stop else None

    ident = P.sb("ident", [128, 128])
    ones = P.sb("ones", [128, 128])
    mones = P.sb("mones", [128, 128])
    zeros = P.sb("zeros", [128, 128])
    Utri = P.sb("Utri", [128, 128])
    NEGT = P.sb("NEGT", [128, 128])
    MsT = P.sb("MsT", [128, 128])
    P.memset(ones[:], 1.0)
    ones_r = P.sb("ones_r", [128, 128])
    P.cp(R(ones_r[:]), ones[:], e="act")
    P.memset(mones[:], -1.0)
    P.memset(zeros[:], 0.0)
    P.asel(ident[:], ones[:], [[-1, 128]], ALU.is_equal, 0.0, 0, 1)
    P.asel(Utri[:], ones[:, :], [[1, 128]], ALU.is_ge, 0.0, 0, -1)
    P.asel(NEGT[:], zeros[:, :], [[1, 128]], ALU.is_ge, NEG, 0, -1)
    P.asel(MsT[:], ones[:, :], [[1, 128]], ALU.is_gt, 0.0, 0, -1)

    xres = [P.sb("xres%d" % i, [128, D]) for i in range(NTILE)]
    nw = P.sb("nw", [128, 2, KC])
    fnw = P.sb("fnw", [128, KC])
    P.dma("sp", nw[:], norm_w.rearrange("l (k p) -> p l k", p=128), writes=[nw[:]], allow_slow_non_contiguous=True)
    P.dma("sp", fnw[:], final_norm_w.rearrange("(k p) -> p k", p=128), writes=[fnw[:]],
          allow_slow_non_contiguous=True)
    ssq = P.sb("ssq", [128, 4])
    rstd = P.sb("rstd", [128, 4])
    P.split(ssq, 4)
    P.split(rstd, 4)
    phase_mark = P.sb_off
    hT = P.sb("hT", [128, KC, NTOK], BF16)
    xn = [P.sb("xn%d" % i, [128, D]) for i in range(2)]

    def tile_rows(i):
        return 128 if i < 16 else 48

    def tile_col(i):
        return PCOL + i * 128 if i < 16 else S1COL

    def norm_to_hT(layer, hT):
        for i in range(NTILE):
            nt = tile_rows(i)
            xt = xres[i]
            xb = xn[i % 2]
            sl = slice(i % 4, i % 4 + 1)
            P.act(xb[0:nt, :], xt[0:nt, :], AF.Square, accum_out=ssq[0:nt, sl])
            P.act(rstd[0:nt, sl], ssq[0:nt, sl], AF.Sqrt, scale=1.0 / D, bias=RMS_EPS)
            P.recip(rstd[0:nt, sl], rstd[0:nt, sl])
            P.ts(xb[0:nt, :], xt[0:nt, :], rstd[0:nt, sl], ALU.mult)
            c0 = tile_col(i)
            for half in range(2):
                ps = P.ps()
                for j in range(4):
                    kc = half * 4 + j
                    P.tr(ps[:, j * 128:j * 128 + nt], xb[0:nt, kc * 128:(kc + 1) * 128], ident[0:nt, 0:nt])
                pv = ps[:, :].rearrange("p (j t) -> p j t", j=4)[:, :, 0:nt]
                P.tt(hT[:, half * 4:half * 4 + 4, c0:c0 + nt], pv,
                     bc(nw[:, layer, half * 4:half * 4 + 4], [128, 4, nt]), ALU.mult)

    for i in range(NTILE):
        if i < 16:
            P.dma("sp", xres[i][:], x_p[i * 128:(i + 1) * 128, :], writes=[xres[i][:]])
        else:
            P.memset(xres[i][:], 0.0)
            P.dma("sp", xres[i][0:16, :], x_s[0:16, :], writes=[xres[i][:]])
            P.dma("sp", xres[i][32:48, :], x_s[16:32, :], writes=[xres[i][:]])
    P.memset(hT[:, :, 0:1], 0.0)
    norm_to_hT(0, hT)

    blocks = [(0, PCOL + i * NBA, NBA, CA, NBA // CA, i * (NBA // CA)) for i in range(TP // NBA)] + \
             [(1, S1COL, 16, 16, 1, NCH - 2), (2, S2COL, 16, 16, 1, NCH - 1)]

    cwt = P.sb("cwt", [32, 4, 128])
    cw = P.sb("cw", [128, 4, 32])
    P.dma("sp", cwt[:], a_conv_w.rearrange("t (g c) -> g t c", c=128), writes=[cwt[:]])
    ps = P.ps()
    for t in range(4):
        P.tr(ps[:, t * 32:(t + 1) * 32], cwt[:, t, :], ident[0:32, 0:32])
    P.cp(cw[:].rearrange("p t g -> p (t g)"), ps[:, 0:128])
    halo_all = P.sb("halo_all", [128, 2, 3, 32])
    hrow = P.sb("hrow", [96, 2, 128])
    for s in range(2):
        P.dma("sp", hrow[:, s, :], conv_s[s].rearrange("t (g c) -> (t g) c", c=128), writes=[hrow[:]])
    ps = P.ps()
    for s in range(2):
        P.tr(ps[:, s * 96:(s + 1) * 96], hrow[:, s, :], ident[0:96, 0:96])
    P.cp(halo_all[:].rearrange("p s t g -> p (s t g)"), ps[:, 0:192])
    fin_all = P.sb("fin_all", [128, 3, 3, 32])
    anw = P.sb("anw", [128, 2])
    P.dma("sp", anw[:], a_norm_w.rearrange("(h p) -> p h", p=128), writes=[anw[:]], allow_slow_non_contiguous=True)
    P.ts(anw[:], anw[:], 0.5, ALU.mult)

    wba = P.sb("wba", [128, KC, 16], BF16)
    P.dma("pool", wba[:], a_w_in.rearrange("(k p) c -> p k c", p=128)[:, :, 6144:6160], writes=[wba[:]])
    NCP = TP // CA
    BA = P.sb("BA", [CA, NCH, 16])
    P.memset(BA[:], 0.0)
    ps = P.ps()
    for c in range(NCP):
        for kc in range(KC):
            P.mm(ps[0:CA, c * 16:(c + 1) * 16], hT[:, kc, PCOL + c * CA:PCOL + (c + 1) * CA], wba[:, kc, :],
                 start=(kc == 0), stop=(kc == KC - 1))
    P.cp(BA[:, 0:NCP, :].rearrange("p c k -> p (c k)"), ps[0:CA, 0:NCP * 16])
    ps = P.ps()
    for s, sc in enumerate((S1COL, S2COL)):
        for kc in range(KC):
            P.mm(ps[0:16, s * 16:(s + 1) * 16], hT[:, kc, sc:sc + 16], wba[:, kc, :],
                 start=(kc == 0), stop=(kc == KC - 1))
    P.cp(BA[0:16, NCP:NCP + 2, :].rearrange("p c k -> p (c k)"), ps[0:16, 0:32])
    alg = P.sb("alg", [CA, 8])
    dtb = P.sb("dtb", [CA, 8])
    P.dma("sp", alg[:], a_log.partition_broadcast(CA), writes=[alg[:]])
    P.dma("sp", dtb[:], a_dt_bias.partition_broadcast(CA), writes=[dtb[:]])
    P.act(alg[:], alg[:], AF.Exp)
    P.ts(alg[:], alg[:], -1.0, ALU.mult)
    beta = P.sb("beta", [CA, NCH, 8])
    gg = P.sb("gg", [CA, NCH, 8])
    Gc = P.sb("Gc", [CA, NCH, 8])
    Glb = P.sb("Glb", [128, NCH, 8])
    gl = P.sb("gl", [128, NCH, 8])
    eG = P.sb("eG", [CA, NCH, 8])
    bG = P.sb("bG", [CA, NCH, 8])
    dte = P.sb("dte", [CA, NCH, 8])
    P.act(beta[:], BA[:, :, 0:8], AF.Sigmoid)
    P.tt(gg[:], BA[:, :, 8:16], bcm(dtb[:], [CA, NCH, 8]), ALU.add)
    P.act(gg[:], gg[:], AF.Exp)
    P.act(gg[:], gg[:], AF.Ln, bias=1.0)
    P.tt(gg[:], gg[:], bcm(alg[:], [CA, NCH, 8]), ALU.mult)
    psG = P.ps()
    psL = P.ps()
    g2 = gg[:].rearrange("p c k -> p (c k)")
    GP = NCP * 8
    P.mm(psG[0:CA, 0:GP], Utri[0:CA, 0:CA], g2[:, 0:GP])
    P.mm(psG[0:16, GP:GP + 16], Utri[0:16, 0:16], g2[0:16, GP:GP + 16])
    P.mm(psL[:, 0:GP], ones[0:CA, :], g2[:, 0:GP])
    P.mm(psL[:, GP:GP + 16], ones[0:16, :], g2[0:16, GP:GP + 16])
    P.memset(Gc[:], 0.0)
    P.cp(Gc[:, 0:NCP, :].rearrange("p c k -> p (c k)"), psG[0:CA, 0:GP])
    P.cp(Gc[0:16, NCP:NCP + 2, :].rearrange("p c k -> p (c k)"), psG[0:16, GP:GP + 16])
    P.cp(Glb[:].rearrange("p c k -> p (c k)"), psL[:, 0:GP + 16])
    P.act(gl[:], Glb[:], AF.Exp)
    P.act(eG[:], Gc[:], AF.Exp)
    P.tt(bG[:], beta[:], eG[:], ALU.mult)
    hbeta = eG
    P.ts(hbeta[:], beta[:], 0.5, ALU.mult)
    P.tt(dte[:], Glb[0:CA], Gc[:], ALU.subtract)
    P.act(dte[:], dte[:], AF.Exp)

    Wh = [P.sb("Wh0", [128, KC, 768], BF16)] * 2
    Wo = [P.sb("Wo0", [128, 2, D], BF16)] * 2
    w_in_v = a_w_in.rearrange("(k p) c -> p k c", p=128)

    def load_head_w(h):
        sl = h % 2
        for (c0, n, o) in ((h * 128, 128, 0), (1024 + h * 128, 128, 128), (2048 + h * 256, 256, 256),
                           (4096 + h * 256, 256, 512)):
            P.dma("pool", Wh[sl][:, :, o:o + n], w_in_v[:, :, c0:c0 + n], writes=[Wh[sl][:]])

    def load_head_wo(h):
        sl = h % 2
        P.dma("pool", Wo[sl][:], a_w_out[h * 256:(h + 1) * 256, :].rearrange("(hh p) c -> p hh c", p=128),
              writes=[Wo[sl][:]])

    NB_ = NBA
    NC_ = NBA // CA
    pre = P.sb("pre", [128, 4, NB_ + 3])
    acc = P.sb("acc", [128, 4, NB_])
    P.split(acc, 4)
    P.split(pre, 4)
    qkv = P.sb("qkv", [128, 4, NB_])
    zs2 = [P.sb("zs%d" % i, [128, 2, NB_]) for i in range(2)]
    sqr = P.sb("sqr", [128, 2, NB_])
    sq = sqr
    rq = acc[:, 2:4]
    oT = P.sb("oTp", [128, 2, NB_])
    osq = P.sb("osq2", [128, 2, NB_])
    qT = P.sb("qT", [128, NB_])
    kT = P.sb("kT", [128, NB_])
    kbT = P.sb("kbT", [128, NB_])
    qgT2 = [P.sb("qgT%d" % i, [128, NB_]) for i in range(2)]
    dg = P.sb("dg", [128, 2, NB_])
    eGb = P.sb("eGb", [128, NB_])
    betab = P.sb("betab", [128, NB_])
    kbeG2 = [P.sb("kbeG%d" % i, [CA, NC_, 128]) for i in range(2)]
    ktk2 = [P.sb("ktk%d" % i, [CA, NC_, 128]) for i in range(2)]
    vb2 = [P.sb("vb%d" % i, [CA, NC_, 256]) for i in range(2)]
    gU = P.sb("gU", [CA, NC_, CA])
    DT = P.sb("DT", [CA, NC_, CA])
    DTs = P.sb("DTs", [CA, NC_, CA])
    qkT2 = [P.sb("qkT%d" % i, [CA, NC_, CA]) for i in range(2)]
    Mn = [P.sb("Mn%d" % i, [CA, NC_, CA]) for i in range(2)]
    MT0a2 = [P.sb("MT0a%d" % i, [CA, NC_, CA]) for i in range(2)]
    MTTa = [P.sb("MTTa%d" % i, [CA, NC_, 2, CA]) for i in range(2)]
    u0 = P.sb("u0", [CA, NC_, 256])
    wkT = P.sb("wkT", [128, NB_])
    uu = [P.sb("uu%d" % i, [CA, 256]) for i in range(2)]
    Sp_ = [P.sb("Sp%d" % i, [128, 256]) for i in range(2)]
    Ss_ = [P.sb("Ss%d" % i, [128, 256]) for i in range(2)]
    Sst = [Sp_, Ss_, Ss_]
    ors = P.sb("ors", [128, NB_])
    og = P.sb("og", [128, 2, NB_], BF16)
    print("SBUF left after layer-A alloc:", P.sb_top - P.sb_off)
    for t_, n_ in [(qkv, 4), (oT, 2), (osq, 2), (sqr, 2), (dg, 2), (gU, NC_), (DT, NC_), (DTs, NC_), (u0, NC_),
                   (og, 2)] + [(x_, 2) for x_ in zs2] + [(x_, NC_) for x_ in kbeG2 + ktk2 + vb2 + qkT2 + Mn + MT0a2 + MTTa]:
        P.split(t_, n_)

    DONE = object()

    def A_s1(it, h, blk):
        (seq, t0, NB, C, nch, c0) = blk
        par = it % 2
        W = Wh[0]
        ggrp = (h, 8 + h, 16 + 2 * h, 17 + 2 * h)
        zs, qgT, ktk, qkT = zs2[par], qgT2[par], ktk2[par], qkT2[par]
        kbeG, vb, MT0a = kbeG2[par], vb2[par], MT0a2[par]
        first_of_seq = (seq == 0 and t0 == PCOL) or seq > 0
        if seq == 0 and t0 == PCOL:
            load_head_w(h)
        if first_of_seq:
            if seq == 0:
                P.memset(pre[:, :, 0:3], 0.0)
            else:
                for gi, g in enumerate(ggrp):
                    with P.only(pre=[gi]):
                        P.cp(pre[:, gi, 0:3], halo_all[:, seq - 1, :, g], e="pool")
        for m in range(6):
            ps = P.ps("a1a")
            for kc in range(KC):
                P.mm(ps[:, 0:NB], W[:, kc, m * 128:(m + 1) * 128], hT[:, kc, t0:t0 + NB],
                     start=(kc == 0), stop=(kc == KC - 1))
            if m < 4:
                with P.only(pre=[m]):
                    P.cp(pre[:, m, 3:3 + NB], ps[:, 0:NB], e="act")
            else:
                P.act(zs[:, m - 4, 0:NB], ps[:, 0:NB], AF.Tanh, scale=0.5)
                P.stt(zs[:, m - 4, 0:NB], zs[:, m - 4, 0:NB], 1.0, ps[:, 0:NB], ALU.add, ALU.mult)
            yield
        for gi, g in enumerate(ggrp):
            with P.only(acc=[gi], pre=[gi]):
                P.act(acc[:, gi, 0:NB], pre[:, gi, 3:3 + NB], AF.Copy, scale=cw[:, 3, g:g + 1])
                for tap in (2, 1, 0):
                    P.stt(acc[:, gi, 0:NB], pre[:, gi, tap:tap + NB], cw[:, tap, g:g + 1], acc[:, gi, 0:NB],
                          ALU.mult, ALU.add)
            yield
        P.act(qkv[:, :, 0:NB], acc[:, :, 0:NB], AF.Tanh, scale=0.5)
        P.stt(qkv[:, :, 0:NB], qkv[:, :, 0:NB], 1.0, acc[:, :, 0:NB], ALU.add, ALU.mult)
        last = (t0 + NB == PCOL + TP) or seq > 0
        for gi, g in enumerate(ggrp):
            with P.only(pre=[gi]):
                if last:
                    P.cp(fin_all[:, seq, :, g], pre[:, gi, NB:NB + 3], e="pool")
                else:
                    P.cp(pre[:, gi, 0:3], pre[:, gi, NB:NB + 3], e="pool")
        yield
        P.act(R(sq[:, :, 0:NB]), qkv[:, 0:2, 0:NB], AF.Square)
        for j in range(2):
            ps = P.ps("a1m")
            P.mmr(ps[:, 0:NB], ones_r[:, :], sq[:, j, 0:NB])
            with P.only(acc=[2 + j]):
                if j == 0:
                    P.act(rq[:, j, 0:NB], ps[:, 0:NB], AF.Ln, scale=128.0, bias=512.0 * 1e-6)
                else:
                    P.act(rq[:, j, 0:NB], ps[:, 0:NB], AF.Ln, bias=4e-6)
        yield
        with P.only(acc=[2, 3]):
            P.act(rq[:, :, 0:NB], rq[:, :, 0:NB], AF.Exp, scale=-0.5)
        with P.only(acc=[2]):
            P.tt(R(qT[:, 0:NB]), qkv[:, 0, 0:NB], rq[:, 0, 0:NB], ALU.mult)
        with P.only(acc=[3]):
            P.tt(R(kT[:, 0:NB]), qkv[:, 1, 0:NB], rq[:, 1, 0:NB], ALU.mult)
        yield
        cs = slice(c0, c0 + nch)
        idb = bcm(ident[0:C, 0:C], [C, nch, C])
        dgv = dg[0:C, :, 0:NB].rearrange("p a (c i) -> p a c i", c=nch)
        P.tt(dgv[:, 0], idb, bc(Gc[0:C, cs, h], [C, nch, C]), ALU.mult)
        P.tt(dgv[:, 1], idb, bc(beta[0:C, cs, h], [C, nch, C]), ALU.mult)
        ps = P.ps("a1m")
        P.mm(ps[:, 0:NB], ones[0:C, :], dg[0:C, 0, 0:NB])
        P.act(eGb[:, 0:NB], ps[:, 0:NB], AF.Exp)
        ps = P.ps("a1m")
        P.mm(ps[:, 0:NB], ones[0:C, :], dg[0:C, 1, 0:NB])
        P.cp(betab[:, 0:NB], ps[:, 0:NB], e="act")
        yield
        P.tt(R(kbT[:, 0:NB]), kT[:, 0:NB], betab[:, 0:NB], ALU.mult)
        P.tt(R(qgT[:, 0:NB]), qT[:, 0:NB], eGb[:, 0:NB], ALU.mult)
        yield
        for c4 in range(0, nch, 4):
            n4 = min(4, nch - c4)
            ps = P.ps("a1m")
            for j in range(n4):
                c = c4 + j
                P.tr(ps[0:C, j * 128:(j + 1) * 128], kT[:, c * C:(c + 1) * C], ident[:, :])
            pv = ps[0:C, 0:n4 * 128].rearrange("p (j d) -> p j d", j=n4)
            P.tt(R(kbeG[0:C, c4:c4 + n4, :]), pv, bc(bG[0:C, c0 + c4:c0 + c4 + n4, h], [C, n4, 128]), ALU.mult)
            P.tt(R(ktk[0:C, c4:c4 + n4, :]), pv, bc(dte[0:C, c0 + c4:c0 + c4 + n4, h], [C, n4, 128]), ALU.mult)
            yield
        for c2 in range(0, nch, 2):
            n2 = min(2, nch - c2)
            ps = P.ps("a1m")
            for j in range(n2):
                c = c2 + j
                for half in range(2):
                    P.tr(ps[0:C, j * 256 + half * 128:j * 256 + (half + 1) * 128],
                         qkv[:, 2 + half, c * C:(c + 1) * C], ident[:, :])
            pv = ps[0:C, 0:n2 * 256].rearrange("p (j d) -> p j d", j=n2)
            P.tt(R(vb[0:C, c2:c2 + n2, :]), pv, bc(hbeta[0:C, c0 + c2:c0 + c2 + n2, h], [C, n2, 256]), ALU.mult)
            yield
        P.tt(gU[0:C, 0:nch, 0:C], bcm(Utri[0:C, 0:C], [C, nch, C]), bc(gg[0:C, cs, h], [C, nch, C]), ALU.mult)
        ps = P.ps("a1m")
        for c in range(nch):
            o = ps[0:C, c * C:(c + 1) * C]
            P.mm(o, ones[0:C, 0:C], gU[0:C, c, 0:C], start=True, stop=False)
            P.mm(o, gU[0:C, c, 0:C], mones[0:C, 0:C], start=False, stop=False)
            P.mm(o, ident[0:C, 0:C], NEGT[0:C, 0:C], start=False, stop=True)
        pv = ps[0:C, 0:nch * C].rearrange("p (c i) -> p c i", c=nch)
        P.act(DT[0:C, 0:nch, 0:C], pv, AF.Exp)
        P.tt(DTs[0:C, 0:nch, 0:C], DT[0:C, 0:nch, 0:C], bcm(MsT[0:C, 0:C], [C, nch, C]), ALU.mult, e="pool")
        yield
        ps = P.ps("a1m")
        for c in range(nch):
            P.mmr(ps[0:C, c * C:(c + 1) * C], kT[:, c * C:(c + 1) * C], kbT[:, c * C:(c + 1) * C])
        for c in range(nch):
            P.mmr(ps[0:C, 256 + c * C:256 + (c + 1) * C], kT[:, c * C:(c + 1) * C], qT[:, c * C:(c + 1) * C])
        pv = ps[0:C, 0:nch * C].rearrange("p (c i) -> p c i", c=nch)
        pv2 = ps[0:C, 256:256 + nch * C].rearrange("p (c i) -> p c i", c=nch)
        P.stt(R(MT0a[0:C, 0:nch, 0:C]), pv, -1.0, DTs[0:C, 0:nch, 0:C], ALU.mult, ALU.mult)
        P.tt(R(qkT[0:C, 0:nch, 0:C]), pv2, DT[0:C, 0:nch, 0:C], ALU.mult)
        yield
        ps = P.ps("a1m")
        for c in range(nch):
            P.tr(ps[0:C, c * C:(c + 1) * C], MT0a[0:C, c, 0:C], ident[0:C, 0:C])
        P.cp(R(Mn[0][0:C, 0:nch, 0:C]), ps[0:C, 0:nch * C].rearrange("p (c i) -> p c i", c=nch), e="act")
        yield
        TTf = yield from neumann(P, ident, MT0a[0:C, 0:nch, 0:C], None, Mn, MTTa, nch, C, pool="a1b")
        yield "DRAIN2"
        for c2 in range(0, nch, 2):
            n2 = min(2, nch - c2)
            ps = P.ps("a1b")
            for j in range(n2):
                P.mmr(ps[0:C, j * 256:(j + 1) * 256], TTf[:, c2 + j, :], vb[0:C, c2 + j, :])
            P.cp(u0[0:C, c2:c2 + n2, :], ps[0:C, 0:n2 * 256].rearrange("p (j d) -> p j d", j=n2), e="act")
        ps = P.ps("a1b")
        for c in range(nch):
            P.mmr(ps[:, c * C:(c + 1) * C], kbeG[0:C, c, :], TTf[:, c, :])
        P.cp(R(wkT[:, 0:NB]), ps[:, 0:NB], e="act")
        yield

    spar = [0, 0, 0]

    def A_s2(it, h, blk):
        (seq, t0, NB, C, nch, c0) = blk
        par = it % 2
        WO = Wo[0]
        zs, qgT, ktk, qkT = zs2[par], qgT2[par], ktk2[par], qkT2[par]
        first_of_seq = (seq == 0 and t0 == PCOL) or seq > 0
        if seq == 0 and t0 == PCOL:
            load_head_wo(h)
        if first_of_seq:
            spar[seq] = 0
            if seq == 0:
                P.cp(R(Sst[0][0][:]), zeros[:, 0:1].broadcast_to([128, 256]), e="act")
            else:
                P.dma("sp", Sst[seq][1][:], delta_s[seq - 1, h], writes=[Sst[seq][1][:]])
                P.cp(R(Sst[seq][0][:]), Sst[seq][1][:], e="act")
        pso = P.psb[7]
        for c in range(nch):
            Sc = Sst[seq][spar[seq]]
            Sn = Sst[seq][1 - spar[seq]]
            u = uu[c % 2]
            ps = P.ps("a2")
            P.mmr(ps[0:C, 0:256], wkT[:, c * C:(c + 1) * C], Sc[:, :])
            P.tt(R(u[0:C, :]), u0[0:C, c, :], ps[0:C, 0:256], ALU.subtract)
            yield
            ps2 = P.ps("a2")
            P.mmr(ps2[:, 0:256], ktk[0:C, c, :], u[0:C, :])
            P.stt(R(Sn[:, :]), Sc[:, :], gl[:, c0 + c, h:h + 1], ps2[:, 0:256], ALU.mult, ALU.add)
            for half in range(2):
                oo = pso[:, (half * nch + c) * C:(half * nch + c + 1) * C]
                P.mmr(oo, Sc[:, half * 128:(half + 1) * 128], qgT[:, c * C:(c + 1) * C], start=True, stop=False)
                P.mmr(oo, u[0:C, half * 128:(half + 1) * 128], qkT[0:C, c, 0:C], start=False, stop=True)
            spar[seq] = 1 - spar[seq]
            yield
        P.cp(oT[:, :, 0:NB], pso[:, 0:2 * NB].rearrange("p (a t) -> p a t", a=2), e="act")
        P.act(R(osq[:, :, 0:NB]), oT[:, :, 0:NB], AF.Square)
        ps = P.ps("a2")
        P.mmr(ps[:, 0:NB], ones_r[:, :], osq[:, 0, 0:NB], start=True, stop=False)
        P.mmr(ps[:, 0:NB], ones_r[:, :], osq[:, 1, 0:NB], start=False, stop=True)
        P.act(ors[:, 0:NB], ps[:, 0:NB], AF.Ln, scale=1.0 / 256.0, bias=RMS_EPS)
        yield
        P.act(ors[:, 0:NB], ors[:, 0:NB], AF.Exp, scale=-0.5)
        for half in range(2):
            P.stt(oT[:, half, 0:NB], oT[:, half, 0:NB], anw[:, half:half + 1], ors[:, 0:NB], ALU.mult, ALU.mult)
        P.tt(og[:, :, 0:NB], oT[:, :, 0:NB], zs[:, :, 0:NB], ALU.mult)
        yield
        for tt0 in range(0, NB, 128):
            nt = min(128, NB - tt0)
            if seq == 0:
                tile_i, prow = (t0 - PCOL + tt0) // 128, 0
            else:
                tile_i, prow = 16, (seq - 1) * 32
            for nh in range(2):
                ps = P.ps("a2")
                for half in range(2):
                    P.mm(ps[prow:prow + nt, :], og[:, half, tt0:tt0 + nt], WO[:, half, nh * 512:(nh + 1) * 512],
                         start=(half == 0), stop=(half == 1))
                xr = xres[tile_i][prow:prow + nt, nh * 512:(nh + 1) * 512]
                P.tt(xr, xr, ps[prow:prow + nt, :], ALU.add)
            yield
        last = (t0 + NB == PCOL + TP) or seq > 0
        if last:
            P.dma("sp", o_delta[seq, h], Sst[seq][spar[seq]][:], reads=[Sst[seq][spar[seq]][:]], final=True)

    def pipeline(items, s1, s2, ratio):
        g2 = None
        for it, item in enumerate(list(items) + [None]):
            g1 = s1(it, *item) if item is not None else None
            while g1 is not None or g2 is not None:
                if g2 is not None:
                    if next(g2, DONE) is DONE:
                        g2 = None
                if g1 is not None:
                    for _ in range(ratio if g2 is not None else 1000000):
                        r = next(g1, DONE)
                        if r is DONE:
                            g1 = None
                            break
                        if r == "DRAIN2":
                            while g2 is not None:
                                if next(g2, DONE) is DONE:
                                    g2 = None
            g2 = s2(it, *item) if item is not None else None

    pipeline([(h, blk) for h in range(H_A) for blk in blocks], A_s1, A_s2, 3)

    ps = P.ps()
    for s in range(3):
        P.tr(ps[0:96, s * 128:(s + 1) * 128], fin_all[:, s].rearrange("p t g -> p (t g)"), ident[:, :])
    for s in range(3):
        P.cp(acc[0:96, s, 0:128], ps[0:96, s * 128:(s + 1) * 128])
        P.dma("sp", o_conv[s].rearrange("t (g c) -> (t g) c", c=128), acc[0:96, s, 0:128], reads=[acc[:]], final=True)

    if stop == "A":
        for i in range(NTILE):
            nt = tile_rows(i)
            P.dma("sp", dbg[i * 128:i * 128 + nt, :], xres[i][0:nt, :], reads=[xres[i][:]], final=True)
        P.finish()
        return nc

    P.barrier()
    P.sb_off = phase_mark
    hT = P.sb("hT2", [128, KC, NTOK], BF16)
    shout = P.sb("shout", [128, 3, KC])
    P.memset(hT[:, :, 0:1], 0.0)
    mark_b0 = P.sb_off
    xn = [P.sb("xnB%d" % i, [128, D]) for i in range(2)]

    def norm_to_hT_B():
        for i in range(NTILE):
            nt = tile_rows(i)
            xt = xres[i]
            xb = xn[i % 2]
            sl = slice(i % 4, i % 4 + 1)
            P.act(xb[0:nt, :], xt[0:nt, :], AF.Square, accum_out=ssq[0:nt, sl])
            P.act(rstd[0:nt, sl], ssq[0:nt, sl], AF.Sqrt, scale=1.0 / D, bias=RMS_EPS)
            P.recip(rstd[0:nt, sl], rstd[0:nt, sl])
            P.ts(xb[0:nt, :], xt[0:nt, :], rstd[0:nt, sl], ALU.mult)
            c0 = tile_col(i)
            for half in range(2):
                ps = P.ps(2)
                for j in range(4):
                    kc = half * 4 + j
                    P.tr(ps[:, j * 128:j * 128 + nt], xb[0:nt, kc * 128:(kc + 1) * 128], ident[0:nt, 0:nt])
                pv4 = ps[:, :].rearrange("p (j t) -> p j t", j=4)
                pv = pv4[:, :, 0:nt]
                P.tt(hT[:, half * 4:half * 4 + 4, c0:c0 + nt], pv,
                     bc(nw[:, 1, half * 4:half * 4 + 4], [128, 4, nt]), ALU.mult)
                lastcols = {15: [(0, 127)], 16: [(1, 15), (2, 47)]}.get(i, [])
                for (sq_, col) in lastcols:
                    P.tt(shout[:, sq_, half * 4:half * 4 + 4], pv4[:, :, col], nw[:, 1, half * 4:half * 4 + 4], ALU.mult)

    norm_to_hT_B()
    P.barrier()
    P.sb_off = mark_b0
    for s_ in range(3):
        P.dma("sp", o_shift[s_].rearrange("(k p) -> p k", p=128), shout[:, s_, :], reads=[shout[:]], final=True,
              allow_slow_non_contiguous=True)
    shin = P.sb("shin", [128, 2, KC])
    P.dma("sp", shin[:], shift_s.rearrange("s (k p) -> p s k", p=128), writes=[shin[:]], allow_slow_non_contiguous=True)
    P.cp(hT[:, :, S1COL - 1], shin[:, 0, :])
    P.cp(hT[:, :, S2COL - 1], shin[:, 1, :])

    vecs = P.sb("vecs", [128, 13, KC])
    P.dma("sp", vecs[:, 0:6, :], b_mu.rearrange("g (k p) -> p g k", p=128), writes=[vecs[:]], allow_slow_non_contiguous=True)
    for vi, v_ in enumerate((b_w0, b_a0, b_k_k, b_k_a, b_r_k, b_gn_w, b_gn_b)):
        P.dma("sp", vecs[:, 6 + vi, :], v_.rearrange("(k p) -> p k", p=128), writes=[vecs[:]], allow_slow_non_contiguous=True)
    V_W0, V_A0, V_KK, V_KA, V_RK, V_GW, V_GB = range(6, 13)
    hvec = P.sb("hvec", [128, 2, KC])
    P.ts(hvec[:], vecs[:, 6:8, :], 0.5, ALU.mult)
    blk1 = P.sb("blk1", [128, 128])
    cst32 = P.sb("cst32", [128, 192])
    P.asel(cst32[:, 0:64], ones[:, 0:64], [[0, 64]], ALU.is_ge, 0.0, 63, -1)
    P.asel(cst32[:, 64:128], ones[:, 0:64], [[0, 64]], ALU.is_ge, 0.0, -64, 1)
    P.cp(R(blk1[:]), cst32[:, 0:128])
    CB = 128
    MXT = P.sb("MXT", [CB, 2 * CB])
    P.cp(MXT[:, 0:CB], MsT[0:CB, 0:CB], e="pool")
    P.cp(MXT[:, CB:2 * CB], Utri[0:CB, 0:CB], e="pool")
    MsL = P.sb("MsL", [CB, CB])
    Sh = P.sb("Sh", [128, 64])
    P.asel(cst32[:, 128:192], ones[:, 0:64], [[-1, 64]], ALU.is_equal, 0.0, -64, 1)
    P.cp(R(Sh[:]), cst32[:, 128:192])
    P.asel(MsL[:], ones[0:CB, 0:CB], [[-1, CB]], ALU.is_gt, 0.0, 0, 1)
    rmask = P.sb("rmask", [128, 256])
    P.memset(rmask[:], 1.0)
    for c in range(256 // CB):
        P.memset(rmask[:, c * CB:c * CB + 1], 0.0)

    NBB = 256
    t1T = P.sb("t1T", [64, NTOK], BF16)
    a1T = P.sb("a1T", [64, NTOK], BF16)
    w2b = P.sb("w2b", [64, D], BF16)
    a2b = P.sb("a2b", [64, D], BF16)
    blk_mark = P.sb_off
    lw1 = P.sb("lw1", [128, KC, 2, 64], BF16)
    lw1p = P.sb("lw1p", [128, KC, 2, 64], BF16)
    lw1pp = P.sb("lw1pp", [128, KC, 2, 64], BF16)
    P.dma("pool", lw1[:, :, 0, :], b_w_w1.rearrange("(k p) c -> p k c", p=128), writes=[lw1[:]])
    P.dma("pool", lw1[:, :, 1, :], b_a_w1.rearrange("(k p) c -> p k c", p=128), writes=[lw1[:]])
    for j in range(2):
        P.tt(lw1p[:, :, j, :], lw1[:, :, j, :], bc(vecs[:, 4 + j, :], [128, KC, 64]), ALU.mult)
    P.tt(lw1pp[:], lw1[:], lw1p[:], ALU.subtract)
    P.dma("pool", w2b[:], b_w_w2[:, :], writes=[w2b[:]])
    P.dma("pool", a2b[:], b_a_w2[:, :], writes=[a2b[:]])
    col_ranges = [(PCOL + i * 512, 512) for i in range(4)] + [(S1COL, 16), (S2COL, 16)]
    for (cc0, n) in col_ranges:
        for j, dst in enumerate((t1T, a1T)):
            ps = P.ps(2)
            for kc in range(KC):
                P.mm(ps[0:64, 0:n], lw1pp[:, kc, j, :], hT[:, kc, cc0:cc0 + n], start=(kc == 0), stop=False)
                P.mm(ps[0:64, 0:n], lw1p[:, kc, j, :], hT[:, kc, cc0 - 1:cc0 - 1 + n], start=False, stop=(kc == KC - 1))
            P.act(dst[:, cc0:cc0 + n], ps[0:64, 0:n], AF.Tanh if j == 0 else AF.Copy)

    P.barrier()
    P.sb_off = blk_mark
    Wp = [P.sb("Wp0", [128, KC, 4, 128], BF16)] * 2
    Wq = P.sb("Wq", [128, KC, 4, 128], BF16)
    Wob = [P.sb("Wob0", [128, D], BF16)] * 2
    b_in_v = b_w_in.rearrange("(k p) (g c) -> p k g c", p=128, g=4)

    def load_pair_w(pr):
        for g_ in range(4):
            P.dma("pool", Wp[pr % 2][:, :, g_, :], b_in_v[:, :, g_, pr * 128:(pr + 1) * 128], writes=[Wp[pr % 2][:]])
        P.dma("pool", Wob[pr % 2][:], b_w_out[pr * 128:(pr + 1) * 128, :], writes=[Wob[pr % 2][:]])

    NCB = NBB // CB
    NX = 2 * NCB
    rkvT = P.sb("rkvT", [128, 3, NBB])
    zsB2 = [P.sb("zsB%d" % i, [128, NBB]) for i in range(2)]
    s2tmp = P.sb("s2tmp", [128, 2, NBB])
    lwT = P.sb("lwT", [128, NBB])
    aT = P.sb("aT", [128, NBB])
    cwv = P.sb("cwv", [128, NBB])
    eW = P.sb("eW", [128, 3, NBB])
    tmpB = P.sb("tmpB", [128, 4, NBB])
    kkT = P.sb("kkT", [128, NBB])
    k2T = P.sb("k2T", [128, NBB])
    arT2 = [P.sb("arT%d" % i, [128, 2, NBB]) for i in range(2)]
    bkT = P.sb("bkT", [128, 2, NBB])
    bkh = P.sb("bkh", [128, 2, NBB])
    Wc = P.sb("Wc", [128, NCB])
    rkb = P.sb("rkb", [128, NBB])
    vt = P.sb("vt", [CB, NCB, 128])
    bht = P.sb("bht", [CB, NCB, 128])
    kht = P.sb("kht", [CB, NCB, 128])
    XA = P.sb("XA", [CB, NX, 2 * CB])
    XB = P.sb("XB", [CB, NX, 2 * CB])
    MnB = [P.sb("MnB%d" % i, [CB, NX, CB]) for i in range(2)]
    MTTb = [P.sb("MTTb%d" % i, [CB, NX, 2, CB]) for i in range(2)]
    for m_ in MTTb:
        P.split(m_, 2)
    Rsb = [P.sb("Rsb%d" % i, [CB, 128]) for i in range(2)]
    Usb = [P.sb("Usb%d" % i, [CB, 128]) for i in range(2)]
    StP = [[P.sb("StP%d_%d" % (i, hd), [64, 64]) for hd in range(2)] for i in range(2)]
    StS = [[P.sb("StS%d_%d" % (i, hd), [64, 64]) for hd in range(2)] for i in range(2)]
    StB = [StP, StS, StS]
    stio = P.sb("stio", [64, 128])
    ar12 = [P.sb("ar1_%d" % i, [64, 2, NBB]) for i in range(2)]
    bk1 = P.sb("bk1", [64, 2, NBB])
    Wc1 = P.sb("Wc1", [64, NCB])
    sqB = P.sb("sqB", [128, 4, NBB])
    oTB = sqB[:, 2]
    ocB = s2tmp[:, 0]
    osB = sqB[:, 3]
    ogB = P.sb("ogB", [128, NBB], BF16)
    print("SBUF left after layer-B alloc:", P.sb_top - P.sb_off)
    for t_, n_ in [(rkvT, 3), (eW, 3), (tmpB, 4), (sqB, 4), (bkT, 2), (bkh, 2), (XA, NX), (XB, NX), (bk1, 2),
                   (vt, NCB), (bht, NCB), (kht, NCB), (s2tmp, 2)] + [(x_, 2) for x_ in arT2 + ar12] + \
            [(x_, NX) for x_ in MnB]:
        P.split(t_, n_)
    blocksB = [(0, PCOL + i * NBB, NBB, CB, NBB // CB) for i in range(TP // NBB)] + \
              [(1, S1COL, 16, 16, 1), (2, S2COL, 16, 16, 1)]
    ENH = -float(np.exp(-0.5))

    load_pair_w(0)
    nblkB = 0
    for pr in range(8):
        W = Wp[pr % 2]
        WO = Wob[pr % 2]
        for g_ in range(4):
            P.tt(Wq[:, :, g_, :], W[:, :, g_, :], bc(vecs[:, g_, :], [128, KC, 128]), ALU.mult)
        P.tt(W[:], W[:], Wq[:], ALU.subtract)
        Wr = W
        spar = [0, 0, 0]
        cur_seq = -1
        for (seq, t0, NB, C, nch) in blocksB:
            nx = 2 * nch
            bpar = nblkB % 2
            nblkB += 1
            zsB, arT, ar1 = zsB2[bpar], arT2[bpar], ar12[bpar]
            if seq != cur_seq:
                cur_seq = seq
                spar[seq] = 0
                if seq == 0:
                    for hd in range(2):
                        P.cp(R(StB[0][0][hd][:]), zeros[0:64, 0:64], e="act")
                else:
                    P.dma("sp", stio[:].rearrange("v (h k) -> v h k", h=2),
                          wkv_s[seq - 1, 2 * pr:2 * pr + 2].rearrange("h v k -> v h k"), writes=[stio[:]])
                    for hd in range(2):
                        ps = P.ps("b1")
                        P.tr(ps[0:64, 0:64], stio[:, hd * 64:(hd + 1) * 64], ident[0:64, 0:64])
                        P.cp(R(StB[seq][0][hd][:]), ps[0:64, 0:64])
            for g_ in range(4):
                ps = P.ps("b1")
                for kc in range(KC):
                    P.mm(ps[:, 0:NB], Wr[:, kc, g_, :], hT[:, kc, t0:t0 + NB], start=(kc == 0), stop=False)
                    P.mm(ps[:, 0:NB], Wq[:, kc, g_, :], hT[:, kc, t0 - 1:t0 - 1 + NB], start=False, stop=(kc == KC - 1))
                if g_ < 3:
                    P.cp(rkvT[:, g_, 0:NB], ps[:, 0:NB], e="act")
                else:
                    P.act(zsB[:, 0:NB], ps[:, 0:NB], AF.Tanh, scale=0.5)
                    P.stt(zsB[:, 0:NB], zsB[:, 0:NB], 1.0, ps[:, 0:NB], ALU.add, ALU.mult)
            ps = P.ps("b1")
            P.mm(ps[:, 0:NB], w2b[:, pr * 128:(pr + 1) * 128], t1T[:, t0:t0 + NB])
            P.act(lwT[:, 0:NB], ps[:, 0:NB], AF.Tanh, scale=0.5, bias=hvec[:, 0, pr:pr + 1])
            P.ts(lwT[:, 0:NB], lwT[:, 0:NB], 0.5 * ENH, ALU.mult, 0.5 * ENH, ALU.add)
            ps = P.ps("b1")
            P.mm(ps[:, 0:NB], a2b[:, pr * 128:(pr + 1) * 128], a1T[:, t0:t0 + NB])
            P.act(aT[:, 0:NB], ps[:, 0:NB], AF.Tanh, scale=0.5, bias=hvec[:, 1, pr:pr + 1])
            P.ts(aT[:, 0:NB], aT[:, 0:NB], 0.5, ALU.mult, 0.5, ALU.add)
            rT = rkvT[:, 0, 0:NB]
            kT_ = rkvT[:, 1, 0:NB]
            vT_ = rkvT[:, 2, 0:NB]
            P.ts(kkT[:, 0:NB], kT_, vecs[:, V_KK, pr:pr + 1], ALU.mult)
            P.act(R(sqB[:, 0, 0:NB]), kT_, AF.Square, scale=vecs[:, V_KK, pr:pr + 1])
            ps = P.ps("b1")
            P.mmr(ps[:, 0:NB], blk1[:, :], sqB[:, 0, 0:NB])
            P.act(tmpB[:, 1, 0:NB], ps[:, 0:NB], AF.Ln, bias=1e-6)
            P.act(tmpB[:, 1, 0:NB], tmpB[:, 1, 0:NB], AF.Exp, scale=-0.5)
            P.tt(kkT[:, 0:NB], kkT[:, 0:NB], tmpB[:, 1, 0:NB], ALU.mult)
            P.ts(tmpB[:, 2, 0:NB], aT[:, 0:NB], -1.0, ALU.add, vecs[:, V_KA, pr:pr + 1], ALU.mult)
            P.ts(tmpB[:, 2, 0:NB], tmpB[:, 2, 0:NB], 1.0, ALU.add)
            P.tt(k2T[:, 0:NB], kT_, tmpB[:, 2, 0:NB], ALU.mult)
            P.scan(cwv[:, 0:NB], rmask[:, 0:NB], lwT[:, 0:NB])
            P.act(eW[:, 0, 0:NB], cwv[:, 0:NB], AF.Exp)
            P.act(eW[:, 1, 0:NB], cwv[:, 0:NB], AF.Exp, scale=-1.0)
            P.tt(tmpB[:, 3, 0:NB], cwv[:, 0:NB], lwT[:, 0:NB], ALU.subtract)
            P.act(eW[:, 2, 0:NB], tmpB[:, 3, 0:NB], AF.Exp)
            P.stt(R(arT[:, 0, 0:NB]), kkT[:, 0:NB], -1.0, eW[:, 2, 0:NB], ALU.mult, ALU.mult)
            P.tt(R(arT[:, 1, 0:NB]), rT, eW[:, 0, 0:NB], ALU.mult)
            P.tt(tmpB[:, 0, 0:NB], kkT[:, 0:NB], aT[:, 0:NB], ALU.mult)
            P.tt(R(bkT[:, 0, 0:NB]), tmpB[:, 0, 0:NB], eW[:, 1, 0:NB], ALU.mult)
            P.tt(R(bkT[:, 1, 0:NB]), k2T[:, 0:NB], eW[:, 1, 0:NB], ALU.mult)
            ewc = eW[:, 0, 0:NB].rearrange("p (c i) -> p c i", c=nch)[:, :, C - 1]
            P.cp(Wc[:, 0:nch], ewc)
            bkv = bkT[:, :, 0:NB].rearrange("p a (c i) -> p a c i", c=nch)
            bhv = bkh[:, :, 0:NB].rearrange("p a (c i) -> p a c i", c=nch)
            for a_ in range(2):
                P.tt(bhv[:, a_], bkv[:, a_], bc(Wc[:, 0:nch], [128, nch, C]), ALU.mult)
            P.stt(R(sqB[:, 1, 0:NB]), rT, vecs[:, V_RK, pr:pr + 1], k2T[:, 0:NB], ALU.mult, ALU.mult)
            ps = P.ps("b1")
            P.mmr(ps[:, 0:NB], blk1[:, :], sqB[:, 1, 0:NB])
            P.tt(rkb[:, 0:NB], ps[:, 0:NB], vT_, ALU.mult)
            ps = P.ps("b1")
            P.mmr(ps[0:64, 0:2 * NB], Sh[:, :], arT[:, :, 0:NB])
            P.cp(R(ar1[:, :, 0:NB]), ps[0:64, 0:2 * NB].rearrange("p (a t) -> p a t", a=2), e="act")
            ps = P.ps("b1")
            P.mmr(ps[0:64, 0:2 * NB], Sh[:, :], bkT[:, :, 0:NB])
            P.cp(R(bk1[:, :, 0:NB]), ps[0:64, 0:2 * NB].rearrange("p (a t) -> p a t", a=2), e="act")
            ps = P.ps("b1")
            P.mm(ps[0:64, 0:nch], cst32[:, 128:192], Wc[:, 0:nch])
            P.cp(Wc1[:, 0:nch], ps[0:64, 0:nch])
            AR = [arT[0:64], ar1[:]]
            BK = [bkT[0:64], bk1[:]]
            WC = [Wc[0:64], Wc1[:]]
            for (src, dst) in ((vT_, vt), (bkh[:, 0, 0:NB], bht), (bkh[:, 1, 0:NB], kht)):
                ps = P.ps("b1")
                for c in range(nch):
                    P.tr(ps[0:C, c * 128:(c + 1) * 128], src[:, c * C:(c + 1) * C], ident[:, :])
                P.cp(R(dst[0:C, 0:nch, :]), ps[0:C, 0:nch * 128].rearrange("p (c d) -> p c d", c=nch), e="act")
            psN = P.ps("b1")
            for hd in range(2):
                hs = slice(hd * 64, (hd + 1) * 64)
                psA = P.ps("b1")
                psB_ = P.ps("b1")
                for c in range(nch):
                    csl = slice(c * C, (c + 1) * C)
                    x_ = hd * nch + c
                    P.mmr(psA[0:C, c * 2 * C:(c + 1) * 2 * C], BK[hd][:, 0, csl], AR[hd][:, :, csl])
                    P.mmr(psB_[0:C, c * 2 * C:(c + 1) * 2 * C], BK[hd][:, 1, csl], AR[hd][:, :, csl])
                    P.mmr(psN[0:C, x_ * C:(x_ + 1) * C], AR[hd][:, 0, csl], BK[hd][:, 0, csl])
                for (psx, dstx) in ((psA, XA), (psB_, XB)):
                    pv = psx[0:C, 0:nch * 2 * C].rearrange("p (c a i) -> p c a i", c=nch, a=2)
                    dv = dstx[0:C, hd * nch:(hd + 1) * nch, :].rearrange("p c (a i) -> p c a i", a=2)[:, :, :, 0:C]
                    mv = MXT[0:C, :].rearrange("p (a i) -> p a i", a=2)[:, :, 0:C].unsqueeze(1).broadcast_to([C, nch, 2, C])
                    P.tt(R(dv), pv, mv, ALU.mult)
            P.tt(R(MnB[0][0:C, 0:nx, 0:C]), psN[0:C, 0:nx * C].rearrange("p (x i) -> p x i", x=nx),
                 bcm(MsL[0:C, 0:C], [C, nx, C]), ALU.mult)
            gen_ = neumann(P, ident, XA[0:C, 0:nx, 0:C], None, MnB, MTTb, nx, C, pool="b1")
            while True:
                try:
                    next(gen_)
                except StopIteration as e_:
                    TTf = e_.value
                    break
            pso = P.psb[7]
            for c in range(nch):
                csl = slice(c * C, (c + 1) * C)
                Sc = StB[seq][spar[seq]]
                Sn = StB[seq][1 - spar[seq]]
                Rb = Rsb[c % 2]
                Ub = Usb[c % 2]
                ps = P.ps("b2")
                for hd in range(2):
                    hs = slice(hd * 64, (hd + 1) * 64)
                    x_ = hd * nch + c
                    P.mmr(ps[0:C, hs], AR[hd][:, 0, csl], Sc[hd][:, :], start=True, stop=False)
                    P.mmr(ps[0:C, hs], XB[0:C, x_, 0:C], vt[0:C, c, hs], start=False, stop=True)
                P.cp(R(Rb[0:C, :]), ps[0:C, 0:128], e="act")
                ps = P.ps("b2")
                for hd in range(2):
                    hs = slice(hd * 64, (hd + 1) * 64)
                    x_ = hd * nch + c
                    P.mmr(ps[0:C, hs], TTf[:, x_, :], Rb[0:C, hs])
                P.cp(R(Ub[0:C, :]), ps[0:C, 0:128], e="act")
                for hd in range(2):
                    hs = slice(hd * 64, (hd + 1) * 64)
                    x_ = hd * nch + c
                    oo = pso[hs, csl]
                    mmf = P.mmr if hd == 0 else P.mm
                    mmf(oo, Sc[hd][:, :], AR[hd][:, 1, csl], start=True, stop=False)
                    mmf(oo, Ub[0:C, hs], XA[0:C, x_, CB:CB + C], start=False, stop=False)
                    mmf(oo, vt[0:C, c, hs], XB[0:C, x_, CB:CB + C], start=False, stop=True)
                ps2 = P.ps("b2")
                for hd in range(2):
                    hs = slice(hd * 64, (hd + 1) * 64)
                    P.mmr(ps2[0:64, hs], bht[0:C, c, hs], Ub[0:C, hs], start=True, stop=False)
                    P.mmr(ps2[0:64, hs], kht[0:C, c, hs], vt[0:C, c, hs], start=False, stop=True)
                for hd in range(2):
                    hs = slice(hd * 64, (hd + 1) * 64)
                    P.stt(R(Sn[hd][:, :]), Sc[hd][:, :], WC[hd][:, c:c + 1], ps2[0:64, hs], ALU.mult, ALU.add)
                spar[seq] = 1 - spar[seq]
            P.cp(R(oTB[:, 0:NB]), pso[:, 0:NB], e="act")
            ps = P.ps("b2")
            P.mmr(ps[:, 0:NB], blk1[:, :], oTB[:, 0:NB])
            P.stt(ocB[:, 0:NB], ps[:, 0:NB], -1.0 / 64.0, oTB[:, 0:NB], ALU.mult, ALU.add)
            P.act(R(osB[:, 0:NB]), ocB[:, 0:NB], AF.Square)
            ps = P.ps("b2")
            P.mmr(ps[:, 0:NB], blk1[:, :], osB[:, 0:NB])
            P.act(s2tmp[:, 1, 0:NB], ps[:, 0:NB], AF.Ln, scale=1.0 / 64.0, bias=GN_EPS)
            P.act(s2tmp[:, 1, 0:NB], s2tmp[:, 1, 0:NB], AF.Exp, scale=-0.5)
            P.tt(ocB[:, 0:NB], ocB[:, 0:NB], s2tmp[:, 1, 0:NB], ALU.mult)
            P.ts(ocB[:, 0:NB], ocB[:, 0:NB], vecs[:, V_GW, pr:pr + 1], ALU.mult, vecs[:, V_GB, pr:pr + 1], ALU.add)
            P.tt(ocB[:, 0:NB], ocB[:, 0:NB], rkb[:, 0:NB], ALU.add)
            P.stt(ogB[:, 0:NB], ocB[:, 0:NB], 0.5, zsB[:, 0:NB], ALU.mult, ALU.mult)
            for tt0 in range(0, NB, 128):
                nt = min(128, NB - tt0)
                if seq == 0:
                    tile_i, prow = (t0 - PCOL + tt0) // 128, 0
                else:
                    tile_i, prow = 16, (seq - 1) * 32
                for nh in range(2):
                    ps = P.ps("b2")
                    P.mm(ps[prow:prow + nt, :], ogB[:, tt0:tt0 + nt], WO[:, nh * 512:(nh + 1) * 512])
                    xr = xres[tile_i][prow:prow + nt, nh * 512:(nh + 1) * 512]
                    P.tt(xr, xr, ps[prow:prow + nt, :], ALU.add)
            last = (t0 + NB == PCOL + TP) or seq > 0
            if last:
                Sf = StB[seq][spar[seq]]
                ps = P.ps("b2")
                for hd in range(2):
                    P.tr(ps[0:64, hd * 64:(hd + 1) * 64], Sf[hd][:, :], ident[0:64, 0:64])
                P.cp(stio[:, :], ps[0:64, 0:128])
                P.dma("sp", o_wkv[seq, 2 * pr:2 * pr + 2].rearrange("h v k -> v h k"),
                      stio[:].rearrange("v (h k) -> v h k", h=2), reads=[stio[:]], final=True)
        if pr + 1 < 8:
            load_pair_w(pr + 1)

    P.barrier()
    P.sb_off = blk_mark
    xn = [P.sb("xnF%d" % i, [128, D]) for i in range(3)]
    fnwb = P.sb("fnwb", [128, D])
    P.dma("sp", fnwb[:], final_norm_w.partition_broadcast(128), writes=[fnwb[:]])
    for i in range(NTILE):
        nt = tile_rows(i)
        xt = xres[i]
        xb = xn[i % 3]
        sl = slice(i % 4, i % 4 + 1)
        P.act(xb[0:nt, :], xt[0:nt, :], AF.Square, accum_out=ssq[0:nt, sl])
        P.act(rstd[0:nt, sl], ssq[0:nt, sl], AF.Sqrt, scale=1.0 / D, bias=RMS_EPS)
        P.recip(rstd[0:nt, sl], rstd[0:nt, sl])
        P.stt(xb[0:nt, :], xt[0:nt, :], rstd[0:nt, sl], fnwb[0:nt, :], ALU.mult, ALU.mult)
        if i < 16:
            P.dma("sp", y_p[i * 128:(i + 1) * 128, :], xb[:, :], reads=[xb[:]], final=True)
        else:
            P.dma("sp", y_s[0:16, :], xb[0:16, :], reads=[xb[:]], final=True)
            P.dma("sp", y_s[16:32, :], xb[32:48, :], reads=[xb[:]], final=True)
    P.finish()
    return nc


_NC_CACHE = {}


def make_in_maps(inputs):
    g = lambda k: np.ascontiguousarray(np.asarray(inputs[k], dtype=np.float32))
    xp, xs = g("x_prompt"), g("x_sample")
    cc, sd, ss, sw = g("cache_conv_a"), g("state_delta_a"), g("state_shift_b"), g("state_wkv_b")
    shared = {
        "norm_w": g("norm_w"), "final_norm_w": g("final_norm_w"), "a_w_in": g("a_w_in")[0],
        "a_conv_w": g("a_conv_w")[0], "a_log": g("a_log")[0], "a_dt_bias": g("a_dt_bias")[0],
        "a_norm_w": g("a_norm_w")[0], "a_w_out": g("a_w_out")[0], "b_mu": g("b_mu")[0],
        "b_w_in": g("b_w_in")[0], "b_w0": g("b_w0")[0], "b_w_w1": g("b_w_w1")[0], "b_w_w2": g("b_w_w2")[0],
        "b_a0": g("b_a0")[0], "b_a_w1": g("b_a_w1")[0], "b_a_w2": g("b_a_w2")[0], "b_k_k": g("b_k_k")[0],
        "b_k_a": g("b_k_a")[0], "b_r_k": g("b_r_k")[0].reshape(-1), "b_gn_w": g("b_gn_w")[0],
        "b_gn_b": g("b_gn_b")[0], "b_w_out": g("b_w_out")[0],
    }
    maps = []
    for i in range(8):
        m = dict(shared)
        m["x_p"] = xp[i]
        m["x_s"] = np.ascontiguousarray(xs[2 * i:2 * i + 2].reshape(2 * TS, D))
        m["conv_s"] = np.ascontiguousarray(cc[0, 2 * i:2 * i + 2])
        m["delta_s"] = np.ascontiguousarray(sd[0, 2 * i:2 * i + 2])
        m["shift_s"] = np.ascontiguousarray(ss[0, 2 * i:2 * i + 2])
        m["wkv_s"] = np.ascontiguousarray(sw[0, 2 * i:2 * i + 2])
        maps.append(m)
    return maps


def kernel(**inputs):
    if "nc" not in _NC_CACHE:
        _NC_CACHE["nc"] = build()
    nc = _NC_CACHE["nc"]
    maps = make_in_maps(inputs)
    res = run_bass_kernel_spmd(nc, maps, core_ids=list(range(8)))
    R = res.results
    y_prompt = np.stack([R[i]["y_p"] for i in range(8)], 0)
    y_sample = np.concatenate([R[i]["y_s"].reshape(2, TS, D) for i in range(8)], 0)

    def pick(name, sl):
        return np.stack([R[i][name][sl] for i in range(8)], 0)[None] if isinstance(sl, int) else \
            np.concatenate([R[i][name][sl] for i in range(8)], 0)[None]

    p_conv, s_conv = pick("o_conv", 0), pick("o_conv", slice(1, 3))
    p_delta, s_delta = pick("o_delta", 0), pick("o_delta", slice(1, 3))
    p_shift, s_shift = pick("o_shift", 0), pick("o_shift", slice(1, 3))
    p_wkv, s_wkv = pick("o_wkv", 0), pick("o_wkv", slice(1, 3))
    return (y_prompt, y_sample, p_conv, p_delta, p_shift, p_wkv, s_conv, s_delta, s_shift, s_wkv)
```

```python
import numpy as np
import concourse.bass as bass
import concourse.mybir as mybir
from concourse.bass_utils import run_bass_kernel_spmd

F32 = mybir.dt.float32
BF16 = mybir.dt.bfloat16
AF = mybir.ActivationFunctionType
ALU = mybir.AluOpType
AX = mybir.AxisListType

D = 1024
KC = 8
TP = 2048
TS = 16
NTILE = 17
H_A = 8
RMS_EPS = 1e-6
GN_EPS = 64e-5
NEG = -30000.0


class Trk:
    __slots__ = ("lw", "rd", "dsem", "dcount")

    def __init__(self):
        self.lw = None
        self.rd = []
        self.dsem = None
        self.dcount = 0


def fsz(ap):
    n = 1
    for d in ap.shape[1:]:
        n *= d
    return n


class Prog:
    WINDOW = 4000
    LAT = 700.0
    LAT_SAME = 200.0
    PE_SWITCH = 0.0
    EPS = 150.0

    def __init__(self, nc):
        self.nc = nc
        self.h = {"pe": nc.tensor, "act": nc.scalar, "dve": nc.vector, "pool": nc.gpsimd, "sp": nc.sync}
        self.sem = {k: nc.alloc_semaphore("sem_" + k) for k in self.h}
        self.cnt = {k: 0 for k in self.h}
        self.known = {k: {} for k in self.h}
        self.trk = {}
        self.ops = []
        self.nps = 0
        self.npool = {}
        self.psb = []
        self.nsem = 0
        self.sb_off = nc.sbuf_base
        self.sb_top = nc.sbuf_top
        self.seg_start = 0
        self.sel = {}
        self.pecls = {}

    def sb(self, name, shape, dt=F32):
        n = 1
        for d in shape[1:]:
            n *= d
        nbytes = n * (2 if dt == BF16 else 4)
        off = (self.sb_off + 63) // 64 * 64
        assert off + nbytes <= self.sb_top, "SBUF overflow at %s: need %d have %d" % (name, nbytes, self.sb_top - off)
        t = self.nc.alloc_sbuf_tensor_at(name, list(shape), dt, offset=off)
        self.sb_off = off + nbytes
        self.trk[t.name] = Trk()
        return t

    def barrier(self):
        self.ops.append(("fence", None, None, (), 0.0, False, None))
        for t in self._all_trk():
            t.lw = None
            t.rd = []

    def _all_trk(self):
        for t in self.trk.values():
            if isinstance(t, list):
                for x in t:
                    yield x
            else:
                yield t

    def init_psum(self):
        for i in range(8):
            t = self.nc.alloc_psum_tensor("psb%d" % i, [128, 512], F32)
            self.trk["psb%d" % i] = Trk()
            self.psb.append(t)

    POOLS = {0: (0, 1, 2, 3, 4, 5, 6), 2: (0, 1, 2, 3, 4, 5, 6),
             "a1a": (0, 1), "a1m": (2,), "a1b": (3, 4), "a2": (5, 6),
             "b1": (0, 1, 2, 3, 4), "b2": (5, 6)}

    def ps(self, pool=0):
        banks = self.POOLS[pool]
        n = self.npool.get(pool, 0)
        self.npool[pool] = n + 1
        return self.psb[banks[n % len(banks)]]

    def split(self, tensor, n):
        self.trk[tensor.name] = [Trk() for _ in range(n)]

    def only(self, **sel):
        prog = self

        class _Ctx:
            def __enter__(self_):
                self_.old = dict(prog.sel)
                prog.sel.update(sel)

            def __exit__(self_, *a):
                prog.sel = self_.old
        return _Ctx()

    def _tks(self, ap):
        t = self.trk[ap.tensor.name]
        if isinstance(t, list):
            idx = self.sel.get(ap.tensor.name.rsplit("_", 1)[0])
            if idx is not None:
                return [t[i] for i in idx]
            try:
                pat = ap.ap
                F = 1
                for d in ap.tensor.shape[1:]:
                    F *= d
                pstride = pat[0][0]
                off = int(ap.offset)
                if pstride != F:
                    return list(t)
                f0 = off % F
                ext = 1
                for st, cnt in pat[1:]:
                    ext += (cnt - 1) * abs(st)
                gsz = F // len(t)
                g0 = f0 // gsz
                g1 = (f0 + ext - 1) // gsz
                if g0 < 0 or g1 >= len(t):
                    return list(t)
                return [t[i] for i in range(g0, g1 + 1)]
            except Exception:
                return list(t)
        return [t]

    def _record(self, kind, e, payload, reads, writes, cost, final=False):
        rt = []
        for a in reads:
            for t in self._tks(a):
                if t not in rt:
                    rt.append(t)
        wt = []
        for a in writes:
            for t in self._tks(a):
                if t not in wt:
                    wt.append(t)
        i = len(self.ops)
        preds = set()
        for t in rt:
            if t.lw is not None:
                preds.add(t.lw)
        for t in wt:
            if t.lw is not None:
                preds.add(t.lw)
            preds.update(t.rd)
        preds.discard(i)
        for t in rt:
            t.rd.append(i)
        for t in wt:
            t.lw = i
            t.rd = []
        t0 = (wt + rt)[0] if kind == "dma" else None
        self.ops.append((kind, e, payload, tuple(preds), float(cost), final, t0))
        return i

    def op(self, e, fn, reads, writes, cost=None):
        if cost is None:
            n = fsz(writes[0]) if writes else 64
            cost = {"act": 200.0 + 0.85 * n, "dve": 110.0 + 1.05 * n, "pool": 260.0 + 1.0 * n, "pe": 150.0}[e]
        return self._record("op", e, fn, reads, writes, cost)

    def dma(self, q, out, in_, reads=(), writes=(), final=False, **kw):
        return self._record("dma", q, (out, in_, kw), reads, writes, 150.0 if q == "sp" else 1200.0, final)

    def _wait(self, e, key, semh, val):
        k = self.known[e]
        if k.get(key, 0) < val:
            self.h[e].wait_ge(semh, val)
            k[key] = val

    def _schedule(self, lo, hi):
        ops = self.ops
        n = hi - lo
        indeg = [0] * n
        succ = [[] for _ in range(n)]
        for i in range(lo, hi):
            ps_ = [p for p in ops[i][3] if p >= lo]
            indeg[i - lo] = len(ps_)
            for p in ps_:
                succ[p - lo].append(i)
        blev = [0.0] * n
        for k in range(n - 1, -1, -1):
            o = ops[lo + k]
            c = o[4] + (2500.0 if o[0] == "dma" else 0.0)
            m = 0.0
            for j in succ[k]:
                v = blev[j - lo] + self.LAT
                if v > m:
                    m = v
            blev[k] = c + m
        dready = [0.0] * n
        etime = {k: 0.0 for k in self.h}
        ready = {k: [] for k in self.h}
        for i in range(lo, hi):
            if indeg[i - lo] == 0:
                ready[ops[i][1]].append(i)
        order = []
        done = [False] * n
        lastcls = None
        minp = lo
        W = self.WINDOW
        EPS = self.EPS
        while len(order) < n:
            while minp < hi and done[minp - lo]:
                minp += 1
            lim = minp + W
            best = None
            for e, lst in ready.items():
                if not lst:
                    continue
                te = etime[e]
                cand = None
                for i in lst:
                    if i >= lim:
                        continue
                    dr = dready[i - lo]
                    st = te if dr <= te + EPS else dr
                    if e == "pe" and self.pecls.get(i) != lastcls:
                        st += self.PE_SWITCH
                    key = (st, -blev[i - lo], i)
                    if cand is None or key < cand:
                        cand = key
                if cand is not None and (best is None or cand < best[0]):
                    best = (cand, e)
            (st, _, i), e = best
            st = max(st, dready[i - lo], etime[e])
            ready[e].remove(i)
            kind = ops[i][0]
            cost = ops[i][4]
            if e == "pe":
                cl = self.pecls.get(i)
                if cl != lastcls:
                    cost += self.PE_SWITCH
                lastcls = cl
            etime[e] = st + cost
            f = st + cost + (2500.0 if kind == "dma" else 0.0)
            done[i - lo] = True
            order.append(i)
            for j in succ[i - lo]:
                ej = ops[j][1]
                v = f + (0.0 if (e == "pe" and ej == "pe") else (self.LAT_SAME if ej == e else self.LAT))
                if v > dready[j - lo]:
                    dready[j - lo] = v
                indeg[j - lo] -= 1
                if indeg[j - lo] == 0:
                    ready[ops[j][1]].append(j)
        return order, max(etime.values())

    def finish(self):
        ops = self.ops
        bounds = [i for i, o in enumerate(ops) if o[0] == "fence"] + [len(ops)]
        needs_inc = [False] * len(ops)
        info = {}
        clock = {}
        finals = []
        lo = 0
        est_total = 0.0
        nwait = 0
        last_inc = {k: None for k in self.h}
        for b in bounds:
            order, est = self._schedule(lo, b)
            est_total += est
            pos = {i: k for k, i in enumerate(order)}
            kept = {}
            lastop = {}
            for i in order:
                kind, e = ops[i][0], ops[i][1]
                if kind == "op":
                    lastop[e] = i
                best = {}
                keep = []
                for p in ops[i][3]:
                    if p < lo:
                        continue
                    if ops[p][0] != "op":
                        keep.append(p)
                        continue
                    f = ops[p][1]
                    if f == "pe" and e == "pe":
                        continue
                    if f not in best or pos[p] > pos[best[f]]:
                        best[f] = p
                for p in best.values():
                    needs_inc[p] = True
                    keep.append(p)
                kept[i] = keep
            for i in lastop.values():
                needs_inc[i] = True
            for i in order:
                kind, e, payload, preds, cost, final, t0 = ops[i]
                kn = self.known[e]
                for p in sorted(kept[i], key=lambda x: pos[x]):
                    if p < lo:
                        continue
                    pi = info[p]
                    if pi[0] == "op":
                        f, c = pi[1], pi[2]
                        if f == "pe" and e == "pe":
                            continue
                        if kn.get(f, 0) < c:
                            self.h[e].wait_ge(self.sem[f], c)
                            nwait += 1
                            kn[f] = c
                            for g, v in clock[p].items():
                                if kn.get(g, 0) < v:
                                    kn[g] = v
                    else:
                        if kn.get(pi[3], 0) < pi[2]:
                            self.h[e].wait_ge(pi[1], pi[2])
                            nwait += 1
                            kn[pi[3]] = pi[2]
                if kind == "op":
                    ins = payload(self.h[e])
                    if needs_inc[i]:
                        self.cnt[e] += 1
                        ins.then_inc(self.sem[e], 1)
                        info[i] = ("op", e, self.cnt[e])
                        snap = {g: v for g, v in kn.items() if g in self.h}
                        snap[e] = self.cnt[e]
                        clock[i] = snap
                    else:
                        info[i] = ("op", e, self.cnt[e] + 1)
                        clock[i] = {}
                else:
                    out, in_, kw = payload
                    if t0.dsem is None:
                        t0.dsem = self.nc.alloc_semaphore("dsem%d" % self.nsem)
                        self.nsem += 1
                    ins = self.h[e].dma_start(out=out, in_=in_, **kw)
                    t0.dcount += 16
                    ins.then_inc(t0.dsem, 16)
                    info[i] = ("dma", t0.dsem, t0.dcount, "d%d" % id(t0))
                    if final:
                        finals.append(info[i])
            lo = b + 1
            if b < len(ops):
                for e in self.h:
                    for f in self.h:
                        if f != e and self.cnt[f] > 0:
                            self._wait(e, f, self.sem[f], self.cnt[f])
                    for t in self._all_trk():
                        if t.dsem is not None and t.dcount > 0:
                            self._wait(e, "d%d" % id(t), t.dsem, t.dcount)
        fmax = {}
        for (_, semh, c, key) in finals:
            if key not in fmax or c > fmax[key][1]:
                fmax[key] = (semh, c)
        for key, (semh, c) in fmax.items():
            self._wait("sp", key, semh, c)
        print("scheduler estimate: %.1f us, %d ops, %d waits, incs %s" % (est_total / 1e3, len(ops), nwait, dict(self.cnt)))

    def mmr(self, out, lhsT, rhs, start=True, stop=True):
        return self.mm(out, R(lhsT), R(rhs), start=start, stop=stop)

    def mm(self, out, lhsT, rhs, start=True, stop=True):
        passes = 4.0 if rhs.dtype == F32 else 1.0
        cost = 70.0 + passes * 0.42 * (fsz(rhs) + min(fsz(lhsT), 128))
        i = self.op("pe", lambda h: h.matmul(out, lhsT=lhsT, rhs=rhs, start=start, stop=stop),
                    [lhsT, rhs], [out], cost=cost)
        self.pecls[i] = str(rhs.dtype)
        return i

    def tr(self, out, in_, ident):
        i = self.op("pe", lambda h: h.transpose(out, in_, ident), [in_, ident], [out], cost=160.0)
        self.pecls[i] = "tr"
        return i

    def act(self, out, in_, func, e="act", **kw):
        rd = [in_] + [v for v in kw.values() if hasattr(v, "tensor")]
        wr = [out]
        if "accum_out" in kw:
            wr.append(kw["accum_out"])
            rd.remove(kw["accum_out"])
        return self.op("act", lambda h: h.activation(out=out, in_=in_, func=func, **kw), rd, wr)

    def tt(self, out, in0, in1, op, e="dve"):
        return self.op(e, lambda h: h.tensor_tensor(out=out, in0=in0, in1=in1, op=op), [in0, in1], [out])

    def ts(self, out, in0, s1, op0, s2=None, op1=None, e="dve"):
        rd = [in0] + [v for v in (s1, s2) if hasattr(v, "tensor")]
        if op1 is None:
            return self.op(e, lambda h: h.tensor_scalar(out=out, in0=in0, scalar1=s1, scalar2=None, op0=op0),
                           rd, [out])
        return self.op(e, lambda h: h.tensor_scalar(out=out, in0=in0, scalar1=s1, scalar2=s2, op0=op0, op1=op1),
                       rd, [out])

    def stt(self, out, in0, scalar, in1, op0, op1):
        rd = [in0, in1] + ([scalar] if hasattr(scalar, "tensor") else [])
        return self.op("dve", lambda h: h.scalar_tensor_tensor(out=out, in0=in0, scalar=scalar, in1=in1,
                                                                 op0=op0, op1=op1), rd, [out])

    def cp(self, out, in_, e="dve"):
        if e == "act":
            return self.act(out, in_, AF.Copy)
        return self.op(e, lambda h: h.tensor_copy(out=out, in_=in_), [in_], [out])

    def scan(self, out, d0, d1):
        return self.op("dve", lambda h: h.tensor_tensor_scan(out=out, data0=d0, data1=d1, initial=0.0,
                                                              op0=ALU.mult, op1=ALU.add),
                       [d0, d1], [out], cost=110.0 + 2.1 * fsz(out))

    def rsqrt_pool(self, out, in_, mhalf):
        return self.op("pool", lambda h: h.tensor_tensor(out=out, in0=in_, in1=mhalf, op=ALU.pow), [in_, mhalf], [out])

    def recip(self, out, in_):
        return self.op("dve", lambda h: h.reciprocal(out=out, in_=in_), [in_], [out], cost=110.0 + 3.0 * fsz(out))

    def memset(self, ap, val, e="pool"):
        return self.op(e, lambda h: h.memset(ap, val), [], [ap])

    def asel(self, out, in_, pattern, cmp, fill, base, cm):
        return self.op("pool", lambda h: h.affine_select(out=out, in_=in_, pattern=pattern, compare_op=cmp,
                                                          fill=fill, base=base, channel_multiplier=cm),
                       [in_], [out])


F32R = mybir.dt.float32r


def R(ap):
    return ap.bitcast(F32R)


def neumann(P, ident, MT0, M0, Mbuf, MTT, nx, C, pool=0):
    L = {128: 7, 64: 6, 16: 4}[C]
    G = 512 // (2 * C)
    ngrp = (nx + G - 1) // G
    names = [m.name.rsplit("_", 1)[0] for m in MTT]
    split = ngrp > 1 and all(isinstance(P.trk[m.name], list) for m in MTT)

    def grp_only(g):
        if not split:
            return P.only()
        return P.only(**{nm: [g] for nm in names})

    def grp3(ps, n, w):
        return ps[0:C, 0:n * w].rearrange("p (x i) -> p x i", x=n)

    psa = P.ps(pool)
    psb = P.ps(pool)
    for x in range(nx):
        P.mm(psa[0:C, x * C:(x + 1) * C], R(MT0[:, x, :]), R(Mbuf[0][0:C, x, 0:C]))
        P.mm(psb[0:C, x * C:(x + 1) * C], R(Mbuf[0][0:C, x, 0:C]), R(MT0[:, x, :]))
    P.cp(R(Mbuf[1][0:C, 0:nx, 0:C]), grp3(psa, nx, C), e="act")
    P.cp(R(MTT[0][0:C, 0:nx, 0, 0:C]), grp3(psb, nx, C), e="act")
    P.tt(R(MTT[0][0:C, 0:nx, 1, 0:C]), MT0, bcm(ident[0:C, 0:C], [C, nx, C]), ALU.add)
    yield
    cm, ct = 1, 0
    for lev in range(2, L + 1):
        last = lev == L
        Mc = Mbuf[cm]
        cur = MTT[ct]
        nxt = MTT[1 - ct]
        if not last:
            psa = P.ps(pool)
            for x in range(nx):
                P.mm(psa[0:C, x * C:(x + 1) * C], R(cur[0:C, x, 0, 0:C]), R(Mc[0:C, x, 0:C]))
        for x0 in range(0, nx, G):
            n = min(G, nx - x0)
            psx = P.ps(pool)
            with grp_only(x0 // G):
                for j in range(n):
                    x = x0 + j
                    if last:
                        P.mm(psx[0:C, j * C:(j + 1) * C], R(Mc[0:C, x, 0:C]), R(cur[0:C, x, 1, 0:C]))
                    else:
                        P.mm(psx[0:C, j * 2 * C:(j + 1) * 2 * C], R(Mc[0:C, x, 0:C]), R(cur[0:C, x, :, 0:C]))
                if last:
                    P.tt(R(nxt[0:C, x0:x0 + n, 1, 0:C]), grp3(psx, n, C), cur[0:C, x0:x0 + n, 1, 0:C], ALU.add)
                else:
                    pv = psx[0:C, 0:n * 2 * C].rearrange("p (x a i) -> p x a i", x=n, a=2)
                    P.cp(R(nxt[0:C, x0:x0 + n, 0, 0:C]), pv[:, :, 0, :], e="act")
                    P.tt(R(nxt[0:C, x0:x0 + n, 1, 0:C]), pv[:, :, 1, :], cur[0:C, x0:x0 + n, 1, 0:C], ALU.add)
        if not last:
            P.cp(R(Mbuf[1 - cm][0:C, 0:nx, 0:C]), grp3(psa, nx, C), e="act")
        cm = 1 - cm
        ct = 1 - ct
        yield
    return MTT[ct][0:C, 0:nx, 1, 0:C]


def bc(ap, shape):
    return ap.unsqueeze(len(ap.shape)).broadcast_to(list(shape))


def bcm(ap, shape):
    return ap.unsqueeze(1).broadcast_to(list(shape))


PCOL = 1
S1COL = TP + 1 + 1
S2COL = S1COL + 32
NTOK = S2COL + 16 + 1
NBA = 256
CA = 128
NCH = TP // CA + 2


def build(stop=None):
    nc = bass.Bass("TRN2", target_bir_lowering=False)
    P = Prog(nc)
    P.init_psum()

    def din(name, shape):
        return nc.dram_tensor(name, list(shape), F32, kind="ExternalInput").ap()

    def dout(name, shape):
        return nc.dram_tensor(name, list(shape), F32, kind="ExternalOutput").ap()

    x_p = din("x_p", [TP, D])
    x_s = din("x_s", [2 * TS, D])
    conv_s = din("conv_s", [2, 3, 4096])
    delta_s = din("delta_s", [2, 8, 128, 256])
    shift_s = din("shift_s", [2, D])
    wkv_s = din("wkv_s", [2, 16, 64, 64])
    norm_w = din("norm_w", [2, D])
    final_norm_w = din("final_norm_w", [D])
    a_w_in = din("a_w_in", [D, 6160])
    a_conv_w = din("a_conv_w", [4, 4096])
    a_log = din("a_log", [8])
    a_dt_bias = din("a_dt_bias", [8])
    a_norm_w = din("a_norm_w", [256])
    a_w_out = din("a_w_out", [2048, D])
    b_mu = din("b_mu", [6, D])
    b_w_in = din("b_w_in", [D, 4096])
    b_w0 = din("b_w0", [D])
    b_w_w1 = din("b_w_w1", [D, 64])
    b_w_w2 = din("b_w_w2", [64, D])
    b_a0 = din("b_a0", [D])
    b_a_w1 = din("b_a_w1", [D, 64])
    b_a_w2 = din("b_a_w2", [64, D])
    b_k_k = din("b_k_k", [D])
    b_k_a = din("b_k_a", [D])
    b_r_k = din("b_r_k", [D])
    b_gn_w = din("b_gn_w", [D])
    b_gn_b = din("b_gn_b", [D])
    b_w_out = din("b_w_out", [D, D])

    y_p = dout("y_p", [TP, D])
    y_s = dout("y_s", [2 * TS, D])
    o_conv = dout("o_conv", [3, 3, 4096])
    o_delta = dout("o_delta", [3, 8, 128, 256])
    o_shift = dout("o_shift", [3, D])
    o_wkv = dout("o_wkv", [3, 16, 64, 64])
    dbg = dout("dbg", [NTILE * 128, D]) if stop else None

    ident = P.sb("ident", [128, 128])
    ones = P.sb("ones", [128, 128])
    mones = P.sb("mones", [128, 128])
    zeros = P.sb("zeros", [128, 128])
    Utri = P.sb("Utri", [128, 128])
    NEGT = P.sb("NEGT", [128, 128])
    MsT = P.sb("MsT", [128, 128])
    P.memset(ones[:], 1.0)
    ones_r = P.sb("ones_r", [128, 128])
    P.cp(R(ones_r[:]), ones[:], e="act")
    P.memset(mones[:], -1.0)
    P.memset(zeros[:], 0.0)
    P.asel(ident[:], ones[:], [[-1, 128]], ALU.is_equal, 0.0, 0, 1)
    P.asel(Utri[:], ones[:, :], [[1, 128]], ALU.is_ge, 0.0, 0, -1)
    P.asel(NEGT[:], zeros[:, :], [[1, 128]], ALU.is_ge, NEG, 0, -1)
    P.asel(MsT[:], ones[:, :], [[1, 128]], ALU.is_gt, 0.0, 0, -1)

    xres = [P.sb("xres%d" % i, [128, D]) for i in range(NTILE)]
    nw = P.sb("nw", [128, 2, KC])
    fnw = P.sb("fnw", [128, KC])
    P.dma("sp", nw[:], norm_w.rearrange("l (k p) -> p l k", p=128), writes=[nw[:]], allow_slow_non_contiguous=True)
    P.dma("sp", fnw[:], final_norm_w.rearrange("(k p) -> p k", p=128), writes=[fnw[:]],
          allow_slow_non_contiguous=True)
    ssq = P.sb("ssq", [128, 4])
    rstd = P.sb("rstd", [128, 4])
    P.split(ssq, 4)
    P.split(rstd, 4)
    phase_mark = P.sb_off
    hT = P.sb("hT", [128, KC, NTOK], BF16)
    xn = [P.sb("xn%d" % i, [128, D]) for i in range(2)]

    def tile_rows(i):
        return 128 if i < 16 else 48

    def tile_col(i):
        return PCOL + i * 128 if i < 16 else S1COL

    def norm_to_hT(layer, hT):
        for i in range(NTILE):
            nt = tile_rows(i)
            xt = xres[i]
            xb = xn[i % 2]
            sl = slice(i % 4, i % 4 + 1)
            P.act(xb[0:nt, :], xt[0:nt, :], AF.Square, accum_out=ssq[0:nt, sl])
            P.act(rstd[0:nt, sl], ssq[0:nt, sl], AF.Sqrt, scale=1.0 / D, bias=RMS_EPS)
            P.recip(rstd[0:nt, sl], rstd[0:nt, sl])
            P.ts(xb[0:nt, :], xt[0:nt, :], rstd[0:nt, sl], ALU.mult)
            c0 = tile_col(i)
            for half in range(2):
                ps = P.ps()
                for j in range(4):
                    kc = half * 4 + j
                    P.tr(ps[:, j * 128:j * 128 + nt], xb[0:nt, kc * 128:(kc + 1) * 128], ident[0:nt, 0:nt])
                pv = ps[:, :].rearrange("p (j t) -> p j t", j=4)[:, :, 0:nt]
                P.tt(hT[:, half * 4:half * 4 + 4, c0:c0 + nt], pv,
                     bc(nw[:, layer, half * 4:half * 4 + 4], [128, 4, nt]), ALU.mult)

    for i in range(NTILE):
        if i < 16:
            P.dma("sp", xres[i][:], x_p[i * 128:(i + 1) * 128, :], writes=[xres[i][:]])
        else:
            P.memset(xres[i][:], 0.0)
            P.dma("sp", xres[i][0:16, :], x_s[0:16, :], writes=[xres[i][:]])
            P.dma("sp", xres[i][32:48, :], x_s[16:32, :], writes=[xres[i][:]])
    P.memset(hT[:, :, 0:1], 0.0)
    norm_to_hT(0, hT)

    blocks = [(0, PCOL + i * NBA, NBA, CA, NBA // CA, i * (NBA // CA)) for i in range(TP // NBA)] + \
             [(1, S1COL, 16, 16, 1, NCH - 2), (2, S2COL, 16, 16, 1, NCH - 1)]

    cwt = P.sb("cwt", [32, 4, 128])
    cw = P.sb("cw", [128, 4, 32])
    P.dma("sp", cwt[:], a_conv_w.rearrange("t (g c) -> g t c", c=128), writes=[cwt[:]])
    ps = P.ps()
    for t in range(4):
        P.tr(ps[:, t * 32:(t + 1) * 32], cwt[:, t, :], ident[0:32, 0:32])
    P.cp(cw[:].rearrange("p t g -> p (t g)"), ps[:, 0:128])
    halo_all = P.sb("halo_all", [128, 2, 3, 32])
    hrow = P.sb("hrow", [96, 2, 128])
    for s in range(2):
        P.dma("sp", hrow[:, s, :], conv_s[s].rearrange("t (g c) -> (t g) c", c=128), writes=[hrow[:]])
    ps = P.ps()
    for s in range(2):
        P.tr(ps[:, s * 96:(s + 1) * 96], hrow[:, s, :], ident[0:96, 0:96])
    P.cp(halo_all[:].rearrange("p s t g -> p (s t g)"), ps[:, 0:192])
    fin_all = P.sb("fin_all", [128, 3, 3, 32])
    anw = P.sb("anw", [128, 2])
    P.dma("sp", anw[:], a_norm_w.rearrange("(h p) -> p h", p=128), writes=[anw[:]], allow_slow_non_contiguous=True)
    P.ts(anw[:], anw[:], 0.5, ALU.mult)

    wba = P.sb("wba", [128, KC, 16], BF16)
    P.dma("pool", wba[:], a_w_in.rearrange("(k p) c -> p k c", p=128)[:, :, 6144:6160], writes=[wba[:]])
    NCP = TP // CA
    BA = P.sb("BA", [CA, NCH, 16])
    P.memset(BA[:], 0.0)
    ps = P.ps()
    for c in range(NCP):
        for kc in range(KC):
            P.mm(ps[0:CA, c * 16:(c + 1) * 16], hT[:, kc, PCOL + c * CA:PCOL + (c + 1) * CA], wba[:, kc, :],
                 start=(kc == 0), stop=(kc == KC - 1))
    P.cp(BA[:, 0:NCP, :].rearrange("p c k -> p (c k)"), ps[0:CA, 0:NCP * 16])
    ps = P.ps()
    for s, sc in enumerate((S1COL, S2COL)):
        for kc in range(KC):
            P.mm(ps[0:16, s * 16:(s + 1) * 16], hT[:, kc, sc:sc + 16], wba[:, kc, :],
                 start=(kc == 0), stop=(kc == KC - 1))
    P.cp(BA[0:16, NCP:NCP + 2, :].rearrange("p c k -> p (c k)"), ps[0:16, 0:32])
    alg = P.sb("alg", [CA, 8])
    dtb = P.sb("dtb", [CA, 8])
    P.dma("sp", alg[:], a_log.partition_broadcast(CA), writes=[alg[:]])
    P.dma("sp", dtb[:], a_dt_bias.partition_broadcast(CA), writes=[dtb[:]])
    P.act(alg[:], alg[:], AF.Exp)
    P.ts(alg[:], alg[:], -1.0, ALU.mult)
    beta = P.sb("beta", [CA, NCH, 8])
    gg = P.sb("gg", [CA, NCH, 8])
    Gc = P.sb("Gc", [CA, NCH, 8])
    Glb = P.sb("Glb", [128, NCH, 8])
    gl = P.sb("gl", [128, NCH, 8])
    eG = P.sb("eG", [CA, NCH, 8])
    bG = P.sb("bG", [CA, NCH, 8])
    dte = P.sb("dte", [CA, NCH, 8])
    P.act(beta[:], BA[:, :, 0:8], AF.Sigmoid)
    P.tt(gg[:], BA[:, :, 8:16], bcm(dtb[:], [CA, NCH, 8]), ALU.add)
    P.act(gg[:], gg[:], AF.Exp)
    P.act(gg[:], gg[:], AF.Ln, bias=1.0)
    P.tt(gg[:], gg[:], bcm(alg[:], [CA, NCH, 8]), ALU.mult)
    psG = P.ps()
    psL = P.ps()
    g2 = gg[:].rearrange("p c k -> p (c k)")
    GP = NCP * 8
    P.mm(psG[0:CA, 0:GP], Utri[0:CA, 0:CA], g2[:, 0:GP])
    P.mm(psG[0:16, GP:GP + 16], Utri[0:16, 0:16], g2[0:16, GP:GP + 16])
    P.mm(psL[:, 0:GP], ones[0:CA, :], g2[:, 0:GP])
    P.mm(psL[:, GP:GP + 16], ones[0:16, :], g2[0:16, GP:GP + 16])
    P.memset(Gc[:], 0.0)
    P.cp(Gc[:, 0:NCP, :].rearrange("p c k -> p (c k)"), psG[0:CA, 0:GP])
    P.cp(Gc[0:16, NCP:NCP + 2, :].rearrange("p c k -> p (c k)"), psG[0:16, GP:GP + 16])
    P.cp(Glb[:].rearrange("p c k -> p (c k)"), psL[:, 0:GP + 16])
    P.act(gl[:], Glb[:], AF.Exp)
    P.act(eG[:], Gc[:], AF.Exp)
    P.tt(bG[:], beta[:], eG[:], ALU.mult)
    hbeta = eG
    P.ts(hbeta[:], beta[:], 0.5, ALU.mult)
    P.tt(dte[:], Glb[0:CA], Gc[:], ALU.subtract)
    P.act(dte[:], dte[:], AF.Exp)

    Wh = [P.sb("Wh0", [128, 6, KC, 128], BF16)] * 2
    P.split(Wh[0], 6)
    Wo = [P.sb("Wo0", [128, 2, D], BF16)] * 2
    w_in_v = a_w_in.rearrange("(k p) c -> p k c", p=128)

    def load_head_w(h):
        sl = h % 2
        for (c0, m_) in ((h * 128, 0), (1024 + h * 128, 1), (2048 + h * 256, 2), (2048 + h * 256 + 128, 3),
                         (4096 + h * 256, 4), (4096 + h * 256 + 128, 5)):
            P.dma("pool", Wh[sl][:, m_, :, :], w_in_v[:, :, c0:c0 + 128], writes=[Wh[sl][:, m_, :, :]])

    def load_head_wo(h):
        sl = h % 2
        P.dma("pool", Wo[sl][:], a_w_out[h * 256:(h + 1) * 256, :].rearrange("(hh p) c -> p hh c", p=128),
              writes=[Wo[sl][:]])

    NB_ = NBA
    NC_ = NBA // CA
    pre = P.sb("pre", [128, 4, NB_ + 3])
    acc = P.sb("acc", [128, 4, NB_])
    P.split(acc, 4)
    P.split(pre, 4)
    qkv = P.sb("qkv", [128, 4, NB_])
    zs2 = [P.sb("zs%d" % i, [128, 2, NB_]) for i in range(2)]
    sqr = P.sb("sqr", [128, 2, NB_])
    sq = sqr
    rq = acc[:, 2:4]
    oT = P.sb("oTp", [128, 2, NB_])
    osq = P.sb("osq2", [128, 2, NB_])
    qT = P.sb("qT", [128, NB_])
    kT = P.sb("kT", [128, NB_])
    kbT = P.sb("kbT", [128, NB_])
    qgT2 = [P.sb("qgT%d" % i, [128, NB_]) for i in range(2)]
    dg = P.sb("dg", [128, 2, NB_])
    eGb = P.sb("eGb", [128, NB_])
    betab = P.sb("betab", [128, NB_])
    kbeG2 = [P.sb("kbeG%d" % i, [CA, NC_, 128]) for i in range(2)]
    ktk2 = [P.sb("ktk%d" % i, [CA, NC_, 128]) for i in range(2)]
    vb2 = [P.sb("vb%d" % i, [CA, NC_, 256]) for i in range(2)]
    gU = P.sb("gU", [CA, NC_, CA])
    DT = P.sb("DT", [CA, NC_, CA])
    DTs = P.sb("DTs", [CA, NC_, CA])
    qkT2 = [P.sb("qkT%d" % i, [CA, NC_, CA]) for i in range(2)]
    Mn = [P.sb("Mn%d" % i, [CA, NC_, CA]) for i in range(2)]
    MT0a2 = [P.sb("MT0a%d" % i, [CA, NC_, CA]) for i in range(2)]
    MTTa = [P.sb("MTTa%d" % i, [CA, NC_, 2, CA]) for i in range(2)]
    u0 = P.sb("u0", [CA, NC_, 256])
    wkT = P.sb("wkT", [128, NB_])
    uu = [P.sb("uu%d" % i, [CA, 256]) for i in range(2)]
    Sp_ = [P.sb("Sp%d" % i, [128, 256]) for i in range(2)]
    Ss_ = [P.sb("Ss%d" % i, [128, 256]) for i in range(2)]
    Sst = [Sp_, Ss_, Ss_]
    ors = P.sb("ors", [128, NB_])
    og = P.sb("og", [128, 2, NB_], BF16)
    print("SBUF left after layer-A alloc:", P.sb_top - P.sb_off)
    for t_, n_ in [(qkv, 4), (oT, 2), (osq, 2), (sqr, 2), (dg, 2), (gU, NC_), (DT, NC_), (DTs, NC_), (u0, NC_),
                   (og, 2)] + [(x_, 2) for x_ in zs2] + [(x_, NC_) for x_ in kbeG2 + ktk2 + vb2 + qkT2 + Mn + MT0a2 + MTTa]:
        P.split(t_, n_)

    DONE = object()

    def A_s1(it, h, blk):
        (seq, t0, NB, C, nch, c0) = blk
        par = it % 2
        W = Wh[0]
        ggrp = (h, 8 + h, 16 + 2 * h, 17 + 2 * h)
        zs, qgT, ktk, qkT = zs2[par], qgT2[par], ktk2[par], qkT2[par]
        kbeG, vb, MT0a = kbeG2[par], vb2[par], MT0a2[par]
        first_of_seq = (seq == 0 and t0 == PCOL) or seq > 0
        if seq == 0 and t0 == PCOL:
            load_head_w(h)
        if first_of_seq:
            if seq == 0:
                P.memset(pre[:, :, 0:3], 0.0)
            else:
                for gi, g in enumerate(ggrp):
                    with P.only(pre=[gi]):
                        P.cp(pre[:, gi, 0:3], halo_all[:, seq - 1, :, g], e="pool")
        for m in range(6):
            ps = P.ps("a1a")
            for kc in range(KC):
                P.mm(ps[:, 0:NB], W[:, m, kc, :], hT[:, kc, t0:t0 + NB],
                     start=(kc == 0), stop=(kc == KC - 1))
            if m < 4:
                with P.only(pre=[m]):
                    P.cp(pre[:, m, 3:3 + NB], ps[:, 0:NB], e="act")
            else:
                P.act(zs[:, m - 4, 0:NB], ps[:, 0:NB], AF.Tanh, scale=0.5)
                P.stt(zs[:, m - 4, 0:NB], zs[:, m - 4, 0:NB], 1.0, ps[:, 0:NB], ALU.add, ALU.mult)
            yield
        for gi, g in enumerate(ggrp):
            with P.only(acc=[gi], pre=[gi]):
                P.act(acc[:, gi, 0:NB], pre[:, gi, 3:3 + NB], AF.Copy, scale=cw[:, 3, g:g + 1])
                for tap in (2, 1, 0):
                    P.stt(acc[:, gi, 0:NB], pre[:, gi, tap:tap + NB], cw[:, tap, g:g + 1], acc[:, gi, 0:NB],
                          ALU.mult, ALU.add)
            yield
        P.act(qkv[:, :, 0:NB], acc[:, :, 0:NB], AF.Tanh, scale=0.5)
        P.stt(qkv[:, :, 0:NB], qkv[:, :, 0:NB], 1.0, acc[:, :, 0:NB], ALU.add, ALU.mult)
        last = (t0 + NB == PCOL + TP) or seq > 0
        for gi, g in enumerate(ggrp):
            with P.only(pre=[gi]):
                if last:
                    P.cp(fin_all[:, seq, :, g], pre[:, gi, NB:NB + 3], e="pool")
                else:
                    P.cp(pre[:, gi, 0:3], pre[:, gi, NB:NB + 3], e="pool")
        yield
        P.act(R(sq[:, :, 0:NB]), qkv[:, 0:2, 0:NB], AF.Square)
        for j in range(2):
            ps = P.ps("a1m")
            P.mmr(ps[:, 0:NB], ones_r[:, :], sq[:, j, 0:NB])
            with P.only(acc=[2 + j]):
                if j == 0:
                    P.act(rq[:, j, 0:NB], ps[:, 0:NB], AF.Ln, scale=128.0, bias=512.0 * 1e-6)
                else:
                    P.act(rq[:, j, 0:NB], ps[:, 0:NB], AF.Ln, bias=4e-6)
        yield
        with P.only(acc=[2, 3]):
            P.act(rq[:, :, 0:NB], rq[:, :, 0:NB], AF.Exp, scale=-0.5)
        with P.only(acc=[2]):
            P.tt(R(qT[:, 0:NB]), qkv[:, 0, 0:NB], rq[:, 0, 0:NB], ALU.mult)
        with P.only(acc=[3]):
            P.tt(R(kT[:, 0:NB]), qkv[:, 1, 0:NB], rq[:, 1, 0:NB], ALU.mult)
        yield
        cs = slice(c0, c0 + nch)
        idb = bcm(ident[0:C, 0:C], [C, nch, C])
        dgv = dg[0:C, :, 0:NB].rearrange("p a (c i) -> p a c i", c=nch)
        P.tt(dgv[:, 0], idb, bc(Gc[0:C, cs, h], [C, nch, C]), ALU.mult)
        P.tt(dgv[:, 1], idb, bc(beta[0:C, cs, h], [C, nch, C]), ALU.mult)
        ps = P.ps("a1m")
        P.mm(ps[:, 0:NB], ones[0:C, :], dg[0:C, 0, 0:NB])
        P.act(eGb[:, 0:NB], ps[:, 0:NB], AF.Exp)
        ps = P.ps("a1m")
        P.mm(ps[:, 0:NB], ones[0:C, :], dg[0:C, 1, 0:NB])
        P.cp(betab[:, 0:NB], ps[:, 0:NB], e="act")
        yield
        P.tt(R(kbT[:, 0:NB]), kT[:, 0:NB], betab[:, 0:NB], ALU.mult)
        P.tt(R(qgT[:, 0:NB]), qT[:, 0:NB], eGb[:, 0:NB], ALU.mult)
        yield
        for c4 in range(0, nch, 4):
            n4 = min(4, nch - c4)
            ps = P.ps("a1m")
            for j in range(n4):
                c = c4 + j
                P.tr(ps[0:C, j * 128:(j + 1) * 128], kT[:, c * C:(c + 1) * C], ident[:, :])
            pv = ps[0:C, 0:n4 * 128].rearrange("p (j d) -> p j d", j=n4)
            P.tt(R(kbeG[0:C, c4:c4 + n4, :]), pv, bc(bG[0:C, c0 + c4:c0 + c4 + n4, h], [C, n4, 128]), ALU.mult)
            P.tt(R(ktk[0:C, c4:c4 + n4, :]), pv, bc(dte[0:C, c0 + c4:c0 + c4 + n4, h], [C, n4, 128]), ALU.mult)
            yield
        for c2 in range(0, nch, 2):
            n2 = min(2, nch - c2)
            ps = P.ps("a1m")
            for j in range(n2):
                c = c2 + j
                for half in range(2):
                    P.tr(ps[0:C, j * 256 + half * 128:j * 256 + (half + 1) * 128],
                         qkv[:, 2 + half, c * C:(c + 1) * C], ident[:, :])
            pv = ps[0:C, 0:n2 * 256].rearrange("p (j d) -> p j d", j=n2)
            P.tt(R(vb[0:C, c2:c2 + n2, :]), pv, bc(hbeta[0:C, c0 + c2:c0 + c2 + n2, h], [C, n2, 256]), ALU.mult)
            yield
        P.tt(gU[0:C, 0:nch, 0:C], bcm(Utri[0:C, 0:C], [C, nch, C]), bc(gg[0:C, cs, h], [C, nch, C]), ALU.mult)
        ps = P.ps("a1m")
        for c in range(nch):
            o = ps[0:C, c * C:(c + 1) * C]
            P.mm(o, ones[0:C, 0:C], gU[0:C, c, 0:C], start=True, stop=False)
            P.mm(o, gU[0:C, c, 0:C], mones[0:C, 0:C], start=False, stop=False)
            P.mm(o, ident[0:C, 0:C], NEGT[0:C, 0:C], start=False, stop=True)
        pv = ps[0:C, 0:nch * C].rearrange("p (c i) -> p c i", c=nch)
        P.act(DT[0:C, 0:nch, 0:C], pv, AF.Exp)
        P.tt(DTs[0:C, 0:nch, 0:C], DT[0:C, 0:nch, 0:C], bcm(MsT[0:C, 0:C], [C, nch, C]), ALU.mult, e="pool")
        yield
        ps = P.ps("a1m")
        for c in range(nch):
            P.mmr(ps[0:C, c * C:(c + 1) * C], kT[:, c * C:(c + 1) * C], kbT[:, c * C:(c + 1) * C])
        for c in range(nch):
            P.mmr(ps[0:C, 256 + c * C:256 + (c + 1) * C], kT[:, c * C:(c + 1) * C], qT[:, c * C:(c + 1) * C])
        pv = ps[0:C, 0:nch * C].rearrange("p (c i) -> p c i", c=nch)
        pv2 = ps[0:C, 256:256 + nch * C].rearrange("p (c i) -> p c i", c=nch)
        P.stt(R(MT0a[0:C, 0:nch, 0:C]), pv, -1.0, DTs[0:C, 0:nch, 0:C], ALU.mult, ALU.mult)
        P.tt(R(qkT[0:C, 0:nch, 0:C]), pv2, DT[0:C, 0:nch, 0:C], ALU.mult)
        yield
        ps = P.ps("a1m")
        for c in range(nch):
            P.tr(ps[0:C, c * C:(c + 1) * C], MT0a[0:C, c, 0:C], ident[0:C, 0:C])
        P.cp(R(Mn[0][0:C, 0:nch, 0:C]), ps[0:C, 0:nch * C].rearrange("p (c i) -> p c i", c=nch), e="act")
        yield
        TTf = yield from neumann(P, ident, MT0a[0:C, 0:nch, 0:C], None, Mn, MTTa, nch, C, pool="a1b")
        yield "DRAIN2"
        for c2 in range(0, nch, 2):
            n2 = min(2, nch - c2)
            ps = P.ps("a1b")
            for j in range(n2):
                P.mmr(ps[0:C, j * 256:(j + 1) * 256], TTf[:, c2 + j, :], vb[0:C, c2 + j, :])
            P.cp(u0[0:C, c2:c2 + n2, :], ps[0:C, 0:n2 * 256].rearrange("p (j d) -> p j d", j=n2), e="act")
        ps = P.ps("a1b")
        for c in range(nch):
            P.mmr(ps[:, c * C:(c + 1) * C], kbeG[0:C, c, :], TTf[:, c, :])
        P.cp(R(wkT[:, 0:NB]), ps[:, 0:NB], e="act")
        yield

    spar = [0, 0, 0]

    def A_s2(it, h, blk):
        (seq, t0, NB, C, nch, c0) = blk
        par = it % 2
        WO = Wo[0]
        zs, qgT, ktk, qkT = zs2[par], qgT2[par], ktk2[par], qkT2[par]
        first_of_seq = (seq == 0 and t0 == PCOL) or seq > 0
        if seq == 0 and t0 == PCOL:
            load_head_wo(h)
        if first_of_seq:
            spar[seq] = 0
            if seq == 0:
                P.cp(R(Sst[0][0][:]), zeros[:, 0:1].broadcast_to([128, 256]), e="act")
            else:
                P.dma("sp", Sst[seq][1][:], delta_s[seq - 1, h], writes=[Sst[seq][1][:]])
                P.cp(R(Sst[seq][0][:]), Sst[seq][1][:], e="act")
        pso = P.psb[7]
        for c in range(nch):
            Sc = Sst[seq][spar[seq]]
            Sn = Sst[seq][1 - spar[seq]]
            u = uu[c % 2]
            ps = P.ps("a2")
            P.mmr(ps[0:C, 0:256], wkT[:, c * C:(c + 1) * C], Sc[:, :])
            P.tt(R(u[0:C, :]), u0[0:C, c, :], ps[0:C, 0:256], ALU.subtract)
            yield
            ps2 = P.ps("a2")
            P.mmr(ps2[:, 0:256], ktk[0:C, c, :], u[0:C, :])
            P.stt(R(Sn[:, :]), Sc[:, :], gl[:, c0 + c, h:h + 1], ps2[:, 0:256], ALU.mult, ALU.add)
            for half in range(2):
                oo = pso[:, (half * nch + c) * C:(half * nch + c + 1) * C]
                P.mmr(oo, Sc[:, half * 128:(half + 1) * 128], qgT[:, c * C:(c + 1) * C], start=True, stop=False)
                P.mmr(oo, u[0:C, half * 128:(half + 1) * 128], qkT[0:C, c, 0:C], start=False, stop=True)
            spar[seq] = 1 - spar[seq]
            yield
        P.cp(oT[:, :, 0:NB], pso[:, 0:2 * NB].rearrange("p (a t) -> p a t", a=2), e="act")
        P.act(R(osq[:, :, 0:NB]), oT[:, :, 0:NB], AF.Square)
        ps = P.ps("a2")
        P.mmr(ps[:, 0:NB], ones_r[:, :], osq[:, 0, 0:NB], start=True, stop=False)
        P.mmr(ps[:, 0:NB], ones_r[:, :], osq[:, 1, 0:NB], start=False, stop=True)
        P.act(ors[:, 0:NB], ps[:, 0:NB], AF.Ln, scale=1.0 / 256.0, bias=RMS_EPS)
        yield
        P.act(ors[:, 0:NB], ors[:, 0:NB], AF.Exp, scale=-0.5)
        for half in range(2):
            P.stt(oT[:, half, 0:NB], oT[:, half, 0:NB], anw[:, half:half + 1], ors[:, 0:NB], ALU.mult, ALU.mult)
        P.tt(og[:, :, 0:NB], oT[:, :, 0:NB], zs[:, :, 0:NB], ALU.mult)
        yield
        for tt0 in range(0, NB, 128):
            nt = min(128, NB - tt0)
            if seq == 0:
                tile_i, prow = (t0 - PCOL + tt0) // 128, 0
            else:
                tile_i, prow = 16, (seq - 1) * 32
            for nh in range(2):
                ps = P.ps("a2")
                for half in range(2):
                    P.mm(ps[prow:prow + nt, :], og[:, half, tt0:tt0 + nt], WO[:, half, nh * 512:(nh + 1) * 512],
                         start=(half == 0), stop=(half == 1))
                xr = xres[tile_i][prow:prow + nt, nh * 512:(nh + 1) * 512]
                P.tt(xr, xr, ps[prow:prow + nt, :], ALU.add)
            yield
        last = (t0 + NB == PCOL + TP) or seq > 0
        if last:
            P.dma("sp", o_delta[seq, h], Sst[seq][spar[seq]][:], reads=[Sst[seq][spar[seq]][:]], final=True)

    def pipeline(items, s1, s2, ratio):
        g2 = None
        for it, item in enumerate(list(items) + [None]):
            g1 = s1(it, *item) if item is not None else None
            while g1 is not None or g2 is not None:
                if g2 is not None:
                    if next(g2, DONE) is DONE:
                        g2 = None
                if g1 is not None:
                    for _ in range(ratio if g2 is not None else 1000000):
                        r = next(g1, DONE)
                        if r is DONE:
                            g1 = None
                            break
                        if r == "DRAIN2":
                            while g2 is not None:
                                if next(g2, DONE) is DONE:
                                    g2 = None
            g2 = s2(it, *item) if item is not None else None

    pipeline([(h, blk) for h in range(H_A) for blk in blocks], A_s1, A_s2, 3)

    ps = P.ps()
    for s in range(3):
        P.tr(ps[0:96, s * 128:(s + 1) * 128], fin_all[:, s].rearrange("p t g -> p (t g)"), ident[:, :])
    for s in range(3):
        P.cp(acc[0:96, s, 0:128], ps[0:96, s * 128:(s + 1) * 128])
        P.dma("sp", o_conv[s].rearrange("t (g c) -> (t g) c", c=128), acc[0:96, s, 0:128], reads=[acc[:]], final=True)

    if stop == "A":
        for i in range(NTILE):
            nt = tile_rows(i)
            P.dma("sp", dbg[i * 128:i * 128 + nt, :], xres[i][0:nt, :], reads=[xres[i][:]], final=True)
        P.finish()
        return nc

    P.barrier()
    P.sb_off = phase_mark
    hT = P.sb("hT2", [128, KC, NTOK], BF16)
    shout = P.sb("shout", [128, 3, KC])
    P.memset(hT[:, :, 0:1], 0.0)
    mark_b0 = P.sb_off
    xn = [P.sb("xnB%d" % i, [128, D]) for i in range(2)]

    def norm_to_hT_B():
        for i in range(NTILE):
            nt = tile_rows(i)
            xt = xres[i]
            xb = xn[i % 2]
            sl = slice(i % 4, i % 4 + 1)
            P.act(xb[0:nt, :], xt[0:nt, :], AF.Square, accum_out=ssq[0:nt, sl])
            P.act(rstd[0:nt, sl], ssq[0:nt, sl], AF.Sqrt, scale=1.0 / D, bias=RMS_EPS)
            P.recip(rstd[0:nt, sl], rstd[0:nt, sl])
            P.ts(xb[0:nt, :], xt[0:nt, :], rstd[0:nt, sl], ALU.mult)
            c0 = tile_col(i)
            for half in range(2):
                ps = P.ps(2)
                for j in range(4):
                    kc = half * 4 + j
                    P.tr(ps[:, j * 128:j * 128 + nt], xb[0:nt, kc * 128:(kc + 1) * 128], ident[0:nt, 0:nt])
                pv4 = ps[:, :].rearrange("p (j t) -> p j t", j=4)
                pv = pv4[:, :, 0:nt]
                P.tt(hT[:, half * 4:half * 4 + 4, c0:c0 + nt], pv,
                     bc(nw[:, 1, half * 4:half * 4 + 4], [128, 4, nt]), ALU.mult)
                lastcols = {15: [(0, 127)], 16: [(1, 15), (2, 47)]}.get(i, [])
                for (sq_, col) in lastcols:
                    P.tt(shout[:, sq_, half * 4:half * 4 + 4], pv4[:, :, col], nw[:, 1, half * 4:half * 4 + 4], ALU.mult)

    norm_to_hT_B()
    P.barrier()
    P.sb_off = mark_b0
    for s_ in range(3):
        P.dma("sp", o_shift[s_].rearrange("(k p) -> p k", p=128), shout[:, s_, :], reads=[shout[:]], final=True,
              allow_slow_non_contiguous=True)
    shin = P.sb("shin", [128, 2, KC])
    P.dma("sp", shin[:], shift_s.rearrange("s (k p) -> p s k", p=128), writes=[shin[:]], allow_slow_non_contiguous=True)
    P.cp(hT[:, :, S1COL - 1], shin[:, 0, :])
    P.cp(hT[:, :, S2COL - 1], shin[:, 1, :])

    vecs = P.sb("vecs", [128, 13, KC])
    P.dma("sp", vecs[:, 0:6, :], b_mu.rearrange("g (k p) -> p g k", p=128), writes=[vecs[:]], allow_slow_non_contiguous=True)
    for vi, v_ in enumerate((b_w0, b_a0, b_k_k, b_k_a, b_r_k, b_gn_w, b_gn_b)):
        P.dma("sp", vecs[:, 6 + vi, :], v_.rearrange("(k p) -> p k", p=128), writes=[vecs[:]], allow_slow_non_contiguous=True)
    V_W0, V_A0, V_KK, V_KA, V_RK, V_GW, V_GB = range(6, 13)
    hvec = P.sb("hvec", [128, 2, KC])
    P.ts(hvec[:], vecs[:, 6:8, :], 0.5, ALU.mult)
    blk1 = P.sb("blk1", [128, 128])
    cst32 = P.sb("cst32", [128, 192])
    P.asel(cst32[:, 0:64], ones[:, 0:64], [[0, 64]], ALU.is_ge, 0.0, 63, -1)
    P.asel(cst32[:, 64:128], ones[:, 0:64], [[0, 64]], ALU.is_ge, 0.0, -64, 1)
    P.cp(R(blk1[:]), cst32[:, 0:128])
    CB = 128
    MXT = P.sb("MXT", [CB, 2 * CB])
    P.cp(MXT[:, 0:CB], MsT[0:CB, 0:CB], e="pool")
    P.cp(MXT[:, CB:2 * CB], Utri[0:CB, 0:CB], e="pool")
    MsL = P.sb("MsL", [CB, CB])
    Sh = P.sb("Sh", [128, 64])
    P.asel(cst32[:, 128:192], ones[:, 0:64], [[-1, 64]], ALU.is_equal, 0.0, -64, 1)
    P.cp(R(Sh[:]), cst32[:, 128:192])
    P.asel(MsL[:], ones[0:CB, 0:CB], [[-1, CB]], ALU.is_gt, 0.0, 0, 1)
    rmask = P.sb("rmask", [128, 256])
    P.memset(rmask[:], 1.0)
    for c in range(256 // CB):
        P.memset(rmask[:, c * CB:c * CB + 1], 0.0)

    NBB = 256
    t1T = P.sb("t1T", [64, NTOK], BF16)
    a1T = P.sb("a1T", [64, NTOK], BF16)
    w2b = P.sb("w2b", [64, D], BF16)
    a2b = P.sb("a2b", [64, D], BF16)
    blk_mark = P.sb_off
    lw1 = P.sb("lw1", [128, KC, 2, 64], BF16)
    lw1p = P.sb("lw1p", [128, KC, 2, 64], BF16)
    lw1pp = P.sb("lw1pp", [128, KC, 2, 64], BF16)
    P.dma("pool", lw1[:, :, 0, :], b_w_w1.rearrange("(k p) c -> p k c", p=128), writes=[lw1[:]])
    P.dma("pool", lw1[:, :, 1, :], b_a_w1.rearrange("(k p) c -> p k c", p=128), writes=[lw1[:]])
    for j in range(2):
        P.tt(lw1p[:, :, j, :], lw1[:, :, j, :], bc(vecs[:, 4 + j, :], [128, KC, 64]), ALU.mult)
    P.tt(lw1pp[:], lw1[:], lw1p[:], ALU.subtract)
    P.dma("pool", w2b[:], b_w_w2[:, :], writes=[w2b[:]])
    P.dma("pool", a2b[:], b_a_w2[:, :], writes=[a2b[:]])
    col_ranges = [(PCOL + i * 512, 512) for i in range(4)] + [(S1COL, 16), (S2COL, 16)]
    for (cc0, n) in col_ranges:
        for j, dst in enumerate((t1T, a1T)):
            ps = P.ps(2)
            for kc in range(KC):
                P.mm(ps[0:64, 0:n], lw1pp[:, kc, j, :], hT[:, kc, cc0:cc0 + n], start=(kc == 0), stop=False)
                P.mm(ps[0:64, 0:n], lw1p[:, kc, j, :], hT[:, kc, cc0 - 1:cc0 - 1 + n], start=False, stop=(kc == KC - 1))
            P.act(dst[:, cc0:cc0 + n], ps[0:64, 0:n], AF.Tanh if j == 0 else AF.Copy)

    P.barrier()
    P.sb_off = blk_mark
    Wp = [P.sb("Wp0", [128, 4, KC, 128], BF16)] * 2
    Wq = P.sb("Wq", [128, 4, KC, 128], BF16)
    P.split(Wp[0], 4)
    P.split(Wq, 4)
    Wob = [P.sb("Wob0", [128, D], BF16)] * 2
    b_in_v = b_w_in.rearrange("(k p) (g c) -> p k g c", p=128, g=4)

    def load_pair_w(pr):
        for g_ in range(4):
            P.dma("pool", Wp[pr % 2][:, g_, :, :], b_in_v[:, :, g_, pr * 128:(pr + 1) * 128],
                  writes=[Wp[pr % 2][:, g_, :, :]])
        P.dma("pool", Wob[pr % 2][:], b_w_out[pr * 128:(pr + 1) * 128, :], writes=[Wob[pr % 2][:]])

    NCB = NBB // CB
    NX = 2 * NCB
    rkvT = P.sb("rkvT", [128, 3, NBB])
    zsB2 = [P.sb("zsB%d" % i, [128, NBB]) for i in range(2)]
    s2tmp = P.sb("s2tmp", [128, 2, NBB])
    lwT = P.sb("lwT", [128, NBB])
    aT = P.sb("aT", [128, NBB])
    cwv = P.sb("cwv", [128, NBB])
    eW = P.sb("eW", [128, 3, NBB])
    tmpB = P.sb("tmpB", [128, 4, NBB])
    kkT = P.sb("kkT", [128, NBB])
    k2T = P.sb("k2T", [128, NBB])
    arT2 = [P.sb("arT%d" % i, [128, 2, NBB]) for i in range(2)]
    bkT = P.sb("bkT", [128, 2, NBB])
    bkh = P.sb("bkh", [128, 2, NBB])
    Wc = P.sb("Wc", [128, NCB])
    rkb = P.sb("rkb", [128, NBB])
    vt = P.sb("vt", [CB, NCB, 128])
    bht = P.sb("bht", [CB, NCB, 128])
    kht = P.sb("kht", [CB, NCB, 128])
    XA = P.sb("XA", [CB, NX, 2 * CB])
    XB = P.sb("XB", [CB, NX, 2 * CB])
    MnB = [P.sb("MnB%d" % i, [CB, NX, CB]) for i in range(2)]
    MTTb = [P.sb("MTTb%d" % i, [CB, NX, 2, CB]) for i in range(2)]
    for m_ in MTTb:
        P.split(m_, 2)
    Rsb = [P.sb("Rsb%d" % i, [CB, 128]) for i in range(2)]
    Usb = [P.sb("Usb%d" % i, [CB, 128]) for i in range(2)]
    StP = [[P.sb("StP%d_%d" % (i, hd), [64, 64]) for hd in range(2)] for i in range(2)]
    StS = [[P.sb("StS%d_%d" % (i, hd), [64, 64]) for hd in range(2)] for i in range(2)]
    StB = [StP, StS, StS]
    stio = P.sb("stio", [64, 128])
    ar12 = [P.sb("ar1_%d" % i, [64, 2, NBB]) for i in range(2)]
    bk1 = P.sb("bk1", [64, 2, NBB])
    Wc1 = P.sb("Wc1", [64, NCB])
    sqB = P.sb("sqB", [128, 4, NBB])
    oTB = sqB[:, 2]
    ocB = s2tmp[:, 0]
    osB = sqB[:, 3]
    ogB = P.sb("ogB", [128, NBB], BF16)
    print("SBUF left after layer-B alloc:", P.sb_top - P.sb_off)
    for t_, n_ in [(rkvT, 3), (eW, 3), (tmpB, 4), (sqB, 4), (bkT, 2), (bkh, 2), (XA, NX), (XB, NX), (bk1, 2),
                   (vt, NCB), (bht, NCB), (kht, NCB), (s2tmp, 2)] + [(x_, 2) for x_ in arT2 + ar12] + \
            [(x_, NX) for x_ in MnB]:
        P.split(t_, n_)
    blocksB = [(0, PCOL + i * NBB, NBB, CB, NBB // CB) for i in range(TP // NBB)] + \
              [(1, S1COL, 16, 16, 1), (2, S2COL, 16, 16, 1)]
    ENH = -float(np.exp(-0.5))

    load_pair_w(0)
    nblkB = 0
    for pr in range(8):
        W = Wp[pr % 2]
        WO = Wob[pr % 2]
        for g_ in range(4):
            P.tt(Wq[:, g_, :, :], W[:, g_, :, :], bc(vecs[:, g_, :], [128, KC, 128]), ALU.mult)
            P.tt(W[:, g_, :, :], W[:, g_, :, :], Wq[:, g_, :, :], ALU.subtract)
        Wr = W
        spar = [0, 0, 0]
        cur_seq = -1
        for (seq, t0, NB, C, nch) in blocksB:
            nx = 2 * nch
            bpar = nblkB % 2
            nblkB += 1
            zsB, arT, ar1 = zsB2[bpar], arT2[bpar], ar12[bpar]
            if seq != cur_seq:
                cur_seq = seq
                spar[seq] = 0
                if seq == 0:
                    for hd in range(2):
                        P.cp(R(StB[0][0][hd][:]), zeros[0:64, 0:64], e="act")
                else:
                    P.dma("sp", stio[:].rearrange("v (h k) -> v h k", h=2),
                          wkv_s[seq - 1, 2 * pr:2 * pr + 2].rearrange("h v k -> v h k"), writes=[stio[:]])
                    for hd in range(2):
                        ps = P.ps("b1")
                        P.tr(ps[0:64, 0:64], stio[:, hd * 64:(hd + 1) * 64], ident[0:64, 0:64])
                        P.cp(R(StB[seq][0][hd][:]), ps[0:64, 0:64])
            for g_ in range(4):
                ps = P.ps("b1")
                for kc in range(KC):
                    P.mm(ps[:, 0:NB], Wr[:, g_, kc, :], hT[:, kc, t0:t0 + NB], start=(kc == 0), stop=False)
                    P.mm(ps[:, 0:NB], Wq[:, g_, kc, :], hT[:, kc, t0 - 1:t0 - 1 + NB], start=False, stop=(kc == KC - 1))
                if g_ < 3:
                    P.cp(rkvT[:, g_, 0:NB], ps[:, 0:NB], e="act")
                else:
                    P.act(zsB[:, 0:NB], ps[:, 0:NB], AF.Tanh, scale=0.5)
                    P.stt(zsB[:, 0:NB], zsB[:, 0:NB], 1.0, ps[:, 0:NB], ALU.add, ALU.mult)
            ps = P.ps("b1")
            P.mm(ps[:, 0:NB], w2b[:, pr * 128:(pr + 1) * 128], t1T[:, t0:t0 + NB])
            P.act(lwT[:, 0:NB], ps[:, 0:NB], AF.Tanh, scale=0.5, bias=hvec[:, 0, pr:pr + 1])
            P.ts(lwT[:, 0:NB], lwT[:, 0:NB], 0.5 * ENH, ALU.mult, 0.5 * ENH, ALU.add)
            ps = P.ps("b1")
            P.mm(ps[:, 0:NB], a2b[:, pr * 128:(pr + 1) * 128], a1T[:, t0:t0 + NB])
            P.act(aT[:, 0:NB], ps[:, 0:NB], AF.Tanh, scale=0.5, bias=hvec[:, 1, pr:pr + 1])
            P.ts(aT[:, 0:NB], aT[:, 0:NB], 0.5, ALU.mult, 0.5, ALU.add)
            rT = rkvT[:, 0, 0:NB]
            kT_ = rkvT[:, 1, 0:NB]
            vT_ = rkvT[:, 2, 0:NB]
            P.ts(kkT[:, 0:NB], kT_, vecs[:, V_KK, pr:pr + 1], ALU.mult)
            P.act(R(sqB[:, 0, 0:NB]), kT_, AF.Square, scale=vecs[:, V_KK, pr:pr + 1])
            ps = P.ps("b1")
            P.mmr(ps[:, 0:NB], blk1[:, :], sqB[:, 0, 0:NB])
            P.act(tmpB[:, 1, 0:NB], ps[:, 0:NB], AF.Ln, bias=1e-6)
            P.act(tmpB[:, 1, 0:NB], tmpB[:, 1, 0:NB], AF.Exp, scale=-0.5)
            P.tt(kkT[:, 0:NB], kkT[:, 0:NB], tmpB[:, 1, 0:NB], ALU.mult)
            P.ts(tmpB[:, 2, 0:NB], aT[:, 0:NB], -1.0, ALU.add, vecs[:, V_KA, pr:pr + 1], ALU.mult)
            P.ts(tmpB[:, 2, 0:NB], tmpB[:, 2, 0:NB], 1.0, ALU.add)
            P.tt(k2T[:, 0:NB], kT_, tmpB[:, 2, 0:NB], ALU.mult)
            P.scan(cwv[:, 0:NB], rmask[:, 0:NB], lwT[:, 0:NB])
            P.act(eW[:, 0, 0:NB], cwv[:, 0:NB], AF.Exp)
            P.act(eW[:, 1, 0:NB], cwv[:, 0:NB], AF.Exp, scale=-1.0)
            P.tt(tmpB[:, 3, 0:NB], cwv[:, 0:NB], lwT[:, 0:NB], ALU.subtract)
            P.act(eW[:, 2, 0:NB], tmpB[:, 3, 0:NB], AF.Exp)
            P.stt(R(arT[:, 0, 0:NB]), kkT[:, 0:NB], -1.0, eW[:, 2, 0:NB], ALU.mult, ALU.mult)
            P.tt(R(arT[:, 1, 0:NB]), rT, eW[:, 0, 0:NB], ALU.mult)
            P.tt(tmpB[:, 0, 0:NB], kkT[:, 0:NB], aT[:, 0:NB], ALU.mult)
            P.tt(R(bkT[:, 0, 0:NB]), tmpB[:, 0, 0:NB], eW[:, 1, 0:NB], ALU.mult)
            P.tt(R(bkT[:, 1, 0:NB]), k2T[:, 0:NB], eW[:, 1, 0:NB], ALU.mult)
            ewc = eW[:, 0, 0:NB].rearrange("p (c i) -> p c i", c=nch)[:, :, C - 1]
            P.cp(Wc[:, 0:nch], ewc)
            bkv = bkT[:, :, 0:NB].rearrange("p a (c i) -> p a c i", c=nch)
            bhv = bkh[:, :, 0:NB].rearrange("p a (c i) -> p a c i", c=nch)
            for a_ in range(2):
                P.tt(bhv[:, a_], bkv[:, a_], bc(Wc[:, 0:nch], [128, nch, C]), ALU.mult)
            P.stt(R(sqB[:, 1, 0:NB]), rT, vecs[:, V_RK, pr:pr + 1], k2T[:, 0:NB], ALU.mult, ALU.mult)
            ps = P.ps("b1")
            P.mmr(ps[:, 0:NB], blk1[:, :], sqB[:, 1, 0:NB])
            P.tt(rkb[:, 0:NB], ps[:, 0:NB], vT_, ALU.mult)
            ps = P.ps("b1")
            P.mmr(ps[0:64, 0:2 * NB], Sh[:, :], arT[:, :, 0:NB])
            P.cp(R(ar1[:, :, 0:NB]), ps[0:64, 0:2 * NB].rearrange("p (a t) -> p a t", a=2), e="act")
            ps = P.ps("b1")
            P.mmr(ps[0:64, 0:2 * NB], Sh[:, :], bkT[:, :, 0:NB])
            P.cp(R(bk1[:, :, 0:NB]), ps[0:64, 0:2 * NB].rearrange("p (a t) -> p a t", a=2), e="act")
            ps = P.ps("b1")
            P.mm(ps[0:64, 0:nch], cst32[:, 128:192], Wc[:, 0:nch])
            P.cp(Wc1[:, 0:nch], ps[0:64, 0:nch])
            AR = [arT[0:64], ar1[:]]
            BK = [bkT[0:64], bk1[:]]
            WC = [Wc[0:64], Wc1[:]]
            for (src, dst) in ((vT_, vt), (bkh[:, 0, 0:NB], bht), (bkh[:, 1, 0:NB], kht)):
                ps = P.ps("b1")
                for c in range(nch):
                    P.tr(ps[0:C, c * 128:(c + 1) * 128], src[:, c * C:(c + 1) * C], ident[:, :])
                P.cp(R(dst[0:C, 0:nch, :]), ps[0:C, 0:nch * 128].rearrange("p (c d) -> p c d", c=nch), e="act")
            psN = P.ps("b1")
            for hd in range(2):
                hs = slice(hd * 64, (hd + 1) * 64)
                psA = P.ps("b1")
                psB_ = P.ps("b1")
                for c in range(nch):
                    csl = slice(c * C, (c + 1) * C)
                    x_ = hd * nch + c
                    P.mmr(psA[0:C, c * 2 * C:(c + 1) * 2 * C], BK[hd][:, 0, csl], AR[hd][:, :, csl])
                    P.mmr(psB_[0:C, c * 2 * C:(c + 1) * 2 * C], BK[hd][:, 1, csl], AR[hd][:, :, csl])
                    P.mmr(psN[0:C, x_ * C:(x_ + 1) * C], AR[hd][:, 0, csl], BK[hd][:, 0, csl])
                for (psx, dstx) in ((psA, XA), (psB_, XB)):
                    pv = psx[0:C, 0:nch * 2 * C].rearrange("p (c a i) -> p c a i", c=nch, a=2)
                    dv = dstx[0:C, hd * nch:(hd + 1) * nch, :].rearrange("p c (a i) -> p c a i", a=2)[:, :, :, 0:C]
                    mv = MXT[0:C, :].rearrange("p (a i) -> p a i", a=2)[:, :, 0:C].unsqueeze(1).broadcast_to([C, nch, 2, C])
                    P.tt(R(dv), pv, mv, ALU.mult)
            P.tt(R(MnB[0][0:C, 0:nx, 0:C]), psN[0:C, 0:nx * C].rearrange("p (x i) -> p x i", x=nx),
                 bcm(MsL[0:C, 0:C], [C, nx, C]), ALU.mult)
            gen_ = neumann(P, ident, XA[0:C, 0:nx, 0:C], None, MnB, MTTb, nx, C, pool="b1")
            while True:
                try:
                    next(gen_)
                except StopIteration as e_:
                    TTf = e_.value
                    break
            pso = P.psb[7]
            for c in range(nch):
                csl = slice(c * C, (c + 1) * C)
                Sc = StB[seq][spar[seq]]
                Sn = StB[seq][1 - spar[seq]]
                Rb = Rsb[c % 2]
                Ub = Usb[c % 2]
                ps = P.ps("b2")
                for hd in range(2):
                    hs = slice(hd * 64, (hd + 1) * 64)
                    x_ = hd * nch + c
                    P.mmr(ps[0:C, hs], AR[hd][:, 0, csl], Sc[hd][:, :], start=True, stop=False)
                    P.mmr(ps[0:C, hs], XB[0:C, x_, 0:C], vt[0:C, c, hs], start=False, stop=True)
                P.cp(R(Rb[0:C, :]), ps[0:C, 0:128], e="act")
                ps = P.ps("b2")
                for hd in range(2):
                    hs = slice(hd * 64, (hd + 1) * 64)
                    x_ = hd * nch + c
                    P.mmr(ps[0:C, hs], TTf[:, x_, :], Rb[0:C, hs])
                P.cp(R(Ub[0:C, :]), ps[0:C, 0:128], e="act")
                for hd in range(2):
                    hs = slice(hd * 64, (hd + 1) * 64)
                    x_ = hd * nch + c
                    oo = pso[hs, csl]
                    mmf = P.mmr if hd == 0 else P.mm
                    mmf(oo, Sc[hd][:, :], AR[hd][:, 1, csl], start=True, stop=False)
                    mmf(oo, Ub[0:C, hs], XA[0:C, x_, CB:CB + C], start=False, stop=False)
                    mmf(oo, vt[0:C, c, hs], XB[0:C, x_, CB:CB + C], start=False, stop=True)
                ps2 = P.ps("b2")
                for hd in range(2):
                    hs = slice(hd * 64, (hd + 1) * 64)
                    P.mmr(ps2[0:64, hs], bht[0:C, c, hs], Ub[0:C, hs], start=True, stop=False)
                    P.mmr(ps2[0:64, hs], kht[0:C, c, hs], vt[0:C, c, hs], start=False, stop=True)
                for hd in range(2):
                    hs = slice(hd * 64, (hd + 1) * 64)
                    P.stt(R(Sn[hd][:, :]), Sc[hd][:, :], WC[hd][:, c:c + 1], ps2[0:64, hs], ALU.mult, ALU.add)
                spar[seq] = 1 - spar[seq]
            P.cp(R(oTB[:, 0:NB]), pso[:, 0:NB], e="act")
            ps = P.ps("b2")
            P.mmr(ps[:, 0:NB], blk1[:, :], oTB[:, 0:NB])
            P.stt(ocB[:, 0:NB], ps[:, 0:NB], -1.0 / 64.0, oTB[:, 0:NB], ALU.mult, ALU.add)
            P.act(R(osB[:, 0:NB]), ocB[:, 0:NB], AF.Square)
            ps = P.ps("b2")
            P.mmr(ps[:, 0:NB], blk1[:, :], osB[:, 0:NB])
            P.act(s2tmp[:, 1, 0:NB], ps[:, 0:NB], AF.Ln, scale=1.0 / 64.0, bias=GN_EPS)
            P.act(s2tmp[:, 1, 0:NB], s2tmp[:, 1, 0:NB], AF.Exp, scale=-0.5)
            P.tt(ocB[:, 0:NB], ocB[:, 0:NB], s2tmp[:, 1, 0:NB], ALU.mult)
            P.ts(ocB[:, 0:NB], ocB[:, 0:NB], vecs[:, V_GW, pr:pr + 1], ALU.mult, vecs[:, V_GB, pr:pr + 1], ALU.add)
            P.tt(ocB[:, 0:NB], ocB[:, 0:NB], rkb[:, 0:NB], ALU.add)
            P.stt(ogB[:, 0:NB], ocB[:, 0:NB], 0.5, zsB[:, 0:NB], ALU.mult, ALU.mult)
            for tt0 in range(0, NB, 128):
                nt = min(128, NB - tt0)
                if seq == 0:
                    tile_i, prow = (t0 - PCOL + tt0) // 128, 0
                else:
                    tile_i, prow = 16, (seq - 1) * 32
                for nh in range(2):
                    ps = P.ps("b2")
                    P.mm(ps[prow:prow + nt, :], ogB[:, tt0:tt0 + nt], WO[:, nh * 512:(nh + 1) * 512])
                    xr = xres[tile_i][prow:prow + nt, nh * 512:(nh + 1) * 512]
                    P.tt(xr, xr, ps[prow:prow + nt, :], ALU.add)
            last = (t0 + NB == PCOL + TP) or seq > 0
            if last:
                Sf = StB[seq][spar[seq]]
                ps = P.ps("b2")
                for hd in range(2):
                    P.tr(ps[0:64, hd * 64:(hd + 1) * 64], Sf[hd][:, :], ident[0:64, 0:64])
                P.cp(stio[:, :], ps[0:64, 0:128])
                P.dma("sp", o_wkv[seq, 2 * pr:2 * pr + 2].rearrange("h v k -> v h k"),
                      stio[:].rearrange("v (h k) -> v h k", h=2), reads=[stio[:]], final=True)
        if pr + 1 < 8:
            load_pair_w(pr + 1)

    P.barrier()
    P.sb_off = blk_mark
    xn = [P.sb("xnF%d" % i, [128, D]) for i in range(3)]
    fnwb = P.sb("fnwb", [128, D])
    P.dma("sp", fnwb[:], final_norm_w.partition_broadcast(128), writes=[fnwb[:]])
    for i in range(NTILE):
        nt = tile_rows(i)
        xt = xres[i]
        xb = xn[i % 3]
        sl = slice(i % 4, i % 4 + 1)
        P.act(xb[0:nt, :], xt[0:nt, :], AF.Square, accum_out=ssq[0:nt, sl])
        P.act(rstd[0:nt, sl], ssq[0:nt, sl], AF.Sqrt, scale=1.0 / D, bias=RMS_EPS)
        P.recip(rstd[0:nt, sl], rstd[0:nt, sl])
        P.stt(xb[0:nt, :], xt[0:nt, :], rstd[0:nt, sl], fnwb[0:nt, :], ALU.mult, ALU.mult)
        if i < 16:
            P.dma("sp", y_p[i * 128:(i + 1) * 128, :], xb[:, :], reads=[xb[:]], final=True)
        else:
            P.dma("sp", y_s[0:16, :], xb[0:16, :], reads=[xb[:]], final=True)
            P.dma("sp", y_s[16:32, :], xb[32:48, :], reads=[xb[:]], final=True)
    P.finish()
    return nc


_NC_CACHE = {}


def make_in_maps(inputs):
    g = lambda k: np.ascontiguousarray(np.asarray(inputs[k], dtype=np.float32))
    xp, xs = g("x_prompt"), g("x_sample")
    cc, sd, ss, sw = g("cache_conv_a"), g("state_delta_a"), g("state_shift_b"), g("state_wkv_b")
    shared = {
        "norm_w": g("norm_w"), "final_norm_w": g("final_norm_w"), "a_w_in": g("a_w_in")[0],
        "a_conv_w": g("a_conv_w")[0], "a_log": g("a_log")[0], "a_dt_bias": g("a_dt_bias")[0],
        "a_norm_w": g("a_norm_w")[0], "a_w_out": g("a_w_out")[0], "b_mu": g("b_mu")[0],
        "b_w_in": g("b_w_in")[0], "b_w0": g("b_w0")[0], "b_w_w1": g("b_w_w1")[0], "b_w_w2": g("b_w_w2")[0],
        "b_a0": g("b_a0")[0], "b_a_w1": g("b_a_w1")[0], "b_a_w2": g("b_a_w2")[0], "b_k_k": g("b_k_k")[0],
        "b_k_a": g("b_k_a")[0], "b_r_k": g("b_r_k")[0].reshape(-1), "b_gn_w": g("b_gn_w")[0],
        "b_gn_b": g("b_gn_b")[0], "b_w_out": g("b_w_out")[0],
    }
    maps = []
    for i in range(8):
        m = dict(shared)
        m["x_p"] = xp[i]
        m["x_s"] = np.ascontiguousarray(xs[2 * i:2 * i + 2].reshape(2 * TS, D))
        m["conv_s"] = np.ascontiguousarray(cc[0, 2 * i:2 * i + 2])
        m["delta_s"] = np.ascontiguousarray(sd[0, 2 * i:2 * i + 2])
        m["shift_s"] = np.ascontiguousarray(ss[0, 2 * i:2 * i + 2])
        m["wkv_s"] = np.ascontiguousarray(sw[0, 2 * i:2 * i + 2])
        maps.append(m)
    return maps


def kernel(**inputs):
    if "nc" not in _NC_CACHE:
        _NC_CACHE["nc"] = build()
    nc = _NC_CACHE["nc"]
    maps = make_in_maps(inputs)
    res = run_bass_kernel_spmd(nc, maps, core_ids=list(range(8)))
    R = res.results
    y_prompt = np.stack([R[i]["y_p"] for i in range(8)], 0)
    y_sample = np.concatenate([R[i]["y_s"].reshape(2, TS, D) for i in range(8)], 0)

    def pick(name, sl):
        return np.stack([R[i][name][sl] for i in range(8)], 0)[None] if isinstance(sl, int) else \
            np.concatenate([R[i][name][sl] for i in range(8)], 0)[None]

    p_conv, s_conv = pick("o_conv", 0), pick("o_conv", slice(1, 3))
    p_delta, s_delta = pick("o_delta", 0), pick("o_delta", slice(1, 3))
    p_shift, s_shift = pick("o_shift", 0), pick("o_shift", slice(1, 3))
    p_wkv, s_wkv = pick("o_wkv", 0), pick("o_wkv", slice(1, 3))
    return (y_prompt, y_sample, p_conv, p_delta, p_shift, p_wkv, s_conv, s_delta, s_shift, s_wkv)
```

```python
import numpy as np
import concourse.bass as bass
import concourse.mybir as mybir
from concourse.bass_utils import run_bass_kernel_spmd

F32 = mybir.dt.float32
BF16 = mybir.dt.bfloat16
AF = mybir.ActivationFunctionType
ALU = mybir.AluOpType
AX = mybir.AxisListType

D = 1024
KC = 8
TP = 2048
TS = 16
NTILE = 17
H_A = 8
RMS_EPS = 1e-6
GN_EPS = 64e-5
NEG = -30000.0


class Trk:
    __slots__ = ("lw", "rd", "dsem", "dcount")

    def __init__(self):
        self.lw = None
        self.rd = []
        self.dsem = None
        self.dcount = 0


def fsz(ap):
    n = 1
    for d in ap.shape[1:]:
        n *= d
    return n


class Prog:
    WINDOW = 4000
    LAT = 900.0
    LAT_SAME = 300.0
    PE_SWITCH = 0.0
    EPS = 150.0

    def __init__(self, nc):
        self.nc = nc
        self.h = {"pe": nc.tensor, "act": nc.scalar, "dve": nc.vector, "pool": nc.gpsimd, "sp": nc.sync}
        self.sem = {k: nc.alloc_semaphore("sem_" + k) for k in self.h}
        self.cnt = {k: 0 for k in self.h}
        self.known = {k: {} for k in self.h}
        self.trk = {}
        self.ops = []
        self.nps = 0
        self.npool = {}
        self.psb = []
        self.nsem = 0
        self.sb_off = nc.sbuf_base
        self.sb_top = nc.sbuf_top
        self.seg_start = 0
        self.sel = {}
        self.pecls = {}

    def sb(self, name, shape, dt=F32):
        n = 1
        for d in shape[1:]:
            n *= d
        nbytes = n * (2 if dt == BF16 else 4)
        off = (self.sb_off + 63) // 64 * 64
        assert off + nbytes <= self.sb_top, "SBUF overflow at %s: need %d have %d" % (name, nbytes, self.sb_top - off)
        t = self.nc.alloc_sbuf_tensor_at(name, list(shape), dt, offset=off)
        self.sb_off = off + nbytes
        self.trk[t.name] = Trk()
        return t

    def barrier(self):
        self.ops.append(("fence", None, None, (), 0.0, False, None))
        for t in self._all_trk():
            t.lw = None
            t.rd = []

    def _all_trk(self):
        for t in self.trk.values():
            if isinstance(t, list):
                for x in t:
                    yield x
            else:
                yield t

    def init_psum(self):
        for i in range(8):
            t = self.nc.alloc_psum_tensor("psb%d" % i, [128, 512], F32)
            self.trk["psb%d" % i] = Trk()
            self.psb.append(t)

    POOLS = {0: (0, 1, 2, 3, 4, 5, 6), 2: (0, 1, 2, 3, 4, 5, 6),
             "a1a": (0, 1), "a1m": (2,), "a1b": (3, 4), "a2": (5, 6),
             "b1": (0, 1, 2, 3, 4), "b2": (5, 6)}

    def ps(self, pool=0):
        banks = self.POOLS[pool]
        n = self.npool.get(pool, 0)
        self.npool[pool] = n + 1
        return self.psb[banks[n % len(banks)]]

    def split(self, tensor, n):
        self.trk[tensor.name] = [Trk() for _ in range(n)]

    def only(self, **sel):
        prog = self

        class _Ctx:
            def __enter__(self_):
                self_.old = dict(prog.sel)
                prog.sel.update(sel)

            def __exit__(self_, *a):
                prog.sel = self_.old
        return _Ctx()

    def _tks(self, ap):
        t = self.trk[ap.tensor.name]
        if isinstance(t, list):
            idx = self.sel.get(ap.tensor.name.rsplit("_", 1)[0])
            if idx is not None:
                return [t[i] for i in idx]
            try:
                pat = ap.ap
                F = 1
                for d in ap.tensor.shape[1:]:
                    F *= d
                pstride = pat[0][0]
                off = int(ap.offset)
                if pstride != F:
                    return list(t)
                f0 = off % F
                ext = 1
                for st, cnt in pat[1:]:
                    ext += (cnt - 1) * abs(st)
                gsz = F // len(t)
                g0 = f0 // gsz
                g1 = (f0 + ext - 1) // gsz
                if g0 < 0 or g1 >= len(t):
                    return list(t)
                return [t[i] for i in range(g0, g1 + 1)]
            except Exception:
                return list(t)
        return [t]

    def _record(self, kind, e, payload, reads, writes, cost, final=False):
        rt = []
        for a in reads:
            for t in self._tks(a):
                if t not in rt:
                    rt.append(t)
        wt = []
        for a in writes:
            for t in self._tks(a):
                if t not in wt:
                    wt.append(t)
        i = len(self.ops)
        preds = set()
        for t in rt:
            if t.lw is not None:
                preds.add(t.lw)
        for t in wt:
            if t.lw is not None:
                preds.add(t.lw)
            preds.update(t.rd)
        preds.discard(i)
        for t in rt:
            t.rd.append(i)
        for t in wt:
            t.lw = i
            t.rd = []
        t0 = (wt + rt)[0] if kind == "dma" else None
        self.ops.append((kind, e, payload, tuple(preds), float(cost), final, t0))
        return i

    def op(self, e, fn, reads, writes, cost=None):
        if cost is None:
            n = fsz(writes[0]) if writes else 64
            cost = {"act": 200.0 + 0.85 * n, "dve": 110.0 + 1.05 * n, "pool": 260.0 + 1.0 * n, "pe": 150.0}[e]
        return self._record("op", e, fn, reads, writes, cost)

    def dma(self, q, out, in_, reads=(), writes=(), final=False, **kw):
        return self._record("dma", q, (out, in_, kw), reads, writes, 150.0 if q == "sp" else 1200.0, final)

    def _wait(self, e, key, semh, val):
        k = self.known[e]
        if k.get(key, 0) < val:
            self.h[e].wait_ge(semh, val)
            k[key] = val

    def _schedule(self, lo, hi):
        ops = self.ops
        n = hi - lo
        indeg = [0] * n
        succ = [[] for _ in range(n)]
        for i in range(lo, hi):
            ps_ = [p for p in ops[i][3] if p >= lo]
            indeg[i - lo] = len(ps_)
            for p in ps_:
                succ[p - lo].append(i)
        blev = [0.0] * n
        for k in range(n - 1, -1, -1):
            o = ops[lo + k]
            c = o[4] + (2500.0 if o[0] == "dma" else 0.0)
            m = 0.0
            for j in succ[k]:
                v = blev[j - lo] + self.LAT
                if v > m:
                    m = v
            blev[k] = c + m
        dready = [0.0] * n
        etime = {k: 0.0 for k in self.h}
        ready = {k: [] for k in self.h}
        for i in range(lo, hi):
            if indeg[i - lo] == 0:
                ready[ops[i][1]].append(i)
        order = []
        done = [False] * n
        lastcls = None
        minp = lo
        W = self.WINDOW
        EPS = self.EPS
        while len(order) < n:
            while minp < hi and done[minp - lo]:
                minp += 1
            lim = minp + W
            best = None
            for e, lst in ready.items():
                if not lst:
                    continue
                te = etime[e]
                cand = None
                for i in lst:
                    if i >= lim:
                        continue
                    dr = dready[i - lo]
                    st = te if dr <= te + EPS else dr
                    if e == "pe" and self.pecls.get(i) != lastcls:
                        st += self.PE_SWITCH
                    key = (st, -blev[i - lo], i)
                    if cand is None or key < cand:
                        cand = key
                if cand is not None and (best is None or cand < best[0]):
                    best = (cand, e)
            (st, _, i), e = best
            st = max(st, dready[i - lo], etime[e])
            ready[e].remove(i)
            kind = ops[i][0]
            cost = ops[i][4]
            if e == "pe":
                cl = self.pecls.get(i)
                if cl != lastcls:
                    cost += self.PE_SWITCH
                lastcls = cl
            etime[e] = st + cost
            f = st + cost + (2500.0 if kind == "dma" else 0.0)
            done[i - lo] = True
            order.append(i)
            for j in succ[i - lo]:
                ej = ops[j][1]
                v = f + (0.0 if (e == "pe" and ej == "pe") else (self.LAT_SAME if ej == e else self.LAT))
                if v > dready[j - lo]:
                    dready[j - lo] = v
                indeg[j - lo] -= 1
                if indeg[j - lo] == 0:
                    ready[ops[j][1]].append(j)
        return order, max(etime.values())

    def finish(self):
        ops = self.ops
        bounds = [i for i, o in enumerate(ops) if o[0] == "fence"] + [len(ops)]
        needs_inc = [False] * len(ops)
        info = {}
        clock = {}
        finals = []
        lo = 0
        est_total = 0.0
        nwait = 0
        last_inc = {k: None for k in self.h}
        for b in bounds:
            order, est = self._schedule(lo, b)
            est_total += est
            pos = {i: k for k, i in enumerate(order)}
            kept = {}
            lastop = {}
            for i in order:
                kind, e = ops[i][0], ops[i][1]
                if kind == "op":
                    lastop[e] = i
                best = {}
                keep = []
                for p in ops[i][3]:
                    if p < lo:
                        continue
                    if ops[p][0] != "op":
                        keep.append(p)
                        continue
                    f = ops[p][1]
                    if f == "pe" and e == "pe":
                        continue
                    if f not in best or pos[p] > pos[best[f]]:
                        best[f] = p
                for p in best.values():
                    needs_inc[p] = True
                    keep.append(p)
                kept[i] = keep
            for i in lastop.values():
                needs_inc[i] = True
            for i in order:
                kind, e, payload, preds, cost, final, t0 = ops[i]
                kn = self.known[e]
                for p in sorted(kept[i], key=lambda x: pos[x]):
                    if p < lo:
                        continue
                    pi = info[p]
                    if pi[0] == "op":
                        f, c = pi[1], pi[2]
                        if f == "pe" and e == "pe":
                            continue
                        if kn.get(f, 0) < c:
                            self.h[e].wait_ge(self.sem[f], c)
                            nwait += 1
                            kn[f] = c
                            for g, v in clock[p].items():
                                if kn.get(g, 0) < v:
                                    kn[g] = v
                    else:
                        if kn.get(pi[3], 0) < pi[2]:
                            self.h[e].wait_ge(pi[1], pi[2])
                            nwait += 1
                            kn[pi[3]] = pi[2]
                if kind == "op":
                    ins = payload(self.h[e])
                    if needs_inc[i]:
                        self.cnt[e] += 1
                        ins.then_inc(self.sem[e], 1)
                        info[i] = ("op", e, self.cnt[e])
                        snap = {g: v for g, v in kn.items() if g in self.h}
                        snap[e] = self.cnt[e]
                        clock[i] = snap
                    else:
                        info[i] = ("op", e, self.cnt[e] + 1)
                        clock[i] = {}
                else:
                    out, in_, kw = payload
                    if t0.dsem is None:
                        t0.dsem = self.nc.alloc_semaphore("dsem%d" % self.nsem)
                        self.nsem += 1
                    ins = self.h[e].dma_start(out=out, in_=in_, **kw)
                    t0.dcount += 16
                    ins.then_inc(t0.dsem, 16)
                    info[i] = ("dma", t0.dsem, t0.dcount, "d%d" % id(t0))
                    if final:
                        finals.append(info[i])
            lo = b + 1
            if b < len(ops):
                for e in self.h:
                    for f in self.h:
                        if f != e and self.cnt[f] > 0:
                            self._wait(e, f, self.sem[f], self.cnt[f])
                    for t in self._all_trk():
                        if t.dsem is not None and t.dcount > 0:
                            self._wait(e, "d%d" % id(t), t.dsem, t.dcount)
        fmax = {}
        for (_, semh, c, key) in finals:
            if key not in fmax or c > fmax[key][1]:
                fmax[key] = (semh, c)
        for key, (semh, c) in fmax.items():
            self._wait("sp", key, semh, c)
        print("scheduler estimate: %.1f us, %d ops, %d waits, incs %s" % (est_total / 1e3, len(ops), nwait, dict(self.cnt)))

    def mmr(self, out, lhsT, rhs, start=True, stop=True):
        return self.mm(out, R(lhsT), R(rhs), start=start, stop=stop)

    def mm(self, out, lhsT, rhs, start=True, stop=True):
        passes = 4.0 if rhs.dtype == F32 else 1.0
        cost = 70.0 + passes * 0.42 * (fsz(rhs) + min(fsz(lhsT), 128))
        i = self.op("pe", lambda h: h.matmul(out, lhsT=lhsT, rhs=rhs, start=start, stop=stop),
                    [lhsT, rhs], [out], cost=cost)
        self.pecls[i] = str(rhs.dtype)
        return i

    def tr(self, out, in_, ident):
        i = self.op("pe", lambda h: h.transpose(out, in_, ident), [in_, ident], [out], cost=160.0)
        self.pecls[i] = "tr"
        return i

    def act(self, out, in_, func, e="act", **kw):
        rd = [in_] + [v for v in kw.values() if hasattr(v, "tensor")]
        wr = [out]
        if "accum_out" in kw:
            wr.append(kw["accum_out"])
            rd.remove(kw["accum_out"])
        return self.op("act", lambda h: h.activation(out=out, in_=in_, func=func, **kw), rd, wr)

    def tt(self, out, in0, in1, op, e="dve"):
        return self.op(e, lambda h: h.tensor_tensor(out=out, in0=in0, in1=in1, op=op), [in0, in1], [out])

    def ts(self, out, in0, s1, op0, s2=None, op1=None, e="dve"):
        rd = [in0] + [v for v in (s1, s2) if hasattr(v, "tensor")]
        if op1 is None:
            return self.op(e, lambda h: h.tensor_scalar(out=out, in0=in0, scalar1=s1, scalar2=None, op0=op0),
                           rd, [out])
        return self.op(e, lambda h: h.tensor_scalar(out=out, in0=in0, scalar1=s1, scalar2=s2, op0=op0, op1=op1),
                       rd, [out])

    def stt(self, out, in0, scalar, in1, op0, op1):
        rd = [in0, in1] + ([scalar] if hasattr(scalar, "tensor") else [])
        return self.op("dve", lambda h: h.scalar_tensor_tensor(out=out, in0=in0, scalar=scalar, in1=in1,
                                                                 op0=op0, op1=op1), rd, [out])

    def cp(self, out, in_, e="dve"):
        if e == "act":
            return self.act(out, in_, AF.Copy)
        return self.op(e, lambda h: h.tensor_copy(out=out, in_=in_), [in_], [out])

    def scan(self, out, d0, d1):
        return self.op("dve", lambda h: h.tensor_tensor_scan(out=out, data0=d0, data1=d1, initial=0.0,
                                                              op0=ALU.mult, op1=ALU.add),
                       [d0, d1], [out], cost=110.0 + 2.1 * fsz(out))

    def rsqrt_pool(self, out, in_, mhalf):
        return self.op("pool", lambda h: h.tensor_tensor(out=out, in0=in_, in1=mhalf, op=ALU.pow), [in_, mhalf], [out])

    def recip(self, out, in_):
        return self.op("dve", lambda h: h.reciprocal(out=out, in_=in_), [in_], [out], cost=110.0 + 3.0 * fsz(out))

    def memset(self, ap, val, e="pool"):
        return self.op(e, lambda h: h.memset(ap, val), [], [ap])

    def asel(self, out, in_, pattern, cmp, fill, base, cm):
        return self.op("pool", lambda h: h.affine_select(out=out, in_=in_, pattern=pattern, compare_op=cmp,
                                                          fill=fill, base=base, channel_multiplier=cm),
                       [in_], [out])


F32R = mybir.dt.float32r


def R(ap):
    return ap.bitcast(F32R)


def neumann(P, ident, MT0, M0, Mbuf, MTT, nx, C, pool=0):
    L = {128: 7, 64: 6, 16: 4}[C]
    G = 512 // (2 * C)
    ngrp = (nx + G - 1) // G
    names = [m.name.rsplit("_", 1)[0] for m in MTT]
    split = ngrp > 1 and all(isinstance(P.trk[m.name], list) for m in MTT)

    def grp_only(g):
        if not split:
            return P.only()
        return P.only(**{nm: [g] for nm in names})

    def grp3(ps, n, w):
        return ps[0:C, 0:n * w].rearrange("p (x i) -> p x i", x=n)

    psa = P.ps(pool)
    psb = P.ps(pool)
    for x in range(nx):
        P.mm(psa[0:C, x * C:(x + 1) * C], R(MT0[:, x, :]), R(Mbuf[0][0:C, x, 0:C]))
        P.mm(psb[0:C, x * C:(x + 1) * C], R(Mbuf[0][0:C, x, 0:C]), R(MT0[:, x, :]))
    P.cp(R(Mbuf[1][0:C, 0:nx, 0:C]), grp3(psa, nx, C), e="act")
    P.cp(R(MTT[0][0:C, 0:nx, 0, 0:C]), grp3(psb, nx, C), e="act")
    P.tt(R(MTT[0][0:C, 0:nx, 1, 0:C]), MT0, bcm(ident[0:C, 0:C], [C, nx, C]), ALU.add)
    yield
    cm, ct = 1, 0
    for lev in range(2, L + 1):
        last = lev == L
        Mc = Mbuf[cm]
        cur = MTT[ct]
        nxt = MTT[1 - ct]
        if not last:
            psa = P.ps(pool)
            for x in range(nx):
                P.mm(psa[0:C, x * C:(x + 1) * C], R(cur[0:C, x, 0, 0:C]), R(Mc[0:C, x, 0:C]))
        for x0 in range(0, nx, G):
            n = min(G, nx - x0)
            psx = P.ps(pool)
            with grp_only(x0 // G):
                for j in range(n):
                    x = x0 + j
                    if last:
                        P.mm(psx[0:C, j * C:(j + 1) * C], R(Mc[0:C, x, 0:C]), R(cur[0:C, x, 1, 0:C]))
                    else:
                        P.mm(psx[0:C, j * 2 * C:(j + 1) * 2 * C], R(Mc[0:C, x, 0:C]), R(cur[0:C, x, :, 0:C]))
                if last:
                    P.tt(R(nxt[0:C, x0:x0 + n, 1, 0:C]), grp3(psx, n, C), cur[0:C, x0:x0 + n, 1, 0:C], ALU.add)
                else:
                    pv = psx[0:C, 0:n * 2 * C].rearrange("p (x a i) -> p x a i", x=n, a=2)
                    P.cp(R(nxt[0:C, x0:x0 + n, 0, 0:C]), pv[:, :, 0, :], e="act")
                    P.tt(R(nxt[0:C, x0:x0 + n, 1, 0:C]), pv[:, :, 1, :], cur[0:C, x0:x0 + n, 1, 0:C], ALU.add)
        if not last:
            P.cp(R(Mbuf[1 - cm][0:C, 0:nx, 0:C]), grp3(psa, nx, C), e="act")
        cm = 1 - cm
        ct = 1 - ct
        yield
    return MTT[ct][0:C, 0:nx, 1, 0:C]


def bc(ap, shape):
    return ap.unsqueeze(len(ap.shape)).broadcast_to(list(shape))


def bcm(ap, shape):
    return ap.unsqueeze(1).broadcast_to(list(shape))


PCOL = 1
S1COL = TP + 1 + 1
S2COL = S1COL + 32
NTOK = S2COL + 16 + 1
NBA = 256
CA = 128
NCH = TP // CA + 2


def build(stop=None):
    nc = bass.Bass("TRN2", target_bir_lowering=False)
    P = Prog(nc)
    P.init_psum()

    def din(name, shape):
        return nc.dram_tensor(name, list(shape), F32, kind="ExternalInput").ap()

    def dout(name, shape):
        return nc.dram_tensor(name, list(shape), F32, kind="ExternalOutput").ap()

    x_p = din("x_p", [TP, D])
    x_s = din("x_s", [2 * TS, D])
    conv_s = din("conv_s", [2, 3, 4096])
    delta_s = din("delta_s", [2, 8, 128, 256])
    shift_s = din("shift_s", [2, D])
    wkv_s = din("wkv_s", [2, 16, 64, 64])
    norm_w = din("norm_w", [2, D])
    final_norm_w = din("final_norm_w", [D])
    a_w_in = din("a_w_in", [D, 6160])
    a_conv_w = din("a_conv_w", [4, 4096])
    a_log = din("a_log", [8])
    a_dt_bias = din("a_dt_bias", [8])
    a_norm_w = din("a_norm_w", [256])
    a_w_out = din("a_w_out", [2048, D])
    b_mu = din("b_mu", [6, D])
    b_w_in = din("b_w_in", [D, 4096])
    b_w0 = din("b_w0", [D])
    b_w_w1 = din("b_w_w1", [D, 64])
    b_w_w2 = din("b_w_w2", [64, D])
    b_a0 = din("b_a0", [D])
    b_a_w1 = din("b_a_w1", [D, 64])
    b_a_w2 = din("b_a_w2", [64, D])
    b_k_k = din("b_k_k", [D])
    b_k_a = din("b_k_a", [D])
    b_r_k = din("b_r_k", [D])
    b_gn_w = din("b_gn_w", [D])
    b_gn_b = din("b_gn_b", [D])
    b_w_out = din("b_w_out", [D, D])

    y_p = dout("y_p", [TP, D])
    y_s = dout("y_s", [2 * TS, D])
    o_conv = dout("o_conv", [3, 3, 4096])
    o_delta = dout("o_delta", [3, 8, 128, 256])
    o_shift = dout("o_shift", [3, D])
    o_wkv = dout("o_wkv", [3, 16, 64, 64])
    dbg = dout("dbg", [NTILE * 128, D]) if stop else None

    ident = P.sb("ident", [128, 128])
    ones = P.sb("ones", [128, 128])
    mones = P.sb("mones", [128, 128])
    zeros = P.sb("zeros", [128, 128])
    Utri = P.sb("Utri", [128, 128])
    NEGT = P.sb("NEGT", [128, 128])
    MsT = P.sb("MsT", [128, 128])
    P.memset(ones[:], 1.0)
    ones_r = P.sb("ones_r", [128, 128])
    P.cp(R(ones_r[:]), ones[:], e="act")
    P.memset(mones[:], -1.0)
    P.memset(zeros[:], 0.0)
    P.asel(ident[:], ones[:], [[-1, 128]], ALU.is_equal, 0.0, 0, 1)
    P.asel(Utri[:], ones[:, :], [[1, 128]], ALU.is_ge, 0.0, 0, -1)
    P.asel(NEGT[:], zeros[:, :], [[1, 128]], ALU.is_ge, NEG, 0, -1)
    P.asel(MsT[:], ones[:, :], [[1, 128]], ALU.is_gt, 0.0, 0, -1)

    xres = [P.sb("xres%d" % i, [128, D]) for i in range(NTILE)]
    nw = P.sb("nw", [128, 2, KC])
    fnw = P.sb("fnw", [128, KC])
    P.dma("sp", nw[:], norm_w.rearrange("l (k p) -> p l k", p=128), writes=[nw[:]], allow_slow_non_contiguous=True)
    P.dma("sp", fnw[:], final_norm_w.rearrange("(k p) -> p k", p=128), writes=[fnw[:]],
          allow_slow_non_contiguous=True)
    ssq = P.sb("ssq", [128, 4])
    rstd = P.sb("rstd", [128, 4])
    P.split(ssq, 4)
    P.split(rstd, 4)
    phase_mark = P.sb_off
    hT = P.sb("hT", [128, KC, NTOK], BF16)
    xn = [P.sb("xn%d" % i, [128, D]) for i in range(2)]

    def tile_rows(i):
        return 128 if i < 16 else 48

    def tile_col(i):
        return PCOL + i * 128 if i < 16 else S1COL

    def norm_to_hT(layer, hT):
        for i in range(NTILE):
            nt = tile_rows(i)
            xt = xres[i]
            xb = xn[i % 2]
            sl = slice(i % 4, i % 4 + 1)
            P.act(xb[0:nt, :], xt[0:nt, :], AF.Square, accum_out=ssq[0:nt, sl])
            P.act(rstd[0:nt, sl], ssq[0:nt, sl], AF.Sqrt, scale=1.0 / D, bias=RMS_EPS)
            P.recip(rstd[0:nt, sl], rstd[0:nt, sl])
            P.ts(xb[0:nt, :], xt[0:nt, :], rstd[0:nt, sl], ALU.mult)
            c0 = tile_col(i)
            for half in range(2):
                ps = P.ps()
                for j in range(4):
                    kc = half * 4 + j
                    P.tr(ps[:, j * 128:j * 128 + nt], xb[0:nt, kc * 128:(kc + 1) * 128], ident[0:nt, 0:nt])
                pv = ps[:, :].rearrange("p (j t) -> p j t", j=4)[:, :, 0:nt]
                P.tt(hT[:, half * 4:half * 4 + 4, c0:c0 + nt], pv,
                     bc(nw[:, layer, half * 4:half * 4 + 4], [128, 4, nt]), ALU.mult)

    for i in range(NTILE):
        if i < 16:
            P.dma("sp", xres[i][:], x_p[i * 128:(i + 1) * 128, :], writes=[xres[i][:]])
        else:
            P.memset(xres[i][:], 0.0)
            P.dma("sp", xres[i][0:16, :], x_s[0:16, :], writes=[xres[i][:]])
            P.dma("sp", xres[i][32:48, :], x_s[16:32, :], writes=[xres[i][:]])
    P.memset(hT[:, :, 0:1], 0.0)
    norm_to_hT(0, hT)

    blocks = [(0, PCOL + i * NBA, NBA, CA, NBA // CA, i * (NBA // CA)) for i in range(TP // NBA)] + \
             [(1, S1COL, 16, 16, 1, NCH - 2), (2, S2COL, 16, 16, 1, NCH - 1)]

    cwt = P.sb("cwt", [32, 4, 128])
    cw = P.sb("cw", [128, 4, 32])
    P.dma("sp", cwt[:], a_conv_w.rearrange("t (g c) -> g t c", c=128), writes=[cwt[:]])
    ps = P.ps()
    for t in range(4):
        P.tr(ps[:, t * 32:(t + 1) * 32], cwt[:, t, :], ident[0:32, 0:32])
    P.cp(cw[:].rearrange("p t g -> p (t g)"), ps[:, 0:128])
    halo_all = P.sb("halo_all", [128, 2, 3, 32])
    hrow = P.sb("hrow", [96, 2, 128])
    for s in range(2):
        P.dma("sp", hrow[:, s, :], conv_s[s].rearrange("t (g c) -> (t g) c", c=128), writes=[hrow[:]])
    ps = P.ps()
    for s in range(2):
        P.tr(ps[:, s * 96:(s + 1) * 96], hrow[:, s, :], ident[0:96, 0:96])
    P.cp(halo_all[:].rearrange("p s t g -> p (s t g)"), ps[:, 0:192])
    fin_all = P.sb("fin_all", [128, 3, 3, 32])
    anw = P.sb("anw", [128, 2])
    P.dma("sp", anw[:], a_norm_w.rearrange("(h p) -> p h", p=128), writes=[anw[:]], allow_slow_non_contiguous=True)
    P.ts(anw[:], anw[:], 0.5, ALU.mult)

    wba = P.sb("wba", [128, KC, 16], BF16)
    P.dma("pool", wba[:], a_w_in.rearrange("(k p) c -> p k c", p=128)[:, :, 6144:6160], writes=[wba[:]])
    NCP = TP // CA
    BA = P.sb("BA", [CA, NCH, 16])
    P.memset(BA[:], 0.0)
    ps = P.ps()
    for c in range(NCP):
        for kc in range(KC):
            P.mm(ps[0:CA, c * 16:(c + 1) * 16], hT[:, kc, PCOL + c * CA:PCOL + (c + 1) * CA], wba[:, kc, :],
                 start=(kc == 0), stop=(kc == KC - 1))
    P.cp(BA[:, 0:NCP, :].rearrange("p c k -> p (c k)"), ps[0:CA, 0:NCP * 16])
    ps = P.ps()
    for s, sc in enumerate((S1COL, S2COL)):
        for kc in range(KC):
            P.mm(ps[0:16, s * 16:(s + 1) * 16], hT[:, kc, sc:sc + 16], wba[:, kc, :],
                 start=(kc == 0), stop=(kc == KC - 1))
    P.cp(BA[0:16, NCP:NCP + 2, :].rearrange("p c k -> p (c k)"), ps[0:16, 0:32])
    alg = P.sb("alg", [CA, 8])
    dtb = P.sb("dtb", [CA, 8])
    P.dma("sp", alg[:], a_log.partition_broadcast(CA), writes=[alg[:]])
    P.dma("sp", dtb[:], a_dt_bias.partition_broadcast(CA), writes=[dtb[:]])
    P.act(alg[:], alg[:], AF.Exp)
    P.ts(alg[:], alg[:], -1.0, ALU.mult)
    beta = P.sb("beta", [CA, NCH, 8])
    gg = P.sb("gg", [CA, NCH, 8])
    Gc = P.sb("Gc", [CA, NCH, 8])
    Glb = P.sb("Glb", [128, NCH, 8])
    gl = P.sb("gl", [128, NCH, 8])
    eG = P.sb("eG", [CA, NCH, 8])
    bG = P.sb("bG", [CA, NCH, 8])
    dte = P.sb("dte", [CA, NCH, 8])
    P.act(beta[:], BA[:, :, 0:8], AF.Sigmoid)
    P.tt(gg[:], BA[:, :, 8:16], bcm(dtb[:], [CA, NCH, 8]), ALU.add)
    P.act(gg[:], gg[:], AF.Exp)
    P.act(gg[:], gg[:], AF.Ln, bias=1.0)
    P.tt(gg[:], gg[:], bcm(alg[:], [CA, NCH, 8]), ALU.mult)
    psG = P.ps()
    psL = P.ps()
    g2 = gg[:].rearrange("p c k -> p (c k)")
    GP = NCP * 8
    P.mm(psG[0:CA, 0:GP], Utri[0:CA, 0:CA], g2[:, 0:GP])
    P.mm(psG[0:16, GP:GP + 16], Utri[0:16, 0:16], g2[0:16, GP:GP + 16])
    P.mm(psL[:, 0:GP], ones[0:CA, :], g2[:, 0:GP])
    P.mm(psL[:, GP:GP + 16], ones[0:16, :], g2[0:16, GP:GP + 16])
    P.memset(Gc[:], 0.0)
    P.cp(Gc[:, 0:NCP, :].rearrange("p c k -> p (c k)"), psG[0:CA, 0:GP])
    P.cp(Gc[0:16, NCP:NCP + 2, :].rearrange("p c k -> p (c k)"), psG[0:16, GP:GP + 16])
    P.cp(Glb[:].rearrange("p c k -> p (c k)"), psL[:, 0:GP + 16])
    P.act(gl[:], Glb[:], AF.Exp)
    P.act(eG[:], Gc[:], AF.Exp)
    P.tt(bG[:], beta[:], eG[:], ALU.mult)
    hbeta = eG
    P.ts(hbeta[:], beta[:], 0.5, ALU.mult)
    P.tt(dte[:], Glb[0:CA], Gc[:], ALU.subtract)
    P.act(dte[:], dte[:], AF.Exp)

    Wh = [P.sb("Wh0", [128, 6, KC, 128], BF16)] * 2
    P.split(Wh[0], 6)
    Wo = [P.sb("Wo0", [128, 2, D], BF16)] * 2
    w_in_v = a_w_in.rearrange("(k p) c -> p k c", p=128)

    def load_head_w(h):
        sl = h % 2
        for (c0, m_) in ((h * 128, 0), (1024 + h * 128, 1), (2048 + h * 256, 2), (2048 + h * 256 + 128, 3),
                         (4096 + h * 256, 4), (4096 + h * 256 + 128, 5)):
            P.dma("pool", Wh[sl][:, m_, :, :], w_in_v[:, :, c0:c0 + 128], writes=[Wh[sl][:, m_, :, :]])

    def load_head_wo(h):
        sl = h % 2
        P.dma("pool", Wo[sl][:], a_w_out[h * 256:(h + 1) * 256, :].rearrange("(hh p) c -> p hh c", p=128),
              writes=[Wo[sl][:]])

    NB_ = NBA
    NC_ = NBA // CA
    pre = P.sb("pre", [128, 4, NB_ + 3])
    acc = P.sb("acc", [128, 4, NB_])
    P.split(acc, 4)
    P.split(pre, 4)
    qkv = P.sb("qkv", [128, 4, NB_])
    zs2 = [P.sb("zs%d" % i, [128, 2, NB_]) for i in range(2)]
    sqr = P.sb("sqr", [128, 2, NB_])
    sq = sqr
    rq = acc[:, 2:4]
    oT = P.sb("oTp", [128, 2, NB_])
    osq = P.sb("osq2", [128, 2, NB_])
    qT = P.sb("qT", [128, NB_])
    kT = P.sb("kT", [128, NB_])
    kbT = P.sb("kbT", [128, NB_])
    qgT2 = [P.sb("qgT%d" % i, [128, NB_]) for i in range(2)]
    dg = P.sb("dg", [128, 2, NB_])
    eGb = P.sb("eGb", [128, NB_])
    betab = P.sb("betab", [128, NB_])
    kbeG2 = [P.sb("kbeG%d" % i, [CA, NC_, 128]) for i in range(2)]
    ktk2 = [P.sb("ktk%d" % i, [CA, NC_, 128]) for i in range(2)]
    vb2 = [P.sb("vb%d" % i, [CA, NC_, 256]) for i in range(2)]
    gU = P.sb("gU", [CA, NC_, CA])
    DT = P.sb("DT", [CA, NC_, CA])
    DTs = P.sb("DTs", [CA, NC_, CA])
    qkT2 = [P.sb("qkT%d" % i, [CA, NC_, CA]) for i in range(2)]
    Mn = [P.sb("Mn%d" % i, [CA, NC_, CA]) for i in range(2)]
    MT0a2 = [P.sb("MT0a%d" % i, [CA, NC_, CA]) for i in range(2)]
    MTTa = [P.sb("MTTa%d" % i, [CA, NC_, 2, CA]) for i in range(2)]
    u0 = P.sb("u0", [CA, NC_, 256])
    wkT = P.sb("wkT", [128, NB_])
    uu = [P.sb("uu%d" % i, [CA, 256]) for i in range(2)]
    Sp_ = [P.sb("Sp%d" % i, [128, 256]) for i in range(2)]
    Ss_ = [P.sb("Ss%d" % i, [128, 256]) for i in range(2)]
    Sst = [Sp_, Ss_, Ss_]
    ors = P.sb("ors", [128, NB_])
    og = P.sb("og", [128, 2, NB_], BF16)
    print("SBUF left after layer-A alloc:", P.sb_top - P.sb_off)
    for t_, n_ in [(qkv, 4), (oT, 2), (osq, 2), (sqr, 2), (dg, 2), (gU, NC_), (DT, NC_), (DTs, NC_), (u0, NC_),
                   (og, 2)] + [(x_, 2) for x_ in zs2] + [(x_, NC_) for x_ in kbeG2 + ktk2 + vb2 + qkT2 + Mn + MT0a2 + MTTa]:
        P.split(t_, n_)

    DONE = object()

    def A_s1(it, h, blk):
        (seq, t0, NB, C, nch, c0) = blk
        par = it % 2
        W = Wh[0]
        ggrp = (h, 8 + h, 16 + 2 * h, 17 + 2 * h)
        zs, qgT, ktk, qkT = zs2[par], qgT2[par], ktk2[par], qkT2[par]
        kbeG, vb, MT0a = kbeG2[par], vb2[par], MT0a2[par]
        first_of_seq = (seq == 0 and t0 == PCOL) or seq > 0
        if seq == 0 and t0 == PCOL:
            load_head_w(h)
        if first_of_seq:
            if seq == 0:
                P.memset(pre[:, :, 0:3], 0.0)
            else:
                for gi, g in enumerate(ggrp):
                    with P.only(pre=[gi]):
                        P.cp(pre[:, gi, 0:3], halo_all[:, seq - 1, :, g], e="pool")
        for m in range(6):
            ps = P.ps("a1a")
            for kc in range(KC):
                P.mm(ps[:, 0:NB], W[:, m, kc, :], hT[:, kc, t0:t0 + NB],
                     start=(kc == 0), stop=(kc == KC - 1))
            if m < 4:
                with P.only(pre=[m]):
                    P.cp(pre[:, m, 3:3 + NB], ps[:, 0:NB], e="act")
            else:
                P.act(zs[:, m - 4, 0:NB], ps[:, 0:NB], AF.Tanh, scale=0.5)
                P.stt(zs[:, m - 4, 0:NB], zs[:, m - 4, 0:NB], 1.0, ps[:, 0:NB], ALU.add, ALU.mult)
            yield
        for gi, g in enumerate(ggrp):
            with P.only(acc=[gi], pre=[gi]):
                P.act(acc[:, gi, 0:NB], pre[:, gi, 3:3 + NB], AF.Copy, scale=cw[:, 3, g:g + 1])
                for tap in (2, 1, 0):
                    P.stt(acc[:, gi, 0:NB], pre[:, gi, tap:tap + NB], cw[:, tap, g:g + 1], acc[:, gi, 0:NB],
                          ALU.mult, ALU.add)
            yield
        P.act(qkv[:, :, 0:NB], acc[:, :, 0:NB], AF.Tanh, scale=0.5)
        P.stt(qkv[:, :, 0:NB], qkv[:, :, 0:NB], 1.0, acc[:, :, 0:NB], ALU.add, ALU.mult)
        last = (t0 + NB == PCOL + TP) or seq > 0
        for gi, g in enumerate(ggrp):
            with P.only(pre=[gi]):
                if last:
                    P.cp(fin_all[:, seq, :, g], pre[:, gi, NB:NB + 3], e="pool")
                else:
                    P.cp(pre[:, gi, 0:3], pre[:, gi, NB:NB + 3], e="pool")
        yield
        P.act(R(sq[:, :, 0:NB]), qkv[:, 0:2, 0:NB], AF.Square)
        for j in range(2):
            ps = P.ps("a1m")
            P.mmr(ps[:, 0:NB], ones_r[:, :], sq[:, j, 0:NB])
            with P.only(acc=[2 + j]):
                if j == 0:
                    P.act(rq[:, j, 0:NB], ps[:, 0:NB], AF.Ln, scale=128.0, bias=512.0 * 1e-6)
                else:
                    P.act(rq[:, j, 0:NB], ps[:, 0:NB], AF.Ln, bias=4e-6)
        yield
        with P.only(acc=[2, 3]):
            P.act(rq[:, :, 0:NB], rq[:, :, 0:NB], AF.Exp, scale=-0.5)
        with P.only(acc=[2]):
            P.tt(R(qT[:, 0:NB]), qkv[:, 0, 0:NB], rq[:, 0, 0:NB], ALU.mult)
        with P.only(acc=[3]):
            P.tt(R(kT[:, 0:NB]), qkv[:, 1, 0:NB], rq[:, 1, 0:NB], ALU.mult)
        yield
        cs = slice(c0, c0 + nch)
        idb = bcm(ident[0:C, 0:C], [C, nch, C])
        dgv = dg[0:C, :, 0:NB].rearrange("p a (c i) -> p a c i", c=nch)
        P.tt(dgv[:, 0], idb, bc(Gc[0:C, cs, h], [C, nch, C]), ALU.mult)
        P.tt(dgv[:, 1], idb, bc(beta[0:C, cs, h], [C, nch, C]), ALU.mult)
        ps = P.ps("a1m")
        P.mm(ps[:, 0:NB], ones[0:C, :], dg[0:C, 0, 0:NB])
        P.act(eGb[:, 0:NB], ps[:, 0:NB], AF.Exp)
        ps = P.ps("a1m")
        P.mm(ps[:, 0:NB], ones[0:C, :], dg[0:C, 1, 0:NB])
        P.cp(betab[:, 0:NB], ps[:, 0:NB], e="act")
        yield
        P.tt(R(kbT[:, 0:NB]), kT[:, 0:NB], betab[:, 0:NB], ALU.mult)
        P.tt(R(qgT[:, 0:NB]), qT[:, 0:NB], eGb[:, 0:NB], ALU.mult)
        yield
        for c4 in range(0, nch, 4):
            n4 = min(4, nch - c4)
            ps = P.ps("a1m")
            for j in range(n4):
                c = c4 + j
                P.tr(ps[0:C, j * 128:(j + 1) * 128], kT[:, c * C:(c + 1) * C], ident[:, :])
            pv = ps[0:C, 0:n4 * 128].rearrange("p (j d) -> p j d", j=n4)
            P.tt(R(kbeG[0:C, c4:c4 + n4, :]), pv, bc(bG[0:C, c0 + c4:c0 + c4 + n4, h], [C, n4, 128]), ALU.mult)
            P.tt(R(ktk[0:C, c4:c4 + n4, :]), pv, bc(dte[0:C, c0 + c4:c0 + c4 + n4, h], [C, n4, 128]), ALU.mult)
            yield
        for c2 in range(0, nch, 2):
            n2 = min(2, nch - c2)
            ps = P.ps("a1m")
            for j in range(n2):
                c = c2 + j
                for half in range(2):
                    P.tr(ps[0:C, j * 256 + half * 128:j * 256 + (half + 1) * 128],
                         qkv[:, 2 + half, c * C:(c + 1) * C], ident[:, :])
            pv = ps[0:C, 0:n2 * 256].rearrange("p (j d) -> p j d", j=n2)
            P.tt(R(vb[0:C, c2:c2 + n2, :]), pv, bc(hbeta[0:C, c0 + c2:c0 + c2 + n2, h], [C, n2, 256]), ALU.mult)
            yield
        P.tt(gU[0:C, 0:nch, 0:C], bcm(Utri[0:C, 0:C], [C, nch, C]), bc(gg[0:C, cs, h], [C, nch, C]), ALU.mult)
        ps = P.ps("a1m")
        for c in range(nch):
            o = ps[0:C, c * C:(c + 1) * C]
            P.mm(o, ones[0:C, 0:C], gU[0:C, c, 0:C], start=True, stop=False)
            P.mm(o, gU[0:C, c, 0:C], mones[0:C, 0:C], start=False, stop=False)
            P.mm(o, ident[0:C, 0:C], NEGT[0:C, 0:C], start=False, stop=True)
        pv = ps[0:C, 0:nch * C].rearrange("p (c i) -> p c i", c=nch)
        P.act(DT[0:C, 0:nch, 0:C], pv, AF.Exp)
        P.tt(DTs[0:C, 0:nch, 0:C], DT[0:C, 0:nch, 0:C], bcm(MsT[0:C, 0:C], [C, nch, C]), ALU.mult, e="pool")
        yield
        ps = P.ps("a1m")
        for c in range(nch):
            P.mmr(ps[0:C, c * C:(c + 1) * C], kT[:, c * C:(c + 1) * C], kbT[:, c * C:(c + 1) * C])
        for c in range(nch):
            P.mmr(ps[0:C, 256 + c * C:256 + (c + 1) * C], kT[:, c * C:(c + 1) * C], qT[:, c * C:(c + 1) * C])
        pv = ps[0:C, 0:nch * C].rearrange("p (c i) -> p c i", c=nch)
        pv2 = ps[0:C, 256:256 + nch * C].rearrange("p (c i) -> p c i", c=nch)
        P.stt(R(MT0a[0:C, 0:nch, 0:C]), pv, -1.0, DTs[0:C, 0:nch, 0:C], ALU.mult, ALU.mult)
        P.tt(R(qkT[0:C, 0:nch, 0:C]), pv2, DT[0:C, 0:nch, 0:C], ALU.mult)
        yield
        ps = P.ps("a1m")
        for c in range(nch):
            P.tr(ps[0:C, c * C:(c + 1) * C], MT0a[0:C, c, 0:C], ident[0:C, 0:C])
        P.cp(R(Mn[0][0:C, 0:nch, 0:C]), ps[0:C, 0:nch * C].rearrange("p (c i) -> p c i", c=nch), e="act")
        yield
        TTf = yield from neumann(P, ident, MT0a[0:C, 0:nch, 0:C], None, Mn, MTTa, nch, C, pool="a1b")
        yield "DRAIN2"
        for c2 in range(0, nch, 2):
            n2 = min(2, nch - c2)
            ps = P.ps("a1b")
            for j in range(n2):
                P.mmr(ps[0:C, j * 256:(j + 1) * 256], TTf[:, c2 + j, :], vb[0:C, c2 + j, :])
            P.cp(u0[0:C, c2:c2 + n2, :], ps[0:C, 0:n2 * 256].rearrange("p (j d) -> p j d", j=n2), e="act")
        ps = P.ps("a1b")
        for c in range(nch):
            P.mmr(ps[:, c * C:(c + 1) * C], kbeG[0:C, c, :], TTf[:, c, :])
        P.cp(R(wkT[:, 0:NB]), ps[:, 0:NB], e="act")
        yield

    spar = [0, 0, 0]

    def A_s2(it, h, blk):
        (seq, t0, NB, C, nch, c0) = blk
        par = it % 2
        WO = Wo[0]
        zs, qgT, ktk, qkT = zs2[par], qgT2[par], ktk2[par], qkT2[par]
        first_of_seq = (seq == 0 and t0 == PCOL) or seq > 0
        if seq == 0 and t0 == PCOL:
            load_head_wo(h)
        if first_of_seq:
            spar[seq] = 0
            if seq == 0:
                P.cp(R(Sst[0][0][:]), zeros[:, 0:1].broadcast_to([128, 256]), e="act")
            else:
                P.dma("sp", Sst[seq][1][:], delta_s[seq - 1, h], writes=[Sst[seq][1][:]])
                P.cp(R(Sst[seq][0][:]), Sst[seq][1][:], e="act")
        pso = P.psb[7]
        for c in range(nch):
            Sc = Sst[seq][spar[seq]]
            Sn = Sst[seq][1 - spar[seq]]
            u = uu[c % 2]
            ps = P.ps("a2")
            P.mmr(ps[0:C, 0:256], wkT[:, c * C:(c + 1) * C], Sc[:, :])
            P.tt(R(u[0:C, :]), u0[0:C, c, :], ps[0:C, 0:256], ALU.subtract)
            yield
            ps2 = P.ps("a2")
            P.mmr(ps2[:, 0:256], ktk[0:C, c, :], u[0:C, :])
            P.stt(R(Sn[:, :]), Sc[:, :], gl[:, c0 + c, h:h + 1], ps2[:, 0:256], ALU.mult, ALU.add)
            for half in range(2):
                oo = pso[:, (half * nch + c) * C:(half * nch + c + 1) * C]
                P.mmr(oo, Sc[:, half * 128:(half + 1) * 128], qgT[:, c * C:(c + 1) * C], start=True, stop=False)
                P.mmr(oo, u[0:C, half * 128:(half + 1) * 128], qkT[0:C, c, 0:C], start=False, stop=True)
            spar[seq] = 1 - spar[seq]
            yield
        P.cp(oT[:, :, 0:NB], pso[:, 0:2 * NB].rearrange("p (a t) -> p a t", a=2), e="act")
        P.act(R(osq[:, :, 0:NB]), oT[:, :, 0:NB], AF.Square)
        ps = P.ps("a2")
        P.mmr(ps[:, 0:NB], ones_r[:, :], osq[:, 0, 0:NB], start=True, stop=False)
        P.mmr(ps[:, 0:NB], ones_r[:, :], osq[:, 1, 0:NB], start=False, stop=True)
        P.act(ors[:, 0:NB], ps[:, 0:NB], AF.Ln, scale=1.0 / 256.0, bias=RMS_EPS)
        yield
        P.act(ors[:, 0:NB], ors[:, 0:NB], AF.Exp, scale=-0.5)
        for half in range(2):
            P.stt(oT[:, half, 0:NB], oT[:, half, 0:NB], anw[:, half:half + 1], ors[:, 0:NB], ALU.mult, ALU.mult)
        P.tt(og[:, :, 0:NB], oT[:, :, 0:NB], zs[:, :, 0:NB], ALU.mult)
        yield
        for tt0 in range(0, NB, 128):
            nt = min(128, NB - tt0)
            if seq == 0:
                tile_i, prow = (t0 - PCOL + tt0) // 128, 0
            else:
                tile_i, prow = 16, (seq - 1) * 32
            for nh in range(2):
                ps = P.ps("a2")
                for half in range(2):
                    P.mm(ps[prow:prow + nt, :], og[:, half, tt0:tt0 + nt], WO[:, half, nh * 512:(nh + 1) * 512],
                         start=(half == 0), stop=(half == 1))
                xr = xres[tile_i][prow:prow + nt, nh * 512:(nh + 1) * 512]
                P.tt(xr, xr, ps[prow:prow + nt, :], ALU.add)
            yield
        last = (t0 + NB == PCOL + TP) or seq > 0
        if last:
            P.dma("sp", o_delta[seq, h], Sst[seq][spar[seq]][:], reads=[Sst[seq][spar[seq]][:]], final=True)

    def pipeline(items, s1, s2, ratio):
        g2 = None
        for it, item in enumerate(list(items) + [None]):
            g1 = s1(it, *item) if item is not None else None
            while g1 is not None or g2 is not None:
                if g2 is not None:
                    if next(g2, DONE) is DONE:
                        g2 = None
                if g1 is not None:
                    for _ in range(ratio if g2 is not None else 1000000):
                        r = next(g1, DONE)
                        if r is DONE:
                            g1 = None
                            break
                        if r == "DRAIN2":
                            while g2 is not None:
                                if next(g2, DONE) is DONE:
                                    g2 = None
            g2 = s2(it, *item) if item is not None else None

    pipeline([(h, blk) for h in range(H_A) for blk in blocks], A_s1, A_s2, 3)

    ps = P.ps()
    for s in range(3):
        P.tr(ps[0:96, s * 128:(s + 1) * 128], fin_all[:, s].rearrange("p t g -> p (t g)"), ident[:, :])
    for s in range(3):
        P.cp(acc[0:96, s, 0:128], ps[0:96, s * 128:(s + 1) * 128])
        P.dma("sp", o_conv[s].rearrange("t (g c) -> (t g) c", c=128), acc[0:96, s, 0:128], reads=[acc[:]], final=True)

    if stop == "A":
        for i in range(NTILE):
            nt = tile_rows(i)
            P.dma("sp", dbg[i * 128:i * 128 + nt, :], xres[i][0:nt, :], reads=[xres[i][:]], final=True)
        P.finish()
        return nc

    P.barrier()
    P.sb_off = phase_mark
    hT = P.sb("hT2", [128, KC, NTOK], BF16)
    shout = P.sb("shout", [128, 3, KC])
    P.memset(hT[:, :, 0:1], 0.0)
    mark_b0 = P.sb_off
    xn = [P.sb("xnB%d" % i, [128, D]) for i in range(2)]

    def norm_to_hT_B():
        for i in range(NTILE):
            nt = tile_rows(i)
            xt = xres[i]
            xb = xn[i % 2]
            sl = slice(i % 4, i % 4 + 1)
            P.act(xb[0:nt, :], xt[0:nt, :], AF.Square, accum_out=ssq[0:nt, sl])
            P.act(rstd[0:nt, sl], ssq[0:nt, sl], AF.Sqrt, scale=1.0 / D, bias=RMS_EPS)
            P.recip(rstd[0:nt, sl], rstd[0:nt, sl])
            P.ts(xb[0:nt, :], xt[0:nt, :], rstd[0:nt, sl], ALU.mult)
            c0 = tile_col(i)
            for half in range(2):
                ps = P.ps(2)
                for j in range(4):
                    kc = half * 4 + j
                    P.tr(ps[:, j * 128:j * 128 + nt], xb[0:nt, kc * 128:(kc + 1) * 128], ident[0:nt, 0:nt])
                pv4 = ps[:, :].rearrange("p (j t) -> p j t", j=4)
                pv = pv4[:, :, 0:nt]
                P.tt(hT[:, half * 4:half * 4 + 4, c0:c0 + nt], pv,
                     bc(nw[:, 1, half * 4:half * 4 + 4], [128, 4, nt]), ALU.mult)
                lastcols = {15: [(0, 127)], 16: [(1, 15), (2, 47)]}.get(i, [])
                for (sq_, col) in lastcols:
                    P.tt(shout[:, sq_, half * 4:half * 4 + 4], pv4[:, :, col], nw[:, 1, half * 4:half * 4 + 4], ALU.mult)

    norm_to_hT_B()
    P.barrier()
    P.sb_off = mark_b0
    for s_ in range(3):
        P.dma("sp", o_shift[s_].rearrange("(k p) -> p k", p=128), shout[:, s_, :], reads=[shout[:]], final=True,
              allow_slow_non_contiguous=True)
    shin = P.sb("shin", [128, 2, KC])
    P.dma("sp", shin[:], shift_s.rearrange("s (k p) -> p s k", p=128), writes=[shin[:]], allow_slow_non_contiguous=True)
    P.cp(hT[:, :, S1COL - 1], shin[:, 0, :])
    P.cp(hT[:, :, S2COL - 1], shin[:, 1, :])

    vecs = P.sb("vecs", [128, 13, KC])
    P.dma("sp", vecs[:, 0:6, :], b_mu.rearrange("g (k p) -> p g k", p=128), writes=[vecs[:]], allow_slow_non_contiguous=True)
    for vi, v_ in enumerate((b_w0, b_a0, b_k_k, b_k_a, b_r_k, b_gn_w, b_gn_b)):
        P.dma("sp", vecs[:, 6 + vi, :], v_.rearrange("(k p) -> p k", p=128), writes=[vecs[:]], allow_slow_non_contiguous=True)
    V_W0, V_A0, V_KK, V_KA, V_RK, V_GW, V_GB = range(6, 13)
    hvec = P.sb("hvec", [128, 2, KC])
    P.ts(hvec[:], vecs[:, 6:8, :], 0.5, ALU.mult)
    blk1 = P.sb("blk1", [128, 128])
    cst32 = P.sb("cst32", [128, 192])
    P.asel(cst32[:, 0:64], ones[:, 0:64], [[0, 64]], ALU.is_ge, 0.0, 63, -1)
    P.asel(cst32[:, 64:128], ones[:, 0:64], [[0, 64]], ALU.is_ge, 0.0, -64, 1)
    P.cp(R(blk1[:]), cst32[:, 0:128])
    CB = 128
    MXT = P.sb("MXT", [CB, 2 * CB])
    P.cp(MXT[:, 0:CB], MsT[0:CB, 0:CB], e="pool")
    P.cp(MXT[:, CB:2 * CB], Utri[0:CB, 0:CB], e="pool")
    MsL = P.sb("MsL", [CB, CB])
    Sh = P.sb("Sh", [128, 64])
    P.asel(cst32[:, 128:192], ones[:, 0:64], [[-1, 64]], ALU.is_equal, 0.0, -64, 1)
    P.cp(R(Sh[:]), cst32[:, 128:192])
    P.asel(MsL[:], ones[0:CB, 0:CB], [[-1, CB]], ALU.is_gt, 0.0, 0, 1)
    rmask = P.sb("rmask", [128, 256])
    P.memset(rmask[:], 1.0)
    for c in range(256 // CB):
        P.memset(rmask[:, c * CB:c * CB + 1], 0.0)

    NBB = 256
    t1T = P.sb("t1T", [64, NTOK], BF16)
    a1T = P.sb("a1T", [64, NTOK], BF16)
    w2b = P.sb("w2b", [64, D], BF16)
    a2b = P.sb("a2b", [64, D], BF16)
    blk_mark = P.sb_off
    lw1 = P.sb("lw1", [128, KC, 2, 64], BF16)
    lw1p = P.sb("lw1p", [128, KC, 2, 64], BF16)
    lw1pp = P.sb("lw1pp", [128, KC, 2, 64], BF16)
    P.dma("pool", lw1[:, :, 0, :], b_w_w1.rearrange("(k p) c -> p k c", p=128), writes=[lw1[:]])
    P.dma("pool", lw1[:, :, 1, :], b_a_w1.rearrange("(k p) c -> p k c", p=128), writes=[lw1[:]])
    for j in range(2):
        P.tt(lw1p[:, :, j, :], lw1[:, :, j, :], bc(vecs[:, 4 + j, :], [128, KC, 64]), ALU.mult)
    P.tt(lw1pp[:], lw1[:], lw1p[:], ALU.subtract)
    P.dma("pool", w2b[:], b_w_w2[:, :], writes=[w2b[:]])
    P.dma("pool", a2b[:], b_a_w2[:, :], writes=[a2b[:]])
    col_ranges = [(PCOL + i * 512, 512) for i in range(4)] + [(S1COL, 16), (S2COL, 16)]
    for (cc0, n) in col_ranges:
        for j, dst in enumerate((t1T, a1T)):
            ps = P.ps(2)
            for kc in range(KC):
                P.mm(ps[0:64, 0:n], lw1pp[:, kc, j, :], hT[:, kc, cc0:cc0 + n], start=(kc == 0), stop=False)
                P.mm(ps[0:64, 0:n], lw1p[:, kc, j, :], hT[:, kc, cc0 - 1:cc0 - 1 + n], start=False, stop=(kc == KC - 1))
            P.act(dst[:, cc0:cc0 + n], ps[0:64, 0:n], AF.Tanh if j == 0 else AF.Copy)

    P.barrier()
    P.sb_off = blk_mark
    Wp = [P.sb("Wp0", [128, 4, KC, 128], BF16)] * 2
    Wq = P.sb("Wq", [128, 4, KC, 128], BF16)
    P.split(Wp[0], 4)
    P.split(Wq, 4)
    Wob = [P.sb("Wob0", [128, D], BF16)] * 2
    b_in_v = b_w_in.rearrange("(k p) (g c) -> p k g c", p=128, g=4)

    def load_pair_w(pr):
        for g_ in range(4):
            P.dma("pool", Wp[pr % 2][:, g_, :, :], b_in_v[:, :, g_, pr * 128:(pr + 1) * 128],
                  writes=[Wp[pr % 2][:, g_, :, :]])
        P.dma("pool", Wob[pr % 2][:], b_w_out[pr * 128:(pr + 1) * 128, :], writes=[Wob[pr % 2][:]])

    NCB = NBB // CB
    NX = 2 * NCB
    rkvT = P.sb("rkvT", [128, 3, NBB])
    zsB2 = [P.sb("zsB%d" % i, [128, NBB]) for i in range(2)]
    s2tmp = P.sb("s2tmp", [128, 2, NBB])
    lwT = P.sb("lwT", [128, NBB])
    aT = P.sb("aT", [128, NBB])
    cwv = P.sb("cwv", [128, NBB])
    eW = P.sb("eW", [128, 3, NBB])
    tmpB = P.sb("tmpB", [128, 4, NBB])
    kkT = P.sb("kkT", [128, NBB])
    k2T = P.sb("k2T", [128, NBB])
    arT2 = [P.sb("arT%d" % i, [128, 2, NBB]) for i in range(2)]
    bkT = P.sb("bkT", [128, 2, NBB])
    bkh = P.sb("bkh", [128, 2, NBB])
    Wc = P.sb("Wc", [128, NCB])
    rkb = P.sb("rkb", [128, NBB])
    vt = P.sb("vt", [CB, NCB, 128])
    bht = P.sb("bht", [CB, NCB, 128])
    kht = P.sb("kht", [CB, NCB, 128])
    XA = P.sb("XA", [CB, NX, 2 * CB])
    XB = P.sb("XB", [CB, NX, 2 * CB])
    MnB = [P.sb("MnB%d" % i, [CB, NX, CB]) for i in range(2)]
    MTTb = [P.sb("MTTb%d" % i, [CB, NX, 2, CB]) for i in range(2)]
    for m_ in MTTb:
        P.split(m_, 2)
    Rsb = [P.sb("Rsb%d" % i, [CB, 128]) for i in range(2)]
    Usb = [P.sb("Usb%d" % i, [CB, 128]) for i in range(2)]
    StP = [[P.sb("StP%d_%d" % (i, hd), [64, 64]) for hd in range(2)] for i in range(2)]
    StS = [[P.sb("StS%d_%d" % (i, hd), [64, 64]) for hd in range(2)] for i in range(2)]
    StB = [StP, StS, StS]
    stio = P.sb("stio", [64, 128])
    ar12 = [P.sb("ar1_%d" % i, [64, 2, NBB]) for i in range(2)]
    bk1 = P.sb("bk1", [64, 2, NBB])
    Wc1 = P.sb("Wc1", [64, NCB])
    sqB = P.sb("sqB", [128, 4, NBB])
    oTB = sqB[:, 2]
    ocB = s2tmp[:, 0]
    osB = sqB[:, 3]
    ogB = P.sb("ogB", [128, NBB], BF16)
    print("SBUF left after layer-B alloc:", P.sb_top - P.sb_off)
    for t_, n_ in [(rkvT, 3), (eW, 3), (tmpB, 4), (sqB, 4), (bkT, 2), (bkh, 2), (XA, NX), (XB, NX), (bk1, 2),
                   (vt, NCB), (bht, NCB), (kht, NCB), (s2tmp, 2)] + [(x_, 2) for x_ in arT2 + ar12] + \
            [(x_, NX) for x_ in MnB]:
        P.split(t_, n_)
    blocksB = [(0, PCOL + i * NBB, NBB, CB, NBB // CB) for i in range(TP // NBB)] + \
              [(1, S1COL, 16, 16, 1), (2, S2COL, 16, 16, 1)]
    ENH = -float(np.exp(-0.5))

    load_pair_w(0)
    nblkB = 0
    for pr in range(8):
        W = Wp[pr % 2]
        WO = Wob[pr % 2]
        for g_ in range(4):
            P.tt(Wq[:, g_, :, :], W[:, g_, :, :], bc(vecs[:, g_, :], [128, KC, 128]), ALU.mult)
            P.tt(W[:, g_, :, :], W[:, g_, :, :], Wq[:, g_, :, :], ALU.subtract)
        Wr = W
        spar = [0, 0, 0]
        cur_seq = -1
        for (seq, t0, NB, C, nch) in blocksB:
            nx = 2 * nch
            bpar = nblkB % 2
            nblkB += 1
            zsB, arT, ar1 = zsB2[bpar], arT2[bpar], ar12[bpar]
            if seq != cur_seq:
                cur_seq = seq
                spar[seq] = 0
                if seq == 0:
                    for hd in range(2):
                        P.cp(R(StB[0][0][hd][:]), zeros[0:64, 0:64], e="act")
                else:
                    P.dma("sp", stio[:].rearrange("v (h k) -> v h k", h=2),
                          wkv_s[seq - 1, 2 * pr:2 * pr + 2].rearrange("h v k -> v h k"), writes=[stio[:]])
                    for hd in range(2):
                        ps = P.ps("b1")
                        P.tr(ps[0:64, 0:64], stio[:, hd * 64:(hd + 1) * 64], ident[0:64, 0:64])
                        P.cp(R(StB[seq][0][hd][:]), ps[0:64, 0:64])
            for g_ in range(4):
                ps = P.ps("b1")
                for kc in range(KC):
                    P.mm(ps[:, 0:NB], Wr[:, g_, kc, :], hT[:, kc, t0:t0 + NB], start=(kc == 0), stop=False)
                    P.mm(ps[:, 0:NB], Wq[:, g_, kc, :], hT[:, kc, t0 - 1:t0 - 1 + NB], start=False, stop=(kc == KC - 1))
                if g_ < 3:
                    P.cp(rkvT[:, g_, 0:NB], ps[:, 0:NB], e="act")
                else:
                    P.act(zsB[:, 0:NB], ps[:, 0:NB], AF.Tanh, scale=0.5)
                    P.stt(zsB[:, 0:NB], zsB[:, 0:NB], 1.0, ps[:, 0:NB], ALU.add, ALU.mult)
            ps = P.ps("b1")
            P.mm(ps[:, 0:NB], w2b[:, pr * 128:(pr + 1) * 128], t1T[:, t0:t0 + NB])
            P.act(lwT[:, 0:NB], ps[:, 0:NB], AF.Tanh, scale=0.5, bias=hvec[:, 0, pr:pr + 1])
            P.ts(lwT[:, 0:NB], lwT[:, 0:NB], 0.5 * ENH, ALU.mult, 0.5 * ENH, ALU.add)
            ps = P.ps("b1")
            P.mm(ps[:, 0:NB], a2b[:, pr * 128:(pr + 1) * 128], a1T[:, t0:t0 + NB])
            P.act(aT[:, 0:NB], ps[:, 0:NB], AF.Tanh, scale=0.5, bias=hvec[:, 1, pr:pr + 1])
            P.ts(aT[:, 0:NB], aT[:, 0:NB], 0.5, ALU.mult, 0.5, ALU.add)
            rT = rkvT[:, 0, 0:NB]
            kT_ = rkvT[:, 1, 0:NB]
            vT_ = rkvT[:, 2, 0:NB]
            P.ts(kkT[:, 0:NB], kT_, vecs[:, V_KK, pr:pr + 1], ALU.mult)
            P.act(R(sqB[:, 0, 0:NB]), kT_, AF.Square, scale=vecs[:, V_KK, pr:pr + 1])
            ps = P.ps("b1")
            P.mmr(ps[:, 0:NB], blk1[:, :], sqB[:, 0, 0:NB])
            P.act(tmpB[:, 1, 0:NB], ps[:, 0:NB], AF.Ln, bias=1e-6)
            P.act(tmpB[:, 1, 0:NB], tmpB[:, 1, 0:NB], AF.Exp, scale=-0.5)
            P.tt(kkT[:, 0:NB], kkT[:, 0:NB], tmpB[:, 1, 0:NB], ALU.mult)
            P.ts(tmpB[:, 2, 0:NB], aT[:, 0:NB], -1.0, ALU.add, vecs[:, V_KA, pr:pr + 1], ALU.mult)
            P.ts(tmpB[:, 2, 0:NB], tmpB[:, 2, 0:NB], 1.0, ALU.add)
            P.tt(k2T[:, 0:NB], kT_, tmpB[:, 2, 0:NB], ALU.mult)
            P.scan(cwv[:, 0:NB], rmask[:, 0:NB], lwT[:, 0:NB])
            P.act(eW[:, 0, 0:NB], cwv[:, 0:NB], AF.Exp)
            P.act(eW[:, 1, 0:NB], cwv[:, 0:NB], AF.Exp, scale=-1.0)
            P.tt(tmpB[:, 3, 0:NB], cwv[:, 0:NB], lwT[:, 0:NB], ALU.subtract)
            P.act(eW[:, 2, 0:NB], tmpB[:, 3, 0:NB], AF.Exp)
            P.stt(R(arT[:, 0, 0:NB]), kkT[:, 0:NB], -1.0, eW[:, 2, 0:NB], ALU.mult, ALU.mult)
            P.tt(R(arT[:, 1, 0:NB]), rT, eW[:, 0, 0:NB], ALU.mult)
            P.tt(tmpB[:, 0, 0:NB], kkT[:, 0:NB], aT[:, 0:NB], ALU.mult)
            P.tt(R(bkT[:, 0, 0:NB]), tmpB[:, 0, 0:NB], eW[:, 1, 0:NB], ALU.mult)
            P.tt(R(bkT[:, 1, 0:NB]), k2T[:, 0:NB], eW[:, 1, 0:NB], ALU.mult)
            ewc = eW[:, 0, 0:NB].rearrange("p (c i) -> p c i", c=nch)[:, :, C - 1]
            P.cp(Wc[:, 0:nch], ewc)
            bkv = bkT[:, :, 0:NB].rearrange("p a (c i) -> p a c i", c=nch)
            bhv = bkh[:, :, 0:NB].rearrange("p a (c i) -> p a c i", c=nch)
            for a_ in range(2):
                P.tt(bhv[:, a_], bkv[:, a_], bc(Wc[:, 0:nch], [128, nch, C]), ALU.mult)
            P.stt(R(sqB[:, 1, 0:NB]), rT, vecs[:, V_RK, pr:pr + 1], k2T[:, 0:NB], ALU.mult, ALU.mult)
            ps = P.ps("b1")
            P.mmr(ps[:, 0:NB], blk1[:, :], sqB[:, 1, 0:NB])
            P.tt(rkb[:, 0:NB], ps[:, 0:NB], vT_, ALU.mult)
            ps = P.ps("b1")
            P.mmr(ps[0:64, 0:2 * NB], Sh[:, :], arT[:, :, 0:NB])
            P.cp(R(ar1[:, :, 0:NB]), ps[0:64, 0:2 * NB].rearrange("p (a t) -> p a t", a=2), e="act")
            ps = P.ps("b1")
            P.mmr(ps[0:64, 0:2 * NB], Sh[:, :], bkT[:, :, 0:NB])
            P.cp(R(bk1[:, :, 0:NB]), ps[0:64, 0:2 * NB].rearrange("p (a t) -> p a t", a=2), e="act")
            ps = P.ps("b1")
            P.mm(ps[0:64, 0:nch], cst32[:, 128:192], Wc[:, 0:nch])
            P.cp(Wc1[:, 0:nch], ps[0:64, 0:nch])
            AR = [arT[0:64], ar1[:]]
            BK = [bkT[0:64], bk1[:]]
            WC = [Wc[0:64], Wc1[:]]
            for (src, dst) in ((vT_, vt), (bkh[:, 0, 0:NB], bht), (bkh[:, 1, 0:NB], kht)):
                ps = P.ps("b1")
                for c in range(nch):
                    P.tr(ps[0:C, c * 128:(c + 1) * 128], src[:, c * C:(c + 1) * C], ident[:, :])
                P.cp(R(dst[0:C, 0:nch, :]), ps[0:C, 0:nch * 128].rearrange("p (c d) -> p c d", c=nch), e="act")
            psN = P.ps("b1")
            for hd in range(2):
                hs = slice(hd * 64, (hd + 1) * 64)
                psA = P.ps("b1")
                psB_ = P.ps("b1")
                for c in range(nch):
                    csl = slice(c * C, (c + 1) * C)
                    x_ = hd * nch + c
                    P.mmr(psA[0:C, c * 2 * C:(c + 1) * 2 * C], BK[hd][:, 0, csl], AR[hd][:, :, csl])
                    P.mmr(psB_[0:C, c * 2 * C:(c + 1) * 2 * C], BK[hd][:, 1, csl], AR[hd][:, :, csl])
                    P.mmr(psN[0:C, x_ * C:(x_ + 1) * C], AR[hd][:, 0, csl], BK[hd][:, 0, csl])
                for (psx, dstx) in ((psA, XA), (psB_, XB)):
                    pv = psx[0:C, 0:nch * 2 * C].rearrange("p (c a i) -> p c a i", c=nch, a=2)
                    dv = dstx[0:C, hd * nch:(hd + 1) * nch, :].rearrange("p c (a i) -> p c a i", a=2)[:, :, :, 0:C]
                    mv = MXT[0:C, :].rearrange("p (a i) -> p a i", a=2)[:, :, 0:C].unsqueeze(1).broadcast_to([C, nch, 2, C])
                    P.tt(R(dv), pv, mv, ALU.mult)
            P.tt(R(MnB[0][0:C, 0:nx, 0:C]), psN[0:C, 0:nx * C].rearrange("p (x i) -> p x i", x=nx),
                 bcm(MsL[0:C, 0:C], [C, nx, C]), ALU.mult)
            gen_ = neumann(P, ident, XA[0:C, 0:nx, 0:C], None, MnB, MTTb, nx, C, pool="b1")
            while True:
                try:
                    next(gen_)
                except StopIteration as e_:
                    TTf = e_.value
                    break
            pso = P.psb[7]
            for c in range(nch):
                csl = slice(c * C, (c + 1) * C)
                Sc = StB[seq][spar[seq]]
                Sn = StB[seq][1 - spar[seq]]
                Rb = Rsb[c % 2]
                Ub = Usb[c % 2]
                ps = P.ps("b2")
                for hd in range(2):
                    hs = slice(hd * 64, (hd + 1) * 64)
                    x_ = hd * nch + c
                    P.mmr(ps[0:C, hs], AR[hd][:, 0, csl], Sc[hd][:, :], start=True, stop=False)
                    P.mmr(ps[0:C, hs], XB[0:C, x_, 0:C], vt[0:C, c, hs], start=False, stop=True)
                P.cp(R(Rb[0:C, :]), ps[0:C, 0:128], e="act")
                ps = P.ps("b2")
                for hd in range(2):
                    hs = slice(hd * 64, (hd + 1) * 64)
                    x_ = hd * nch + c
                    P.mmr(ps[0:C, hs], TTf[:, x_, :], Rb[0:C, hs])
                P.cp(R(Ub[0:C, :]), ps[0:C, 0:128], e="act")
                for hd in range(2):
                    hs = slice(hd * 64, (hd + 1) * 64)
                    x_ = hd * nch + c
                    oo = pso[hs, csl]
                    mmf = P.mmr if hd == 0 else P.mm
                    mmf(oo, Sc[hd][:, :], AR[hd][:, 1, csl], start=True, stop=False)
                    mmf(oo, Ub[0:C, hs], XA[0:C, x_, CB:CB + C], start=False, stop=False)
                    mmf(oo, vt[0:C, c, hs], XB[0:C, x_, CB:CB + C], start=False, stop=True)
                ps2 = P.ps("b2")
                for hd in range(2):
                    hs = slice(hd * 64, (hd + 1) * 64)
                    P.mmr(ps2[0:64, hs], bht[0:C, c, hs], Ub[0:C, hs], start=True, stop=False)
                    P.mmr(ps2[0:64, hs], kht[0:C, c, hs], vt[0:C, c, hs], start=False, stop=True)
                for hd in range(2):
                    hs = slice(hd * 64, (hd + 1) * 64)
                    P.stt(R(Sn[hd][:, :]), Sc[hd][:, :], WC[hd][:, c:c + 1], ps2[0:64, hs], ALU.mult, ALU.add)
                spar[seq] = 1 - spar[seq]
            P.cp(R(oTB[:, 0:NB]), pso[:, 0:NB], e="act")
            ps = P.ps("b2")
            P.mmr(ps[:, 0:NB], blk1[:, :], oTB[:, 0:NB])
            P.stt(ocB[:, 0:NB], ps[:, 0:NB], -1.0 / 64.0, oTB[:, 0:NB], ALU.mult, ALU.add)
            P.act(R(osB[:, 0:NB]), ocB[:, 0:NB], AF.Square)
            ps = P.ps("b2")
            P.mmr(ps[:, 0:NB], blk1[:, :], osB[:, 0:NB])
            P.act(s2tmp[:, 1, 0:NB], ps[:, 0:NB], AF.Ln, scale=1.0 / 64.0, bias=GN_EPS)
            P.act(s2tmp[:, 1, 0:NB], s2tmp[:, 1, 0:NB], AF.Exp, scale=-0.5)
            P.tt(ocB[:, 0:NB], ocB[:, 0:NB], s2tmp[:, 1, 0:NB], ALU.mult)
            P.ts(ocB[:, 0:NB], ocB[:, 0:NB], vecs[:, V_GW, pr:pr + 1], ALU.mult, vecs[:, V_GB, pr:pr + 1], ALU.add)
            P.tt(ocB[:, 0:NB], ocB[:, 0:NB], rkb[:, 0:NB], ALU.add)
            P.stt(ogB[:, 0:NB], ocB[:, 0:NB], 0.5, zsB[:, 0:NB], ALU.mult, ALU.mult)
            for tt0 in range(0, NB, 128):
                nt = min(128, NB - tt0)
                if seq == 0:
                    tile_i, prow = (t0 - PCOL + tt0) // 128, 0
                else:
                    tile_i, prow = 16, (seq - 1) * 32
                for nh in range(2):
                    ps = P.ps("b2")
                    P.mm(ps[prow:prow + nt, :], ogB[:, tt0:tt0 + nt], WO[:, nh * 512:(nh + 1) * 512])
                    xr = xres[tile_i][prow:prow + nt, nh * 512:(nh + 1) * 512]
                    P.tt(xr, xr, ps[prow:prow + nt, :], ALU.add)
            last = (t0 + NB == PCOL + TP) or seq > 0
            if last:
                Sf = StB[seq][spar[seq]]
                ps = P.ps("b2")
                for hd in range(2):
                    P.tr(ps[0:64, hd * 64:(hd + 1) * 64], Sf[hd][:, :], ident[0:64, 0:64])
                P.cp(stio[:, :], ps[0:64, 0:128])
                P.dma("sp", o_wkv[seq, 2 * pr:2 * pr + 2].rearrange("h v k -> v h k"),
                      stio[:].rearrange("v (h k) -> v h k", h=2), reads=[stio[:]], final=True)
        if pr + 1 < 8:
            load_pair_w(pr + 1)

    P.barrier()
    P.sb_off = blk_mark
    xn = [P.sb("xnF%d" % i, [128, D]) for i in range(3)]
    fnwb = P.sb("fnwb", [128, D])
    P.dma("sp", fnwb[:], final_norm_w.partition_broadcast(128), writes=[fnwb[:]])
    for i in range(NTILE):
        nt = tile_rows(i)
        xt = xres[i]
        xb = xn[i % 3]
        sl = slice(i % 4, i % 4 + 1)
        P.act(xb[0:nt, :], xt[0:nt, :], AF.Square, accum_out=ssq[0:nt, sl])
        P.act(rstd[0:nt, sl], ssq[0:nt, sl], AF.Sqrt, scale=1.0 / D, bias=RMS_EPS)
        P.recip(rstd[0:nt, sl], rstd[0:nt, sl])
        P.stt(xb[0:nt, :], xt[0:nt, :], rstd[0:nt, sl], fnwb[0:nt, :], ALU.mult, ALU.mult)
        if i < 16:
            P.dma("sp", y_p[i * 128:(i + 1) * 128, :], xb[:, :], reads=[xb[:]], final=True)
        else:
            P.dma("sp", y_s[0:16, :], xb[0:16, :], reads=[xb[:]], final=True)
            P.dma("sp", y_s[16:32, :], xb[32:48, :], reads=[xb[:]], final=True)
    P.finish()
    return nc


_NC_CACHE = {}


def make_in_maps(inputs):
    g = lambda k: np.ascontiguousarray(np.asarray(inputs[k], dtype=np.float32))
    xp, xs = g("x_prompt"), g("x_sample")
    cc, sd, ss, sw = g("cache_conv_a"), g("state_delta_a"), g("state_shift_b"), g("state_wkv_b")
    shared = {
        "norm_w": g("norm_w"), "final_norm_w": g("final_norm_w"), "a_w_in": g("a_w_in")[0],
        "a_conv_w": g("a_conv_w")[0], "a_log": g("a_log")[0], "a_dt_bias": g("a_dt_bias")[0],
        "a_norm_w": g("a_norm_w")[0], "a_w_out": g("a_w_out")[0], "b_mu": g("b_mu")[0],
        "b_w_in": g("b_w_in")[0], "b_w0": g("b_w0")[0], "b_w_w1": g("b_w_w1")[0], "b_w_w2": g("b_w_w2")[0],
        "b_a0": g("b_a0")[0], "b_a_w1": g("b_a_w1")[0], "b_a_w2": g("b_a_w2")[0], "b_k_k": g("b_k_k")[0],
        "b_k_a": g("b_k_a")[0], "b_r_k": g("b_r_k")[0].reshape(-1), "b_gn_w": g("b_gn_w")[0],
        "b_gn_b": g("b_gn_b")[0], "b_w_out": g("b_w_out")[0],
    }
    maps = []
    for i in range(8):
        m = dict(shared)
        m["x_p"] = xp[i]
        m["x_s"] = np.ascontiguousarray(xs[2 * i:2 * i + 2].reshape(2 * TS, D))
        m["conv_s"] = np.ascontiguousarray(cc[0, 2 * i:2 * i + 2])
        m["delta_s"] = np.ascontiguousarray(sd[0, 2 * i:2 * i + 2])
        m["shift_s"] = np.ascontiguousarray(ss[0, 2 * i:2 * i + 2])
        m["wkv_s"] = np.ascontiguousarray(sw[0, 2 * i:2 * i + 2])
        maps.append(m)
    return maps


def kernel(**inputs):
    if "nc" not in _NC_CACHE:
        _NC_CACHE["nc"] = build()
    nc = _NC_CACHE["nc"]
    maps = make_in_maps(inputs)
    res = run_bass_kernel_spmd(nc, maps, core_ids=list(range(8)))
    R = res.results
    y_prompt = np.stack([R[i]["y_p"] for i in range(8)], 0)
    y_sample = np.concatenate([R[i]["y_s"].reshape(2, TS, D) for i in range(8)], 0)

    def pick(name, sl):
        return np.stack([R[i][name][sl] for i in range(8)], 0)[None] if isinstance(sl, int) else \
            np.concatenate([R[i][name][sl] for i in range(8)], 0)[None]

    p_conv, s_conv = pick("o_conv", 0), pick("o_conv", slice(1, 3))
    p_delta, s_delta = pick("o_delta", 0), pick("o_delta", slice(1, 3))
    p_shift, s_shift = pick("o_shift", 0), pick("o_shift", slice(1, 3))
    p_wkv, s_wkv = pick("o_wkv", 0), pick("o_wkv", slice(1, 3))
    return (y_prompt, y_sample, p_conv, p_delta, p_shift, p_wkv, s_conv, s_delta, s_shift, s_wkv)
```
